# Optimizing a Trainium2 kernel written in Bass

```python
import jax
import jax.numpy as jnp
from jax import lax
import numpy as np


D_MODEL = 1024
BATCH = 8
SEQ = 8192
DEPTH = 1

GRID_W = 64
CTX_LEN = 256
D_MIX = D_MODEL
GLA_WIDTH = D_MIX // 2
GLA_HEADS = 4
GLA_DV = GLA_WIDTH // GLA_HEADS
GLA_DK = GLA_DV // 2
GLA_QK = GLA_HEADS * GLA_DK
GLA_LOWRANK = 16
GLA_TAU = 16.0
GLA_CHUNK = 64
SC_WIDTH = D_MIX - GLA_WIDTH
D_FF = 2816
EPS = 1e-6

COL_K = 0
COL_V = COL_K + GLA_QK
COL_AF = COL_V + GLA_WIDTH
COL_AB = COL_AF + GLA_LOWRANK
COL_Q = COL_AB + GLA_LOWRANK
COL_OG = COL_Q + GLA_QK
COL_SB = COL_OG + GLA_WIDTH
COL_SC = COL_SB + SC_WIDTH
COL_SX = COL_SC + SC_WIDTH
D_IN = COL_SX + SC_WIDTH

kernel_name = "hybrid_gla_shortconv_dit_layer"


def rmsnorm(x, g):
    xf = x.astype(jnp.float32)
    y = xf * lax.rsqrt(jnp.mean(xf * xf, axis=-1, keepdims=True) + EPS)
    return (y * g.astype(jnp.float32)).astype(x.dtype)


def adaln(cond, w, b):
    return jax.nn.silu(cond) @ w + b


def modulate(h, shift, scale):
    return h * (1.0 + scale) + shift


def flip(a):
    return a[:, ::-1]


def gate_logdecay(a_low, w, b):
    a = (a_low @ w + b).astype(jnp.float32)
    return (jax.nn.log_sigmoid(a) / GLA_TAU).reshape(a.shape[0], a.shape[1], GLA_HEADS, GLA_DK)


def gla_inputs(p, w_af, b_af, w_ab, b_ab):
    bsz, t, _ = p.shape
    k = p[..., COL_K:COL_V].reshape(bsz, t, GLA_HEADS, GLA_DK)
    v = p[..., COL_V:COL_AF].reshape(bsz, t, GLA_HEADS, GLA_DV)
    g_f = gate_logdecay(p[..., COL_AF:COL_AB], w_af, b_af)
    g_b = gate_logdecay(p[..., COL_AB:COL_Q], w_ab, b_ab)
    return k, v, g_f, g_b


def gla_final_state(k, v, g):
    b = jnp.cumsum(g.astype(jnp.float32), axis=1)
    w = jnp.exp(b[:, -1:] - b)
    return jnp.einsum('bthk,bthv->bhkv', k.astype(jnp.float32) * w, v.astype(jnp.float32))


def gla_chunked(q, k, v, g, s0, strict):
    bsz, t, h, dk = q.shape
    dv = v.shape[-1]
    n = t // GLA_CHUNK

    def chunks(a):
        a = a.astype(jnp.float32).reshape(bsz, n, GLA_CHUNK, h, a.shape[-1])
        return jnp.moveaxis(a, 1, 0)

    pos = jnp.arange(GLA_CHUNK)
    mask = (pos[None, :] < pos[:, None]) if strict else (pos[None, :] <= pos[:, None])
    mask = mask[None, :, :, None, None]

    def step(s, inp):
        qc, kc, vc, gc = inp
        b = jnp.cumsum(gc, axis=1)
        o_inter = jnp.einsum('bihk,bhkv->bihv', qc * jnp.exp(b), s)
        decay = jnp.exp(jnp.where(mask, b[:, :, None] - b[:, None, :], -jnp.inf))
        att = jnp.einsum('bihk,bjhk,bijhk->bijh', qc, kc, decay)
        o_intra = jnp.einsum('bijh,bjhv->bihv', att, vc)
        b_last = b[:, -1]
        s_new = s * jnp.exp(b_last)[..., None] + jnp.einsum('bjhk,bjhv->bhkv', kc * jnp.exp(b_last[:, None] - b), vc)
        return s_new, o_inter + o_intra

    _, o = lax.scan(step, s0.astype(jnp.float32), (chunks(q), chunks(k), chunks(v), chunks(g)))
    return jnp.moveaxis(o, 0, 1).reshape(bsz, t, h, dv).astype(v.dtype)


def dwconv_row(u, w, b, rows, width):
    bsz, t, ch = u.shape
    up = jnp.pad(u.reshape(bsz, rows, width, ch), ((0, 0), (0, 0), (1, 1), (0, 0)))
    y = w[0] * up[:, :, :-2] + w[1] * up[:, :, 1:-1] + w[2] * up[:, :, 2:] + b
    return y.reshape(bsz, t, ch)


def dwconv_grid(u, w, b, rows, width):
    bsz, t, ch = u.shape
    up = jnp.pad(u.reshape(bsz, rows, width, ch), ((0, 0), (1, 1), (1, 1), (0, 0)))
    y = b
    for dr in range(3):
        for dc in range(3):
            y = y + w[dr, dc] * up[:, dr:dr + rows, dc:dc + width]
    return y.reshape(bsz, t, ch)


def token_mixers(p, s_f, s_b, rows, width, w_af, b_af, w_ab, b_ab, g_head, w_sc, b_sc):
    bsz, t, _ = p.shape
    k, v, g_f, g_b = gla_inputs(p, w_af, b_af, w_ab, b_ab)
    q = p[..., COL_Q:COL_OG].reshape(bsz, t, GLA_HEADS, GLA_DK) * (GLA_DK ** -0.5)
    og = p[..., COL_OG:COL_SB]
    o_f = gla_chunked(q, k, v, g_f, s_f, False)
    o_b = flip(gla_chunked(flip(q), flip(k), flip(v), flip(g_b), s_b, True))
    o_gla = rmsnorm(o_f + o_b, g_head).reshape(bsz, t, GLA_WIDTH) * jax.nn.silu(og)
    sb = p[..., COL_SB:COL_SC]
    sc = p[..., COL_SC:COL_SX]
    sx = p[..., COL_SX:D_IN]
    o_sc = sb * dwconv_row(sc * sx, w_sc, b_sc, rows, width)
    return jnp.concatenate([o_gla, o_sc], axis=-1)


def conv_ffn(h, rows, width, w_up, w_cf, b_cf, w_down):
    u, gate = jnp.split(h @ w_up, 2, axis=-1)
    u = dwconv_grid(u, w_cf, b_cf, rows, width)
    return (jax.nn.silu(u) * gate) @ w_down


def setup_inputs(seed: int = 0) -> dict:
    key = jax.random.key(seed)
    ks = jax.random.split(key, 24)

    def nrm(k, shape, s):
        return jax.random.normal(k, shape, jnp.float32) * s

    L = DEPTH
    return {
        'x': nrm(ks[0], (BATCH, SEQ, D_MODEL), 1.0),
        'c': nrm(ks[1], (BATCH, D_MODEL), 1.0),
        'ctx': nrm(ks[2], (BATCH, CTX_LEN, D_MODEL), 1.0),
        'c_ctx': nrm(ks[3], (D_MODEL,), 1.0),
        'w_ada': nrm(ks[4], (L, D_MODEL, 6 * D_MODEL), D_MODEL ** -0.5),
        'b_ada': nrm(ks[5], (L, 6 * D_MODEL), 0.01),
        'g_pre_mix': 1.0 + nrm(ks[6], (L, D_MODEL), 0.05),
        'g_post_mix': 1.0 + nrm(ks[7], (L, D_MODEL), 0.05),
        'g_pre_ffn': 1.0 + nrm(ks[8], (L, D_MODEL), 0.05),
        'g_post_ffn': 1.0 + nrm(ks[9], (L, D_MODEL), 0.05),
        'w_in': nrm(ks[10], (L, D_MODEL, D_IN), D_MODEL ** -0.5),
        'w_af': nrm(ks[11], (L, GLA_LOWRANK, GLA_QK), GLA_LOWRANK ** -0.5),
        'b_af': nrm(ks[12], (L, GLA_QK), 0.1),
        'w_ab': nrm(ks[13], (L, GLA_LOWRANK, GLA_QK), GLA_LOWRANK ** -0.5),
        'b_ab': nrm(ks[14], (L, GLA_QK), 0.1),
        'g_head': 1.0 + nrm(ks[15], (L, GLA_DV), 0.05),
        'w_sc': nrm(ks[16], (L, 3, SC_WIDTH), 0.5),
        'b_sc': nrm(ks[17], (L, SC_WIDTH), 0.01),
        'w_out': nrm(ks[18], (L, D_MIX, D_MODEL), D_MIX ** -0.5),
        'w_up': nrm(ks[19], (L, D_MODEL, 2 * D_FF), D_MODEL ** -0.5),
        'w_cf': nrm(ks[20], (L, 3, 3, D_FF), 1.0 / 3.0),
        'b_cf': nrm(ks[21], (L, D_FF), 0.01),
        'w_down': nrm(ks[22], (L, D_FF, D_MODEL), D_FF ** -0.5),
    }


def reference(x, c, ctx, c_ctx, w_ada, b_ada, g_pre_mix, g_post_mix, g_pre_ffn, g_post_ffn,
              w_in, w_af, b_af, w_ab, b_ab, g_head, w_sc, b_sc, w_out, w_up, w_cf, b_cf, w_down):
    rows = x.shape[1] // GRID_W
    ctx_len = ctx.shape[1]
    for i in range(DEPTH):
        update_ctx = i + 1 < DEPTH
        sh1, sc1, gt1, sh2, sc2, gt2 = jnp.split(adaln(c, w_ada[i], b_ada[i])[:, None, :], 6, axis=-1)
        csh1, csc1, cgt1, csh2, csc2, cgt2 = jnp.split(adaln(c_ctx, w_ada[i], b_ada[i]), 6, axis=-1)
        gla_p = (w_af[i], b_af[i], w_ab[i], b_ab[i])

        hc = modulate(rmsnorm(ctx, g_pre_mix[i]), csh1, csc1)
        pc = hc @ (w_in[i] if update_ctx else w_in[i][:, :COL_Q])
        kc, vc, gfc, gbc = gla_inputs(pc, *gla_p)
        s_f = gla_final_state(kc, vc, gfc)
        s_b = gla_final_state(flip(kc), flip(vc), flip(gbc))

        hx = modulate(rmsnorm(x, g_pre_mix[i]), sh1, sc1)
        yx = token_mixers(hx @ w_in[i], s_f, s_b, rows, GRID_W, *gla_p, g_head[i], w_sc[i], b_sc[i])
        x = x + gt1 * rmsnorm(yx @ w_out[i], g_post_mix[i])

        hx = modulate(rmsnorm(x, g_pre_ffn[i]), sh2, sc2)
        x = x + gt2 * rmsnorm(conv_ffn(hx, rows, GRID_W, w_up[i], w_cf[i], b_cf[i], w_down[i]), g_post_ffn[i])

        if update_ctx:
            zero_state = jnp.zeros_like(s_f)
            yc = token_mixers(pc, zero_state, zero_state, 1, ctx_len, *gla_p, g_head[i], w_sc[i], b_sc[i])
            ctx = ctx + cgt1 * rmsnorm(yc @ w_out[i], g_post_mix[i])
            hc = modulate(rmsnorm(ctx, g_pre_ffn[i]), csh2, csc2)
            ctx = ctx + cgt2 * rmsnorm(conv_ffn(hc, 1, ctx_len, w_up[i], w_cf[i], b_cf[i], w_down[i]), g_post_ffn[i])
    return x
```

```python
import os
import numpy as np
import concourse.bass as bass
import concourse.mybir as mybir
from concourse.bass_utils import run_bass_kernel_spmd

F32 = mybir.dt.float32
BF16 = mybir.dt.bfloat16
AF = mybir.ActivationFunctionType
ALU = mybir.AluOpType

D = 1024
DFF = 2816
NFC = 22
CTX = 256
EPS = 1e-6
SB_LO = 16512
SB_HI = 229344
EPOCH = 24000

C_K, C_V, C_AF, C_AB, C_Q, C_OG, C_SB, C_SC, C_SX = 0, 256, 768, 784, 800, 1056, 1568, 2080, 2592

K_ID, K_LFI, K_LFR, K_LBI, K_LBR, K_MF, K_MB, K_NC = 0, 128, 256, 384, 512, 640, 768, 896
NCST = 897
R_GPM, R_GQM, R_GPF, R_GQF, R_GH, R_BADA = 0, 1024, 2048, 3072, 4096, 4608
NROW = 4608 + 6144
Q_C, Q_CC, Q_BSC, Q_BCF, Q_WSC, Q_WCF = 0, 8, 16, 20, 42, 54
NCOL = 54 + 198


class Buf:
    __slots__ = ("name", "w", "r")

    def __init__(self, name):
        self.name = name
        self.w = None
        self.r = {}


class Prog:
    ENGS = ("pe", "act", "dve", "pool", "sp")

    def __init__(self):
        self.ops = {e: [] for e in self.ENGS}
        self.waited = {e: {} for e in self.ENGS}
        self.dma_cum = {}
        self.n = 0

    def _need(self, eng, tok, waits):
        key, val = tok
        if self.waited[eng].get(key, -1) >= val:
            return
        self.waited[eng][key] = val
        waits.append(tok)
        if not key.startswith("dma:"):
            self.ops[key][val]["sig"] = True

    def add(self, eng, fn, reads=(), writes=(), dma=None, ndma=1):
        idx = len(self.ops[eng])
        waits = []
        mykey = ("dma:" + dma) if dma is not None else eng
        for b in reads:
            if b.w is not None:
                if b.w[0] == eng and eng == "pe" and dma is None:
                    continue
                self._need(eng, b.w, waits)
        for b in writes:
            if b.w is not None and not (b.w[0] == mykey and (eng == "pe" or dma is not None)):
                self._need(eng, b.w, waits)
            for k, v in b.r.items():
                if k == mykey and (eng == "pe" or dma is not None):
                    continue
                self._need(eng, (k, v), waits)
        if dma is not None:
            cum = self.dma_cum.get(dma, 0) + 16 * ndma
            self.dma_cum[dma] = cum
            tok = ("dma:" + dma, cum)
        else:
            tok = (eng, idx)
        for b in reads:
            if b.r.get(tok[0], -1) < tok[1]:
                b.r[tok[0]] = tok[1]
        for b in writes:
            b.w = tok
            b.r = {}
        self.ops[eng].append({"fn": fn, "waits": waits, "sig": False, "dma": dma})
        self.n += 1
        return tok

    def barrier(self):
        toks = []
        for e in self.ENGS:
            if self.ops[e]:
                for i in range(len(self.ops[e]) - 1, -1, -1):
                    if self.ops[e][i]["dma"] is None and self.ops[e][i]["fn"] is not None:
                        toks.append((e, i))
                        break
        for k, v in self.dma_cum.items():
            toks.append(("dma:" + k, v))
        for e in self.ENGS:
            waits = []
            for t in toks:
                if t[0] == e:
                    continue
                self._need(e, t, waits)
            self.ops[e].append({"fn": None, "waits": waits, "sig": False, "dma": None})

    def final_wait(self, eng, toks):
        waits = []
        for t in toks:
            self._need(eng, t, waits)
        self.ops[eng].append({"fn": None, "waits": waits, "sig": False, "dma": None})

    def emit(self, nc):
        engsem = {}
        signum = {}
        for e in self.ENGS:
            cnt = 0
            signum[e] = {}
            for i, op in enumerate(self.ops[e]):
                if op["sig"]:
                    signum[e][i] = cnt
                    cnt += 1
            nep = cnt // EPOCH + 1
            engsem[e] = [nc.alloc_semaphore(f"s_{e}_{j}") for j in range(nep)]
        dmasem = {k: nc.alloc_semaphore("d_" + k) for k in self.dma_cum}

        def run(ename, eng):
            for i, op in enumerate(self.ops[ename]):
                for key, val in op["waits"]:
                    if key.startswith("dma:"):
                        eng.wait_ge(dmasem[key[4:]], val)
                    else:
                        s = signum[key][val]
                        eng.wait_ge(engsem[key][s // EPOCH], s % EPOCH + 1)
                if op["fn"] is None:
                    continue
                ins = op["fn"](eng)
                if op["dma"] is not None:
                    ins.then_inc(dmasem[op["dma"]], 16)
                elif op["sig"]:
                    s = signum[ename][i]
                    ins.then_inc(engsem[ename][s // EPOCH], 1)

        with nc.Block() as block:
            @block.tensor
            def _(eng):
                run("pe", eng)

            @block.scalar
            def _(eng):
                run("act", eng)

            @block.vector
            def _(eng):
                run("dve", eng)

            @block.gpsimd
            def _(eng):
                run("pool", eng)

            @block.sync
            def _(eng):
                run("sp", eng)


class Alloc:
    def __init__(self, nc, lo, hi):
        self.nc, self.lo, self.hi, self.p, self.n = nc, lo, hi, lo, 0

    def t(self, shape, dt, name="t"):
        nb = 1
        for s in shape[1:]:
            nb *= s
        nb *= 2 if dt == BF16 else 4
        nb = (nb + 31) // 32 * 32
        off = self.p
        self.p += nb
        assert self.p <= self.hi, f"SBUF overflow {name} {self.p} > {self.hi}"
        self.n += 1
        return self.nc.alloc_sbuf_tensor_at(f"{name}{self.n}", list(shape), dt, offset=off)


def build(NT=16, debug=False, stage=9):
    nc = bass.Bass("TRN2", target_bir_lowering=False)
    SEQ = NT * 512
    P = Prog()

    def finish():
        fin = [("dma:" + k, v) for k, v in P.dma_cum.items()]
        P.final_wait("sp", fin)
        P.emit(nc)
        return nc

    def din(name, shape, dt=F32):
        return nc.dram_tensor(name, list(shape), dt, kind="ExternalInput").ap()

    x_d = din("x", [SEQ, D])
    ctx_d = din("ctx", [CTX, D])
    rows_d = din("rows", [128, NROW])
    cols_d = din("cols", [128, NCOL])
    cst_d = din("cst", [128, NCST])
    wada_d = din("w_ada", [D, 6 * D])
    wtm_d = din("w_tm", [D, 1536])
    wfm_d = din("w_fm", [D, 1600])
    wgate_d = din("w_gate", [96, 512])
    wout_d = din("w_out", [D, D])
    wup_d = din("w_up", [D, 2 * DFF])
    wdown_d = din("w_down", [DFF, D])
    out_d = nc.dram_tensor("out", [SEQ, D], F32, kind="ExternalOutput").ap()
    if debug:
        ob_d = nc.dram_tensor("ob_d", [SEQ, 512], F32, kind="ExternalOutput").ap()
        x1_d = nc.dram_tensor("x1_d", [SEQ, D], F32, kind="ExternalOutput").ap()
    else:
        ob_d = nc.dram_tensor("ob_d", [SEQ, 512], F32).ap()
        x1_d = nc.dram_tensor("x1_d", [SEQ, D], F32).ap()
    hx2_d = nc.dram_tensor("hx2_d", [8, 128, SEQ + 128], BF16).ap()
    wupb_d = nc.dram_tensor("wupb_d", [NFC // 2, 128, 8, 512], BF16).ap()

    pb = [nc.alloc_psum_tensor(f"pb{i}", [128, 512], F32) for i in range(8)]
    PB = [Buf(f"pb{i}") for i in range(8)]

    G = Alloc(nc, SB_LO, SB_HI)
    ident = G.t([128, 128], BF16, "ident")
    Lm = G.t([128, 4, 128], BF16, "Lm")
    maskT = G.t([128, 2, 512], BF16, "maskT")
    ncol = G.t([128, 1], BF16, "ncol")
    ones_r = G.t([1, 128], BF16, "ones_r")
    A1 = G.t([128, D], F32, "A1")
    cA1 = G.t([128, D], F32, "cA1")
    G1 = G.t([128, D], F32, "G1")
    A2 = G.t([128, D], F32, "A2")
    G2 = G.t([128, D], F32, "G2")
    ghead = G.t([128, 512], F32, "ghead")
    B1 = G.t([1, D], BF16, "B1")
    cB1 = G.t([1, D], BF16, "cB1")
    B2 = G.t([1, D], BF16, "B2")
    colp = G.t([128, NCOL], F32, "colp")
    wgate = G.t([96, 512], BF16, "wgate")
    dsc = G.t([128, 12, 128], BF16, "dsc")
    S = [G.t([128, 2, 128], F32, "S") for _ in range(2)]
    Sb = [G.t([128, 2, 128], BF16, "Sb") for _ in range(2)]
    alT = G.t([96, 512], BF16, "alT")
    stat = G.t([128, 64], F32, "stat")
    fstat = G.t([128, 8], F32, "fstat")
    b_fst = [Buf(f"fst{i}") for i in range(4)]
    epsc = G.t([128, 1], F32, "epsc")
    g_const = Buf("consts")
    bS = [Buf("S0"), Buf("S1")]
    bSb = [Buf("Sb0"), Buf("Sb1")]
    OV = G.p

    Z = Alloc(nc, OV, SB_HI)
    cstf = Z.t([128, NCST], F32, "cstf")
    rowp = Z.t([128, NROW], F32, "rowp")
    wa = [Z.t([128, 8, D], F32, "wa") for _ in range(2)]
    modr = [Z.t([128, D], F32, "modr") for _ in range(2)]
    rep = Z.t([128, 16, 128], F32, "rep")
    onesf = Z.t([128, 128], F32, "onesf")
    scl = Z.t([128, 16], F32, "scl")
    wgf = Z.t([96, 512], F32, "wgf")
    zt = Z.t([128, 64], BF16, "zt")
    b_cst, b_row, b_col, b_wgf = Buf("cstf"), Buf("rowp"), Buf("colp"), Buf("wgf")
    b_wa = [Buf("wa0"), Buf("wa1")]
    b_modr = [Buf("modr0"), Buf("modr1")]
    b_rep, b_scl, b_zt = Buf("rep"), Buf("scl"), Buf("zt")

    P.add("sp", lambda e: e.dma_start(out=cstf[:], in_=cst_d[:, :]), writes=[b_cst], dma="ld0")
    P.add("sp", lambda e: e.dma_start(out=colp[:], in_=cols_d[:, :]), writes=[b_col], dma="ld1")
    P.add("sp", lambda e: e.dma_start(out=wgf[:], in_=wgate_d[:, :]), writes=[b_wgf], dma="ld2")
    P.add("sp", lambda e: e.dma_start(out=rowp[:, 0:4608], in_=rows_d[:, 0:4608]), writes=[b_row], dma="ld3", ndma=2)
    P.add("sp", lambda e: e.dma_start(out=rowp[:, 4608:NROW], in_=rows_d[:, 4608:NROW]), writes=[b_row], dma="ld3", ndma=0)

    b_wupb = Buf("wupb")
    wup_v = wup_d.rearrange("(k p) j -> p k j", p=128)
    for pc in range(NFC // 2):
        for ug in range(2):
            off = ug * DFF + pc * 256
            P.add("pool", lambda e, pc=pc, ug=ug, off=off: e.dma_start(out=wupb_d[pc, :, :, ug * 256:(ug + 1) * 256], in_=wup_v[:, :, off:off + 256]),
                  writes=[b_wupb], dma="wupc", ndma=1)

    P.add("dve", lambda e: e.tensor_copy(out=ident[:], in_=cstf[:, K_ID:K_ID + 128]), reads=[b_cst], writes=[g_const])
    for j, k0 in enumerate((K_LFI, K_LFR, K_LBI, K_LBR)):
        P.add("dve", lambda e, j=j, k0=k0: e.tensor_copy(out=Lm[:, j, :], in_=cstf[:, k0:k0 + 128]), reads=[b_cst], writes=[g_const])
    for d_, k0 in enumerate((K_MF, K_MB)):
        for h in range(4):
            P.add("dve", lambda e, d_=d_, k0=k0, h=h: e.tensor_copy(out=maskT[:, d_, h * 128:(h + 1) * 128], in_=cstf[:, k0:k0 + 128]),
                  reads=[b_cst], writes=[g_const])
    P.add("dve", lambda e: e.tensor_copy(out=ncol[:], in_=cstf[:, K_NC:K_NC + 1]), reads=[b_cst], writes=[g_const])
    P.add("dve", lambda e: e.memset(ones_r[:], 1.0), writes=[g_const])
    P.add("dve", lambda e: e.memset(epsc[:], EPS), writes=[g_const])
    P.add("dve", lambda e: e.memset(onesf[:], 1.0), writes=[b_rep])
    P.add("dve", lambda e: e.memset(zt[:], 0.0), writes=[b_zt])
    P.add("dve", lambda e: e.memset(alT[64:96, :], 1.0), writes=[g_const])
    P.add("dve", lambda e: e.tensor_copy(out=wgate[:], in_=wgf[:]), reads=[b_wgf], writes=[g_const])
    P.add("dve", lambda e: e.tensor_copy(out=ghead[:], in_=rowp[:, R_GH:R_GH + 512]), reads=[b_row], writes=[g_const])
    for d_ in range(2):
        P.add("dve", lambda e, d_=d_: e.memset(S[d_][:], 0.0), writes=[bS[d_]])
        P.add("dve", lambda e, d_=d_: e.memset(Sb[d_][:], 0.0), writes=[bSb[d_]])
    for t in range(3):
        for j in range(4):
            P.add("dve", lambda e, t=t, j=j: e.tensor_scalar(out=dsc[:, t * 4 + j, :], in0=ident[:], scalar1=colp[:, Q_WSC + t * 4 + j:Q_WSC + t * 4 + j + 1],
                                                              scalar2=None, op0=ALU.mult), reads=[g_const, b_col], writes=[g_const])
    b_hx2 = Buf("hx2_d")
    for k in range(8):
        P.add("sp", lambda e, k=k: e.dma_start(out=hx2_d[k, :, 0:64], in_=zt[:, 0:64]), reads=[b_zt], writes=[b_hx2], dma="zm", ndma=1)
        P.add("sp", lambda e, k=k: e.dma_start(out=hx2_d[k, :, SEQ + 64:SEQ + 128], in_=zt[:, 0:64]), reads=[b_zt], writes=[b_hx2], dma="zm", ndma=1)

    P.add("act", lambda e: e.activation(out=scl[:], in_=colp[:, Q_C:Q_C + 16], func=AF.Silu), reads=[b_col], writes=[b_scl])
    for j in range(16):
        P.add("dve", lambda e, j=j: e.tensor_scalar(out=rep[:, j, :], in0=onesf[:], scalar1=scl[:, j:j + 1], scalar2=None, op0=ALU.mult),
              reads=[b_scl, b_rep], writes=[b_rep])
    wada_v = wada_d.rearrange("(k p) j -> p k j", p=128)

    def mod_piece(m, v, slot):
        for hf in range(2):
            bank = 2 * slot + hf
            for k in range(8):
                P.add("pe", lambda e, k=k, hf=hf, bank=bank: e.matmul(pb[bank][:, :], lhsT=rep[:, v * 8 + k, :], rhs=wa[m % 2][:, k, hf * 512:(hf + 1) * 512],
                                                                     start=(k == 0), stop=(k == 7)),
                      reads=[b_rep, b_wa[m % 2]], writes=[PB[bank]])
            P.add("dve", lambda e, hf=hf, bank=bank: e.tensor_tensor(out=modr[slot][:, hf * 512:(hf + 1) * 512], in0=pb[bank][:, :],
                                                                      in1=rowp[:, R_BADA + m * D + hf * 512:R_BADA + m * D + (hf + 1) * 512], op=ALU.add),
                  reads=[PB[bank], b_row], writes=[b_modr[slot]])

    for m in range(6):
        for q in range(2):
            P.add("sp", lambda e, m=m, q=q: e.dma_start(out=wa[m % 2][:, q * 4:(q + 1) * 4, :], in_=wada_v[:, q * 4:(q + 1) * 4, m * D:(m + 1) * D]),
                  writes=[b_wa[m % 2]], dma=f"wa{m % 2}", ndma=1)
        mod_piece(m, 0, 0)
        if m < 2:
            mod_piece(m, 1, 1)
        if m == 0:
            P.add("act", lambda e: e.copy(out=B1[0:1, :], in_=modr[0][0:1, :]), reads=[b_modr[0]], writes=[g_const])
            P.add("act", lambda e: e.copy(out=cB1[0:1, :], in_=modr[1][0:1, :]), reads=[b_modr[1]], writes=[g_const])
        elif m == 1:
            P.add("dve", lambda e: e.scalar_tensor_tensor(out=A1[:], in0=modr[0][:], scalar=1.0, in1=rowp[:, R_GPM:R_GPM + D], op0=ALU.add, op1=ALU.mult),
                  reads=[b_modr[0], b_row], writes=[g_const])
            P.add("dve", lambda e: e.scalar_tensor_tensor(out=cA1[:], in0=modr[1][:], scalar=1.0, in1=rowp[:, R_GPM:R_GPM + D], op0=ALU.add, op1=ALU.mult),
                  reads=[b_modr[1], b_row], writes=[g_const])
        elif m == 2:
            P.add("dve", lambda e: e.tensor_tensor(out=G1[:], in0=modr[0][:], in1=rowp[:, R_GQM:R_GQM + D], op=ALU.mult), reads=[b_modr[0], b_row], writes=[g_const])
        elif m == 3:
            P.add("act", lambda e: e.copy(out=B2[0:1, :], in_=modr[0][0:1, :]), reads=[b_modr[0]], writes=[g_const])
        elif m == 4:
            P.add("dve", lambda e: e.scalar_tensor_tensor(out=A2[:], in0=modr[0][:], scalar=1.0, in1=rowp[:, R_GPF:R_GPF + D], op0=ALU.add, op1=ALU.mult),
                  reads=[b_modr[0], b_row], writes=[g_const])
        else:
            P.add("dve", lambda e: e.tensor_tensor(out=G2[:], in0=modr[0][:], in1=rowp[:, R_GQF:R_GQF + D], op=ALU.mult), reads=[b_modr[0], b_row], writes=[g_const])
    P.barrier()
    if stage == 0:
        return finish()

    Y = Alloc(nc, OV, SB_HI)
    wtm = Y.t([128, 8, 1536], BF16, "wtm")
    wfm = Y.t([128, 8, 1600], BF16, "wfm")
    wout = Y.t([128, 8, D], BF16, "wout")
    xt = [Y.t([128, 4, D], F32, "xt") for _ in range(2)]
    hxT = Y.t([128, 8, 512], BF16, "hxT")
    obl = [Y.t([128, 512], F32, "obl") for _ in range(2)]
    g_l = [Y.t([128, 256], BF16, "g_l") for _ in range(2)]
    g_eb = [Y.t([128, 256], F32, "g_eb") for _ in range(2)]
    g_enb = [Y.t([128, 256], F32, "g_enb") for _ in range(2)]
    g_er = [Y.t([128, 256], F32, "g_er") for _ in range(2)]
    g_e = g_er
    g_qz = [Y.t([128, 2, 256], BF16, "g_qz") for _ in range(2)]
    g_kt = [Y.t([128, 256], BF16, "g_kt") for _ in range(2)]
    kqs = [Y.t([128, 512], F32, "kqs") for _ in range(2)]
    g_et = [Y.t([128, 2], F32, "g_et") for _ in range(2)]
    g_kh = [Y.t([128, 256], BF16, "g_kh") for _ in range(2)]
    g_vb = [Y.t([128, 512], BF16, "g_vb") for _ in range(2)]
    g_T = [Y.t([128, 768], BF16, "g_T") for _ in range(2)]
    g_att = [Y.t([128, 512], BF16, "g_att") for _ in range(2)]
    obst = obl
    osum = Y.t([128, 512], F32, "osum")
    ghs4 = Y.t([128, 4, 512], BF16, "ghs4")
    b_ghs = [Buf(f"ghs{j}") for j in range(4)]
    yg = Y.t([128, 512], BF16, "yg")
    junk = Y.t([128, D], BF16, "junk")
    ybf = [Y.t([128, D], BF16, "ybf") for _ in range(2)]
    b_ybf = [Buf("ybf0"), Buf("ybf1")]
    yxT = Y.t([128, 8, 512], BF16, "yxT")
    sbT = [Y.t([128, 512], BF16, "sbT") for _ in range(2)]
    scT = [Y.t([128, 512], BF16, "scT") for _ in range(2)]
    zpad = Y.t([128, 4, 8, 66], BF16, "zpad")
    tmpf = Y.t([128, 512], F32, "tmpf")
    hx2s = [Y.t([128, 8, 128], BF16, "hx2s") for _ in range(2)]

    b_wtm, b_wfm, b_wout = Buf("wtm"), Buf("wfm"), Buf("wout")
    b_xt = [[Buf(f"xt{s}{j}") for j in range(4)] for s in range(2)]
    b_hxT = [Buf(f"hxT{j}") for j in range(4)]
    b_obl = [Buf("obl0"), Buf("obl1")]
    bg = {n: Buf(n) for n in ("osum", "sg", "ghs", "yg", "junk", "ybf", "tmpf", "alT", "stat")}
    bF = [{n: Buf(n + str(i)) for n in ("e", "l", "eb", "enb", "er", "qt", "kt", "qblk", "kqs")} for i in range(2)]
    bB = [{n: Buf(n + str(i)) for n in ("et", "kh", "vb", "T", "att")} for i in range(2)]
    b_obst = b_obl
    b_yxg = [Buf(f"yxg{j}") for j in range(4)]
    b_yxs = [Buf(f"yxs{j}") for j in range(4)]
    b_sbT = [Buf("sbT0"), Buf("sbT1")]
    b_scT = [Buf("scT0"), Buf("scT1")]
    b_zp = [Buf(f"zp{j}") for j in range(4)]
    b_hx2s = [Buf("hx2s0"), Buf("hx2s1")]
    b_ob_d = [Buf(f"ob_d{i}") for i in range(NT * 4)]
    b_x1_d = [Buf(f"x1_d{i}") for i in range(NT * 4)]
    b_hx2_d = [Buf(f"hx2_d{i}") for i in range(NT)]

    wtm_v = wtm_d.rearrange("(k p) j -> p k j", p=128)
    wfm_v = wfm_d.rearrange("(k p) j -> p k j", p=128)
    wout_v = wout_d.rearrange("(k p) j -> p k j", p=128)
    for k in range(8):
        P.add("pool", lambda e, k=k: e.dma_start(out=wtm[:, k, :], in_=wtm_v[:, k, :]), writes=[b_wtm], dma="wtm", ndma=1)
    for k in range(8):
        P.add("pool", lambda e, k=k: e.dma_start(out=wfm[:, k, :], in_=wfm_v[:, k, :]), writes=[b_wfm], dma="wfm", ndma=1)
    for k in range(8):
        P.add("pool", lambda e, k=k: e.dma_start(out=wout[:, k, :], in_=wout_v[:, k, :]), writes=[b_wout], dma="wout", ndma=1)
    P.add("dve", lambda e: e.memset(zpad[:], 0.0), writes=b_zp)
    for i_ in range(2):
        P.add("dve", lambda e, i_=i_: e.memset(g_qz[i_][:], 0.0), writes=[bF[i_]["qt"]])

    sc_i = [0]

    bst8 = [Buf(f"stat{i}") for i in range(8)]

    def stat_slot():
        k = sc_i[0] % 8
        sc_i[0] += 1
        return 8 * k, bst8[k]

    def rstd_from(ss_ap, nfeat, out_ap, rb, wb):
        P.add("act", lambda e: e.activation(out=out_ap, in_=ss_ap, func=AF.Ln, scale=1.0 / nfeat, bias=epsc[:, 0:1]), reads=rb + [g_const], writes=wb)
        P.add("act", lambda e: e.activation(out=out_ap, in_=out_ap, func=AF.Exp, scale=-0.5), reads=wb, writes=wb)

    fcnt = [0]

    def front_stages(src_ap, src_buf, Arow, Brow, dstT, dst_cols, dst_buf):
        n_ = fcnt[0]
        fcnt[0] += 1
        q_, par = n_ % 4, n_ % 2
        banks = (6, 7) if par == 0 else (2, 5)
        ss = fstat[:, 2 * q_:2 * q_ + 1]
        rs = fstat[:, 2 * q_ + 1:2 * q_ + 2]
        bst = b_fst[q_]
        yb = ybf[par]

        def fa():
            P.add("dve", lambda e: e.memset(fstat[:, 2 * q_:2 * q_ + 2], 0.0), writes=[bst])
            P.add("act", lambda e: e.activation(out=junk[:], in_=src_ap, func=AF.Square, accum_out=ss), reads=[src_buf, bst], writes=[bg["junk"], bst])
            rstd_from(ss, D, rs, [bst], [bst])

        def fb():
            P.add("dve", lambda e: e.scalar_tensor_tensor(out=yb[:], in0=src_ap, scalar=rs, in1=Arow[:], op0=ALU.mult, op1=ALU.mult),
                  reads=[src_buf, bst, g_const], writes=[b_ybf[par]])

        def fc():
            for k in range(8):
                bank = banks[k // 4]
                o = pb[bank][:, (k % 4) * 128:(k % 4 + 1) * 128]
                P.add("pe", lambda e, o=o, k=k: e.matmul(o, lhsT=yb[:, k * 128:(k + 1) * 128], rhs=ident[:], start=True, stop=False),
                      reads=[b_ybf[par], g_const], writes=[PB[bank]])
                P.add("pe", lambda e, o=o, k=k: e.matmul(o, lhsT=Brow[0:1, k * 128:(k + 1) * 128], rhs=ones_r[0:1, :], start=False, stop=True),
                      reads=[g_const], writes=[PB[bank]])

        def fd():
            for hf in range(2):
                bank = banks[hf]
                P.add("act", lambda e, hf=hf, bank=bank: e.copy(out=dstT[:, hf * 4:(hf + 1) * 4, dst_cols], in_=pb[bank][:, :].rearrange("p (k t) -> p k t", k=4)),
                      reads=[PB[bank]], writes=[dst_buf])

        return [fa, fb, fc, fd]

    def front(src_ap, src_buf, Arow, Brow, dstT, dst_cols, dst_buf, banks=None):
        for st_ in front_stages(src_ap, src_buf, Arow, Brow, dstT, dst_cols, dst_buf):
            st_()

    def front4(slot):
        if fcnt[0] % 2:
            fcnt[0] += 1
        S_ = [front_stages(xt[slot][:, j, :], b_xt[slot][j], A1, B1, hxT, slice(j * 128, (j + 1) * 128), b_hxT[j]) for j in range(4)]
        for (j, k) in ((0, 0), (1, 0), (0, 1), (1, 1), (2, 0), (3, 0), (0, 2), (0, 3), (2, 1), (1, 2), (1, 3), (3, 1), (2, 2), (2, 3), (3, 2), (3, 3)):
            S_[j][k]()

    def proj_tm(cols, grp, bank, hbuf):
        for k in range(8):
            P.add("pe", lambda e, k=k: e.matmul(pb[bank][:, :], lhsT=hxT[:, k, cols], rhs=wtm[:, k, grp * 512:(grp + 1) * 512], start=(k == 0), stop=(k == 7)),
                  reads=[hbuf, b_wtm], writes=[PB[bank]])

    def proj_fm(c0, M, bank, ncols=512):
        for k in range(8):
            P.add("pe", lambda e, k=k: e.matmul(pb[bank][0:M, 0:ncols], lhsT=wfm[:, k, c0:c0 + M], rhs=hxT[:, k, 0:ncols], start=(k == 0), stop=(k == 7)),
                  reads=b_hxT + [b_wfm], writes=[PB[bank]])

    def gate_lowrank(ncols=512):
        proj_fm(0, 64, 7, ncols)
        P.add("act", lambda e: e.copy(out=alT[0:64, 0:ncols], in_=pb[7][0:64, 0:ncols]), reads=[PB[7]], writes=[bg["alT"]])

    def gla_stages(d, sl, bs, cols, hbuf, full):
        bK, bV, bA = 3 * sl, 3 * sl + 1, 3 * sl + 2
        F, Bb = bF[sl], bB[bs]
        e_, l_, eb_, enb_, er_, qz_, kt_, kq_ = g_e[sl], g_l[sl], g_eb[sl], g_enb[sl], g_er[sl], g_qz[sl], g_kt[sl], kqs[sl]
        et_, kh_, vb_, T_, att_ = g_et[bs], g_kh[bs], g_vb[bs], g_T[bs], g_att[bs]
        Tps = pb[bV].bitcast(BF16)

        def s1():
            proj_tm(cols, 0, bK, hbuf)
            proj_tm(cols, 1, bV, hbuf)
            P.add("pe", lambda e: e.matmul(pb[bA][:, 0:256], lhsT=alT[0:96, cols], rhs=wgate[:, d * 256:(d + 1) * 256], start=True, stop=True),
                  reads=[bg["alT"], g_const], writes=[PB[bA]])

        def s2():
            P.add("act", lambda e: e.activation(out=e_[:], in_=pb[bA][:, 0:256], func=AF.Exp, scale=-1.0), reads=[PB[bA]], writes=[F["er"]])
            P.add("act", lambda e: e.activation(out=l_[:], in_=e_[:], func=AF.Ln, bias=1.0), reads=[F["er"]], writes=[F["l"]])
            P.add("act", lambda e: e.copy(out=kq_[:], in_=pb[bK][:, :]), reads=[PB[bK]], writes=[F["kqs"]])
            P.add("act", lambda e: e.copy(out=vb_[:], in_=pb[bV][:, :]), reads=[PB[bV]], writes=[Bb["vb"]])

        def s3():
            if full:
                P.add("pe", lambda e: e.matmul(pb[bK][:, 0:256], lhsT=Lm[:, 2 * d, :], rhs=l_[:], start=True, stop=True), reads=[F["l"], g_const], writes=[PB[bK]])
            P.add("pe", lambda e: e.matmul(pb[bK][:, 256:512], lhsT=Lm[:, 2 * d + 1, :], rhs=l_[:], start=True, stop=True), reads=[F["l"], g_const], writes=[PB[bK]])
            for p_ in range(2):
                P.add("pe", lambda e, p_=p_: e.matmul(pb[bA][:, 256 + p_:257 + p_], lhsT=l_[:, p_ * 128:(p_ + 1) * 128], rhs=ncol[:, 0:1], start=True, stop=True),
                      reads=[F["l"], g_const], writes=[PB[bA]])

        def s4():
            if full:
                P.add("act", lambda e: e.activation(out=eb_[:], in_=pb[bK][:, 0:256], func=AF.Exp), reads=[PB[bK]], writes=[F["eb"]])
                P.add("act", lambda e: e.activation(out=enb_[:], in_=pb[bK][:, 0:256], func=AF.Exp, scale=-1.0), reads=[PB[bK]], writes=[F["enb"]])
            P.add("act", lambda e: e.activation(out=er_[:], in_=pb[bK][:, 256:512], func=AF.Exp), reads=[PB[bK]], writes=[F["er"]])
            P.add("act", lambda e: e.activation(out=et_[:], in_=pb[bA][:, 256:258], func=AF.Exp), reads=[PB[bA]], writes=[Bb["et"]])

        def s5():
            if full:
                kq3 = kq_[:, 256:512].rearrange("p (a b) -> p a b", a=2)
                eb3 = eb_[:].rearrange("p (a b) -> p a b", a=2)
                for w in range(2):
                    P.add("dve", lambda e, w=w: e.scalar_tensor_tensor(out=qz_[:, :, w * 192:w * 192 + 64], in0=kq3[:, :, w * 64:(w + 1) * 64], scalar=0.125,
                                                                       in1=eb3[:, :, w * 64:(w + 1) * 64], op0=ALU.mult, op1=ALU.mult),
                          reads=[F["kqs"], F["eb"]], writes=[F["qt"]])
                P.add("dve", lambda e: e.tensor_tensor(out=kt_[:], in0=kq_[:, 0:256], in1=enb_[:], op=ALU.mult), reads=[F["kqs"], F["enb"]], writes=[F["kt"]])
            P.add("dve", lambda e: e.tensor_tensor(out=kh_[:], in0=kq_[:, 0:256], in1=er_[:], op=ALU.mult), reads=[F["kqs"], F["er"]], writes=[Bb["kh"]])

        def s6():
            if not full:
                return
            for j in range(4):
                P.add("pe", lambda e, j=j: e.transpose(Tps[:, j * 128:(j + 1) * 128], qz_[:, j // 2, (j % 2) * 128:(j % 2 + 1) * 128], ident[:]),
                      reads=[F["qt"], g_const], writes=[PB[bV]])
            for j in range(2):
                P.add("pe", lambda e, j=j: e.transpose(Tps[:, 512 + j * 128:512 + (j + 1) * 128], kt_[:, j * 128:(j + 1) * 128], ident[:]),
                      reads=[F["kt"], g_const], writes=[PB[bV]])

        def s7():
            if not full:
                return
            P.add("dve", lambda e: e.tensor_copy(out=T_[:], in_=Tps[:, 0:768]), reads=[PB[bV]], writes=[Bb["T"]])

        def s8():
            if not full:
                return
            for pr in range(2):
                P.add("pe", lambda e, pr=pr: e.matmul(pb[bA][:, pr * 256:(pr + 1) * 256], lhsT=T_[:, 512 + pr * 128:512 + (pr + 1) * 128], rhs=T_[:, pr * 256:(pr + 1) * 256],
                                                      start=True, stop=True), reads=[Bb["T"]], writes=[PB[bA]])

        def s9():
            if not full:
                return
            P.add("dve", lambda e: e.tensor_tensor(out=att_[:], in0=pb[bA][:, :], in1=maskT[:, d, :], op=ALU.mult), reads=[PB[bA], g_const], writes=[Bb["att"]])

        return [s1, s2, s3, s4, s5, s6, s7, s8, s9]

    def gla_back(d, bs, full):
        Bb = bB[bs]
        et_, kh_, vb_, T_, att_ = g_et[bs], g_kh[bs], g_vb[bs], g_T[bs], g_att[bs]
        if full:
            for h in range(4):
                pr = h // 2
                P.add("pe", lambda e, h=h, pr=pr: e.matmul(pb[6][:, h * 128:(h + 1) * 128], lhsT=T_[:, h * 128:(h + 1) * 128], rhs=Sb[d][:, pr, :],
                                                           start=True, stop=False), reads=[Bb["T"], bSb[d]], writes=[PB[6]])
                P.add("pe", lambda e, h=h: e.matmul(pb[6][:, h * 128:(h + 1) * 128], lhsT=att_[:, h * 128:(h + 1) * 128], rhs=vb_[:, h * 128:(h + 1) * 128],
                                                    start=False, stop=True), reads=[Bb["att"], Bb["vb"]], writes=[PB[6]])
        for pr in range(2):
            P.add("pe", lambda e, pr=pr: e.matmul(pb[7][:, pr * 256:(pr + 1) * 256], lhsT=kh_[:, pr * 128:(pr + 1) * 128], rhs=vb_[:, pr * 256:(pr + 1) * 256],
                                                  start=True, stop=True), reads=[Bb["kh"], Bb["vb"]], writes=[PB[7]])
        for pr in range(2):
            for w in range(2):
                rsl = slice(w * 64, (w + 1) * 64)
                c0 = pr * 256 + w * 128
                P.add("dve", lambda e, pr=pr, rsl=rsl, c0=c0: e.scalar_tensor_tensor(out=S[d][rsl, pr, :], in0=S[d][rsl, pr, :], scalar=et_[rsl, pr:pr + 1],
                                                                                      in1=pb[7][rsl, c0:c0 + 128], op0=ALU.mult, op1=ALU.add),
                      reads=[bS[d], Bb["et"], PB[7]], writes=[bS[d]])
        P.add("act", lambda e: e.copy(out=Sb[d][:], in_=S[d][:]), reads=[bS[d]], writes=[bSb[d]])

    def gla_tile(d, order, after_back, mid=None):
        st = {}
        for n_, j in enumerate(order):
            st[j] = gla_stages(d, n_ % 2, n_ % 2, slice(j * 128, (j + 1) * 128), b_hxT[j], True)
        c0_, c1_, c2_, c3_ = order
        for k in range(9):
            st[c0_][k]()
            st[c1_][k]()
        if mid is None:
            st[c2_][0]()
            st[c3_][0]()
        for n_, j in enumerate((c0_, c1_)):
            gla_back(d, n_, True)
            if mid is not None:
                mid(j)
            after_back(j)
        for k in range(0 if mid is not None else 1, 9):
            st[c2_][k]()
            st[c3_][k]()
        for n_, j in enumerate((c2_, c3_)):
            gla_back(d, n_, True)
            if mid is not None:
                mid(j)
            after_back(j)

    ctx_v = ctx_d.rearrange("(j p) f -> p j f", p=128)
    for j in range(2):
        P.add("sp", lambda e, j=j: e.dma_start(out=xt[0][:, j, :], in_=ctx_v[:, j, :]), writes=[b_xt[0][j]], dma=f"x0{j}")
    for j in range(2):
        front(xt[0][:, j, :], b_xt[0][j], cA1, cB1, hxT, slice(j * 128, (j + 1) * 128), b_hxT[j])
    gate_lowrank(256)
    for d_, order in ((0, (0, 1)), (1, (1, 0))):
        for j in order:
            cols = slice(j * 128, (j + 1) * 128)
            for st_ in gla_stages(d_, 0, 0, cols, b_hxT[j], False):
                st_()
            gla_back(d_, 0, False)

    if stage == 1:
        return finish()
    x_v = x_d.rearrange("(n j p) f -> n p j f", p=128, j=4)
    ob_v = ob_d.rearrange("(n p) f -> n p f", p=128)
    x1_v = x1_d.rearrange("(n p) f -> n p f", p=128)

    def load_x(i, slot):
        for j in range(4):
            P.add("sp", lambda e, j=j: e.dma_start(out=xt[slot][:, j, :], in_=x_v[i, :, j, :]), writes=[b_xt[slot][j]], dma=f"x{slot}{j}")

    tseq = [i for i in range(NT - 1, -1, -1)] + [i for i in range(NT)]
    tcount = 0
    load_x(tseq[0], 0)
    for i in range(NT - 1, -1, -1):
        slot = tcount % 2
        tcount += 1
        if tcount < len(tseq) and stage >= 3 or tcount < NT:
            load_x(tseq[tcount], tcount % 2)
        front4(slot)
        gate_lowrank()
        def after_a(j, i=i):
            n = i * 4 + j
            P.add("act", lambda e, j=j: e.copy(out=obst[j % 2][:], in_=pb[6][:, :]), reads=[PB[6]], writes=[b_obst[j % 2]])
            P.add("sp", lambda e, j=j, n=n: e.dma_start(out=ob_v[n, :, :], in_=obst[j % 2][:]), reads=[b_obst[j % 2]], writes=[b_ob_d[n]], dma=f"obst{j % 2}")

        gla_tile(1, (3, 2, 1, 0), after_a)

    if stage == 2:
        return finish()
    hx2_v = hx2_d.rearrange("k p t -> p k t")
    for i in range(NT):
        slot = tcount % 2
        tcount += 1
        if tcount < len(tseq):
            load_x(tseq[tcount], tcount % 2)
        front4(slot)
        gate_lowrank()
        def sc_step(j):
            s2 = j % 2
            proj_fm(64 + j * 128, 128, 0)
            P.add("act", lambda e, s2=s2: e.copy(out=sbT[s2][:], in_=pb[0][:, :]), reads=[PB[0]], writes=[b_sbT[s2]])
            proj_fm(64 + 512 + j * 128, 128, 1)
            P.add("act", lambda e, s2=s2: e.copy(out=scT[s2][:], in_=pb[1][:, :]), reads=[PB[1]], writes=[b_scT[s2]])
            proj_fm(64 + 1024 + j * 128, 128, 2)
            P.add("dve", lambda e, j=j, s2=s2: e.tensor_tensor(out=zpad[:, j, :, 1:65], in0=pb[2][:, :].rearrange("p (r c) -> p r c", r=8),
                                                               in1=scT[s2][:].rearrange("p (r c) -> p r c", r=8), op=ALU.mult),
                  reads=[PB[2], b_scT[s2]], writes=[b_zp[j]])
            for t in range(3):
                P.add("pe", lambda e, j=j, t=t: e.matmul(pb[3][:, :], lhsT=dsc[:, t * 4 + j, :], rhs=zpad[:, j, :, t:t + 64], start=(t == 0), stop=(t == 2)),
                      reads=[b_zp[j], g_const], writes=[PB[3]])
            P.add("dve", lambda e, j=j, s2=s2: e.scalar_tensor_tensor(out=yxT[:, 4 + j, :], in0=pb[3][:, :], scalar=colp[:, Q_BSC + j:Q_BSC + j + 1], in1=sbT[s2][:],
                                                                      op0=ALU.add, op1=ALU.mult), reads=[PB[3], b_sbT[s2], g_const], writes=[b_yxs[j]])
        sgt = (tmpf, osum)
        sgb = (bg["tmpf"], bg["osum"])
        for j in range(4):
            cols = slice(j * 128, (j + 1) * 128)
            proj_tm(cols, 2, j, b_hxT[j])
            P.add("act", lambda e, j=j: e.activation(out=sgt[j % 2][:], in_=pb[j][:, :], func=AF.Silu), reads=[PB[j]], writes=[sgb[j % 2]])
            P.add("dve", lambda e, j=j: e.tensor_tensor(out=ghs4[:, j, :], in0=sgt[j % 2][:], in1=ghead[:], op=ALU.mult), reads=[sgb[j % 2], g_const], writes=[b_ghs[j]])

        def load_ob(j, i=i):
            n = i * 4 + j
            P.add("sp", lambda e, j=j, n=n: e.dma_start(out=obl[j % 2][:], in_=ob_v[n, :, :]), reads=[b_ob_d[n]], writes=[b_obl[j % 2]], dma=f"obl{j % 2}")

        load_ob(0)
        load_ob(1)

        def after_b(j, i=i):
            cols = slice(j * 128, (j + 1) * 128)
            P.add("dve", lambda e, j=j: e.tensor_tensor(out=osum[:], in0=pb[6][:, :], in1=obl[j % 2][:], op=ALU.add), reads=[PB[6], b_obl[j % 2]], writes=[bg["osum"]])
            if j < 2:
                load_ob(j + 2)
            c0, bs_ = stat_slot()
            P.add("dve", lambda e, c0=c0: e.memset(stat[:, c0:c0 + 8], 0.0), writes=[bs_])
            for h in range(4):
                P.add("act", lambda e, h=h, c0=c0: e.activation(out=junk[:, h * 128:(h + 1) * 128], in_=osum[:, h * 128:(h + 1) * 128], func=AF.Square,
                                                                accum_out=stat[:, c0 + h:c0 + h + 1]), reads=[bg["osum"], bs_], writes=[bg["junk"], bs_])
            rstd_from(stat[:, c0:c0 + 4], 128, stat[:, c0 + 4:c0 + 8], [bs_], [bs_])
            for h in range(4):
                P.add("dve", lambda e, h=h, c0=c0, j=j: e.scalar_tensor_tensor(out=yg[:, h * 128:(h + 1) * 128], in0=osum[:, h * 128:(h + 1) * 128],
                                                                               scalar=stat[:, c0 + 4 + h:c0 + 5 + h], in1=ghs4[:, j, h * 128:(h + 1) * 128],
                                                                               op0=ALU.mult, op1=ALU.mult), reads=[bg["osum"], bs_, b_ghs[j]], writes=[bg["yg"]])
            Tps = pb[7].bitcast(BF16)
            for h in range(4):
                P.add("pe", lambda e, h=h: e.transpose(Tps[:, h * 128:(h + 1) * 128], yg[:, h * 128:(h + 1) * 128], ident[:]), reads=[bg["yg"], g_const], writes=[PB[7]])
            P.add("act", lambda e, cols=cols: e.copy(out=yxT[:, 0:4, cols], in_=Tps[:, 0:512].rearrange("p (k t) -> p k t", k=4)), reads=[PB[7]], writes=[b_yxg[j]])

        gla_tile(0, (0, 1, 2, 3), after_b, mid=sc_step)
        def oproj_stages(j, i=i, slot=slot):
            cols = slice(j * 128, (j + 1) * 128)
            n = i * 4 + j
            ob_ = 3 * (j % 2)
            c0, bs_ = stat_slot()
            tmp2 = (tmpf, osum)
            tmpb = (bg["tmpf"], bg["osum"])

            def o1():
                for hf in range(2):
                    for k in range(8):
                        rb = [b_yxg[j]] if k < 4 else [b_yxs[k - 4]]
                        P.add("pe", lambda e, k=k, hf=hf: e.matmul(pb[ob_ + hf][:, :], lhsT=yxT[:, k, cols], rhs=wout[:, k, hf * 512:(hf + 1) * 512],
                                                                   start=(k == 0), stop=(k == 7)), reads=rb + [b_wout], writes=[PB[ob_ + hf]])

            def o2():
                P.add("dve", lambda e: e.memset(stat[:, c0:c0 + 4], 0.0), writes=[bs_])
                for hf in range(2):
                    P.add("act", lambda e, hf=hf: e.activation(out=junk[:, hf * 512:(hf + 1) * 512], in_=pb[ob_ + hf][:, :], func=AF.Square,
                                                               accum_out=stat[:, c0 + hf:c0 + hf + 1]), reads=[PB[ob_ + hf], bs_], writes=[bg["junk"], bs_])
                P.add("dve", lambda e: e.tensor_tensor(out=stat[:, c0 + 2:c0 + 3], in0=stat[:, c0:c0 + 1], in1=stat[:, c0 + 1:c0 + 2], op=ALU.add),
                      reads=[bs_], writes=[bs_])
                rstd_from(stat[:, c0 + 2:c0 + 3], D, stat[:, c0 + 3:c0 + 4], [bs_], [bs_])

            def o3():
                for hf in range(2):
                    hs = slice(hf * 512, (hf + 1) * 512)
                    P.add("dve", lambda e, hf=hf, hs=hs: e.scalar_tensor_tensor(out=tmp2[hf][:], in0=pb[ob_ + hf][:, :], scalar=stat[:, c0 + 3:c0 + 4], in1=G1[:, hs],
                                                                               op0=ALU.mult, op1=ALU.mult), reads=[PB[ob_ + hf], bs_, g_const], writes=[tmpb[hf]])
                    P.add("dve", lambda e, hf=hf, hs=hs: e.tensor_tensor(out=xt[slot][:, j, hs], in0=xt[slot][:, j, hs], in1=tmp2[hf][:], op=ALU.add),
                          reads=[tmpb[hf], b_xt[slot][j]], writes=[b_xt[slot][j]])
                P.add("sp", lambda e: e.dma_start(out=x1_v[n, :, :], in_=xt[slot][:, j, :]), reads=[b_xt[slot][j]], writes=[b_x1_d[n]], dma=f"x{slot}{j}")

            fs = front_stages(xt[slot][:, j, :], b_xt[slot][j], A2, B2, hx2s[j % 2], slice(0, 128), b_hx2s[j % 2])

            def o7():
                fs[3]()
                P.add("sp", lambda e: e.dma_start(out=hx2_v[:, :, 64 + i * 512 + j * 128:64 + i * 512 + (j + 1) * 128], in_=hx2s[j % 2][:]),
                      reads=[b_hx2s[j % 2]], writes=[b_hx2_d[i]], dma=f"hx2s{j % 2}")

            return [o1, o2, o3, fs[0], fs[1], fs[2], o7]

        if fcnt[0] % 2:
            fcnt[0] += 1
        OS = [oproj_stages(j) for j in range(4)]
        order = [(0, 0), (1, 0), (0, 1), (0, 2), (0, 3), (1, 1), (2, 0), (1, 2), (0, 4), (1, 3), (0, 5), (2, 1), (3, 0), (0, 6), (2, 2), (1, 4), (2, 3), (1, 5),
                 (3, 1), (1, 6), (3, 2), (2, 4), (3, 3), (2, 5), (2, 6), (3, 4), (3, 5), (3, 6)]
        assert sorted(order) == [(j, k) for j in range(4) for k in range(7)]
        for (j, k) in order:
            OS[j][k]()
    P.barrier()
    if stage == 3:
        return finish()

    Zc = Alloc(nc, OV, SB_HI)
    wdn = Zc.t([128, NFC, D], BF16, "wdn")
    wupS = [Zc.t([128, 8, 512], BF16, "wupS") for _ in range(2)]
    hw = [Zc.t([128, 8, 640], BF16, "hw") for _ in range(2)]
    x1t = [Zc.t([128, 4, D], F32, "x1t") for _ in range(2)]
    hT = Zc.t([128, NFC, 512], BF16, "hT")
    upad = [Zc.t([128, 10, 66], BF16, "upad") for _ in range(2)]
    gts = [Zc.t([128, 512], BF16, "gts") for _ in range(2)]
    sil = [Zc.t([128, 512], BF16, "sil") for _ in range(2)]
    dg = [Zc.t([128, 9, 128], BF16, "dg") for _ in range(2)]
    tmpc = Zc.t([128, 512], F32, "tmpc")
    junkc = Zc.t([128, 512], BF16, "junkc")
    b_wdn = Buf("wdn")
    b_wupS = [Buf("wupS0"), Buf("wupS1")]
    b_hw = [Buf("hw0"), Buf("hw1")]
    b_x1t = [[Buf(f"x1t{s}{j}") for j in range(4)] for s in range(2)]
    b_hT = [Buf(f"hT{c}") for c in range(NFC)]
    b_upad = [Buf("upad0"), Buf("upad1")]
    b_upc = [Buf("upc0"), Buf("upc1")]
    ucar = Zc.t([128, NFC, 2, 64], BF16, "ucar")
    b_ucar = [Buf(f"ucar{c}") for c in range(NFC)]
    b_gts = [Buf("gts0"), Buf("gts1")]
    b_sil = [Buf("sil0"), Buf("sil1")]
    b_dg = [Buf("dg0"), Buf("dg1")]
    b_tmpc, b_junkc, b_statc = Buf("tmpc"), Buf("junkc"), Buf("statc")
    b_out = []

    wdn_v = wdown_d.rearrange("(c p) j -> p c j", p=128)
    for c in range(NFC):
        P.add("pool", lambda e, c=c: e.dma_start(out=wdn[:, c, :], in_=wdn_v[:, c, :]), writes=[b_wdn], dma="wdn", ndma=1)
    for s in range(2):
        P.add("dve", lambda e, s=s: e.memset(upad[s][:], 0.0), writes=[b_upad[s]])
    out_v = out_d.rearrange("(n p) f -> n p f", p=128)
    x1_v4 = x1_d.rearrange("(n j p) f -> n p j f", p=128, j=4)
    def load_c(i):
        slot = i % 2
        P.add("sp", lambda e: e.dma_start(out=hw[slot][:], in_=hx2_v[:, :, i * 512:i * 512 + 640]),
              reads=b_hx2_d[max(0, i - 1):i + 2] + [b_hx2], writes=[b_hw[slot]], dma=f"hw{slot}")
        for j in range(4):
            P.add("sp", lambda e, j=j: e.dma_start(out=x1t[slot][:, j, :], in_=x1_v4[i, :, j, :]), reads=[b_x1_d[i * 4 + j]],
                  writes=[b_x1t[slot][j]], dma=f"x1t{slot}{j}")

    NPC = NFC // 2

    def load_piece(g):
        ps = g % 2
        pc = g % NPC
        P.add("sp", lambda e: e.dma_start(out=wupS[ps][:], in_=wupb_d[pc, :, :, :]), reads=[b_wupb], writes=[b_wupS[ps]], dma=f"wupS{ps}")

    load_c(0)
    load_piece(0)
    piece = 0
    for i in range(NT):
        slot = i % 2
        for c in range(NFC):
            s2 = c % 2
            if c % 2 == 0:
                ps = piece % 2
                piece += 1
                if piece < NT * NPC:
                    load_piece(piece)
                if c == 2 and i + 1 < NT:
                    load_c(i + 1)
            wo = (c % 2) * 128
            for t in range(9):
                P.add("dve", lambda e, c=c, t=t, s2=s2: e.tensor_scalar(out=dg[s2][:, t, :], in0=ident[:], scalar1=colp[:, Q_WCF + c * 9 + t:Q_WCF + c * 9 + t + 1],
                                                                         scalar2=None, op0=ALU.mult), reads=[g_const], writes=[b_dg[s2]])
            ucol = 0 if i == 0 else 128
            for k in range(8):
                P.add("pe", lambda e, k=k, s2=s2, ps=ps, wo=wo, slot=slot, ucol=ucol: e.matmul(pb[s2][:, :], lhsT=wupS[ps][:, k, wo:wo + 128], rhs=hw[slot][:, k, ucol:ucol + 512],
                                                                                           start=(k == 0), stop=(k == 7)),
                      reads=[b_wupS[ps], b_hw[slot]], writes=[PB[s2]])
            if i == 0:
                for k in range(8):
                    P.add("pe", lambda e, k=k, s2=s2, ps=ps, wo=wo, slot=slot: e.matmul(pb[2 + s2][:, 0:128], lhsT=wupS[ps][:, k, wo:wo + 128], rhs=hw[slot][:, k, 512:640], start=(k == 0), stop=(k == 7)),
                          reads=[b_wupS[ps], b_hw[slot]], writes=[PB[2 + s2]])
            for k in range(8):
                P.add("pe", lambda e, k=k, s2=s2, ps=ps, wo=wo, slot=slot: e.matmul(pb[4 + s2][:, :], lhsT=wupS[ps][:, k, 256 + wo:256 + wo + 128], rhs=hw[slot][:, k, 64:576], start=(k == 0), stop=(k == 7)),
                      reads=[b_wupS[ps], b_hw[slot]], writes=[PB[4 + s2]])
            if i == 0:
                P.add("act", lambda e, s2=s2: e.copy(out=upad[s2][:, 0:8, 1:65], in_=pb[s2][:, :].rearrange("p (r c) -> p r c", r=8)), reads=[PB[s2]], writes=[b_upc[s2], b_upad[s2]])
                P.add("act", lambda e, s2=s2: e.copy(out=upad[s2][:, 8:10, 1:65], in_=pb[2 + s2][:, 0:128].rearrange("p (r c) -> p r c", r=2)), reads=[PB[2 + s2]], writes=[b_upad[s2]])
            else:
                P.add("pool", lambda e, c=c, s2=s2: e.tensor_copy(out=upad[s2][:, 0:2, 1:65], in_=ucar[:, c, :, :]), reads=[b_ucar[c]], writes=[b_upc[s2]])
                P.add("act", lambda e, s2=s2: e.copy(out=upad[s2][:, 2:10, 1:65], in_=pb[s2][:, :].rearrange("p (r c) -> p r c", r=8)), reads=[PB[s2]], writes=[b_upad[s2]])
            if i < NT - 1:
                P.add("pool", lambda e, c=c, s2=s2: e.tensor_copy(out=ucar[:, c, :, :], in_=upad[s2][:, 8:10, 1:65]), reads=[b_upad[s2]], writes=[b_ucar[c]])
            P.add("dve", lambda e, s2=s2: e.tensor_copy(out=gts[s2][:], in_=pb[4 + s2][:, :]), reads=[PB[4 + s2]], writes=[b_gts[s2]])
            for t in range(9):
                dr, dc = t // 3, t % 3
                P.add("pe", lambda e, t=t, dr=dr, dc=dc, s2=s2: e.matmul(pb[6 + s2][:, :], lhsT=dg[s2][:, t, :], rhs=upad[s2][:, dr:dr + 8, dc:dc + 64], start=(t == 0), stop=(t == 8)),
                      reads=[b_dg[s2], b_upad[s2], b_upc[s2]], writes=[PB[6 + s2]])
            P.add("act", lambda e, c=c, s2=s2: e.activation(out=sil[s2][:], in_=pb[6 + s2][:, :], func=AF.Silu, bias=colp[:, Q_BCF + c:Q_BCF + c + 1]),
                  reads=[PB[6 + s2], g_const], writes=[b_sil[s2]])
            P.add("dve", lambda e, c=c, s2=s2: e.tensor_tensor(out=hT[:, c, :], in0=sil[s2][:], in1=gts[s2][:], op=ALU.mult), reads=[b_sil[s2], b_gts[s2]], writes=[b_hT[c]])
        for j in range(4):
            cols = slice(j * 128, (j + 1) * 128)
            n = i * 4 + j
            for hf in range(2):
                bank = (2 * j + hf) % 8
                for c in range(NFC):
                    P.add("pe", lambda e, c=c, hf=hf, bank=bank, cols=cols: e.matmul(pb[bank][:, :], lhsT=hT[:, c, cols], rhs=wdn[:, c, hf * 512:(hf + 1) * 512],
                                                                                      start=(c == 0), stop=(c == NFC - 1)), reads=[b_hT[c], b_wdn], writes=[PB[bank]])
            c0, _unused = stat_slot()
            P.add("dve", lambda e, c0=c0: e.memset(stat[:, c0:c0 + 4], 0.0), writes=[b_statc])
            for hf in range(2):
                bank = (2 * j + hf) % 8
                P.add("act", lambda e, hf=hf, bank=bank, c0=c0: e.activation(out=junkc[:], in_=pb[bank][:, :], func=AF.Square, accum_out=stat[:, c0 + hf:c0 + hf + 1]),
                      reads=[PB[bank], b_statc], writes=[b_junkc, b_statc])
            P.add("dve", lambda e, c0=c0: e.tensor_tensor(out=stat[:, c0 + 2:c0 + 3], in0=stat[:, c0:c0 + 1], in1=stat[:, c0 + 1:c0 + 2], op=ALU.add),
                  reads=[b_statc], writes=[b_statc])
            rstd_from(stat[:, c0 + 2:c0 + 3], D, stat[:, c0 + 3:c0 + 4], [b_statc], [b_statc])
            for hf in range(2):
                bank = (2 * j + hf) % 8
                hs = slice(hf * 512, (hf + 1) * 512)
                P.add("dve", lambda e, hf=hf, bank=bank, hs=hs, c0=c0: e.scalar_tensor_tensor(out=tmpc[:], in0=pb[bank][:, :], scalar=stat[:, c0 + 3:c0 + 4], in1=G2[:, hs],
                                                                                             op0=ALU.mult, op1=ALU.mult), reads=[PB[bank], b_statc, g_const], writes=[b_tmpc])
                P.add("dve", lambda e, j=j, hs=hs, slot=slot: e.tensor_tensor(out=x1t[slot][:, j, hs], in0=x1t[slot][:, j, hs], in1=tmpc[:], op=ALU.add),
                      reads=[b_tmpc, b_x1t[slot][j]], writes=[b_x1t[slot][j]])
            bo = Buf(f"out{n}")
            tok = P.add("sp", lambda e, j=j, n=n, slot=slot: e.dma_start(out=out_v[n, :, :], in_=x1t[slot][:, j, :]), reads=[b_x1t[slot][j]], writes=[bo], dma=f"x1t{slot}{j}")
            b_out.append(tok)
    last = {}
    for t in b_out:
        last[t[0]] = max(last.get(t[0], 0), t[1])
    fin = list(last.items())
    if debug:
        for k, v in P.dma_cum.items():
            fin.append(("dma:" + k, v))
    P.final_wait("sp", fin)
    P.emit(nc)
    return nc


def _consts():
    c = np.zeros((128, NCST), np.float32)
    m = np.arange(128)[:, None]
    t = np.arange(128)[None, :]
    c[:, K_ID:K_ID + 128] = (m == t)
    c[:, K_LFI:K_LFI + 128] = (m <= t) * (-1.0 / 16)
    c[:, K_LFR:K_LFR + 128] = (m > t) * (-1.0 / 16)
    c[:, K_LBI:K_LBI + 128] = (m >= t) * (-1.0 / 16)
    c[:, K_LBR:K_LBR + 128] = (m < t) * (-1.0 / 16)
    c[:, K_MF:K_MF + 128] = (m <= t)
    c[:, K_MB:K_MB + 128] = (m > t)
    c[:, K_NC] = -1.0 / 16
    return c


def _colmaj(v, n):
    return np.ascontiguousarray(np.asarray(v, np.float32).reshape(n, 128).T)


def make_in_maps(inputs, NT=16, ncores=8):
    f = lambda a: np.asarray(a, np.float32)
    w_in = f(inputs["w_in"])[0]
    w_tm = np.ascontiguousarray(np.concatenate([w_in[:, C_K:C_V], w_in[:, C_Q:C_OG], w_in[:, C_V:C_AF], w_in[:, C_OG:C_SB]], axis=1))
    gate = np.zeros((D, 64), np.float32)
    gate[:, 0:16] = w_in[:, C_AF:C_AB]
    gate[:, 32:48] = w_in[:, C_AB:C_Q]
    w_fm = np.ascontiguousarray(np.concatenate([gate, w_in[:, C_SB:C_SC], w_in[:, C_SC:C_SX], w_in[:, C_SX:]], axis=1))
    wg = np.zeros((96, 512), np.float32)
    wg[0:16, 0:256] = f(inputs["w_af"])[0]
    wg[32:48, 256:512] = f(inputs["w_ab"])[0]
    wg[64, 0:256] = f(inputs["b_af"])[0]
    wg[64, 256:512] = f(inputs["b_ab"])[0]
    rows1 = np.concatenate([f(inputs["g_pre_mix"])[0], f(inputs["g_post_mix"])[0], f(inputs["g_pre_ffn"])[0], f(inputs["g_post_ffn"])[0],
                            np.tile(f(inputs["g_head"])[0], 4), f(inputs["b_ada"])[0]])
    rows = np.ascontiguousarray(np.broadcast_to(rows1[None, :], (128, NROW)))
    w_sc = f(inputs["w_sc"])[0]
    w_cf = f(inputs["w_cf"])[0].reshape(9, DFF)
    cols_common = np.concatenate([
        _colmaj(f(inputs["b_sc"])[0], 4), _colmaj(f(inputs["b_cf"])[0], NFC),
        np.concatenate([_colmaj(w_sc[t], 4) for t in range(3)], axis=1),
        np.stack([_colmaj(w_cf[t], NFC) for t in range(9)], axis=2).reshape(128, NFC * 9),
    ], axis=1)
    cst = _consts()
    maps = []
    x = inputs["x"]
    for b in range(ncores):
        cols = np.ascontiguousarray(np.concatenate([_colmaj(f(inputs["c"])[b], 8), _colmaj(f(inputs["c_ctx"]), 8), cols_common], axis=1))
        maps.append({
            "x": np.ascontiguousarray(f(x[b])[:NT * 512]), "ctx": np.ascontiguousarray(f(inputs["ctx"][b])),
            "rows": rows, "cols": cols, "cst": cst,
            "w_ada": f(inputs["w_ada"])[0], "w_tm": w_tm, "w_fm": w_fm, "w_gate": wg,
            "w_out": f(inputs["w_out"])[0], "w_up": f(inputs["w_up"])[0], "w_down": f(inputs["w_down"])[0],
        })
    return maps


_NC_CACHE = {}


def kernel(**inputs):
    NT = 16
    if NT not in _NC_CACHE:
        _NC_CACHE[NT] = build(NT)
    nc = _NC_CACHE[NT]
    maps = make_in_maps(inputs, NT, 8)
    res = run_bass_kernel_spmd(nc, maps, core_ids=list(range(8)))
    return np.stack([np.asarray(r["out"], np.float32).reshape(NT * 512, D) for r in res.results], axis=0)
```

```python
import os
import numpy as np
import concourse.bass as bass
import concourse.mybir as mybir
from concourse.bass_utils import run_bass_kernel_spmd

F32 = mybir.dt.float32
BF16 = mybir.dt.bfloat16
AF = mybir.ActivationFunctionType
ALU = mybir.AluOpType

D = 1024
DFF = 2816
NFC = 22
CTX = 256
EPS = 1e-6
SB_LO = 16512
SB_HI = 229344
EPOCH = 24000

C_K, C_V, C_AF, C_AB, C_Q, C_OG, C_SB, C_SC, C_SX = 0, 256, 768, 784, 800, 1056, 1568, 2080, 2592

K_ID, K_LFI, K_LFR, K_LBI, K_LBR, K_MF, K_MB, K_NC = 0, 128, 256, 384, 512, 640, 768, 896
NCST = 897
R_GPM, R_GQM, R_GPF, R_GQF, R_GH, R_BADA = 0, 1024, 2048, 3072, 4096, 4608
NROW = 4608 + 6144
Q_C, Q_CC, Q_BSC, Q_BCF, Q_WSC, Q_WCF = 0, 8, 16, 20, 42, 54
NCOL = 54 + 198


class Buf:
    __slots__ = ("name", "w", "r")

    def __init__(self, name):
        self.name = name
        self.w = None
        self.r = {}


class Prog:
    ENGS = ("pe", "act", "dve", "pool", "sp")

    def __init__(self):
        self.ops = {e: [] for e in self.ENGS}
        self.waited = {e: {} for e in self.ENGS}
        self.dma_cum = {}
        self.n = 0

    def _need(self, eng, tok, waits):
        key, val = tok
        if self.waited[eng].get(key, -1) >= val:
            return
        self.waited[eng][key] = val
        waits.append(tok)
        if not key.startswith("dma:"):
            self.ops[key][val]["sig"] = True

    def add(self, eng, fn, reads=(), writes=(), dma=None, ndma=1):
        idx = len(self.ops[eng])
        waits = []
        mykey = ("dma:" + dma) if dma is not None else eng
        for b in reads:
            if b.w is not None:
                if b.w[0] == eng and eng == "pe" and dma is None:
                    continue
                self._need(eng, b.w, waits)
        for b in writes:
            if b.w is not None and b.w[0] != mykey:
                self._need(eng, b.w, waits)
            for k, v in b.r.items():
                if k == mykey:
                    continue
                self._need(eng, (k, v), waits)
        if dma is not None:
            cum = self.dma_cum.get(dma, 0) + 16 * ndma
            self.dma_cum[dma] = cum
            tok = ("dma:" + dma, cum)
        else:
            tok = (eng, idx)
        for b in reads:
            if b.r.get(tok[0], -1) < tok[1]:
                b.r[tok[0]] = tok[1]
        for b in writes:
            b.w = tok
            b.r = {}
        self.ops[eng].append({"fn": fn, "waits": waits, "sig": False, "dma": dma})
        self.n += 1
        return tok

    def barrier(self):
        toks = []
        for e in self.ENGS:
            if self.ops[e]:
                for i in range(len(self.ops[e]) - 1, -1, -1):
                    if self.ops[e][i]["dma"] is None and self.ops[e][i]["fn"] is not None:
                        toks.append((e, i))
                        break
        for k, v in self.dma_cum.items():
            toks.append(("dma:" + k, v))
        for e in self.ENGS:
            waits = []
            for t in toks:
                if t[0] == e:
                    continue
                self._need(e, t, waits)
            self.ops[e].append({"fn": None, "waits": waits, "sig": False, "dma": None})

    def final_wait(self, eng, toks):
        waits = []
        for t in toks:
            self._need(eng, t, waits)
        self.ops[eng].append({"fn": None, "waits": waits, "sig": False, "dma": None})

    def emit(self, nc):
        engsem = {}
        signum = {}
        for e in self.ENGS:
            cnt = 0
            signum[e] = {}
            for i, op in enumerate(self.ops[e]):
                if op["sig"]:
                    signum[e][i] = cnt
                    cnt += 1
            nep = cnt // EPOCH + 1
            engsem[e] = [nc.alloc_semaphore(f"s_{e}_{j}") for j in range(nep)]
        dmasem = {k: nc.alloc_semaphore("d_" + k) for k in self.dma_cum}

        def run(ename, eng):
            for i, op in enumerate(self.ops[ename]):
                for key, val in op["waits"]:
                    if key.startswith("dma:"):
                        eng.wait_ge(dmasem[key[4:]], val)
                    else:
                        s = signum[key][val]
                        eng.wait_ge(engsem[key][s // EPOCH], s % EPOCH + 1)
                if op["fn"] is None:
                    continue
                ins = op["fn"](eng)
                if op["dma"] is not None:
                    ins.then_inc(dmasem[op["dma"]], 16)
                elif op["sig"]:
                    s = signum[ename][i]
                    ins.then_inc(engsem[ename][s // EPOCH], 1)

        with nc.Block() as block:
            @block.tensor
            def _(eng):
                run("pe", eng)

            @block.scalar
            def _(eng):
                run("act", eng)

            @block.vector
            def _(eng):
                run("dve", eng)

            @block.gpsimd
            def _(eng):
                run("pool", eng)

            @block.sync
            def _(eng):
                run("sp", eng)


class Alloc:
    def __init__(self, nc, lo, hi):
        self.nc, self.lo, self.hi, self.p, self.n = nc, lo, hi, lo, 0

    def t(self, shape, dt, name="t"):
        nb = 1
        for s in shape[1:]:
            nb *= s
        nb *= 2 if dt == BF16 else 4
        nb = (nb + 31) // 32 * 32
        off = self.p
        self.p += nb
        assert self.p <= self.hi, f"SBUF overflow {name} {self.p} > {self.hi}"
        self.n += 1
        return self.nc.alloc_sbuf_tensor_at(f"{name}{self.n}", list(shape), dt, offset=off)


def build(NT=16, debug=False, stage=9):
    nc = bass.Bass("TRN2", target_bir_lowering=False)
    SEQ = NT * 512
    P = Prog()

    def finish():
        fin = [("dma:" + k, v) for k, v in P.dma_cum.items()]
        P.final_wait("sp", fin)
        P.emit(nc)
        return nc

    def din(name, shape, dt=F32):
        return nc.dram_tensor(name, list(shape), dt, kind="ExternalInput").ap()

    x_d = din("x", [SEQ, D])
    ctx_d = din("ctx", [CTX, D])
    rows_d = din("rows", [128, NROW])
    cols_d = din("cols", [128, NCOL])
    cst_d = din("cst", [128, NCST])
    wada_d = din("w_ada", [D, 6 * D])
    wtm_d = din("w_tm", [D, 1536])
    wfm_d = din("w_fm", [D, 1600])
    wgate_d = din("w_gate", [96, 512])
    wout_d = din("w_out", [D, D])
    wup_d = din("w_up", [D, 2 * DFF])
    wdown_d = din("w_down", [DFF, D])
    out_d = nc.dram_tensor("out", [SEQ, D], F32, kind="ExternalOutput").ap()
    if debug:
        ob_d = nc.dram_tensor("ob_d", [SEQ, 512], F32, kind="ExternalOutput").ap()
        x1_d = nc.dram_tensor("x1_d", [SEQ, D], F32, kind="ExternalOutput").ap()
    else:
        ob_d = nc.dram_tensor("ob_d", [SEQ, 512], F32).ap()
        x1_d = nc.dram_tensor("x1_d", [SEQ, D], F32).ap()
    hx2_d = nc.dram_tensor("hx2_d", [8, 128, SEQ + 128], BF16).ap()
    wupb_d = nc.dram_tensor("wupb_d", [NFC // 2, 128, 8, 512], BF16).ap()

    pb = [nc.alloc_psum_tensor(f"pb{i}", [128, 512], F32) for i in range(8)]
    PB = [Buf(f"pb{i}") for i in range(8)]

    G = Alloc(nc, SB_LO, SB_HI)
    ident = G.t([128, 128], BF16, "ident")
    Lm = G.t([128, 4, 128], BF16, "Lm")
    maskT = G.t([128, 2, 512], BF16, "maskT")
    ncol = G.t([128, 1], BF16, "ncol")
    ones_r = G.t([1, 128], BF16, "ones_r")
    A1 = G.t([128, D], F32, "A1")
    cA1 = G.t([128, D], F32, "cA1")
    G1 = G.t([128, D], F32, "G1")
    A2 = G.t([128, D], F32, "A2")
    G2 = G.t([128, D], F32, "G2")
    ghead = G.t([128, 512], F32, "ghead")
    B1 = G.t([1, D], BF16, "B1")
    cB1 = G.t([1, D], BF16, "cB1")
    B2 = G.t([1, D], BF16, "B2")
    colp = G.t([128, NCOL], F32, "colp")
    wgate = G.t([96, 512], BF16, "wgate")
    dsc = G.t([128, 12, 128], BF16, "dsc")
    S = [G.t([128, 2, 128], F32, "S") for _ in range(2)]
    Sb = [G.t([128, 2, 128], BF16, "Sb") for _ in range(2)]
    alT = G.t([96, 512], BF16, "alT")
    stat = G.t([128, 64], F32, "stat")
    fstat = G.t([128, 8], F32, "fstat")
    b_fst = [Buf(f"fst{i}") for i in range(4)]
    epsc = G.t([128, 1], F32, "epsc")
    g_const = Buf("consts")
    bS = [Buf("S0"), Buf("S1")]
    bSb = [Buf("Sb0"), Buf("Sb1")]
    OV = G.p

    Z = Alloc(nc, OV, SB_HI)
    cstf = Z.t([128, NCST], F32, "cstf")
    rowp = Z.t([128, NROW], F32, "rowp")
    wa = [Z.t([128, 8, D], F32, "wa") for _ in range(2)]
    modr = [Z.t([128, D], F32, "modr") for _ in range(2)]
    rep = Z.t([128, 16, 128], F32, "rep")
    onesf = Z.t([128, 128], F32, "onesf")
    scl = Z.t([128, 16], F32, "scl")
    wgf = Z.t([96, 512], F32, "wgf")
    zt = Z.t([128, 64], BF16, "zt")
    b_cst, b_row, b_col, b_wgf = Buf("cstf"), Buf("rowp"), Buf("colp"), Buf("wgf")
    b_wa = [Buf("wa0"), Buf("wa1")]
    b_modr = [Buf("modr0"), Buf("modr1")]
    b_rep, b_scl, b_zt = Buf("rep"), Buf("scl"), Buf("zt")

    P.add("sp", lambda e: e.dma_start(out=cstf[:], in_=cst_d[:, :]), writes=[b_cst], dma="ld0")
    P.add("sp", lambda e: e.dma_start(out=colp[:], in_=cols_d[:, :]), writes=[b_col], dma="ld1")
    P.add("sp", lambda e: e.dma_start(out=wgf[:], in_=wgate_d[:, :]), writes=[b_wgf], dma="ld2")
    P.add("sp", lambda e: e.dma_start(out=rowp[:, 0:4608], in_=rows_d[:, 0:4608]), writes=[b_row], dma="ld3", ndma=2)
    P.add("sp", lambda e: e.dma_start(out=rowp[:, 4608:NROW], in_=rows_d[:, 4608:NROW]), writes=[b_row], dma="ld3", ndma=0)

    b_wupb = Buf("wupb")
    wup_v = wup_d.rearrange("(k p) j -> p k j", p=128)
    for pc in range(NFC // 2):
        for ug in range(2):
            off = ug * DFF + pc * 256
            P.add("pool", lambda e, pc=pc, ug=ug, off=off: e.dma_start(out=wupb_d[pc, :, :, ug * 256:(ug + 1) * 256], in_=wup_v[:, :, off:off + 256]),
                  writes=[b_wupb], dma="wupc", ndma=1)

    P.add("dve", lambda e: e.tensor_copy(out=ident[:], in_=cstf[:, K_ID:K_ID + 128]), reads=[b_cst], writes=[g_const])
    for j, k0 in enumerate((K_LFI, K_LFR, K_LBI, K_LBR)):
        P.add("dve", lambda e, j=j, k0=k0: e.tensor_copy(out=Lm[:, j, :], in_=cstf[:, k0:k0 + 128]), reads=[b_cst], writes=[g_const])
    for d_, k0 in enumerate((K_MF, K_MB)):
        for h in range(4):
            P.add("dve", lambda e, d_=d_, k0=k0, h=h: e.tensor_copy(out=maskT[:, d_, h * 128:(h + 1) * 128], in_=cstf[:, k0:k0 + 128]),
                  reads=[b_cst], writes=[g_const])
    P.add("dve", lambda e: e.tensor_copy(out=ncol[:], in_=cstf[:, K_NC:K_NC + 1]), reads=[b_cst], writes=[g_const])
    P.add("dve", lambda e: e.memset(ones_r[:], 1.0), writes=[g_const])
    P.add("dve", lambda e: e.memset(epsc[:], EPS), writes=[g_const])
    P.add("dve", lambda e: e.memset(onesf[:], 1.0), writes=[b_rep])
    P.add("dve", lambda e: e.memset(zt[:], 0.0), writes=[b_zt])
    P.add("dve", lambda e: e.memset(alT[64:96, :], 1.0), writes=[g_const])
    P.add("dve", lambda e: e.tensor_copy(out=wgate[:], in_=wgf[:]), reads=[b_wgf], writes=[g_const])
    P.add("dve", lambda e: e.tensor_copy(out=ghead[:], in_=rowp[:, R_GH:R_GH + 512]), reads=[b_row], writes=[g_const])
    for d_ in range(2):
        P.add("dve", lambda e, d_=d_: e.memset(S[d_][:], 0.0), writes=[bS[d_]])
        P.add("dve", lambda e, d_=d_: e.memset(Sb[d_][:], 0.0), writes=[bSb[d_]])
    for t in range(3):
        for j in range(4):
            P.add("dve", lambda e, t=t, j=j: e.tensor_scalar(out=dsc[:, t * 4 + j, :], in0=ident[:], scalar1=colp[:, Q_WSC + t * 4 + j:Q_WSC + t * 4 + j + 1],
                                                              scalar2=None, op0=ALU.mult), reads=[g_const, b_col], writes=[g_const])
    b_hx2 = Buf("hx2_d")
    for k in range(8):
        P.add("sp", lambda e, k=k: e.dma_start(out=hx2_d[k, :, 0:64], in_=zt[:, 0:64]), reads=[b_zt], writes=[b_hx2], dma="zm", ndma=1)
        P.add("sp", lambda e, k=k: e.dma_start(out=hx2_d[k, :, SEQ + 64:SEQ + 128], in_=zt[:, 0:64]), reads=[b_zt], writes=[b_hx2], dma="zm", ndma=1)

    P.add("act", lambda e: e.activation(out=scl[:], in_=colp[:, Q_C:Q_C + 16], func=AF.Silu), reads=[b_col], writes=[b_scl])
    for j in range(16):
        P.add("dve", lambda e, j=j: e.tensor_scalar(out=rep[:, j, :], in0=onesf[:], scalar1=scl[:, j:j + 1], scalar2=None, op0=ALU.mult),
              reads=[b_scl, b_rep], writes=[b_rep])
    wada_v = wada_d.rearrange("(k p) j -> p k j", p=128)

    def mod_piece(m, v, slot):
        for hf in range(2):
            bank = 2 * slot + hf
            for k in range(8):
                P.add("pe", lambda e, k=k, hf=hf, bank=bank: e.matmul(pb[bank][:, :], lhsT=rep[:, v * 8 + k, :], rhs=wa[m % 2][:, k, hf * 512:(hf + 1) * 512],
                                                                     start=(k == 0), stop=(k == 7)),
                      reads=[b_rep, b_wa[m % 2]], writes=[PB[bank]])
            P.add("dve", lambda e, hf=hf, bank=bank: e.tensor_tensor(out=modr[slot][:, hf * 512:(hf + 1) * 512], in0=pb[bank][:, :],
                                                                      in1=rowp[:, R_BADA + m * D + hf * 512:R_BADA + m * D + (hf + 1) * 512], op=ALU.add),
                  reads=[PB[bank], b_row], writes=[b_modr[slot]])

    for m in range(6):
        for q in range(2):
            P.add("sp", lambda e, m=m, q=q: e.dma_start(out=wa[m % 2][:, q * 4:(q + 1) * 4, :], in_=wada_v[:, q * 4:(q + 1) * 4, m * D:(m + 1) * D]),
                  writes=[b_wa[m % 2]], dma=f"wa{m % 2}", ndma=1)
        mod_piece(m, 0, 0)
        if m < 2:
            mod_piece(m, 1, 1)
        if m == 0:
            P.add("act", lambda e: e.copy(out=B1[0:1, :], in_=modr[0][0:1, :]), reads=[b_modr[0]], writes=[g_const])
            P.add("act", lambda e: e.copy(out=cB1[0:1, :], in_=modr[1][0:1, :]), reads=[b_modr[1]], writes=[g_const])
        elif m == 1:
            P.add("dve", lambda e: e.scalar_tensor_tensor(out=A1[:], in0=modr[0][:], scalar=1.0, in1=rowp[:, R_GPM:R_GPM + D], op0=ALU.add, op1=ALU.mult),
                  reads=[b_modr[0], b_row], writes=[g_const])
            P.add("dve", lambda e: e.scalar_tensor_tensor(out=cA1[:], in0=modr[1][:], scalar=1.0, in1=rowp[:, R_GPM:R_GPM + D], op0=ALU.add, op1=ALU.mult),
                  reads=[b_modr[1], b_row], writes=[g_const])
        elif m == 2:
            P.add("dve", lambda e: e.tensor_tensor(out=G1[:], in0=modr[0][:], in1=rowp[:, R_GQM:R_GQM + D], op=ALU.mult), reads=[b_modr[0], b_row], writes=[g_const])
        elif m == 3:
            P.add("act", lambda e: e.copy(out=B2[0:1, :], in_=modr[0][0:1, :]), reads=[b_modr[0]], writes=[g_const])
        elif m == 4:
            P.add("dve", lambda e: e.scalar_tensor_tensor(out=A2[:], in0=modr[0][:], scalar=1.0, in1=rowp[:, R_GPF:R_GPF + D], op0=ALU.add, op1=ALU.mult),
                  reads=[b_modr[0], b_row], writes=[g_const])
        else:
            P.add("dve", lambda e: e.tensor_tensor(out=G2[:], in0=modr[0][:], in1=rowp[:, R_GQF:R_GQF + D], op=ALU.mult), reads=[b_modr[0], b_row], writes=[g_const])
    P.barrier()
    if stage == 0:
        return finish()

    Y = Alloc(nc, OV, SB_HI)
    wtm = Y.t([128, 8, 1536], BF16, "wtm")
    wfm = Y.t([128, 8, 1600], BF16, "wfm")
    wout = Y.t([128, 8, D], BF16, "wout")
    xt = [Y.t([128, 4, D], F32, "xt") for _ in range(2)]
    hxT = Y.t([128, 8, 512], BF16, "hxT")
    obl = [Y.t([128, 512], F32, "obl") for _ in range(2)]
    g_l = [Y.t([128, 256], BF16, "g_l") for _ in range(2)]
    g_eb = [Y.t([128, 256], F32, "g_eb") for _ in range(2)]
    g_enb = [Y.t([128, 256], F32, "g_enb") for _ in range(2)]
    g_er = [Y.t([128, 256], F32, "g_er") for _ in range(2)]
    g_e = g_er
    g_qz = [Y.t([128, 2, 256], BF16, "g_qz") for _ in range(2)]
    g_kt = [Y.t([128, 256], BF16, "g_kt") for _ in range(2)]
    kqs = [Y.t([128, 512], F32, "kqs") for _ in range(2)]
    g_et = [Y.t([128, 2], F32, "g_et") for _ in range(2)]
    g_kh = [Y.t([128, 256], BF16, "g_kh") for _ in range(2)]
    g_vb = [Y.t([128, 512], BF16, "g_vb") for _ in range(2)]
    g_T = [Y.t([128, 768], BF16, "g_T") for _ in range(2)]
    g_att = [Y.t([128, 512], BF16, "g_att") for _ in range(2)]
    obst = obl
    osum = Y.t([128, 512], F32, "osum")
    ghs4 = Y.t([128, 4, 512], BF16, "ghs4")
    b_ghs = [Buf(f"ghs{j}") for j in range(4)]
    yg = Y.t([128, 512], BF16, "yg")
    junk = Y.t([128, D], BF16, "junk")
    ybf = [Y.t([128, D], BF16, "ybf") for _ in range(2)]
    b_ybf = [Buf("ybf0"), Buf("ybf1")]
    yxT = Y.t([128, 8, 512], BF16, "yxT")
    sbT = [Y.t([128, 512], BF16, "sbT") for _ in range(2)]
    scT = [Y.t([128, 512], BF16, "scT") for _ in range(2)]
    zpad = Y.t([128, 4, 8, 66], BF16, "zpad")
    tmpf = Y.t([128, 512], F32, "tmpf")
    hx2s = [Y.t([128, 8, 128], BF16, "hx2s") for _ in range(2)]

    b_wtm, b_wfm, b_wout = Buf("wtm"), Buf("wfm"), Buf("wout")
    b_xt = [[Buf(f"xt{s}{j}") for j in range(4)] for s in range(2)]
    b_hxT = [Buf(f"hxT{j}") for j in range(4)]
    b_obl = [Buf("obl0"), Buf("obl1")]
    bg = {n: Buf(n) for n in ("osum", "sg", "ghs", "yg", "junk", "ybf", "tmpf", "alT", "stat")}
    bF = [{n: Buf(n + str(i)) for n in ("e", "l", "eb", "enb", "er", "qt", "kt", "qblk", "kqs")} for i in range(2)]
    bB = [{n: Buf(n + str(i)) for n in ("et", "kh", "vb", "T", "att")} for i in range(2)]
    b_obst = b_obl
    b_yxg = [Buf(f"yxg{j}") for j in range(4)]
    b_yxs = [Buf(f"yxs{j}") for j in range(4)]
    b_sbT = [Buf("sbT0"), Buf("sbT1")]
    b_scT = [Buf("scT0"), Buf("scT1")]
    b_zp = [Buf(f"zp{j}") for j in range(4)]
    b_hx2s = [Buf("hx2s0"), Buf("hx2s1")]
    b_ob_d = [Buf(f"ob_d{i}") for i in range(NT * 4)]
    b_x1_d = [Buf(f"x1_d{i}") for i in range(NT * 4)]
    b_hx2_d = [Buf(f"hx2_d{i}") for i in range(NT)]

    wtm_v = wtm_d.rearrange("(k p) j -> p k j", p=128)
    wfm_v = wfm_d.rearrange("(k p) j -> p k j", p=128)
    wout_v = wout_d.rearrange("(k p) j -> p k j", p=128)
    for k in range(8):
        P.add("pool", lambda e, k=k: e.dma_start(out=wtm[:, k, :], in_=wtm_v[:, k, :]), writes=[b_wtm], dma="wtm", ndma=1)
    for k in range(8):
        P.add("pool", lambda e, k=k: e.dma_start(out=wfm[:, k, :], in_=wfm_v[:, k, :]), writes=[b_wfm], dma="wfm", ndma=1)
    for k in range(8):
        P.add("pool", lambda e, k=k: e.dma_start(out=wout[:, k, :], in_=wout_v[:, k, :]), writes=[b_wout], dma="wout", ndma=1)
    P.add("dve", lambda e: e.memset(zpad[:], 0.0), writes=b_zp)
    for i_ in range(2):
        P.add("dve", lambda e, i_=i_: e.memset(g_qz[i_][:], 0.0), writes=[bF[i_]["qt"]])

    sc_i = [0]

    bst8 = [Buf(f"stat{i}") for i in range(8)]

    def stat_slot():
        k = sc_i[0] % 8
        sc_i[0] += 1
        return 8 * k, bst8[k]

    def rstd_from(ss_ap, nfeat, out_ap, rb, wb):
        P.add("act", lambda e: e.activation(out=out_ap, in_=ss_ap, func=AF.Ln, scale=1.0 / nfeat, bias=epsc[:, 0:1]), reads=rb + [g_const], writes=wb)
        P.add("act", lambda e: e.activation(out=out_ap, in_=out_ap, func=AF.Exp, scale=-0.5), reads=wb, writes=wb)

    fcnt = [0]

    def front_stages(src_ap, src_buf, Arow, Brow, dstT, dst_cols, dst_buf, banks=None):
        n_ = fcnt[0]
        fcnt[0] += 1
        q_, par = n_ % 4, n_ % 2
        if banks is None:
            banks = (6, 7) if par == 0 else (2, 5)
        ss = fstat[:, 2 * q_:2 * q_ + 1]
        rs = fstat[:, 2 * q_ + 1:2 * q_ + 2]
        bst = b_fst[q_]
        yb = ybf[par]

        def fa():
            P.add("dve", lambda e: e.memset(fstat[:, 2 * q_:2 * q_ + 2], 0.0), writes=[bst])
            P.add("act", lambda e: e.activation(out=junk[:], in_=src_ap, func=AF.Square, accum_out=ss), reads=[src_buf, bst], writes=[bg["junk"], bst])
            rstd_from(ss, D, rs, [bst], [bst])

        def fb():
            P.add("dve", lambda e: e.scalar_tensor_tensor(out=yb[:], in0=src_ap, scalar=rs, in1=Arow[:], op0=ALU.mult, op1=ALU.mult),
                  reads=[src_buf, bst, g_const], writes=[b_ybf[par]])

        def fc():
            for k in range(8):
                bank = banks[k // 4]
                o = pb[bank][:, (k % 4) * 128:(k % 4 + 1) * 128]
                P.add("pe", lambda e, o=o, k=k: e.matmul(o, lhsT=yb[:, k * 128:(k + 1) * 128], rhs=ident[:], start=True, stop=False),
                      reads=[b_ybf[par], g_const], writes=[PB[bank]])
                P.add("pe", lambda e, o=o, k=k: e.matmul(o, lhsT=Brow[0:1, k * 128:(k + 1) * 128], rhs=ones_r[0:1, :], start=False, stop=True),
                      reads=[g_const], writes=[PB[bank]])

        def fd():
            for hf in range(2):
                bank = banks[hf]
                P.add("act", lambda e, hf=hf, bank=bank: e.copy(out=dstT[:, hf * 4:(hf + 1) * 4, dst_cols], in_=pb[bank][:, :].rearrange("p (k t) -> p k t", k=4)),
                      reads=[PB[bank]], writes=[dst_buf])

        return [fa, fb, fc, fd]

    def front(src_ap, src_buf, Arow, Brow, dstT, dst_cols, dst_buf, banks=None):
        for st_ in front_stages(src_ap, src_buf, Arow, Brow, dstT, dst_cols, dst_buf):
            st_()

    def front4(slot):
        if fcnt[0] % 2:
            fcnt[0] += 1
        S_ = [front_stages(xt[slot][:, j, :], b_xt[slot][j], A1, B1, hxT, slice(j * 128, (j + 1) * 128), b_hxT[j]) for j in range(4)]
        for (j, k) in ((0, 0), (1, 0), (0, 1), (1, 1), (2, 0), (3, 0), (0, 2), (0, 3), (2, 1), (1, 2), (1, 3), (3, 1), (2, 2), (2, 3), (3, 2), (3, 3)):
            S_[j][k]()

    def proj_tm(cols, grp, bank, hbuf):
        for k in range(8):
            P.add("pe", lambda e, k=k: e.matmul(pb[bank][:, :], lhsT=hxT[:, k, cols], rhs=wtm[:, k, grp * 512:(grp + 1) * 512], start=(k == 0), stop=(k == 7)),
                  reads=[hbuf, b_wtm], writes=[PB[bank]])

    def proj_fm(c0, M, bank, ncols=512):
        for k in range(8):
            P.add("pe", lambda e, k=k: e.matmul(pb[bank][0:M, 0:ncols], lhsT=wfm[:, k, c0:c0 + M], rhs=hxT[:, k, 0:ncols], start=(k == 0), stop=(k == 7)),
                  reads=b_hxT + [b_wfm], writes=[PB[bank]])

    def gate_lowrank(ncols=512):
        proj_fm(0, 64, 7, ncols)
        P.add("act", lambda e: e.copy(out=alT[0:64, 0:ncols], in_=pb[7][0:64, 0:ncols]), reads=[PB[7]], writes=[bg["alT"]])

    def gla_stages(d, sl, bs, cols, hbuf, full):
        bK, bV, bA = 3 * sl, 3 * sl + 1, 3 * sl + 2
        F, Bb = bF[sl], bB[bs]
        e_, l_, eb_, enb_, er_, qz_, kt_, kq_ = g_e[sl], g_l[sl], g_eb[sl], g_enb[sl], g_er[sl], g_qz[sl], g_kt[sl], kqs[sl]
        et_, kh_, vb_, T_, att_ = g_et[bs], g_kh[bs], g_vb[bs], g_T[bs], g_att[bs]
        Tps = pb[bV].bitcast(BF16)

        def s1():
            proj_tm(cols, 0, bK, hbuf)
            proj_tm(cols, 1, bV, hbuf)
            P.add("pe", lambda e: e.matmul(pb[bA][:, 0:256], lhsT=alT[0:96, cols], rhs=wgate[:, d * 256:(d + 1) * 256], start=True, stop=True),
                  reads=[bg["alT"], g_const], writes=[PB[bA]])

        def s2():
            P.add("act", lambda e: e.activation(out=e_[:], in_=pb[bA][:, 0:256], func=AF.Exp, scale=-1.0), reads=[PB[bA]], writes=[F["er"]])
            P.add("act", lambda e: e.activation(out=l_[:], in_=e_[:], func=AF.Ln, bias=1.0), reads=[F["er"]], writes=[F["l"]])
            P.add("act", lambda e: e.copy(out=kq_[:], in_=pb[bK][:, :]), reads=[PB[bK]], writes=[F["kqs"]])
            P.add("act", lambda e: e.copy(out=vb_[:], in_=pb[bV][:, :]), reads=[PB[bV]], writes=[Bb["vb"]])

        def s3():
            if full:
                P.add("pe", lambda e: e.matmul(pb[bK][:, 0:256], lhsT=Lm[:, 2 * d, :], rhs=l_[:], start=True, stop=True), reads=[F["l"], g_const], writes=[PB[bK]])
            P.add("pe", lambda e: e.matmul(pb[bK][:, 256:512], lhsT=Lm[:, 2 * d + 1, :], rhs=l_[:], start=True, stop=True), reads=[F["l"], g_const], writes=[PB[bK]])
            for p_ in range(2):
                P.add("pe", lambda e, p_=p_: e.matmul(pb[bA][:, 256 + p_:257 + p_], lhsT=l_[:, p_ * 128:(p_ + 1) * 128], rhs=ncol[:, 0:1], start=True, stop=True),
                      reads=[F["l"], g_const], writes=[PB[bA]])

        def s4():
            if full:
                P.add("act", lambda e: e.activation(out=eb_[:], in_=pb[bK][:, 0:256], func=AF.Exp), reads=[PB[bK]], writes=[F["eb"]])
                P.add("act", lambda e: e.activation(out=enb_[:], in_=pb[bK][:, 0:256], func=AF.Exp, scale=-1.0), reads=[PB[bK]], writes=[F["enb"]])
            P.add("act", lambda e: e.activation(out=er_[:], in_=pb[bK][:, 256:512], func=AF.Exp), reads=[PB[bK]], writes=[F["er"]])
            P.add("act", lambda e: e.activation(out=et_[:], in_=pb[bA][:, 256:258], func=AF.Exp), reads=[PB[bA]], writes=[Bb["et"]])

        def s5():
            if full:
                kq3 = kq_[:, 256:512].rearrange("p (a b) -> p a b", a=2)
                eb3 = eb_[:].rearrange("p (a b) -> p a b", a=2)
                for w in range(2):
                    P.add("dve", lambda e, w=w: e.scalar_tensor_tensor(out=qz_[:, :, w * 192:w * 192 + 64], in0=kq3[:, :, w * 64:(w + 1) * 64], scalar=0.125,
                                                                       in1=eb3[:, :, w * 64:(w + 1) * 64], op0=ALU.mult, op1=ALU.mult),
                          reads=[F["kqs"], F["eb"]], writes=[F["qt"]])
                P.add("dve", lambda e: e.tensor_tensor(out=kt_[:], in0=kq_[:, 0:256], in1=enb_[:], op=ALU.mult), reads=[F["kqs"], F["enb"]], writes=[F["kt"]])
            P.add("dve", lambda e: e.tensor_tensor(out=kh_[:], in0=kq_[:, 0:256], in1=er_[:], op=ALU.mult), reads=[F["kqs"], F["er"]], writes=[Bb["kh"]])

        def s6():
            if not full:
                return
            for j in range(4):
                P.add("pe", lambda e, j=j: e.transpose(Tps[:, j * 128:(j + 1) * 128], qz_[:, j // 2, (j % 2) * 128:(j % 2 + 1) * 128], ident[:]),
                      reads=[F["qt"], g_const], writes=[PB[bV]])
            for j in range(2):
                P.add("pe", lambda e, j=j: e.transpose(Tps[:, 512 + j * 128:512 + (j + 1) * 128], kt_[:, j * 128:(j + 1) * 128], ident[:]),
                      reads=[F["kt"], g_const], writes=[PB[bV]])

        def s7():
            if not full:
                return
            P.add("dve", lambda e: e.tensor_copy(out=T_[:], in_=Tps[:, 0:768]), reads=[PB[bV]], writes=[Bb["T"]])

        def s8():
            if not full:
                return
            for pr in range(2):
                P.add("pe", lambda e, pr=pr: e.matmul(pb[bA][:, pr * 256:(pr + 1) * 256], lhsT=T_[:, 512 + pr * 128:512 + (pr + 1) * 128], rhs=T_[:, pr * 256:(pr + 1) * 256],
                                                      start=True, stop=True), reads=[Bb["T"]], writes=[PB[bA]])

        def s9():
            if not full:
                return
            P.add("dve", lambda e: e.tensor_tensor(out=att_[:], in0=pb[bA][:, :], in1=maskT[:, d, :], op=ALU.mult), reads=[PB[bA], g_const], writes=[Bb["att"]])

        return [s1, s2, s3, s4, s5, s6, s7, s8, s9]

    def gla_back(d, bs, full):
        Bb = bB[bs]
        et_, kh_, vb_, T_, att_ = g_et[bs], g_kh[bs], g_vb[bs], g_T[bs], g_att[bs]
        if full:
            for h in range(4):
                pr = h // 2
                P.add("pe", lambda e, h=h, pr=pr: e.matmul(pb[6][:, h * 128:(h + 1) * 128], lhsT=T_[:, h * 128:(h + 1) * 128], rhs=Sb[d][:, pr, :],
                                                           start=True, stop=False), reads=[Bb["T"], bSb[d]], writes=[PB[6]])
                P.add("pe", lambda e, h=h: e.matmul(pb[6][:, h * 128:(h + 1) * 128], lhsT=att_[:, h * 128:(h + 1) * 128], rhs=vb_[:, h * 128:(h + 1) * 128],
                                                    start=False, stop=True), reads=[Bb["att"], Bb["vb"]], writes=[PB[6]])
        for pr in range(2):
            P.add("pe", lambda e, pr=pr: e.matmul(pb[7][:, pr * 256:(pr + 1) * 256], lhsT=kh_[:, pr * 128:(pr + 1) * 128], rhs=vb_[:, pr * 256:(pr + 1) * 256],
                                                  start=True, stop=True), reads=[Bb["kh"], Bb["vb"]], writes=[PB[7]])
        for pr in range(2):
            for w in range(2):
                rsl = slice(w * 64, (w + 1) * 64)
                c0 = pr * 256 + w * 128
                P.add("dve", lambda e, pr=pr, rsl=rsl, c0=c0: e.scalar_tensor_tensor(out=S[d][rsl, pr, :], in0=S[d][rsl, pr, :], scalar=et_[rsl, pr:pr + 1],
                                                                                      in1=pb[7][rsl, c0:c0 + 128], op0=ALU.mult, op1=ALU.add),
                      reads=[bS[d], Bb["et"], PB[7]], writes=[bS[d]])
        P.add("act", lambda e: e.copy(out=Sb[d][:], in_=S[d][:]), reads=[bS[d]], writes=[bSb[d]])

    def gla_tile(d, order, after_back, mid=None):
        st = {}
        for n_, j in enumerate(order):
            st[j] = gla_stages(d, n_ % 2, n_ % 2, slice(j * 128, (j + 1) * 128), b_hxT[j], True)
        c0_, c1_, c2_, c3_ = order
        for k in range(9):
            st[c0_][k]()
            st[c1_][k]()
        if mid is None:
            st[c2_][0]()
            st[c3_][0]()
        for n_, j in enumerate((c0_, c1_)):
            gla_back(d, n_, True)
            if mid is not None:
                mid(j)
            after_back(j)
        for k in range(0 if mid is not None else 1, 9):
            st[c2_][k]()
            st[c3_][k]()
        for n_, j in enumerate((c2_, c3_)):
            gla_back(d, n_, True)
            if mid is not None:
                mid(j)
            after_back(j)

    ctx_v = ctx_d.rearrange("(j p) f -> p j f", p=128)
    for j in range(2):
        P.add("sp", lambda e, j=j: e.dma_start(out=xt[0][:, j, :], in_=ctx_v[:, j, :]), writes=[b_xt[0][j]], dma=f"x0{j}")
    for j in range(2):
        front(xt[0][:, j, :], b_xt[0][j], cA1, cB1, hxT, slice(j * 128, (j + 1) * 128), b_hxT[j])
    gate_lowrank(256)
    for d_, order in ((0, (0, 1)), (1, (1, 0))):
        for j in order:
            cols = slice(j * 128, (j + 1) * 128)
            for st_ in gla_stages(d_, 0, 0, cols, b_hxT[j], False):
                st_()
            gla_back(d_, 0, False)

    if stage == 1:
        return finish()
    x_v = x_d.rearrange("(n j p) f -> n p j f", p=128, j=4)
    ob_v = ob_d.rearrange("(n p) f -> n p f", p=128)
    x1_v = x1_d.rearrange("(n p) f -> n p f", p=128)

    def load_x(i, slot):
        for j in range(4):
            P.add("sp", lambda e, j=j: e.dma_start(out=xt[slot][:, j, :], in_=x_v[i, :, j, :]), writes=[b_xt[slot][j]], dma=f"x{slot}{j}")

    tseq = [i for i in range(NT - 1, -1, -1)] + [i for i in range(NT)]
    tcount = 0
    load_x(tseq[0], 0)
    for i in range(NT - 1, -1, -1):
        slot = tcount % 2
        tcount += 1
        if tcount < len(tseq) and stage >= 3 or tcount < NT:
            load_x(tseq[tcount], tcount % 2)
        front4(slot)
        gate_lowrank()
        def after_a(j, i=i):
            n = i * 4 + j
            P.add("act", lambda e, j=j: e.copy(out=obst[j % 2][:], in_=pb[6][:, :]), reads=[PB[6]], writes=[b_obst[j % 2]])
            P.add("sp", lambda e, j=j, n=n: e.dma_start(out=ob_v[n, :, :], in_=obst[j % 2][:]), reads=[b_obst[j % 2]], writes=[b_ob_d[n]], dma=f"obst{j % 2}")

        gla_tile(1, (3, 2, 1, 0), after_a)

    if stage == 2:
        return finish()
    hx2_v = hx2_d.rearrange("k p t -> p k t")
    for i in range(NT):
        slot = tcount % 2
        tcount += 1
        if tcount < len(tseq):
            load_x(tseq[tcount], tcount % 2)
        front4(slot)
        gate_lowrank()
        def sc_step(j):
            s2 = j % 2
            proj_fm(64 + j * 128, 128, 0)
            P.add("act", lambda e, s2=s2: e.copy(out=sbT[s2][:], in_=pb[0][:, :]), reads=[PB[0]], writes=[b_sbT[s2]])
            proj_fm(64 + 512 + j * 128, 128, 1)
            P.add("act", lambda e, s2=s2: e.copy(out=scT[s2][:], in_=pb[1][:, :]), reads=[PB[1]], writes=[b_scT[s2]])
            proj_fm(64 + 1024 + j * 128, 128, 2)
            P.add("dve", lambda e, j=j, s2=s2: e.tensor_tensor(out=zpad[:, j, :, 1:65], in0=pb[2][:, :].rearrange("p (r c) -> p r c", r=8),
                                                               in1=scT[s2][:].rearrange("p (r c) -> p r c", r=8), op=ALU.mult),
                  reads=[PB[2], b_scT[s2]], writes=[b_zp[j]])
            for t in range(3):
                P.add("pe", lambda e, j=j, t=t: e.matmul(pb[3][:, :], lhsT=dsc[:, t * 4 + j, :], rhs=zpad[:, j, :, t:t + 64], start=(t == 0), stop=(t == 2)),
                      reads=[b_zp[j], g_const], writes=[PB[3]])
            P.add("dve", lambda e, j=j, s2=s2: e.scalar_tensor_tensor(out=yxT[:, 4 + j, :], in0=pb[3][:, :], scalar=colp[:, Q_BSC + j:Q_BSC + j + 1], in1=sbT[s2][:],
                                                                      op0=ALU.add, op1=ALU.mult), reads=[PB[3], b_sbT[s2], g_const], writes=[b_yxs[j]])
        sgt = (tmpf, osum)
        sgb = (bg["tmpf"], bg["osum"])
        for j in range(4):
            cols = slice(j * 128, (j + 1) * 128)
            proj_tm(cols, 2, j, b_hxT[j])
            P.add("act", lambda e, j=j: e.activation(out=sgt[j % 2][:], in_=pb[j][:, :], func=AF.Silu), reads=[PB[j]], writes=[sgb[j % 2]])
            P.add("dve", lambda e, j=j: e.tensor_tensor(out=ghs4[:, j, :], in0=sgt[j % 2][:], in1=ghead[:], op=ALU.mult), reads=[sgb[j % 2], g_const], writes=[b_ghs[j]])

        def load_ob(j, i=i):
            n = i * 4 + j
            P.add("sp", lambda e, j=j, n=n: e.dma_start(out=obl[j % 2][:], in_=ob_v[n, :, :]), reads=[b_ob_d[n]], writes=[b_obl[j % 2]], dma=f"obl{j % 2}")

        load_ob(0)
        load_ob(1)

        def after_b(j, i=i):
            cols = slice(j * 128, (j + 1) * 128)
            P.add("dve", lambda e, j=j: e.tensor_tensor(out=osum[:], in0=pb[6][:, :], in1=obl[j % 2][:], op=ALU.add), reads=[PB[6], b_obl[j % 2]], writes=[bg["osum"]])
            if j < 2:
                load_ob(j + 2)
            c0, bs_ = stat_slot()
            P.add("dve", lambda e, c0=c0: e.memset(stat[:, c0:c0 + 8], 0.0), writes=[bs_])
            for h in range(4):
                P.add("act", lambda e, h=h, c0=c0: e.activation(out=junk[:, h * 128:(h + 1) * 128], in_=osum[:, h * 128:(h + 1) * 128], func=AF.Square,
                                                                accum_out=stat[:, c0 + h:c0 + h + 1]), reads=[bg["osum"], bs_], writes=[bg["junk"], bs_])
            rstd_from(stat[:, c0:c0 + 4], 128, stat[:, c0 + 4:c0 + 8], [bs_], [bs_])
            for h in range(4):
                P.add("dve", lambda e, h=h, c0=c0, j=j: e.scalar_tensor_tensor(out=yg[:, h * 128:(h + 1) * 128], in0=osum[:, h * 128:(h + 1) * 128],
                                                                               scalar=stat[:, c0 + 4 + h:c0 + 5 + h], in1=ghs4[:, j, h * 128:(h + 1) * 128],
                                                                               op0=ALU.mult, op1=ALU.mult), reads=[bg["osum"], bs_, b_ghs[j]], writes=[bg["yg"]])
            Tps = pb[7].bitcast(BF16)
            for h in range(4):
                P.add("pe", lambda e, h=h: e.transpose(Tps[:, h * 128:(h + 1) * 128], yg[:, h * 128:(h + 1) * 128], ident[:]), reads=[bg["yg"], g_const], writes=[PB[7]])
            P.add("act", lambda e, cols=cols: e.copy(out=yxT[:, 0:4, cols], in_=Tps[:, 0:512].rearrange("p (k t) -> p k t", k=4)), reads=[PB[7]], writes=[b_yxg[j]])

        def oproj_stages(j, i=i, slot=slot):
            cols = slice(j * 128, (j + 1) * 128)
            n = i * 4 + j
            ob_ = 3 * (j % 2)
            c0, bs_ = stat_slot()
            tmp2 = (tmpf, osum)
            tmpb = (bg["tmpf"], bg["osum"])

            def o1():
                for hf in range(2):
                    for k in range(8):
                        rb = [b_yxg[j]] if k < 4 else [b_yxs[k - 4]]
                        P.add("pe", lambda e, k=k, hf=hf: e.matmul(pb[ob_ + hf][:, :], lhsT=yxT[:, k, cols], rhs=wout[:, k, hf * 512:(hf + 1) * 512],
                                                                   start=(k == 0), stop=(k == 7)), reads=rb + [b_wout], writes=[PB[ob_ + hf]])

            def o2():
                P.add("dve", lambda e: e.memset(stat[:, c0:c0 + 4], 0.0), writes=[bs_])
                for hf in range(2):
                    P.add("act", lambda e, hf=hf: e.activation(out=junk[:, hf * 512:(hf + 1) * 512], in_=pb[ob_ + hf][:, :], func=AF.Square,
                                                               accum_out=stat[:, c0 + hf:c0 + hf + 1]), reads=[PB[ob_ + hf], bs_], writes=[bg["junk"], bs_])
                P.add("dve", lambda e: e.tensor_tensor(out=stat[:, c0 + 2:c0 + 3], in0=stat[:, c0:c0 + 1], in1=stat[:, c0 + 1:c0 + 2], op=ALU.add),
                      reads=[bs_], writes=[bs_])
                rstd_from(stat[:, c0 + 2:c0 + 3], D, stat[:, c0 + 3:c0 + 4], [bs_], [bs_])

            def o3():
                for hf in range(2):
                    hs = slice(hf * 512, (hf + 1) * 512)
                    P.add("dve", lambda e, hf=hf, hs=hs: e.scalar_tensor_tensor(out=tmp2[hf][:], in0=pb[ob_ + hf][:, :], scalar=stat[:, c0 + 3:c0 + 4], in1=G1[:, hs],
                                                                               op0=ALU.mult, op1=ALU.mult), reads=[PB[ob_ + hf], bs_, g_const], writes=[tmpb[hf]])
                    P.add("dve", lambda e, hf=hf, hs=hs: e.tensor_tensor(out=xt[slot][:, j, hs], in0=xt[slot][:, j, hs], in1=tmp2[hf][:], op=ALU.add),
                          reads=[tmpb[hf], b_xt[slot][j]], writes=[b_xt[slot][j]])
                P.add("sp", lambda e: e.dma_start(out=x1_v[n, :, :], in_=xt[slot][:, j, :]), reads=[b_xt[slot][j]], writes=[b_x1_d[n]], dma=f"x{slot}{j}")

            fs = front_stages(xt[slot][:, j, :], b_xt[slot][j], A2, B2, hx2s[j % 2], slice(0, 128), b_hx2s[j % 2], banks=(2, 5))

            def o7():
                fs[3]()
                P.add("sp", lambda e: e.dma_start(out=hx2_v[:, :, 64 + i * 512 + j * 128:64 + i * 512 + (j + 1) * 128], in_=hx2s[j % 2][:]),
                      reads=[b_hx2s[j % 2]], writes=[b_hx2_d[i]], dma=f"hx2s{j % 2}")

            return [o1, o2, o3, fs[0], fs[1], fs[2], o7]

        if fcnt[0] % 2:
            fcnt[0] += 1
        OS = [oproj_stages(j) for j in range(4)]

        def mid_b(j):
            if j == 0:
                sc_step(0)
                sc_step(1)
            elif j == 1:
                sc_step(2)
                sc_step(3)
            elif j == 2:
                OS[0][0]()
            else:
                OS[0][5]()
                OS[1][0]()

        def after_b2(j):
            after_b(j)
            if j == 2:
                for k in (1, 2, 3, 4):
                    OS[0][k]()
            elif j == 3:
                OS[0][6]()
                for k in (1, 2, 3, 4):
                    OS[1][k]()

        gla_tile(0, (0, 1, 2, 3), after_b2, mid=mid_b)
        order = [(2, 0), (1, 5), (3, 0), (1, 6), (2, 1), (2, 2), (2, 3), (3, 1), (2, 4), (3, 2), (2, 5), (3, 3), (2, 6), (3, 4), (3, 5), (3, 6)]
        for (j, k) in order:
            OS[j][k]()
    P.barrier()
    if stage == 3:
        return finish()

    Zc = Alloc(nc, OV, SB_HI)
    wdn = Zc.t([128, NFC, D], BF16, "wdn")
    wupS = [Zc.t([128, 8, 512], BF16, "wupS") for _ in range(2)]
    hw = [Zc.t([128, 8, 640], BF16, "hw") for _ in range(2)]
    x1t = [Zc.t([128, 4, D], F32, "x1t") for _ in range(2)]
    hT = Zc.t([128, NFC, 512], BF16, "hT")
    upad = [Zc.t([128, 10, 66], BF16, "upad") for _ in range(2)]
    gts = [Zc.t([128, 512], BF16, "gts") for _ in range(2)]
    sil = [Zc.t([128, 512], BF16, "sil") for _ in range(2)]
    dg = [Zc.t([128, 9, 128], BF16, "dg") for _ in range(2)]
    tmpc = Zc.t([128, 512], F32, "tmpc")
    junkc = Zc.t([128, 512], BF16, "junkc")
    b_wdn = Buf("wdn")
    b_wupS = [Buf("wupS0"), Buf("wupS1")]
    b_hw = [Buf("hw0"), Buf("hw1")]
    b_x1t = [[Buf(f"x1t{s}{j}") for j in range(4)] for s in range(2)]
    b_hT = [Buf(f"hT{c}") for c in range(NFC)]
    b_upad = [Buf("upad0"), Buf("upad1")]
    b_upc = [Buf("upc0"), Buf("upc1")]
    ucar = Zc.t([128, NFC, 2, 64], BF16, "ucar")
    b_ucar = [Buf(f"ucar{c}") for c in range(NFC)]
    b_gts = [Buf("gts0"), Buf("gts1")]
    b_sil = [Buf("sil0"), Buf("sil1")]
    b_dg = [Buf("dg0"), Buf("dg1")]
    b_tmpc, b_junkc, b_statc = Buf("tmpc"), Buf("junkc"), Buf("statc")
    b_out = []

    wdn_v = wdown_d.rearrange("(c p) j -> p c j", p=128)
    for c in range(NFC):
        P.add("pool", lambda e, c=c: e.dma_start(out=wdn[:, c, :], in_=wdn_v[:, c, :]), writes=[b_wdn], dma="wdn", ndma=1)
    for s in range(2):
        P.add("dve", lambda e, s=s: e.memset(upad[s][:], 0.0), writes=[b_upad[s]])
    out_v = out_d.rearrange("(n p) f -> n p f", p=128)
    x1_v4 = x1_d.rearrange("(n j p) f -> n p j f", p=128, j=4)
    def load_c(i):
        slot = i % 2
        P.add("sp", lambda e: e.dma_start(out=hw[slot][:], in_=hx2_v[:, :, i * 512:i * 512 + 640]),
              reads=b_hx2_d[max(0, i - 1):i + 2] + [b_hx2], writes=[b_hw[slot]], dma=f"hw{slot}")
        for j in range(4):
            P.add("sp", lambda e, j=j: e.dma_start(out=x1t[slot][:, j, :], in_=x1_v4[i, :, j, :]), reads=[b_x1_d[i * 4 + j]],
                  writes=[b_x1t[slot][j]], dma=f"x1t{slot}{j}")

    NPC = NFC // 2

    def load_piece(g):
        ps = g % 2
        pc = g % NPC
        P.add("sp", lambda e: e.dma_start(out=wupS[ps][:], in_=wupb_d[pc, :, :, :]), reads=[b_wupb], writes=[b_wupS[ps]], dma=f"wupS{ps}")

    load_c(0)
    load_piece(0)
    piece = 0
    for i in range(NT):
        slot = i % 2
        for c in range(NFC):
            s2 = c % 2
            if c % 2 == 0:
                ps = piece % 2
                piece += 1
                if piece < NT * NPC:
                    load_piece(piece)
                if c == 2 and i + 1 < NT:
                    load_c(i + 1)
            wo = (c % 2) * 128
            for t in range(9):
                P.add("dve", lambda e, c=c, t=t, s2=s2: e.tensor_scalar(out=dg[s2][:, t, :], in0=ident[:], scalar1=colp[:, Q_WCF + c * 9 + t:Q_WCF + c * 9 + t + 1],
                                                                         scalar2=None, op0=ALU.mult), reads=[g_const], writes=[b_dg[s2]])
            ucol = 0 if i == 0 else 128
            for k in range(8):
                P.add("pe", lambda e, k=k, s2=s2, ps=ps, wo=wo, slot=slot, ucol=ucol: e.matmul(pb[s2][:, :], lhsT=wupS[ps][:, k, wo:wo + 128], rhs=hw[slot][:, k, ucol:ucol + 512],
                                                                                           start=(k == 0), stop=(k == 7)),
                      reads=[b_wupS[ps], b_hw[slot]], writes=[PB[s2]])
            if i == 0:
                for k in range(8):
                    P.add("pe", lambda e, k=k, s2=s2, ps=ps, wo=wo, slot=slot: e.matmul(pb[2 + s2][:, 0:128], lhsT=wupS[ps][:, k, wo:wo + 128], rhs=hw[slot][:, k, 512:640], start=(k == 0), stop=(k == 7)),
                          reads=[b_wupS[ps], b_hw[slot]], writes=[PB[2 + s2]])
            for k in range(8):
                P.add("pe", lambda e, k=k, s2=s2, ps=ps, wo=wo, slot=slot: e.matmul(pb[4 + s2][:, :], lhsT=wupS[ps][:, k, 256 + wo:256 + wo + 128], rhs=hw[slot][:, k, 64:576], start=(k == 0), stop=(k == 7)),
                      reads=[b_wupS[ps], b_hw[slot]], writes=[PB[4 + s2]])
            if i == 0:
                P.add("act", lambda e, s2=s2: e.copy(out=upad[s2][:, 0:8, 1:65], in_=pb[s2][:, :].rearrange("p (r c) -> p r c", r=8)), reads=[PB[s2]], writes=[b_upc[s2], b_upad[s2]])
                P.add("act", lambda e, s2=s2: e.copy(out=upad[s2][:, 8:10, 1:65], in_=pb[2 + s2][:, 0:128].rearrange("p (r c) -> p r c", r=2)), reads=[PB[2 + s2]], writes=[b_upad[s2]])
            else:
                P.add("pool", lambda e, c=c, s2=s2: e.tensor_copy(out=upad[s2][:, 0:2, 1:65], in_=ucar[:, c, :, :]), reads=[b_ucar[c]], writes=[b_upc[s2]])
                P.add("act", lambda e, s2=s2: e.copy(out=upad[s2][:, 2:10, 1:65], in_=pb[s2][:, :].rearrange("p (r c) -> p r c", r=8)), reads=[PB[s2]], writes=[b_upad[s2]])
            if i < NT - 1:
                P.add("pool", lambda e, c=c, s2=s2: e.tensor_copy(out=ucar[:, c, :, :], in_=upad[s2][:, 8:10, 1:65]), reads=[b_upad[s2]], writes=[b_ucar[c]])
            P.add("dve", lambda e, s2=s2: e.tensor_copy(out=gts[s2][:], in_=pb[4 + s2][:, :]), reads=[PB[4 + s2]], writes=[b_gts[s2]])
            for t in range(9):
                dr, dc = t // 3, t % 3
                P.add("pe", lambda e, t=t, dr=dr, dc=dc, s2=s2: e.matmul(pb[6 + s2][:, :], lhsT=dg[s2][:, t, :], rhs=upad[s2][:, dr:dr + 8, dc:dc + 64], start=(t == 0), stop=(t == 8)),
                      reads=[b_dg[s2], b_upad[s2], b_upc[s2]], writes=[PB[6 + s2]])
            P.add("act", lambda e, c=c, s2=s2: e.activation(out=sil[s2][:], in_=pb[6 + s2][:, :], func=AF.Silu, bias=colp[:, Q_BCF + c:Q_BCF + c + 1]),
                  reads=[PB[6 + s2], g_const], writes=[b_sil[s2]])
            P.add("dve", lambda e, c=c, s2=s2: e.tensor_tensor(out=hT[:, c, :], in0=sil[s2][:], in1=gts[s2][:], op=ALU.mult), reads=[b_sil[s2], b_gts[s2]], writes=[b_hT[c]])
        for j in range(4):
            cols = slice(j * 128, (j + 1) * 128)
            n = i * 4 + j
            for hf in range(2):
                bank = (2 * j + hf) % 8
                for c in range(NFC):
                    P.add("pe", lambda e, c=c, hf=hf, bank=bank, cols=cols: e.matmul(pb[bank][:, :], lhsT=hT[:, c, cols], rhs=wdn[:, c, hf * 512:(hf + 1) * 512],
                                                                                      start=(c == 0), stop=(c == NFC - 1)), reads=[b_hT[c], b_wdn], writes=[PB[bank]])
            c0, _unused = stat_slot()
            P.add("dve", lambda e, c0=c0: e.memset(stat[:, c0:c0 + 4], 0.0), writes=[b_statc])
            for hf in range(2):
                bank = (2 * j + hf) % 8
                P.add("act", lambda e, hf=hf, bank=bank, c0=c0: e.activation(out=junkc[:], in_=pb[bank][:, :], func=AF.Square, accum_out=stat[:, c0 + hf:c0 + hf + 1]),
                      reads=[PB[bank], b_statc], writes=[b_junkc, b_statc])
            P.add("dve", lambda e, c0=c0: e.tensor_tensor(out=stat[:, c0 + 2:c0 + 3], in0=stat[:, c0:c0 + 1], in1=stat[:, c0 + 1:c0 + 2], op=ALU.add),
                  reads=[b_statc], writes=[b_statc])
            rstd_from(stat[:, c0 + 2:c0 + 3], D, stat[:, c0 + 3:c0 + 4], [b_statc], [b_statc])
            for hf in range(2):
                bank = (2 * j + hf) % 8
                hs = slice(hf * 512, (hf + 1) * 512)
                P.add("dve", lambda e, hf=hf, bank=bank, hs=hs, c0=c0: e.scalar_tensor_tensor(out=tmpc[:], in0=pb[bank][:, :], scalar=stat[:, c0 + 3:c0 + 4], in1=G2[:, hs],
                                                                                             op0=ALU.mult, op1=ALU.mult), reads=[PB[bank], b_statc, g_const], writes=[b_tmpc])
                P.add("dve", lambda e, j=j, hs=hs, slot=slot: e.tensor_tensor(out=x1t[slot][:, j, hs], in0=x1t[slot][:, j, hs], in1=tmpc[:], op=ALU.add),
                      reads=[b_tmpc, b_x1t[slot][j]], writes=[b_x1t[slot][j]])
            bo = Buf(f"out{n}")
            tok = P.add("sp", lambda e, j=j, n=n, slot=slot: e.dma_start(out=out_v[n, :, :], in_=x1t[slot][:, j, :]), reads=[b_x1t[slot][j]], writes=[bo], dma=f"x1t{slot}{j}")
            b_out.append(tok)
    last = {}
    for t in b_out:
        last[t[0]] = max(last.get(t[0], 0), t[1])
    fin = list(last.items())
    if debug:
        for k, v in P.dma_cum.items():
            fin.append(("dma:" + k, v))
    P.final_wait("sp", fin)
    P.emit(nc)
    return nc


def _consts():
    c = np.zeros((128, NCST), np.float32)
    m = np.arange(128)[:, None]
    t = np.arange(128)[None, :]
    c[:, K_ID:K_ID + 128] = (m == t)
    c[:, K_LFI:K_LFI + 128] = (m <= t) * (-1.0 / 16)
    c[:, K_LFR:K_LFR + 128] = (m > t) * (-1.0 / 16)
    c[:, K_LBI:K_LBI + 128] = (m >= t) * (-1.0 / 16)
    c[:, K_LBR:K_LBR + 128] = (m < t) * (-1.0 / 16)
    c[:, K_MF:K_MF + 128] = (m <= t)
    c[:, K_MB:K_MB + 128] = (m > t)
    c[:, K_NC] = -1.0 / 16
    return c


def _colmaj(v, n):
    return np.ascontiguousarray(np.asarray(v, np.float32).reshape(n, 128).T)


def make_in_maps(inputs, NT=16, ncores=8):
    f = lambda a: np.asarray(a, np.float32)
    w_in = f(inputs["w_in"])[0]
    w_tm = np.ascontiguousarray(np.concatenate([w_in[:, C_K:C_V], w_in[:, C_Q:C_OG], w_in[:, C_V:C_AF], w_in[:, C_OG:C_SB]], axis=1))
    gate = np.zeros((D, 64), np.float32)
    gate[:, 0:16] = w_in[:, C_AF:C_AB]
    gate[:, 32:48] = w_in[:, C_AB:C_Q]
    w_fm = np.ascontiguousarray(np.concatenate([gate, w_in[:, C_SB:C_SC], w_in[:, C_SC:C_SX], w_in[:, C_SX:]], axis=1))
    wg = np.zeros((96, 512), np.float32)
    wg[0:16, 0:256] = f(inputs["w_af"])[0]
    wg[32:48, 256:512] = f(inputs["w_ab"])[0]
    wg[64, 0:256] = f(inputs["b_af"])[0]
    wg[64, 256:512] = f(inputs["b_ab"])[0]
    rows1 = np.concatenate([f(inputs["g_pre_mix"])[0], f(inputs["g_post_mix"])[0], f(inputs["g_pre_ffn"])[0], f(inputs["g_post_ffn"])[0],
                            np.tile(f(inputs["g_head"])[0], 4), f(inputs["b_ada"])[0]])
    rows = np.ascontiguousarray(np.broadcast_to(rows1[None, :], (128, NROW)))
    w_sc = f(inputs["w_sc"])[0]
    w_cf = f(inputs["w_cf"])[0].reshape(9, DFF)
    cols_common = np.concatenate([
        _colmaj(f(inputs["b_sc"])[0], 4), _colmaj(f(inputs["b_cf"])[0], NFC),
        np.concatenate([_colmaj(w_sc[t], 4) for t in range(3)], axis=1),
        np.stack([_colmaj(w_cf[t], NFC) for t in range(9)], axis=2).reshape(128, NFC * 9),
    ], axis=1)
    cst = _consts()
    maps = []
    x = inputs["x"]
    for b in range(ncores):
        cols = np.ascontiguousarray(np.concatenate([_colmaj(f(inputs["c"])[b], 8), _colmaj(f(inputs["c_ctx"]), 8), cols_common], axis=1))
        maps.append({
            "x": np.ascontiguousarray(f(x[b])[:NT * 512]), "ctx": np.ascontiguousarray(f(inputs["ctx"][b])),
            "rows": rows, "cols": cols, "cst": cst,
            "w_ada": f(inputs["w_ada"])[0], "w_tm": w_tm, "w_fm": w_fm, "w_gate": wg,
            "w_out": f(inputs["w_out"])[0], "w_up": f(inputs["w_up"])[0], "w_down": f(inputs["w_down"])[0],
        })
    return maps


_NC_CACHE = {}


def kernel(**inputs):
    NT = 16
    if NT not in _NC_CACHE:
        _NC_CACHE[NT] = build(NT)
    nc = _NC_CACHE[NT]
    maps = make_in_maps(inputs, NT, 8)
    res = run_bass_kernel_spmd(nc, maps, core_ids=list(range(8)))
    return np.stack([np.asarray(r["out"], np.float32).reshape(NT * 512, D) for r in res.results], axis=0)
```

```python
import os
import numpy as np
import concourse.bass as bass
import concourse.mybir as mybir
from concourse.bass_utils import run_bass_kernel_spmd

F32 = mybir.dt.float32
BF16 = mybir.dt.bfloat16
AF = mybir.ActivationFunctionType
ALU = mybir.AluOpType

D = 1024
DFF = 2816
NFC = 22
CTX = 256
EPS = 1e-6
SB_LO = 16512
SB_HI = 229344
EPOCH = 24000
STRICT_SYNC = False

C_K, C_V, C_AF, C_AB, C_Q, C_OG, C_SB, C_SC, C_SX = 0, 256, 768, 784, 800, 1056, 1568, 2080, 2592

K_ID, K_LFI, K_LFR, K_LBI, K_LBR, K_MF, K_MB, K_NC = 0, 128, 256, 384, 512, 640, 768, 896
NCST = 897
R_GPM, R_GQM, R_GPF, R_GQF, R_GH, R_BADA = 0, 1024, 2048, 3072, 4096, 4608
NROW = 4608 + 6144
Q_C, Q_CC, Q_BSC, Q_BCF, Q_WSC, Q_WCF = 0, 8, 16, 20, 42, 54
NCOL = 54 + 198


class Buf:
    __slots__ = ("name", "w", "r")

    def __init__(self, name):
        self.name = name
        self.w = None
        self.r = {}


class Prog:
    ENGS = ("pe", "act", "dve", "pool", "sp")

    def __init__(self):
        self.ops = {e: [] for e in self.ENGS}
        self.waited = {e: {} for e in self.ENGS}
        self.dma_cum = {}
        self.n = 0

    def _need(self, eng, tok, waits):
        key, val = tok
        if self.waited[eng].get(key, -1) >= val:
            return
        self.waited[eng][key] = val
        waits.append(tok)
        if not key.startswith("dma:"):
            self.ops[key][val]["sig"] = True

    def add(self, eng, fn, reads=(), writes=(), dma=None, ndma=1):
        idx = len(self.ops[eng])
        waits = []
        mykey = ("dma:" + dma) if dma is not None else eng
        for b in reads:
            if b.w is not None:
                if b.w[0] == eng and eng == "pe" and dma is None:
                    continue
                self._need(eng, b.w, waits)
        relax = (eng == "pe" or dma is not None) if STRICT_SYNC else True
        for b in writes:
            if b.w is not None and not (b.w[0] == mykey and relax):
                self._need(eng, b.w, waits)
            for k, v in b.r.items():
                if k == mykey and relax:
                    continue
                self._need(eng, (k, v), waits)
        if dma is not None:
            cum = self.dma_cum.get(dma, 0) + 16 * ndma
            self.dma_cum[dma] = cum
            tok = ("dma:" + dma, cum)
        else:
            tok = (eng, idx)
        for b in reads:
            if b.r.get(tok[0], -1) < tok[1]:
                b.r[tok[0]] = tok[1]
        for b in writes:
            b.w = tok
            b.r = {}
        self.ops[eng].append({"fn": fn, "waits": waits, "sig": False, "dma": dma})
        self.n += 1
        return tok

    def barrier(self, skip=()):
        toks = []
        for e in self.ENGS:
            if self.ops[e]:
                for i in range(len(self.ops[e]) - 1, -1, -1):
                    if self.ops[e][i]["dma"] is None and self.ops[e][i]["fn"] is not None:
                        toks.append((e, i))
                        break
        for k, v in self.dma_cum.items():
            if k in skip:
                continue
            toks.append(("dma:" + k, v))
        for e in self.ENGS:
            waits = []
            for t in toks:
                if t[0] == e:
                    continue
                self._need(e, t, waits)
            self.ops[e].append({"fn": None, "waits": waits, "sig": False, "dma": None})

    def final_wait(self, eng, toks):
        waits = []
        for t in toks:
            self._need(eng, t, waits)
        self.ops[eng].append({"fn": None, "waits": waits, "sig": False, "dma": None})

    def emit(self, nc):
        engsem = {}
        signum = {}
        for e in self.ENGS:
            cnt = 0
            signum[e] = {}
            for i, op in enumerate(self.ops[e]):
                if op["sig"]:
                    signum[e][i] = cnt
                    cnt += 1
            nep = cnt // EPOCH + 1
            engsem[e] = [nc.alloc_semaphore(f"s_{e}_{j}") for j in range(nep)]
        dmasem = {k: nc.alloc_semaphore("d_" + k) for k in self.dma_cum}

        def run(ename, eng):
            for i, op in enumerate(self.ops[ename]):
                for key, val in op["waits"]:
                    if key.startswith("dma:"):
                        eng.wait_ge(dmasem[key[4:]], val)
                    else:
                        s = signum[key][val]
                        eng.wait_ge(engsem[key][s // EPOCH], s % EPOCH + 1)
                if op["fn"] is None:
                    continue
                ins = op["fn"](eng)
                if op["dma"] is not None:
                    ins.then_inc(dmasem[op["dma"]], 16)
                elif op["sig"]:
                    s = signum[ename][i]
                    ins.then_inc(engsem[ename][s // EPOCH], 1)

        with nc.Block() as block:
            @block.tensor
            def _(eng):
                run("pe", eng)

            @block.scalar
            def _(eng):
                run("act", eng)

            @block.vector
            def _(eng):
                run("dve", eng)

            @block.gpsimd
            def _(eng):
                run("pool", eng)

            @block.sync
            def _(eng):
                run("sp", eng)


class Alloc:
    def __init__(self, nc, lo, hi):
        self.nc, self.lo, self.hi, self.p, self.n = nc, lo, hi, lo, 0

    def t(self, shape, dt, name="t"):
        nb = 1
        for s in shape[1:]:
            nb *= s
        nb *= 2 if dt == BF16 else 4
        nb = (nb + 31) // 32 * 32
        off = self.p
        self.p += nb
        assert self.p <= self.hi, f"SBUF overflow {name} {self.p} > {self.hi}"
        self.n += 1
        return self.nc.alloc_sbuf_tensor_at(f"{name}{self.n}", list(shape), dt, offset=off)


def build(NT=16, debug=False, stage=9):
    nc = bass.Bass("TRN2", target_bir_lowering=False)
    SEQ = NT * 512
    P = Prog()

    def finish():
        fin = [("dma:" + k, v) for k, v in P.dma_cum.items()]
        P.final_wait("sp", fin)
        P.emit(nc)
        return nc

    def din(name, shape, dt=F32):
        return nc.dram_tensor(name, list(shape), dt, kind="ExternalInput").ap()

    x_d = din("x", [SEQ, D])
    ctx_d = din("ctx", [CTX, D])
    rows_d = din("rows", [128, NROW])
    cols_d = din("cols", [128, NCOL])
    cst_d = din("cst", [128, NCST])
    wada_d = din("w_ada", [D, 6 * D])
    wtm_d = din("w_tm", [D, 1536])
    wfm_d = din("w_fm", [D, 1600])
    wgate_d = din("w_gate", [96, 512])
    wout_d = din("w_out", [D, D])
    wup_d = din("w_up", [D, 2 * DFF])
    wdown_d = din("w_down", [DFF, D])
    out_d = nc.dram_tensor("out", [SEQ, D], F32, kind="ExternalOutput").ap()
    if debug:
        ob_d = nc.dram_tensor("ob_d", [SEQ, 512], F32, kind="ExternalOutput").ap()
        x1_d = nc.dram_tensor("x1_d", [SEQ, D], F32, kind="ExternalOutput").ap()
    else:
        ob_d = nc.dram_tensor("ob_d", [SEQ, 512], F32).ap()
        x1_d = nc.dram_tensor("x1_d", [SEQ, D], F32).ap()
    hx2_d = nc.dram_tensor("hx2_d", [8, 128, SEQ + 128], BF16).ap()
    wupb_d = nc.dram_tensor("wupb_d", [NFC // 2, 128, 8, 512], BF16).ap()

    pb = [nc.alloc_psum_tensor(f"pb{i}", [128, 512], F32) for i in range(8)]
    PB = [Buf(f"pb{i}") for i in range(8)]

    G = Alloc(nc, SB_LO, SB_HI)
    ident = G.t([128, 128], BF16, "ident")
    Lm = G.t([128, 4, 128], BF16, "Lm")
    maskT = G.t([128, 2, 512], BF16, "maskT")
    ncol = G.t([128, 1], BF16, "ncol")
    ones_r = G.t([1, 128], BF16, "ones_r")
    A1 = G.t([128, D], F32, "A1")
    cA1 = G.t([128, D], F32, "cA1")
    G1 = G.t([128, D], F32, "G1")
    A2 = G.t([128, D], F32, "A2")
    G2 = G.t([128, D], F32, "G2")
    ghead = G.t([128, 512], F32, "ghead")
    B1 = G.t([1, D], BF16, "B1")
    cB1 = G.t([1, D], BF16, "cB1")
    B2 = G.t([1, D], BF16, "B2")
    colp = G.t([128, NCOL], F32, "colp")
    wgate = G.t([96, 512], BF16, "wgate")
    dsc = G.t([128, 12, 128], BF16, "dsc")
    S = [G.t([128, 2, 128], F32, "S") for _ in range(2)]
    Sb = [G.t([128, 2, 128], BF16, "Sb") for _ in range(2)]
    alT = G.t([96, 512], BF16, "alT")
    stat = G.t([128, 64], F32, "stat")
    fstat = G.t([128, 8], F32, "fstat")
    b_fst = [Buf(f"fst{i}") for i in range(4)]
    epsc = G.t([128, 1], F32, "epsc")
    g_const = Buf("consts")
    bS = [Buf("S0"), Buf("S1")]
    bSb = [Buf("Sb0"), Buf("Sb1")]
    OV = G.p

    Z = Alloc(nc, OV, SB_HI)
    cstf = Z.t([128, NCST], F32, "cstf")
    rowp = Z.t([128, NROW], F32, "rowp")
    wa = [Z.t([128, 8, D], F32, "wa") for _ in range(2)]
    modr = [Z.t([128, D], F32, "modr") for _ in range(2)]
    rep = Z.t([128, 16, 128], F32, "rep")
    onesf = Z.t([128, 128], F32, "onesf")
    scl = Z.t([128, 16], F32, "scl")
    wgf = Z.t([96, 512], F32, "wgf")
    zt = Z.t([128, 64], BF16, "zt")
    b_cst, b_row, b_col, b_wgf = Buf("cstf"), Buf("rowp"), Buf("colp"), Buf("wgf")
    b_wa = [Buf("wa0"), Buf("wa1")]
    b_modr = [Buf("modr0"), Buf("modr1")]
    b_rep, b_scl, b_zt = Buf("rep"), Buf("scl"), Buf("zt")

    P.add("sp", lambda e: e.dma_start(out=cstf[:], in_=cst_d[:, :]), writes=[b_cst], dma="ld0")
    P.add("sp", lambda e: e.dma_start(out=colp[:], in_=cols_d[:, :]), writes=[b_col], dma="ld1")
    P.add("sp", lambda e: e.dma_start(out=wgf[:], in_=wgate_d[:, :]), writes=[b_wgf], dma="ld2")
    P.add("sp", lambda e: e.dma_start(out=rowp[:, 0:4608], in_=rows_d[:, 0:4608]), writes=[b_row], dma="ld3", ndma=2)
    P.add("sp", lambda e: e.dma_start(out=rowp[:, 4608:NROW], in_=rows_d[:, 4608:NROW]), writes=[b_row], dma="ld3", ndma=0)

    b_wupb = Buf("wupb")
    wup_v = wup_d.rearrange("(k p) j -> p k j", p=128)
    for pc in range(NFC // 2):
        for ug in range(2):
            off = ug * DFF + pc * 256
            P.add("pool", lambda e, pc=pc, ug=ug, off=off: e.dma_start(out=wupb_d[pc, :, :, ug * 256:(ug + 1) * 256], in_=wup_v[:, :, off:off + 256]),
                  writes=[b_wupb], dma="wupc", ndma=1)

    P.add("dve", lambda e: e.tensor_copy(out=ident[:], in_=cstf[:, K_ID:K_ID + 128]), reads=[b_cst], writes=[g_const])
    for j, k0 in enumerate((K_LFI, K_LFR, K_LBI, K_LBR)):
        P.add("dve", lambda e, j=j, k0=k0: e.tensor_copy(out=Lm[:, j, :], in_=cstf[:, k0:k0 + 128]), reads=[b_cst], writes=[g_const])
    for d_, k0 in enumerate((K_MF, K_MB)):
        for h in range(4):
            P.add("dve", lambda e, d_=d_, k0=k0, h=h: e.tensor_copy(out=maskT[:, d_, h * 128:(h + 1) * 128], in_=cstf[:, k0:k0 + 128]),
                  reads=[b_cst], writes=[g_const])
    P.add("dve", lambda e: e.tensor_copy(out=ncol[:], in_=cstf[:, K_NC:K_NC + 1]), reads=[b_cst], writes=[g_const])
    P.add("dve", lambda e: e.memset(ones_r[:], 1.0), writes=[g_const])
    P.add("dve", lambda e: e.memset(epsc[:], EPS), writes=[g_const])
    P.add("dve", lambda e: e.memset(onesf[:], 1.0), writes=[b_rep])
    P.add("dve", lambda e: e.memset(zt[:], 0.0), writes=[b_zt])
    P.add("dve", lambda e: e.memset(alT[64:96, :], 1.0), writes=[g_const])
    P.add("dve", lambda e: e.tensor_copy(out=wgate[:], in_=wgf[:]), reads=[b_wgf], writes=[g_const])
    P.add("dve", lambda e: e.tensor_copy(out=ghead[:], in_=rowp[:, R_GH:R_GH + 512]), reads=[b_row], writes=[g_const])
    for d_ in range(2):
        P.add("dve", lambda e, d_=d_: e.memset(S[d_][:], 0.0), writes=[bS[d_]])
        P.add("dve", lambda e, d_=d_: e.memset(Sb[d_][:], 0.0), writes=[bSb[d_]])
    for t in range(3):
        for j in range(4):
            P.add("dve", lambda e, t=t, j=j: e.tensor_scalar(out=dsc[:, t * 4 + j, :], in0=ident[:], scalar1=colp[:, Q_WSC + t * 4 + j:Q_WSC + t * 4 + j + 1],
                                                              scalar2=None, op0=ALU.mult), reads=[g_const, b_col], writes=[g_const])
    b_hx2 = Buf("hx2_d")
    for k in range(8):
        P.add("sp", lambda e, k=k: e.dma_start(out=hx2_d[k, :, 0:64], in_=zt[:, 0:64]), reads=[b_zt], writes=[b_hx2], dma="zm", ndma=1)
        P.add("sp", lambda e, k=k: e.dma_start(out=hx2_d[k, :, SEQ + 64:SEQ + 128], in_=zt[:, 0:64]), reads=[b_zt], writes=[b_hx2], dma="zm", ndma=1)

    P.add("act", lambda e: e.activation(out=scl[:], in_=colp[:, Q_C:Q_C + 16], func=AF.Silu), reads=[b_col], writes=[b_scl])
    for j in range(16):
        P.add("dve", lambda e, j=j: e.tensor_scalar(out=rep[:, j, :], in0=onesf[:], scalar1=scl[:, j:j + 1], scalar2=None, op0=ALU.mult),
              reads=[b_scl, b_rep], writes=[b_rep])
    wada_v = wada_d.rearrange("(k p) j -> p k j", p=128)

    def mod_piece(m, v, slot):
        for hf in range(2):
            bank = 2 * slot + hf
            for k in range(8):
                P.add("pe", lambda e, k=k, hf=hf, bank=bank: e.matmul(pb[bank][:, :], lhsT=rep[:, v * 8 + k, :], rhs=wa[m % 2][:, k, hf * 512:(hf + 1) * 512],
                                                                     start=(k == 0), stop=(k == 7)),
                      reads=[b_rep, b_wa[m % 2]], writes=[PB[bank]])
            P.add("dve", lambda e, hf=hf, bank=bank: e.tensor_tensor(out=modr[slot][:, hf * 512:(hf + 1) * 512], in0=pb[bank][:, :],
                                                                      in1=rowp[:, R_BADA + m * D + hf * 512:R_BADA + m * D + (hf + 1) * 512], op=ALU.add),
                  reads=[PB[bank], b_row], writes=[b_modr[slot]])

    for m in range(6):
        for q in range(2):
            P.add("sp", lambda e, m=m, q=q: e.dma_start(out=wa[m % 2][:, q * 4:(q + 1) * 4, :], in_=wada_v[:, q * 4:(q + 1) * 4, m * D:(m + 1) * D]),
                  writes=[b_wa[m % 2]], dma=f"wa{m % 2}", ndma=1)
        mod_piece(m, 0, 0)
        if m < 2:
            mod_piece(m, 1, 1)
        if m == 0:
            P.add("act", lambda e: e.copy(out=B1[0:1, :], in_=modr[0][0:1, :]), reads=[b_modr[0]], writes=[g_const])
            P.add("act", lambda e: e.copy(out=cB1[0:1, :], in_=modr[1][0:1, :]), reads=[b_modr[1]], writes=[g_const])
        elif m == 1:
            P.add("dve", lambda e: e.scalar_tensor_tensor(out=A1[:], in0=modr[0][:], scalar=1.0, in1=rowp[:, R_GPM:R_GPM + D], op0=ALU.add, op1=ALU.mult),
                  reads=[b_modr[0], b_row], writes=[g_const])
            P.add("dve", lambda e: e.scalar_tensor_tensor(out=cA1[:], in0=modr[1][:], scalar=1.0, in1=rowp[:, R_GPM:R_GPM + D], op0=ALU.add, op1=ALU.mult),
                  reads=[b_modr[1], b_row], writes=[g_const])
        elif m == 2:
            P.add("dve", lambda e: e.tensor_tensor(out=G1[:], in0=modr[0][:], in1=rowp[:, R_GQM:R_GQM + D], op=ALU.mult), reads=[b_modr[0], b_row], writes=[g_const])
        elif m == 3:
            P.add("act", lambda e: e.copy(out=B2[0:1, :], in_=modr[0][0:1, :]), reads=[b_modr[0]], writes=[g_const])
        elif m == 4:
            P.add("dve", lambda e: e.scalar_tensor_tensor(out=A2[:], in0=modr[0][:], scalar=1.0, in1=rowp[:, R_GPF:R_GPF + D], op0=ALU.add, op1=ALU.mult),
                  reads=[b_modr[0], b_row], writes=[g_const])
        else:
            P.add("dve", lambda e: e.tensor_tensor(out=G2[:], in0=modr[0][:], in1=rowp[:, R_GQF:R_GQF + D], op=ALU.mult), reads=[b_modr[0], b_row], writes=[g_const])
    P.barrier(skip=("wupc",))
    if stage == 0:
        return finish()

    Y = Alloc(nc, OV, SB_HI)
    wtm = Y.t([128, 8, 1536], BF16, "wtm")
    wfm = Y.t([128, 8, 1600], BF16, "wfm")
    wout = Y.t([128, 8, D], BF16, "wout")
    xt = [Y.t([128, 4, D], F32, "xt") for _ in range(2)]
    hxT = Y.t([128, 8, 512], BF16, "hxT")
    obl = [Y.t([128, 512], F32, "obl") for _ in range(2)]
    g_l = [Y.t([128, 256], BF16, "g_l") for _ in range(2)]
    g_eb = [Y.t([128, 256], F32, "g_eb") for _ in range(2)]
    g_enb = [Y.t([128, 256], F32, "g_enb") for _ in range(2)]
    g_er = [Y.t([128, 256], F32, "g_er") for _ in range(2)]
    g_e = g_er
    g_qz = [Y.t([128, 2, 256], BF16, "g_qz") for _ in range(2)]
    g_kt = [Y.t([128, 256], BF16, "g_kt") for _ in range(2)]
    kqs = [Y.t([128, 512], F32, "kqs") for _ in range(2)]
    g_et = [Y.t([128, 2], F32, "g_et") for _ in range(2)]
    g_kh = [Y.t([128, 256], BF16, "g_kh") for _ in range(2)]
    g_vb = [Y.t([128, 512], BF16, "g_vb") for _ in range(2)]
    g_T = [Y.t([128, 768], BF16, "g_T") for _ in range(2)]
    g_att = [Y.t([128, 512], BF16, "g_att") for _ in range(2)]
    obst = obl
    osum = Y.t([128, 512], F32, "osum")
    ghs4 = Y.t([128, 4, 512], BF16, "ghs4")
    b_ghs = [Buf(f"ghs{j}") for j in range(4)]
    yg = Y.t([128, 512], BF16, "yg")
    junk = Y.t([128, D], BF16, "junk")
    ybf = [Y.t([128, D], BF16, "ybf") for _ in range(2)]
    b_ybf = [Buf("ybf0"), Buf("ybf1")]
    yxT = Y.t([128, 8, 512], BF16, "yxT")
    sbT = [Y.t([128, 512], BF16, "sbT") for _ in range(2)]
    scT = [Y.t([128, 512], BF16, "scT") for _ in range(2)]
    zpad = Y.t([128, 4, 8, 66], BF16, "zpad")
    tmpf = Y.t([128, 512], F32, "tmpf")
    hx2s = [Y.t([128, 8, 128], BF16, "hx2s") for _ in range(2)]

    b_wtm, b_wfm, b_wout = Buf("wtm"), Buf("wfm"), Buf("wout")
    b_xt = [[Buf(f"xt{s}{j}") for j in range(4)] for s in range(2)]
    b_hxT = [Buf(f"hxT{j}") for j in range(4)]
    b_obl = [Buf("obl0"), Buf("obl1")]
    bg = {n: Buf(n) for n in ("osum", "sg", "ghs", "yg", "junk", "ybf", "tmpf", "alT", "stat")}
    bF = [{n: Buf(n + str(i)) for n in ("e", "l", "eb", "enb", "er", "qt", "kt", "qblk", "kqs")} for i in range(2)]
    bB = [{n: Buf(n + str(i)) for n in ("et", "kh", "vb", "T", "att")} for i in range(2)]
    b_obst = b_obl
    b_yxg = [Buf(f"yxg{j}") for j in range(4)]
    b_yxs = [Buf(f"yxs{j}") for j in range(4)]
    b_sbT = [Buf("sbT0"), Buf("sbT1")]
    b_scT = [Buf("scT0"), Buf("scT1")]
    b_zp = [Buf(f"zp{j}") for j in range(4)]
    b_hx2s = [Buf("hx2s0"), Buf("hx2s1")]
    b_ob_d = [Buf(f"ob_d{i}") for i in range(NT * 4)]
    b_x1_d = [Buf(f"x1_d{i}") for i in range(NT * 4)]
    b_hx2_d = [Buf(f"hx2_d{i}") for i in range(NT)]

    wtm_v = wtm_d.rearrange("(k p) j -> p k j", p=128)
    wfm_v = wfm_d.rearrange("(k p) j -> p k j", p=128)
    wout_v = wout_d.rearrange("(k p) j -> p k j", p=128)
    for k in range(8):
        P.add("pool", lambda e, k=k: e.dma_start(out=wtm[:, k, :], in_=wtm_v[:, k, :]), writes=[b_wtm], dma="wtm", ndma=1)
    for k in range(8):
        P.add("pool", lambda e, k=k: e.dma_start(out=wfm[:, k, :], in_=wfm_v[:, k, :]), writes=[b_wfm], dma="wfm", ndma=1)
    for k in range(8):
        P.add("pool", lambda e, k=k: e.dma_start(out=wout[:, k, :], in_=wout_v[:, k, :]), writes=[b_wout], dma="wout", ndma=1)
    P.add("dve", lambda e: e.memset(zpad[:], 0.0), writes=b_zp)
    for i_ in range(2):
        P.add("dve", lambda e, i_=i_: e.memset(g_qz[i_][:], 0.0), writes=[bF[i_]["qt"]])

    sc_i = [0]

    bst8 = [Buf(f"stat{i}") for i in range(8)]

    def stat_slot():
        k = sc_i[0] % 8
        sc_i[0] += 1
        return 8 * k, bst8[k]

    def rstd_from(ss_ap, nfeat, out_ap, rb, wb):
        P.add("act", lambda e: e.activation(out=out_ap, in_=ss_ap, func=AF.Ln, scale=1.0 / nfeat, bias=epsc[:, 0:1]), reads=rb + [g_const], writes=wb)
        P.add("act", lambda e: e.activation(out=out_ap, in_=out_ap, func=AF.Exp, scale=-0.5), reads=wb, writes=wb)

    fcnt = [0]

    def front_stages(src_ap, src_buf, Arow, Brow, dstT, dst_cols, dst_buf):
        n_ = fcnt[0]
        fcnt[0] += 1
        q_, par = n_ % 4, n_ % 2
        banks = (6, 7) if par == 0 else (2, 5)
        ss = fstat[:, 2 * q_:2 * q_ + 1]
        rs = fstat[:, 2 * q_ + 1:2 * q_ + 2]
        bst = b_fst[q_]
        yb = ybf[par]

        def fa():
            P.add("dve", lambda e: e.memset(fstat[:, 2 * q_:2 * q_ + 2], 0.0), writes=[bst])
            P.add("act", lambda e: e.activation(out=junk[:], in_=src_ap, func=AF.Square, accum_out=ss), reads=[src_buf, bst], writes=[bg["junk"], bst])
            rstd_from(ss, D, rs, [bst], [bst])

        def fb():
            P.add("dve", lambda e: e.scalar_tensor_tensor(out=yb[:], in0=src_ap, scalar=rs, in1=Arow[:], op0=ALU.mult, op1=ALU.mult),
                  reads=[src_buf, bst, g_const], writes=[b_ybf[par]])

        def fc():
            for k in range(8):
                bank = banks[k // 4]
                o = pb[bank][:, (k % 4) * 128:(k % 4 + 1) * 128]
                P.add("pe", lambda e, o=o, k=k: e.matmul(o, lhsT=yb[:, k * 128:(k + 1) * 128], rhs=ident[:], start=True, stop=False),
                      reads=[b_ybf[par], g_const], writes=[PB[bank]])
                P.add("pe", lambda e, o=o, k=k: e.matmul(o, lhsT=Brow[0:1, k * 128:(k + 1) * 128], rhs=ones_r[0:1, :], start=False, stop=True),
                      reads=[g_const], writes=[PB[bank]])

        def fd():
            for hf in range(2):
                bank = banks[hf]
                P.add("act", lambda e, hf=hf, bank=bank: e.copy(out=dstT[:, hf * 4:(hf + 1) * 4, dst_cols], in_=pb[bank][:, :].rearrange("p (k t) -> p k t", k=4)),
                      reads=[PB[bank]], writes=[dst_buf])

        return [fa, fb, fc, fd]

    def front(src_ap, src_buf, Arow, Brow, dstT, dst_cols, dst_buf, banks=None):
        for st_ in front_stages(src_ap, src_buf, Arow, Brow, dstT, dst_cols, dst_buf):
            st_()

    def front4(slot):
        if fcnt[0] % 2:
            fcnt[0] += 1
        S_ = [front_stages(xt[slot][:, j, :], b_xt[slot][j], A1, B1, hxT, slice(j * 128, (j + 1) * 128), b_hxT[j]) for j in range(4)]
        for (j, k) in ((0, 0), (1, 0), (0, 1), (1, 1), (2, 0), (3, 0), (0, 2), (0, 3), (2, 1), (1, 2), (1, 3), (3, 1), (2, 2), (2, 3), (3, 2), (3, 3)):
            S_[j][k]()

    def proj_tm(cols, grp, bank, hbuf):
        for k in range(8):
            P.add("pe", lambda e, k=k: e.matmul(pb[bank][:, :], lhsT=hxT[:, k, cols], rhs=wtm[:, k, grp * 512:(grp + 1) * 512], start=(k == 0), stop=(k == 7)),
                  reads=[hbuf, b_wtm], writes=[PB[bank]])

    def proj_fm(c0, M, bank, ncols=512):
        for k in range(8):
            P.add("pe", lambda e, k=k: e.matmul(pb[bank][0:M, 0:ncols], lhsT=wfm[:, k, c0:c0 + M], rhs=hxT[:, k, 0:ncols], start=(k == 0), stop=(k == 7)),
                  reads=b_hxT + [b_wfm], writes=[PB[bank]])

    def gate_lowrank(ncols=512):
        proj_fm(0, 64, 7, ncols)
        P.add("act", lambda e: e.copy(out=alT[0:64, 0:ncols], in_=pb[7][0:64, 0:ncols]), reads=[PB[7]], writes=[bg["alT"]])

    def gla_stages(d, sl, bs, cols, hbuf, full):
        bK, bV, bA = 3 * sl, 3 * sl + 1, 3 * sl + 2
        F, Bb = bF[sl], bB[bs]
        e_, l_, eb_, enb_, er_, qz_, kt_, kq_ = g_e[sl], g_l[sl], g_eb[sl], g_enb[sl], g_er[sl], g_qz[sl], g_kt[sl], kqs[sl]
        et_, kh_, vb_, T_, att_ = g_et[bs], g_kh[bs], g_vb[bs], g_T[bs], g_att[bs]
        Tps = pb[bV].bitcast(BF16)

        def s1():
            proj_tm(cols, 0, bK, hbuf)
            proj_tm(cols, 1, bV, hbuf)
            P.add("pe", lambda e: e.matmul(pb[bA][:, 0:256], lhsT=alT[0:96, cols], rhs=wgate[:, d * 256:(d + 1) * 256], start=True, stop=True),
                  reads=[bg["alT"], g_const], writes=[PB[bA]])

        def s2():
            P.add("act", lambda e: e.activation(out=e_[:], in_=pb[bA][:, 0:256], func=AF.Exp, scale=-1.0), reads=[PB[bA]], writes=[F["er"]])
            P.add("act", lambda e: e.activation(out=l_[:], in_=e_[:], func=AF.Ln, bias=1.0), reads=[F["er"]], writes=[F["l"]])
            P.add("act", lambda e: e.copy(out=kq_[:], in_=pb[bK][:, :]), reads=[PB[bK]], writes=[F["kqs"]])
            P.add("act", lambda e: e.copy(out=vb_[:], in_=pb[bV][:, :]), reads=[PB[bV]], writes=[Bb["vb"]])

        def s3():
            if full:
                P.add("pe", lambda e: e.matmul(pb[bK][:, 0:256], lhsT=Lm[:, 2 * d, :], rhs=l_[:], start=True, stop=True), reads=[F["l"], g_const], writes=[PB[bK]])
            P.add("pe", lambda e: e.matmul(pb[bK][:, 256:512], lhsT=Lm[:, 2 * d + 1, :], rhs=l_[:], start=True, stop=True), reads=[F["l"], g_const], writes=[PB[bK]])
            for p_ in range(2):
                P.add("pe", lambda e, p_=p_: e.matmul(pb[bA][:, 256 + p_:257 + p_], lhsT=l_[:, p_ * 128:(p_ + 1) * 128], rhs=ncol[:, 0:1], start=True, stop=True),
                      reads=[F["l"], g_const], writes=[PB[bA]])

        def s4():
            if full:
                P.add("act", lambda e: e.activation(out=eb_[:], in_=pb[bK][:, 0:256], func=AF.Exp), reads=[PB[bK]], writes=[F["eb"]])
                P.add("act", lambda e: e.activation(out=enb_[:], in_=pb[bK][:, 0:256], func=AF.Exp, scale=-1.0), reads=[PB[bK]], writes=[F["enb"]])
            P.add("act", lambda e: e.activation(out=er_[:], in_=pb[bK][:, 256:512], func=AF.Exp), reads=[PB[bK]], writes=[F["er"]])
            P.add("act", lambda e: e.activation(out=et_[:], in_=pb[bA][:, 256:258], func=AF.Exp), reads=[PB[bA]], writes=[Bb["et"]])

        def s5():
            if full:
                kq3 = kq_[:, 256:512].rearrange("p (a b) -> p a b", a=2)
                eb3 = eb_[:].rearrange("p (a b) -> p a b", a=2)
                for w in range(2):
                    P.add("dve", lambda e, w=w: e.scalar_tensor_tensor(out=qz_[:, :, w * 192:w * 192 + 64], in0=kq3[:, :, w * 64:(w + 1) * 64], scalar=0.125,
                                                                       in1=eb3[:, :, w * 64:(w + 1) * 64], op0=ALU.mult, op1=ALU.mult),
                          reads=[F["kqs"], F["eb"]], writes=[F["qt"]])
                P.add("dve", lambda e: e.tensor_tensor(out=kt_[:], in0=kq_[:, 0:256], in1=enb_[:], op=ALU.mult), reads=[F["kqs"], F["enb"]], writes=[F["kt"]])
            P.add("dve", lambda e: e.tensor_tensor(out=kh_[:], in0=kq_[:, 0:256], in1=er_[:], op=ALU.mult), reads=[F["kqs"], F["er"]], writes=[Bb["kh"]])

        def s6():
            if not full:
                return
            for j in range(4):
                P.add("pe", lambda e, j=j: e.transpose(Tps[:, j * 128:(j + 1) * 128], qz_[:, j // 2, (j % 2) * 128:(j % 2 + 1) * 128], ident[:]),
                      reads=[F["qt"], g_const], writes=[PB[bV]])
            for j in range(2):
                P.add("pe", lambda e, j=j: e.transpose(Tps[:, 512 + j * 128:512 + (j + 1) * 128], kt_[:, j * 128:(j + 1) * 128], ident[:]),
                      reads=[F["kt"], g_const], writes=[PB[bV]])

        def s7():
            if not full:
                return
            P.add("dve", lambda e: e.tensor_copy(out=T_[:], in_=Tps[:, 0:768]), reads=[PB[bV]], writes=[Bb["T"]])

        def s8():
            if not full:
                return
            for pr in range(2):
                P.add("pe", lambda e, pr=pr: e.matmul(pb[bA][:, pr * 256:(pr + 1) * 256], lhsT=T_[:, 512 + pr * 128:512 + (pr + 1) * 128], rhs=T_[:, pr * 256:(pr + 1) * 256],
                                                      start=True, stop=True), reads=[Bb["T"]], writes=[PB[bA]])

        def s9():
            if not full:
                return
            P.add("dve", lambda e: e.tensor_tensor(out=att_[:], in0=pb[bA][:, :], in1=maskT[:, d, :], op=ALU.mult), reads=[PB[bA], g_const], writes=[Bb["att"]])

        return [s1, s2, s3, s4, s5, s6, s7, s8, s9]

    def gla_back(d, bs, full):
        Bb = bB[bs]
        et_, kh_, vb_, T_, att_ = g_et[bs], g_kh[bs], g_vb[bs], g_T[bs], g_att[bs]
        if full:
            for h in range(4):
                pr = h // 2
                P.add("pe", lambda e, h=h, pr=pr: e.matmul(pb[6][:, h * 128:(h + 1) * 128], lhsT=T_[:, h * 128:(h + 1) * 128], rhs=Sb[d][:, pr, :],
                                                           start=True, stop=False), reads=[Bb["T"], bSb[d]], writes=[PB[6]])
                P.add("pe", lambda e, h=h: e.matmul(pb[6][:, h * 128:(h + 1) * 128], lhsT=att_[:, h * 128:(h + 1) * 128], rhs=vb_[:, h * 128:(h + 1) * 128],
                                                    start=False, stop=True), reads=[Bb["att"], Bb["vb"]], writes=[PB[6]])
        for pr in range(2):
            P.add("pe", lambda e, pr=pr: e.matmul(pb[7][:, pr * 256:(pr + 1) * 256], lhsT=kh_[:, pr * 128:(pr + 1) * 128], rhs=vb_[:, pr * 256:(pr + 1) * 256],
                                                  start=True, stop=True), reads=[Bb["kh"], Bb["vb"]], writes=[PB[7]])
        for pr in range(2):
            for w in range(2):
                rsl = slice(w * 64, (w + 1) * 64)
                c0 = pr * 256 + w * 128
                P.add("dve", lambda e, pr=pr, rsl=rsl, c0=c0: e.scalar_tensor_tensor(out=S[d][rsl, pr, :], in0=S[d][rsl, pr, :], scalar=et_[rsl, pr:pr + 1],
                                                                                      in1=pb[7][rsl, c0:c0 + 128], op0=ALU.mult, op1=ALU.add),
                      reads=[bS[d], Bb["et"], PB[7]], writes=[bS[d]])
        P.add("act", lambda e: e.copy(out=Sb[d][:], in_=S[d][:]), reads=[bS[d]], writes=[bSb[d]])

    def gla_tile(d, order, after_back, mid=None):
        st = {}
        for n_, j in enumerate(order):
            st[j] = gla_stages(d, n_ % 2, n_ % 2, slice(j * 128, (j + 1) * 128), b_hxT[j], True)
        c0_, c1_, c2_, c3_ = order
        for k in range(9):
            st[c0_][k]()
            st[c1_][k]()
        if mid is None:
            st[c2_][0]()
            st[c3_][0]()
        for n_, j in enumerate((c0_, c1_)):
            gla_back(d, n_, True)
            if mid is not None:
                mid(j)
            after_back(j)
        for k in range(0 if mid is not None else 1, 9):
            st[c2_][k]()
            st[c3_][k]()
        for n_, j in enumerate((c2_, c3_)):
            gla_back(d, n_, True)
            if mid is not None:
                mid(j)
            after_back(j)

    ctx_v = ctx_d.rearrange("(j p) f -> p j f", p=128)
    for j in range(2):
        P.add("sp", lambda e, j=j: e.dma_start(out=xt[0][:, j, :], in_=ctx_v[:, j, :]), writes=[b_xt[0][j]], dma=f"x0{j}")
    for j in range(2):
        front(xt[0][:, j, :], b_xt[0][j], cA1, cB1, hxT, slice(j * 128, (j + 1) * 128), b_hxT[j])
    gate_lowrank(256)
    for d_, order in ((0, (0, 1)), (1, (1, 0))):
        for j in order:
            cols = slice(j * 128, (j + 1) * 128)
            for st_ in gla_stages(d_, 0, 0, cols, b_hxT[j], False):
                st_()
            gla_back(d_, 0, False)

    if stage == 1:
        return finish()
    x_v = x_d.rearrange("(n j p) f -> n p j f", p=128, j=4)
    ob_v = ob_d.rearrange("(n p) f -> n p f", p=128)
    x1_v = x1_d.rearrange("(n p) f -> n p f", p=128)

    def load_x(i, slot):
        for j in range(4):
            P.add("sp", lambda e, j=j: e.dma_start(out=xt[slot][:, j, :], in_=x_v[i, :, j, :]), writes=[b_xt[slot][j]], dma=f"x{slot}{j}")

    tseq = [i for i in range(NT - 1, -1, -1)] + [i for i in range(NT)]
    tcount = 0
    load_x(tseq[0], 0)
    for i in range(NT - 1, -1, -1):
        slot = tcount % 2
        tcount += 1
        if tcount < len(tseq) and stage >= 3 or tcount < NT:
            load_x(tseq[tcount], tcount % 2)
        front4(slot)
        gate_lowrank()
        def after_a(j, i=i):
            n = i * 4 + j
            P.add("act", lambda e, j=j: e.copy(out=obst[j % 2][:], in_=pb[6][:, :]), reads=[PB[6]], writes=[b_obst[j % 2]])
            P.add("sp", lambda e, j=j, n=n: e.dma_start(out=ob_v[n, :, :], in_=obst[j % 2][:]), reads=[b_obst[j % 2]], writes=[b_ob_d[n]], dma=f"obst{j % 2}")

        gla_tile(1, (3, 2, 1, 0), after_a)

    if stage == 2:
        return finish()
    hx2_v = hx2_d.rearrange("k p t -> p k t")
    for i in range(NT):
        slot = tcount % 2
        tcount += 1
        if tcount < len(tseq):
            load_x(tseq[tcount], tcount % 2)
        front4(slot)
        gate_lowrank()
        def sc_step(j):
            s2 = j % 2
            proj_fm(64 + j * 128, 128, 0)
            P.add("act", lambda e, s2=s2: e.copy(out=sbT[s2][:], in_=pb[0][:, :]), reads=[PB[0]], writes=[b_sbT[s2]])
            proj_fm(64 + 512 + j * 128, 128, 1)
            P.add("act", lambda e, s2=s2: e.copy(out=scT[s2][:], in_=pb[1][:, :]), reads=[PB[1]], writes=[b_scT[s2]])
            proj_fm(64 + 1024 + j * 128, 128, 2)
            P.add("dve", lambda e, j=j, s2=s2: e.tensor_tensor(out=zpad[:, j, :, 1:65], in0=pb[2][:, :].rearrange("p (r c) -> p r c", r=8),
                                                               in1=scT[s2][:].rearrange("p (r c) -> p r c", r=8), op=ALU.mult),
                  reads=[PB[2], b_scT[s2]], writes=[b_zp[j]])
            for t in range(3):
                P.add("pe", lambda e, j=j, t=t: e.matmul(pb[3][:, :], lhsT=dsc[:, t * 4 + j, :], rhs=zpad[:, j, :, t:t + 64], start=(t == 0), stop=(t == 2)),
                      reads=[b_zp[j], g_const], writes=[PB[3]])
            P.add("dve", lambda e, j=j, s2=s2: e.scalar_tensor_tensor(out=yxT[:, 4 + j, :], in0=pb[3][:, :], scalar=colp[:, Q_BSC + j:Q_BSC + j + 1], in1=sbT[s2][:],
                                                                      op0=ALU.add, op1=ALU.mult), reads=[PB[3], b_sbT[s2], g_const], writes=[b_yxs[j]])
        sgt = (tmpf, osum)
        sgb = (bg["tmpf"], bg["osum"])
        for j in range(4):
            cols = slice(j * 128, (j + 1) * 128)
            proj_tm(cols, 2, j, b_hxT[j])
            P.add("act", lambda e, j=j: e.activation(out=sgt[j % 2][:], in_=pb[j][:, :], func=AF.Silu), reads=[PB[j]], writes=[sgb[j % 2]])
            P.add("dve", lambda e, j=j: e.tensor_tensor(out=ghs4[:, j, :], in0=sgt[j % 2][:], in1=ghead[:], op=ALU.mult), reads=[sgb[j % 2], g_const], writes=[b_ghs[j]])

        def load_ob(j, i=i):
            n = i * 4 + j
            P.add("sp", lambda e, j=j, n=n: e.dma_start(out=obl[j % 2][:], in_=ob_v[n, :, :]), reads=[b_ob_d[n]], writes=[b_obl[j % 2]], dma=f"obl{j % 2}")

        load_ob(0)
        load_ob(1)

        def after_b(j, i=i):
            cols = slice(j * 128, (j + 1) * 128)
            P.add("dve", lambda e, j=j: e.tensor_tensor(out=osum[:], in0=pb[6][:, :], in1=obl[j % 2][:], op=ALU.add), reads=[PB[6], b_obl[j % 2]], writes=[bg["osum"]])
            if j < 2:
                load_ob(j + 2)
            c0, bs_ = stat_slot()
            P.add("dve", lambda e, c0=c0: e.memset(stat[:, c0:c0 + 8], 0.0), writes=[bs_])
            for h in range(4):
                P.add("act", lambda e, h=h, c0=c0: e.activation(out=junk[:, h * 128:(h + 1) * 128], in_=osum[:, h * 128:(h + 1) * 128], func=AF.Square,
                                                                accum_out=stat[:, c0 + h:c0 + h + 1]), reads=[bg["osum"], bs_], writes=[bg["junk"], bs_])
            rstd_from(stat[:, c0:c0 + 4], 128, stat[:, c0 + 4:c0 + 8], [bs_], [bs_])
            for h in range(4):
                P.add("dve", lambda e, h=h, c0=c0, j=j: e.scalar_tensor_tensor(out=yg[:, h * 128:(h + 1) * 128], in0=osum[:, h * 128:(h + 1) * 128],
                                                                               scalar=stat[:, c0 + 4 + h:c0 + 5 + h], in1=ghs4[:, j, h * 128:(h + 1) * 128],
                                                                               op0=ALU.mult, op1=ALU.mult), reads=[bg["osum"], bs_, b_ghs[j]], writes=[bg["yg"]])
            Tps = pb[7].bitcast(BF16)
            for h in range(4):
                P.add("pe", lambda e, h=h: e.transpose(Tps[:, h * 128:(h + 1) * 128], yg[:, h * 128:(h + 1) * 128], ident[:]), reads=[bg["yg"], g_const], writes=[PB[7]])
            P.add("act", lambda e, cols=cols: e.copy(out=yxT[:, 0:4, cols], in_=Tps[:, 0:512].rearrange("p (k t) -> p k t", k=4)), reads=[PB[7]], writes=[b_yxg[j]])

        gla_tile(0, (0, 1, 2, 3), after_b, mid=sc_step)
        def oproj_stages(j, i=i, slot=slot):
            cols = slice(j * 128, (j + 1) * 128)
            n = i * 4 + j
            ob_ = 3 * (j % 2)
            c0, bs_ = stat_slot()
            tmp2 = (tmpf, osum)
            tmpb = (bg["tmpf"], bg["osum"])

            def o1():
                for hf in range(2):
                    for k in range(8):
                        rb = [b_yxg[j]] if k < 4 else [b_yxs[k - 4]]
                        P.add("pe", lambda e, k=k, hf=hf: e.matmul(pb[ob_ + hf][:, :], lhsT=yxT[:, k, cols], rhs=wout[:, k, hf * 512:(hf + 1) * 512],
                                                                   start=(k == 0), stop=(k == 7)), reads=rb + [b_wout], writes=[PB[ob_ + hf]])

            def o2():
                P.add("dve", lambda e: e.memset(stat[:, c0:c0 + 4], 0.0), writes=[bs_])
                for hf in range(2):
                    P.add("act", lambda e, hf=hf: e.activation(out=junk[:, hf * 512:(hf + 1) * 512], in_=pb[ob_ + hf][:, :], func=AF.Square,
                                                               accum_out=stat[:, c0 + hf:c0 + hf + 1]), reads=[PB[ob_ + hf], bs_], writes=[bg["junk"], bs_])
                P.add("dve", lambda e: e.tensor_tensor(out=stat[:, c0 + 2:c0 + 3], in0=stat[:, c0:c0 + 1], in1=stat[:, c0 + 1:c0 + 2], op=ALU.add),
                      reads=[bs_], writes=[bs_])
                rstd_from(stat[:, c0 + 2:c0 + 3], D, stat[:, c0 + 3:c0 + 4], [bs_], [bs_])

            def o3():
                for hf in range(2):
                    hs = slice(hf * 512, (hf + 1) * 512)
                    P.add("dve", lambda e, hf=hf, hs=hs: e.scalar_tensor_tensor(out=tmp2[hf][:], in0=pb[ob_ + hf][:, :], scalar=stat[:, c0 + 3:c0 + 4], in1=G1[:, hs],
                                                                               op0=ALU.mult, op1=ALU.mult), reads=[PB[ob_ + hf], bs_, g_const], writes=[tmpb[hf]])
                    P.add("dve", lambda e, hf=hf, hs=hs: e.tensor_tensor(out=xt[slot][:, j, hs], in0=xt[slot][:, j, hs], in1=tmp2[hf][:], op=ALU.add),
                          reads=[tmpb[hf], b_xt[slot][j]], writes=[b_xt[slot][j]])
                P.add("sp", lambda e: e.dma_start(out=x1_v[n, :, :], in_=xt[slot][:, j, :]), reads=[b_xt[slot][j]], writes=[b_x1_d[n]], dma=f"xs{slot}{j}")

            fs = front_stages(xt[slot][:, j, :], b_xt[slot][j], A2, B2, hx2s[j % 2], slice(0, 128), b_hx2s[j % 2])

            def o7():
                fs[3]()
                P.add("sp", lambda e: e.dma_start(out=hx2_v[:, :, 64 + i * 512 + j * 128:64 + i * 512 + (j + 1) * 128], in_=hx2s[j % 2][:]),
                      reads=[b_hx2s[j % 2]], writes=[b_hx2_d[i]], dma=f"hx2s{j % 2}")

            return [o1, o2, o3, fs[0], fs[1], fs[2], o7]

        if fcnt[0] % 2:
            fcnt[0] += 1
        OS = [oproj_stages(j) for j in range(4)]
        order = [(0, 0), (1, 0), (0, 1), (0, 2), (0, 3), (1, 1), (2, 0), (1, 2), (0, 4), (1, 3), (0, 5), (2, 1), (3, 0), (0, 6), (2, 2), (1, 4), (2, 3), (1, 5),
                 (3, 1), (1, 6), (3, 2), (2, 4), (3, 3), (2, 5), (2, 6), (3, 4), (3, 5), (3, 6)]
        assert sorted(order) == [(j, k) for j in range(4) for k in range(7)]
        for (j, k) in order:
            OS[j][k]()
    P.barrier()
    if stage == 3:
        return finish()

    Zc = Alloc(nc, OV, SB_HI)
    wdn = Zc.t([128, NFC, D], BF16, "wdn")
    wupS = [Zc.t([128, 8, 512], BF16, "wupS") for _ in range(2)]
    hw = [Zc.t([128, 8, 640], BF16, "hw") for _ in range(2)]
    x1t = [Zc.t([128, 4, D], F32, "x1t") for _ in range(2)]
    hT = Zc.t([128, NFC, 512], BF16, "hT")
    upad = [Zc.t([128, 10, 66], BF16, "upad") for _ in range(2)]
    gts = [Zc.t([128, 512], BF16, "gts") for _ in range(2)]
    sil = [Zc.t([128, 512], BF16, "sil") for _ in range(2)]
    dg = [Zc.t([128, 9, 128], BF16, "dg") for _ in range(2)]
    tmpc = Zc.t([128, 512], F32, "tmpc")
    junkc = Zc.t([128, 512], BF16, "junkc")
    b_wdn = Buf("wdn")
    b_wupS = [Buf("wupS0"), Buf("wupS1")]
    b_hw = [Buf("hw0"), Buf("hw1")]
    b_x1t = [[Buf(f"x1t{s}{j}") for j in range(4)] for s in range(2)]
    b_hT = [Buf(f"hT{c}") for c in range(NFC)]
    b_upad = [Buf("upad0"), Buf("upad1")]
    b_upc = [Buf("upc0"), Buf("upc1")]
    ucar = Zc.t([128, NFC, 2, 64], BF16, "ucar")
    b_ucar = [Buf(f"ucar{c}") for c in range(NFC)]
    b_gts = [Buf("gts0"), Buf("gts1")]
    b_sil = [Buf("sil0"), Buf("sil1")]
    b_dg = [Buf("dg0"), Buf("dg1")]
    b_tmpc, b_junkc, b_statc = Buf("tmpc"), Buf("junkc"), Buf("statc")
    b_out = []

    wdn_v = wdown_d.rearrange("(c p) j -> p c j", p=128)
    for c in range(NFC):
        P.add("pool", lambda e, c=c: e.dma_start(out=wdn[:, c, :], in_=wdn_v[:, c, :]), writes=[b_wdn], dma="wdn", ndma=1)
    for s in range(2):
        P.add("dve", lambda e, s=s: e.memset(upad[s][:], 0.0), writes=[b_upad[s]])
    out_v = out_d.rearrange("(n p) f -> n p f", p=128)
    x1_v4 = x1_d.rearrange("(n j p) f -> n p j f", p=128, j=4)
    def load_c(i):
        slot = i % 2
        P.add("sp", lambda e: e.dma_start(out=hw[slot][:], in_=hx2_v[:, :, i * 512:i * 512 + 640]),
              reads=b_hx2_d[max(0, i - 1):i + 2] + [b_hx2], writes=[b_hw[slot]], dma=f"hw{slot}")
        for j in range(4):
            P.add("sp", lambda e, j=j: e.dma_start(out=x1t[slot][:, j, :], in_=x1_v4[i, :, j, :]), reads=[b_x1_d[i * 4 + j]],
                  writes=[b_x1t[slot][j]], dma=f"x1t{slot}{j}")

    NPC = NFC // 2

    def load_piece(g):
        ps = g % 2
        pc = g % NPC
        P.add("sp", lambda e: e.dma_start(out=wupS[ps][:], in_=wupb_d[pc, :, :, :]), reads=[b_wupb], writes=[b_wupS[ps]], dma=f"wupS{ps}")

    load_c(0)
    load_piece(0)
    piece = 0
    for i in range(NT):
        slot = i % 2
        for c in range(NFC):
            s2 = c % 2
            if c % 2 == 0:
                ps = piece % 2
                piece += 1
                if piece < NT * NPC:
                    load_piece(piece)
                if c == 2 and i + 1 < NT:
                    load_c(i + 1)
            wo = (c % 2) * 128
            for t in range(9):
                P.add("dve", lambda e, c=c, t=t, s2=s2: e.tensor_scalar(out=dg[s2][:, t, :], in0=ident[:], scalar1=colp[:, Q_WCF + c * 9 + t:Q_WCF + c * 9 + t + 1],
                                                                         scalar2=None, op0=ALU.mult), reads=[g_const], writes=[b_dg[s2]])
            ucol = 0 if i == 0 else 128
            for k in range(8):
                P.add("pe", lambda e, k=k, s2=s2, ps=ps, wo=wo, slot=slot, ucol=ucol: e.matmul(pb[s2][:, :], lhsT=wupS[ps][:, k, wo:wo + 128], rhs=hw[slot][:, k, ucol:ucol + 512],
                                                                                           start=(k == 0), stop=(k == 7)),
                      reads=[b_wupS[ps], b_hw[slot]], writes=[PB[s2]])
            if i == 0:
                for k in range(8):
                    P.add("pe", lambda e, k=k, s2=s2, ps=ps, wo=wo, slot=slot: e.matmul(pb[2 + s2][:, 0:128], lhsT=wupS[ps][:, k, wo:wo + 128], rhs=hw[slot][:, k, 512:640], start=(k == 0), stop=(k == 7)),
                          reads=[b_wupS[ps], b_hw[slot]], writes=[PB[2 + s2]])
            for k in range(8):
                P.add("pe", lambda e, k=k, s2=s2, ps=ps, wo=wo, slot=slot: e.matmul(pb[4 + s2][:, :], lhsT=wupS[ps][:, k, 256 + wo:256 + wo + 128], rhs=hw[slot][:, k, 64:576], start=(k == 0), stop=(k == 7)),
                      reads=[b_wupS[ps], b_hw[slot]], writes=[PB[4 + s2]])
            if i == 0:
                P.add("act", lambda e, s2=s2: e.copy(out=upad[s2][:, 0:8, 1:65], in_=pb[s2][:, :].rearrange("p (r c) -> p r c", r=8)), reads=[PB[s2]], writes=[b_upc[s2], b_upad[s2]])
                P.add("act", lambda e, s2=s2: e.copy(out=upad[s2][:, 8:10, 1:65], in_=pb[2 + s2][:, 0:128].rearrange("p (r c) -> p r c", r=2)), reads=[PB[2 + s2]], writes=[b_upad[s2]])
            else:
                P.add("pool", lambda e, c=c, s2=s2: e.tensor_copy(out=upad[s2][:, 0:2, 1:65], in_=ucar[:, c, :, :]), reads=[b_ucar[c]], writes=[b_upc[s2]])
                P.add("act", lambda e, s2=s2: e.copy(out=upad[s2][:, 2:10, 1:65], in_=pb[s2][:, :].rearrange("p (r c) -> p r c", r=8)), reads=[PB[s2]], writes=[b_upad[s2]])
            if i < NT - 1:
                P.add("pool", lambda e, c=c, s2=s2: e.tensor_copy(out=ucar[:, c, :, :], in_=upad[s2][:, 8:10, 1:65]), reads=[b_upad[s2]], writes=[b_ucar[c]])
            P.add("dve", lambda e, s2=s2: e.tensor_copy(out=gts[s2][:], in_=pb[4 + s2][:, :]), reads=[PB[4 + s2]], writes=[b_gts[s2]])
            for t in range(9):
                dr, dc = t // 3, t % 3
                P.add("pe", lambda e, t=t, dr=dr, dc=dc, s2=s2: e.matmul(pb[6 + s2][:, :], lhsT=dg[s2][:, t, :], rhs=upad[s2][:, dr:dr + 8, dc:dc + 64], start=(t == 0), stop=(t == 8)),
                      reads=[b_dg[s2], b_upad[s2], b_upc[s2]], writes=[PB[6 + s2]])
            P.add("act", lambda e, c=c, s2=s2: e.activation(out=sil[s2][:], in_=pb[6 + s2][:, :], func=AF.Silu, bias=colp[:, Q_BCF + c:Q_BCF + c + 1]),
                  reads=[PB[6 + s2], g_const], writes=[b_sil[s2]])
            P.add("dve", lambda e, c=c, s2=s2: e.tensor_tensor(out=hT[:, c, :], in0=sil[s2][:], in1=gts[s2][:], op=ALU.mult), reads=[b_sil[s2], b_gts[s2]], writes=[b_hT[c]])
        for j in range(4):
            cols = slice(j * 128, (j + 1) * 128)
            n = i * 4 + j
            for hf in range(2):
                bank = (2 * j + hf) % 8
                for c in range(NFC):
                    P.add("pe", lambda e, c=c, hf=hf, bank=bank, cols=cols: e.matmul(pb[bank][:, :], lhsT=hT[:, c, cols], rhs=wdn[:, c, hf * 512:(hf + 1) * 512],
                                                                                      start=(c == 0), stop=(c == NFC - 1)), reads=[b_hT[c], b_wdn], writes=[PB[bank]])
            c0, _unused = stat_slot()
            P.add("dve", lambda e, c0=c0: e.memset(stat[:, c0:c0 + 4], 0.0), writes=[b_statc])
            for hf in range(2):
                bank = (2 * j + hf) % 8
                P.add("act", lambda e, hf=hf, bank=bank, c0=c0: e.activation(out=junkc[:], in_=pb[bank][:, :], func=AF.Square, accum_out=stat[:, c0 + hf:c0 + hf + 1]),
                      reads=[PB[bank], b_statc], writes=[b_junkc, b_statc])
            P.add("dve", lambda e, c0=c0: e.tensor_tensor(out=stat[:, c0 + 2:c0 + 3], in0=stat[:, c0:c0 + 1], in1=stat[:, c0 + 1:c0 + 2], op=ALU.add),
                  reads=[b_statc], writes=[b_statc])
            rstd_from(stat[:, c0 + 2:c0 + 3], D, stat[:, c0 + 3:c0 + 4], [b_statc], [b_statc])
            for hf in range(2):
                bank = (2 * j + hf) % 8
                hs = slice(hf * 512, (hf + 1) * 512)
                P.add("dve", lambda e, hf=hf, bank=bank, hs=hs, c0=c0: e.scalar_tensor_tensor(out=tmpc[:], in0=pb[bank][:, :], scalar=stat[:, c0 + 3:c0 + 4], in1=G2[:, hs],
                                                                                             op0=ALU.mult, op1=ALU.mult), reads=[PB[bank], b_statc, g_const], writes=[b_tmpc])
                P.add("dve", lambda e, j=j, hs=hs, slot=slot: e.tensor_tensor(out=x1t[slot][:, j, hs], in0=x1t[slot][:, j, hs], in1=tmpc[:], op=ALU.add),
                      reads=[b_tmpc, b_x1t[slot][j]], writes=[b_x1t[slot][j]])
            bo = Buf(f"out{n}")
            tok = P.add("sp", lambda e, j=j, n=n, slot=slot: e.dma_start(out=out_v[n, :, :], in_=x1t[slot][:, j, :]), reads=[b_x1t[slot][j]], writes=[bo], dma=f"ot{slot}{j}")
            b_out.append(tok)
    last = {}
    for t in b_out:
        last[t[0]] = max(last.get(t[0], 0), t[1])
    fin = list(last.items())
    if debug:
        for k, v in P.dma_cum.items():
            fin.append(("dma:" + k, v))
    P.final_wait("sp", fin)
    P.emit(nc)
    return nc


def _consts():
    c = np.zeros((128, NCST), np.float32)
    m = np.arange(128)[:, None]
    t = np.arange(128)[None, :]
    c[:, K_ID:K_ID + 128] = (m == t)
    c[:, K_LFI:K_LFI + 128] = (m <= t) * (-1.0 / 16)
    c[:, K_LFR:K_LFR + 128] = (m > t) * (-1.0 / 16)
    c[:, K_LBI:K_LBI + 128] = (m >= t) * (-1.0 / 16)
    c[:, K_LBR:K_LBR + 128] = (m < t) * (-1.0 / 16)
    c[:, K_MF:K_MF + 128] = (m <= t)
    c[:, K_MB:K_MB + 128] = (m > t)
    c[:, K_NC] = -1.0 / 16
    return c


def _colmaj(v, n):
    return np.ascontiguousarray(np.asarray(v, np.float32).reshape(n, 128).T)


def make_in_maps(inputs, NT=16, ncores=8):
    f = lambda a: np.asarray(a, np.float32)
    w_in = f(inputs["w_in"])[0]
    w_tm = np.ascontiguousarray(np.concatenate([w_in[:, C_K:C_V], w_in[:, C_Q:C_OG], w_in[:, C_V:C_AF], w_in[:, C_OG:C_SB]], axis=1))
    gate = np.zeros((D, 64), np.float32)
    gate[:, 0:16] = w_in[:, C_AF:C_AB]
    gate[:, 32:48] = w_in[:, C_AB:C_Q]
    w_fm = np.ascontiguousarray(np.concatenate([gate, w_in[:, C_SB:C_SC], w_in[:, C_SC:C_SX], w_in[:, C_SX:]], axis=1))
    wg = np.zeros((96, 512), np.float32)
    wg[0:16, 0:256] = f(inputs["w_af"])[0]
    wg[32:48, 256:512] = f(inputs["w_ab"])[0]
    wg[64, 0:256] = f(inputs["b_af"])[0]
    wg[64, 256:512] = f(inputs["b_ab"])[0]
    rows1 = np.concatenate([f(inputs["g_pre_mix"])[0], f(inputs["g_post_mix"])[0], f(inputs["g_pre_ffn"])[0], f(inputs["g_post_ffn"])[0],
                            np.tile(f(inputs["g_head"])[0], 4), f(inputs["b_ada"])[0]])
    rows = np.ascontiguousarray(np.broadcast_to(rows1[None, :], (128, NROW)))
    w_sc = f(inputs["w_sc"])[0]
    w_cf = f(inputs["w_cf"])[0].reshape(9, DFF)
    cols_common = np.concatenate([
        _colmaj(f(inputs["b_sc"])[0], 4), _colmaj(f(inputs["b_cf"])[0], NFC),
        np.concatenate([_colmaj(w_sc[t], 4) for t in range(3)], axis=1),
        np.stack([_colmaj(w_cf[t], NFC) for t in range(9)], axis=2).reshape(128, NFC * 9),
    ], axis=1)
    cst = _consts()
    maps = []
    x = inputs["x"]
    for b in range(ncores):
        cols = np.ascontiguousarray(np.concatenate([_colmaj(f(inputs["c"])[b], 8), _colmaj(f(inputs["c_ctx"]), 8), cols_common], axis=1))
        maps.append({
            "x": np.ascontiguousarray(f(x[b])[:NT * 512]), "ctx": np.ascontiguousarray(f(inputs["ctx"][b])),
            "rows": rows, "cols": cols, "cst": cst,
            "w_ada": f(inputs["w_ada"])[0], "w_tm": w_tm, "w_fm": w_fm, "w_gate": wg,
            "w_out": f(inputs["w_out"])[0], "w_up": f(inputs["w_up"])[0], "w_down": f(inputs["w_down"])[0],
        })
    return maps


_NC_CACHE = {}


def kernel(**inputs):
    NT = 16
    if NT not in _NC_CACHE:
        _NC_CACHE[NT] = build(NT)
    nc = _NC_CACHE[NT]
    maps = make_in_maps(inputs, NT, 8)
    res = run_bass_kernel_spmd(nc, maps, core_ids=list(range(8)))
    return np.stack([np.asarray(r["out"], np.float32).reshape(NT * 512, D) for r in res.results], axis=0)
```

```python
import os
import numpy as np
import concourse.bass as bass
import concourse.mybir as mybir
from concourse.bass_utils import run_bass_kernel_spmd

F32 = mybir.dt.float32
BF16 = mybir.dt.bfloat16
AF = mybir.ActivationFunctionType
ALU = mybir.AluOpType

D = 1024
DFF = 2816
NFC = 22
CTX = 256
EPS = 1e-6
SB_LO = 16512
SB_HI = 229344
EPOCH = 24000
STRICT_SYNC = False

C_K, C_V, C_AF, C_AB, C_Q, C_OG, C_SB, C_SC, C_SX = 0, 256, 768, 784, 800, 1056, 1568, 2080, 2592

K_ID, K_LFI, K_LFR, K_LBI, K_LBR, K_MF, K_MB, K_NC = 0, 128, 256, 384, 512, 640, 768, 896
NCST = 897
R_GPM, R_GQM, R_GPF, R_GQF, R_GH, R_BADA = 0, 1024, 2048, 3072, 4096, 4608
NROW = 4608 + 6144
Q_C, Q_CC, Q_BSC, Q_BCF, Q_WSC, Q_WCF = 0, 8, 16, 20, 42, 54
NCOL = 54 + 198


class Buf:
    __slots__ = ("name", "w", "r")

    def __init__(self, name):
        self.name = name
        self.w = None
        self.r = {}


class Prog:
    ENGS = ("pe", "act", "dve", "pool", "sp")

    def __init__(self):
        self.ops = {e: [] for e in self.ENGS}
        self.waited = {e: {} for e in self.ENGS}
        self.dma_cum = {}
        self.n = 0

    def _need(self, eng, tok, waits):
        key, val = tok
        if self.waited[eng].get(key, -1) >= val:
            return
        self.waited[eng][key] = val
        waits.append(tok)
        if not key.startswith("dma:"):
            self.ops[key][val]["sig"] = True

    def add(self, eng, fn, reads=(), writes=(), dma=None, ndma=1):
        idx = len(self.ops[eng])
        waits = []
        mykey = ("dma:" + dma) if dma is not None else eng
        for b in reads:
            if b.w is not None:
                if b.w[0] == eng and eng == "pe" and dma is None:
                    continue
                self._need(eng, b.w, waits)
        relax = (eng == "pe" or dma is not None) if STRICT_SYNC else True
        for b in writes:
            if b.w is not None and not (b.w[0] == mykey and relax):
                self._need(eng, b.w, waits)
            for k, v in b.r.items():
                if k == mykey and relax:
                    continue
                self._need(eng, (k, v), waits)
        if dma is not None:
            cum = self.dma_cum.get(dma, 0) + 16 * ndma
            self.dma_cum[dma] = cum
            tok = ("dma:" + dma, cum)
        else:
            tok = (eng, idx)
        for b in reads:
            if b.r.get(tok[0], -1) < tok[1]:
                b.r[tok[0]] = tok[1]
        for b in writes:
            b.w = tok
            b.r = {}
        self.ops[eng].append({"fn": fn, "waits": waits, "sig": False, "dma": dma})
        self.n += 1
        return tok

    def barrier(self, skip=()):
        toks = []
        for e in self.ENGS:
            if self.ops[e]:
                for i in range(len(self.ops[e]) - 1, -1, -1):
                    if self.ops[e][i]["dma"] is None and self.ops[e][i]["fn"] is not None:
                        toks.append((e, i))
                        break
        for k, v in self.dma_cum.items():
            if k in skip:
                continue
            toks.append(("dma:" + k, v))
        for e in self.ENGS:
            waits = []
            for t in toks:
                if t[0] == e:
                    continue
                self._need(e, t, waits)
            self.ops[e].append({"fn": None, "waits": waits, "sig": False, "dma": None})

    def final_wait(self, eng, toks):
        waits = []
        for t in toks:
            self._need(eng, t, waits)
        self.ops[eng].append({"fn": None, "waits": waits, "sig": False, "dma": None})

    def emit(self, nc):
        engsem = {}
        signum = {}
        for e in self.ENGS:
            cnt = 0
            signum[e] = {}
            for i, op in enumerate(self.ops[e]):
                if op["sig"]:
                    signum[e][i] = cnt
                    cnt += 1
            nep = cnt // EPOCH + 1
            engsem[e] = [nc.alloc_semaphore(f"s_{e}_{j}") for j in range(nep)]
        dmasem = {k: nc.alloc_semaphore("d_" + k) for k in self.dma_cum}

        def run(ename, eng):
            for i, op in enumerate(self.ops[ename]):
                for key, val in op["waits"]:
                    if key.startswith("dma:"):
                        eng.wait_ge(dmasem[key[4:]], val)
                    else:
                        s = signum[key][val]
                        eng.wait_ge(engsem[key][s // EPOCH], s % EPOCH + 1)
                if op["fn"] is None:
                    continue
                ins = op["fn"](eng)
                if op["dma"] is not None:
                    ins.then_inc(dmasem[op["dma"]], 16)
                elif op["sig"]:
                    s = signum[ename][i]
                    ins.then_inc(engsem[ename][s // EPOCH], 1)

        with nc.Block() as block:
            @block.tensor
            def _(eng):
                run("pe", eng)

            @block.scalar
            def _(eng):
                run("act", eng)

            @block.vector
            def _(eng):
                run("dve", eng)

            @block.gpsimd
            def _(eng):
                run("pool", eng)

            @block.sync
            def _(eng):
                run("sp", eng)


class Alloc:
    def __init__(self, nc, lo, hi):
        self.nc, self.lo, self.hi, self.p, self.n = nc, lo, hi, lo, 0

    def t(self, shape, dt, name="t"):
        nb = 1
        for s in shape[1:]:
            nb *= s
        nb *= 2 if dt == BF16 else 4
        nb = (nb + 31) // 32 * 32
        off = self.p
        self.p += nb
        assert self.p <= self.hi, f"SBUF overflow {name} {self.p} > {self.hi}"
        self.n += 1
        return self.nc.alloc_sbuf_tensor_at(f"{name}{self.n}", list(shape), dt, offset=off)


def build(NT=16, debug=False, stage=9):
    nc = bass.Bass("TRN2", target_bir_lowering=False)
    SEQ = NT * 512
    P = Prog()

    def finish():
        fin = [("dma:" + k, v) for k, v in P.dma_cum.items()]
        P.final_wait("sp", fin)
        P.emit(nc)
        return nc

    def din(name, shape, dt=F32):
        return nc.dram_tensor(name, list(shape), dt, kind="ExternalInput").ap()

    x_d = din("x", [SEQ, D])
    ctx_d = din("ctx", [CTX, D])
    rows_d = din("rows", [128, NROW])
    cols_d = din("cols", [128, NCOL])
    cst_d = din("cst", [128, NCST])
    wada_d = din("w_ada", [D, 6 * D])
    wtm_d = din("w_tm", [D, 1536])
    wfm_d = din("w_fm", [D, 1600])
    wgate_d = din("w_gate", [96, 512])
    wout_d = din("w_out", [D, D])
    wup_d = din("w_up", [D, 2 * DFF])
    wdown_d = din("w_down", [DFF, D])
    out_d = nc.dram_tensor("out", [SEQ, D], F32, kind="ExternalOutput").ap()
    if debug:
        ob_d = nc.dram_tensor("ob_d", [SEQ, 512], F32, kind="ExternalOutput").ap()
        x1_d = nc.dram_tensor("x1_d", [SEQ, D], F32, kind="ExternalOutput").ap()
    else:
        ob_d = nc.dram_tensor("ob_d", [SEQ, 512], F32).ap()
        x1_d = nc.dram_tensor("x1_d", [SEQ, D], F32).ap()
    hx2_d = nc.dram_tensor("hx2_d", [8, 128, SEQ + 128], BF16).ap()
    wupb_d = nc.dram_tensor("wupb_d", [NFC // 2, 128, 8, 512], BF16).ap()

    pb = [nc.alloc_psum_tensor(f"pb{i}", [128, 512], F32) for i in range(8)]
    PB = [Buf(f"pb{i}") for i in range(8)]

    G = Alloc(nc, SB_LO, SB_HI)
    ident = G.t([128, 128], BF16, "ident")
    Lm = G.t([128, 4, 128], BF16, "Lm")
    maskT = G.t([128, 2, 512], BF16, "maskT")
    ncol = G.t([128, 1], BF16, "ncol")
    ones_r = G.t([1, 128], BF16, "ones_r")
    A1 = G.t([128, D], F32, "A1")
    cA1 = G.t([128, D], F32, "cA1")
    G1 = G.t([128, D], F32, "G1")
    A2 = G.t([128, D], F32, "A2")
    G2 = G.t([128, D], F32, "G2")
    ghead = G.t([128, 512], F32, "ghead")
    B1 = G.t([1, D], BF16, "B1")
    cB1 = G.t([1, D], BF16, "cB1")
    B2 = G.t([1, D], BF16, "B2")
    colp = G.t([128, NCOL], F32, "colp")
    Bc = G.t([128, 24], F32, "Bc")
    wgate = G.t([96, 512], BF16, "wgate")
    dsc = G.t([128, 12, 128], BF16, "dsc")
    S = [G.t([128, 2, 128], F32, "S") for _ in range(2)]
    Sb = [G.t([128, 2, 128], BF16, "Sb") for _ in range(2)]
    alT = G.t([96, 512], BF16, "alT")
    stat = G.t([128, 64], F32, "stat")
    fstat = G.t([128, 8], F32, "fstat")
    b_fst = [Buf(f"fst{i}") for i in range(4)]
    epsc = G.t([128, 1], F32, "epsc")
    g_const = Buf("consts")
    bS = [Buf("S0"), Buf("S1")]
    bSb = [Buf("Sb0"), Buf("Sb1")]
    OV = G.p

    Z = Alloc(nc, OV, SB_HI)
    cstf = Z.t([128, NCST], F32, "cstf")
    rowp = Z.t([128, NROW], F32, "rowp")
    wa = [Z.t([128, 8, D], F32, "wa") for _ in range(2)]
    modr = [Z.t([128, D], F32, "modr") for _ in range(2)]
    rep = Z.t([128, 16, 128], F32, "rep")
    onesf = Z.t([128, 128], F32, "onesf")
    scl = Z.t([128, 16], F32, "scl")
    wgf = Z.t([96, 512], F32, "wgf")
    zt = Z.t([128, 64], BF16, "zt")
    b_cst, b_row, b_col, b_wgf = Buf("cstf"), Buf("rowp"), Buf("colp"), Buf("wgf")
    b_wa = [Buf("wa0"), Buf("wa1")]
    b_modr = [Buf("modr0"), Buf("modr1")]
    b_rep, b_scl, b_zt = Buf("rep"), Buf("scl"), Buf("zt")

    P.add("sp", lambda e: e.dma_start(out=cstf[:], in_=cst_d[:, :]), writes=[b_cst], dma="ld0")
    P.add("sp", lambda e: e.dma_start(out=colp[:], in_=cols_d[:, :]), writes=[b_col], dma="ld1")
    P.add("sp", lambda e: e.dma_start(out=wgf[:], in_=wgate_d[:, :]), writes=[b_wgf], dma="ld2")
    P.add("sp", lambda e: e.dma_start(out=rowp[:, 0:4608], in_=rows_d[:, 0:4608]), writes=[b_row], dma="ld3", ndma=2)
    P.add("sp", lambda e: e.dma_start(out=rowp[:, 4608:NROW], in_=rows_d[:, 4608:NROW]), writes=[b_row], dma="ld3", ndma=0)

    b_wupb = Buf("wupb")
    wup_v = wup_d.rearrange("(k p) j -> p k j", p=128)
    for pc in range(NFC // 2):
        for ug in range(2):
            off = ug * DFF + pc * 256
            P.add("pool", lambda e, pc=pc, ug=ug, off=off: e.dma_start(out=wupb_d[pc, :, :, ug * 256:(ug + 1) * 256], in_=wup_v[:, :, off:off + 256]),
                  writes=[b_wupb], dma="wupc", ndma=1)

    P.add("dve", lambda e: e.tensor_copy(out=ident[:], in_=cstf[:, K_ID:K_ID + 128]), reads=[b_cst], writes=[g_const])
    for j, k0 in enumerate((K_LFI, K_LFR, K_LBI, K_LBR)):
        P.add("dve", lambda e, j=j, k0=k0: e.tensor_copy(out=Lm[:, j, :], in_=cstf[:, k0:k0 + 128]), reads=[b_cst], writes=[g_const])
    for d_, k0 in enumerate((K_MF, K_MB)):
        for h in range(4):
            P.add("dve", lambda e, d_=d_, k0=k0, h=h: e.tensor_copy(out=maskT[:, d_, h * 128:(h + 1) * 128], in_=cstf[:, k0:k0 + 128]),
                  reads=[b_cst], writes=[g_const])
    P.add("dve", lambda e: e.tensor_copy(out=ncol[:], in_=cstf[:, K_NC:K_NC + 1]), reads=[b_cst], writes=[g_const])
    P.add("dve", lambda e: e.memset(ones_r[:], 1.0), writes=[g_const])
    P.add("dve", lambda e: e.memset(epsc[:], EPS), writes=[g_const])
    P.add("dve", lambda e: e.memset(onesf[:], 1.0), writes=[b_rep])
    P.add("dve", lambda e: e.memset(zt[:], 0.0), writes=[b_zt])
    P.add("dve", lambda e: e.memset(alT[64:96, :], 1.0), writes=[g_const])
    P.add("dve", lambda e: e.tensor_copy(out=wgate[:], in_=wgf[:]), reads=[b_wgf], writes=[g_const])
    P.add("dve", lambda e: e.tensor_copy(out=ghead[:], in_=rowp[:, R_GH:R_GH + 512]), reads=[b_row], writes=[g_const])
    for d_ in range(2):
        P.add("dve", lambda e, d_=d_: e.memset(S[d_][:], 0.0), writes=[bS[d_]])
        P.add("dve", lambda e, d_=d_: e.memset(Sb[d_][:], 0.0), writes=[bSb[d_]])
    for t in range(3):
        for j in range(4):
            P.add("dve", lambda e, t=t, j=j: e.tensor_scalar(out=dsc[:, t * 4 + j, :], in0=ident[:], scalar1=colp[:, Q_WSC + t * 4 + j:Q_WSC + t * 4 + j + 1],
                                                              scalar2=None, op0=ALU.mult), reads=[g_const, b_col], writes=[g_const])
    b_hx2 = Buf("hx2_d")
    for k in range(8):
        P.add("sp", lambda e, k=k: e.dma_start(out=hx2_d[k, :, 0:64], in_=zt[:, 0:64]), reads=[b_zt], writes=[b_hx2], dma="zm", ndma=1)
        P.add("sp", lambda e, k=k: e.dma_start(out=hx2_d[k, :, SEQ + 64:SEQ + 128], in_=zt[:, 0:64]), reads=[b_zt], writes=[b_hx2], dma="zm", ndma=1)

    P.add("act", lambda e: e.activation(out=scl[:], in_=colp[:, Q_C:Q_C + 16], func=AF.Silu), reads=[b_col], writes=[b_scl])
    for j in range(16):
        P.add("dve", lambda e, j=j: e.tensor_scalar(out=rep[:, j, :], in0=onesf[:], scalar1=scl[:, j:j + 1], scalar2=None, op0=ALU.mult),
              reads=[b_scl, b_rep], writes=[b_rep])
    wada_v = wada_d.rearrange("(k p) j -> p k j", p=128)

    def mod_piece(m, v, slot):
        for hf in range(2):
            bank = 2 * slot + hf
            for k in range(8):
                P.add("pe", lambda e, k=k, hf=hf, bank=bank: e.matmul(pb[bank][:, :], lhsT=rep[:, v * 8 + k, :], rhs=wa[m % 2][:, k, hf * 512:(hf + 1) * 512],
                                                                     start=(k == 0), stop=(k == 7)),
                      reads=[b_rep, b_wa[m % 2]], writes=[PB[bank]])
            P.add("dve", lambda e, hf=hf, bank=bank: e.tensor_tensor(out=modr[slot][:, hf * 512:(hf + 1) * 512], in0=pb[bank][:, :],
                                                                      in1=rowp[:, R_BADA + m * D + hf * 512:R_BADA + m * D + (hf + 1) * 512], op=ALU.add),
                  reads=[PB[bank], b_row], writes=[b_modr[slot]])

    for m in range(6):
        for q in range(2):
            P.add("sp", lambda e, m=m, q=q: e.dma_start(out=wa[m % 2][:, q * 4:(q + 1) * 4, :], in_=wada_v[:, q * 4:(q + 1) * 4, m * D:(m + 1) * D]),
                  writes=[b_wa[m % 2]], dma=f"wa{m % 2}", ndma=1)
        mod_piece(m, 0, 0)
        if m < 2:
            mod_piece(m, 1, 1)
        if m == 0:
            P.add("act", lambda e: e.copy(out=B1[0:1, :], in_=modr[0][0:1, :]), reads=[b_modr[0]], writes=[g_const])
            P.add("act", lambda e: e.copy(out=cB1[0:1, :], in_=modr[1][0:1, :]), reads=[b_modr[1]], writes=[g_const])
        elif m == 1:
            P.add("dve", lambda e: e.scalar_tensor_tensor(out=A1[:], in0=modr[0][:], scalar=1.0, in1=rowp[:, R_GPM:R_GPM + D], op0=ALU.add, op1=ALU.mult),
                  reads=[b_modr[0], b_row], writes=[g_const])
            P.add("dve", lambda e: e.scalar_tensor_tensor(out=cA1[:], in0=modr[1][:], scalar=1.0, in1=rowp[:, R_GPM:R_GPM + D], op0=ALU.add, op1=ALU.mult),
                  reads=[b_modr[1], b_row], writes=[g_const])
        elif m == 2:
            P.add("dve", lambda e: e.tensor_tensor(out=G1[:], in0=modr[0][:], in1=rowp[:, R_GQM:R_GQM + D], op=ALU.mult), reads=[b_modr[0], b_row], writes=[g_const])
        elif m == 3:
            P.add("act", lambda e: e.copy(out=B2[0:1, :], in_=modr[0][0:1, :]), reads=[b_modr[0]], writes=[g_const])
        elif m == 4:
            P.add("dve", lambda e: e.scalar_tensor_tensor(out=A2[:], in0=modr[0][:], scalar=1.0, in1=rowp[:, R_GPF:R_GPF + D], op0=ALU.add, op1=ALU.mult),
                  reads=[b_modr[0], b_row], writes=[g_const])
        else:
            P.add("dve", lambda e: e.tensor_tensor(out=G2[:], in0=modr[0][:], in1=rowp[:, R_GQF:R_GQF + D], op=ALU.mult), reads=[b_modr[0], b_row], writes=[g_const])
    for v_, Br_ in enumerate((B1, cB1, B2)):
        for k in range(8):
            P.add("pe", lambda e, v_=v_, k=k, Br_=Br_: e.matmul(pb[4][:, v_ * 8 + k:v_ * 8 + k + 1], lhsT=Br_[0:1, k * 128:(k + 1) * 128], rhs=ones_r[0:1, 0:1],
                                                            start=True, stop=True), reads=[g_const], writes=[PB[4]])
    P.add("dve", lambda e: e.tensor_copy(out=Bc[:], in_=pb[4][:, 0:24]), reads=[PB[4]], writes=[g_const])
    P.barrier(skip=("wupc",))
    if stage == 0:
        return finish()

    Y = Alloc(nc, OV, SB_HI)
    wtm = Y.t([128, 8, 1536], BF16, "wtm")
    wfm = Y.t([128, 8, 1600], BF16, "wfm")
    wout = Y.t([128, 8, D], BF16, "wout")
    xt = [Y.t([128, 4, D], F32, "xt") for _ in range(2)]
    hxT = Y.t([128, 8, 512], BF16, "hxT")
    obl = [Y.t([128, 512], F32, "obl") for _ in range(2)]
    g_l = [Y.t([128, 256], BF16, "g_l") for _ in range(2)]
    g_eb = [Y.t([128, 256], F32, "g_eb") for _ in range(2)]
    g_enb = [Y.t([128, 256], F32, "g_enb") for _ in range(2)]
    g_er = [Y.t([128, 256], F32, "g_er") for _ in range(2)]
    g_e = g_er
    g_qz = [Y.t([128, 2, 256], BF16, "g_qz") for _ in range(2)]
    g_kt = [Y.t([128, 256], BF16, "g_kt") for _ in range(2)]
    kqs = [Y.t([128, 512], F32, "kqs") for _ in range(2)]
    g_et = [Y.t([128, 2], F32, "g_et") for _ in range(2)]
    g_kh = [Y.t([128, 256], BF16, "g_kh") for _ in range(2)]
    g_vb = [Y.t([128, 512], BF16, "g_vb") for _ in range(2)]
    g_T = [Y.t([128, 768], BF16, "g_T") for _ in range(2)]
    g_att = [Y.t([128, 512], BF16, "g_att") for _ in range(2)]
    obst = obl
    osum = Y.t([128, 512], F32, "osum")
    ghs4 = Y.t([128, 4, 512], BF16, "ghs4")
    b_ghs = [Buf(f"ghs{j}") for j in range(4)]
    yg = Y.t([128, 512], BF16, "yg")
    junk = Y.t([128, D], BF16, "junk")
    ybf = [Y.t([128, D], BF16, "ybf") for _ in range(2)]
    b_ybf = [Buf("ybf0"), Buf("ybf1")]
    yxT = Y.t([128, 8, 512], BF16, "yxT")
    sbT = [Y.t([128, 512], BF16, "sbT") for _ in range(2)]
    scT = [Y.t([128, 512], BF16, "scT") for _ in range(2)]
    zpad = Y.t([128, 4, 8, 66], BF16, "zpad")
    tmpf = Y.t([128, 512], F32, "tmpf")
    hx2s = [Y.t([128, 8, 128], BF16, "hx2s") for _ in range(2)]

    b_wtm, b_wfm, b_wout = Buf("wtm"), Buf("wfm"), Buf("wout")
    b_xt = [[Buf(f"xt{s}{j}") for j in range(4)] for s in range(2)]
    b_hxT = [Buf(f"hxT{j}") for j in range(4)]
    b_obl = [Buf("obl0"), Buf("obl1")]
    bg = {n: Buf(n) for n in ("osum", "sg", "ghs", "yg", "junk", "ybf", "tmpf", "alT", "stat")}
    bF = [{n: Buf(n + str(i)) for n in ("e", "l", "eb", "enb", "er", "qt", "kt", "qblk", "kqs")} for i in range(2)]
    bB = [{n: Buf(n + str(i)) for n in ("et", "kh", "vb", "T", "att")} for i in range(2)]
    b_obst = b_obl
    b_yxg = [Buf(f"yxg{j}") for j in range(4)]
    b_yxs = [Buf(f"yxs{j}") for j in range(4)]
    b_sbT = [Buf("sbT0"), Buf("sbT1")]
    b_scT = [Buf("scT0"), Buf("scT1")]
    b_zp = [Buf(f"zp{j}") for j in range(4)]
    b_hx2s = [Buf("hx2s0"), Buf("hx2s1")]
    b_ob_d = [Buf(f"ob_d{i}") for i in range(NT * 4)]
    b_x1_d = [Buf(f"x1_d{i}") for i in range(NT * 4)]
    b_hx2_d = [Buf(f"hx2_d{i}") for i in range(NT)]

    wtm_v = wtm_d.rearrange("(k p) j -> p k j", p=128)
    wfm_v = wfm_d.rearrange("(k p) j -> p k j", p=128)
    wout_v = wout_d.rearrange("(k p) j -> p k j", p=128)
    for k in range(8):
        P.add("pool", lambda e, k=k: e.dma_start(out=wtm[:, k, :], in_=wtm_v[:, k, :]), writes=[b_wtm], dma="wtm", ndma=1)
    for k in range(8):
        P.add("pool", lambda e, k=k: e.dma_start(out=wfm[:, k, :], in_=wfm_v[:, k, :]), writes=[b_wfm], dma="wfm", ndma=1)
    for k in range(8):
        P.add("pool", lambda e, k=k: e.dma_start(out=wout[:, k, :], in_=wout_v[:, k, :]), writes=[b_wout], dma="wout", ndma=1)
    P.add("dve", lambda e: e.memset(zpad[:], 0.0), writes=b_zp)
    for i_ in range(2):
        P.add("dve", lambda e, i_=i_: e.memset(g_qz[i_][:], 0.0), writes=[bF[i_]["qt"]])

    sc_i = [0]

    bst8 = [Buf(f"stat{i}") for i in range(8)]

    def stat_slot():
        k = sc_i[0] % 8
        sc_i[0] += 1
        return 8 * k, bst8[k]

    def rstd_from(ss_ap, nfeat, out_ap, rb, wb):
        P.add("act", lambda e: e.activation(out=out_ap, in_=ss_ap, func=AF.Ln, scale=1.0 / nfeat, bias=epsc[:, 0:1]), reads=rb + [g_const], writes=wb)
        P.add("act", lambda e: e.activation(out=out_ap, in_=out_ap, func=AF.Exp, scale=-0.5), reads=wb, writes=wb)

    fcnt = [0]

    def front_stages(src_ap, src_buf, Arow, Brow, dstT, dst_cols, dst_buf):
        n_ = fcnt[0]
        fcnt[0] += 1
        q_, par = n_ % 4, n_ % 2
        banks = (6, 7) if par == 0 else (2, 5)
        ss = fstat[:, 2 * q_:2 * q_ + 1]
        rs = fstat[:, 2 * q_ + 1:2 * q_ + 2]
        bst = b_fst[q_]
        yb = ybf[par]

        def fa():
            P.add("dve", lambda e: e.memset(fstat[:, 2 * q_:2 * q_ + 2], 0.0), writes=[bst])
            P.add("act", lambda e: e.activation(out=junk[:], in_=src_ap, func=AF.Square, accum_out=ss), reads=[src_buf, bst], writes=[bg["junk"], bst])
            rstd_from(ss, D, rs, [bst], [bst])

        def fb():
            P.add("dve", lambda e: e.scalar_tensor_tensor(out=yb[:], in0=src_ap, scalar=rs, in1=Arow[:], op0=ALU.mult, op1=ALU.mult),
                  reads=[src_buf, bst, g_const], writes=[b_ybf[par]])

        bv = {id(B1): 0, id(cB1): 8, id(B2): 16}[id(Brow)]

        def fc():
            for k in range(8):
                bank = banks[k // 4]
                o = pb[bank][:, (k % 4) * 128:(k % 4 + 1) * 128]
                P.add("pe", lambda e, o=o, k=k: e.matmul(o, lhsT=yb[:, k * 128:(k + 1) * 128], rhs=ident[:], start=True, stop=True),
                      reads=[b_ybf[par], g_const], writes=[PB[bank]])

        def fd():
            for k in range(8):
                bank = banks[k // 4]
                src = pb[bank][:, (k % 4) * 128:(k % 4 + 1) * 128]
                if par == 0:
                    P.add("act", lambda e, k=k, src=src: e.activation(out=dstT[:, k, dst_cols], in_=src, func=AF.Identity, bias=Bc[:, bv + k:bv + k + 1]),
                          reads=[PB[bank], g_const], writes=[dst_buf])
                else:
                    P.add("dve", lambda e, k=k, src=src: e.tensor_scalar(out=dstT[:, k, dst_cols], in0=src, scalar1=Bc[:, bv + k:bv + k + 1], scalar2=None, op0=ALU.add),
                          reads=[PB[bank], g_const], writes=[dst_buf])

        return [fa, fb, fc, fd]

    def front(src_ap, src_buf, Arow, Brow, dstT, dst_cols, dst_buf, banks=None):
        for st_ in front_stages(src_ap, src_buf, Arow, Brow, dstT, dst_cols, dst_buf):
            st_()

    def front4(slot):
        if fcnt[0] % 2:
            fcnt[0] += 1
        S_ = [front_stages(xt[slot][:, j, :], b_xt[slot][j], A1, B1, hxT, slice(j * 128, (j + 1) * 128), b_hxT[j]) for j in range(4)]
        for (j, k) in ((0, 0), (1, 0), (0, 1), (1, 1), (2, 0), (3, 0), (0, 2), (0, 3), (2, 1), (1, 2), (1, 3), (3, 1), (2, 2), (2, 3), (3, 2), (3, 3)):
            S_[j][k]()

    def proj_tm(cols, grp, bank, hbuf):
        for k in range(8):
            P.add("pe", lambda e, k=k: e.matmul(pb[bank][:, :], lhsT=hxT[:, k, cols], rhs=wtm[:, k, grp * 512:(grp + 1) * 512], start=(k == 0), stop=(k == 7)),
                  reads=[hbuf, b_wtm], writes=[PB[bank]])

    def proj_fm(c0, M, bank, ncols=512):
        for k in range(8):
            P.add("pe", lambda e, k=k: e.matmul(pb[bank][0:M, 0:ncols], lhsT=wfm[:, k, c0:c0 + M], rhs=hxT[:, k, 0:ncols], start=(k == 0), stop=(k == 7)),
                  reads=b_hxT + [b_wfm], writes=[PB[bank]])

    def gate_lowrank(ncols=512):
        proj_fm(0, 64, 7, ncols)
        P.add("act", lambda e: e.copy(out=alT[0:64, 0:ncols], in_=pb[7][0:64, 0:ncols]), reads=[PB[7]], writes=[bg["alT"]])

    def gla_stages(d, sl, bs, cols, hbuf, full):
        bK, bV, bA = 3 * sl, 3 * sl + 1, 3 * sl + 2
        F, Bb = bF[sl], bB[bs]
        e_, l_, eb_, enb_, er_, qz_, kt_, kq_ = g_e[sl], g_l[sl], g_eb[sl], g_enb[sl], g_er[sl], g_qz[sl], g_kt[sl], kqs[sl]
        et_, kh_, vb_, T_, att_ = g_et[bs], g_kh[bs], g_vb[bs], g_T[bs], g_att[bs]
        Tps = pb[bV].bitcast(BF16)

        def s1():
            proj_tm(cols, 0, bK, hbuf)
            proj_tm(cols, 1, bV, hbuf)
            P.add("pe", lambda e: e.matmul(pb[bA][:, 0:256], lhsT=alT[0:96, cols], rhs=wgate[:, d * 256:(d + 1) * 256], start=True, stop=True),
                  reads=[bg["alT"], g_const], writes=[PB[bA]])

        def s2():
            P.add("act", lambda e: e.activation(out=e_[:], in_=pb[bA][:, 0:256], func=AF.Exp, scale=-1.0), reads=[PB[bA]], writes=[F["er"]])
            P.add("act", lambda e: e.activation(out=l_[:], in_=e_[:], func=AF.Ln, bias=1.0), reads=[F["er"]], writes=[F["l"]])
            P.add("act", lambda e: e.copy(out=kq_[:], in_=pb[bK][:, :]), reads=[PB[bK]], writes=[F["kqs"]])
            P.add("act", lambda e: e.copy(out=vb_[:], in_=pb[bV][:, :]), reads=[PB[bV]], writes=[Bb["vb"]])

        def s3():
            if full:
                P.add("pe", lambda e: e.matmul(pb[bK][:, 0:256], lhsT=Lm[:, 2 * d, :], rhs=l_[:], start=True, stop=True), reads=[F["l"], g_const], writes=[PB[bK]])
            P.add("pe", lambda e: e.matmul(pb[bK][:, 256:512], lhsT=Lm[:, 2 * d + 1, :], rhs=l_[:], start=True, stop=True), reads=[F["l"], g_const], writes=[PB[bK]])
            for p_ in range(2):
                P.add("pe", lambda e, p_=p_: e.matmul(pb[bA][:, 256 + p_:257 + p_], lhsT=l_[:, p_ * 128:(p_ + 1) * 128], rhs=ncol[:, 0:1], start=True, stop=True),
                      reads=[F["l"], g_const], writes=[PB[bA]])

        def s4():
            if full:
                P.add("act", lambda e: e.activation(out=eb_[:], in_=pb[bK][:, 0:256], func=AF.Exp), reads=[PB[bK]], writes=[F["eb"]])
                P.add("act", lambda e: e.activation(out=enb_[:], in_=pb[bK][:, 0:256], func=AF.Exp, scale=-1.0), reads=[PB[bK]], writes=[F["enb"]])
            P.add("act", lambda e: e.activation(out=er_[:], in_=pb[bK][:, 256:512], func=AF.Exp), reads=[PB[bK]], writes=[F["er"]])
            P.add("act", lambda e: e.activation(out=et_[:], in_=pb[bA][:, 256:258], func=AF.Exp), reads=[PB[bA]], writes=[Bb["et"]])

        def s5():
            if full:
                kq3 = kq_[:, 256:512].rearrange("p (a b) -> p a b", a=2)
                eb3 = eb_[:].rearrange("p (a b) -> p a b", a=2)
                for w in range(2):
                    P.add("dve", lambda e, w=w: e.scalar_tensor_tensor(out=qz_[:, :, w * 192:w * 192 + 64], in0=kq3[:, :, w * 64:(w + 1) * 64], scalar=0.125,
                                                                       in1=eb3[:, :, w * 64:(w + 1) * 64], op0=ALU.mult, op1=ALU.mult),
                          reads=[F["kqs"], F["eb"]], writes=[F["qt"]])
                P.add("dve", lambda e: e.tensor_tensor(out=kt_[:], in0=kq_[:, 0:256], in1=enb_[:], op=ALU.mult), reads=[F["kqs"], F["enb"]], writes=[F["kt"]])
            P.add("dve", lambda e: e.tensor_tensor(out=kh_[:], in0=kq_[:, 0:256], in1=er_[:], op=ALU.mult), reads=[F["kqs"], F["er"]], writes=[Bb["kh"]])

        def s6():
            if not full:
                return
            for j in range(4):
                P.add("pe", lambda e, j=j: e.transpose(Tps[:, j * 128:(j + 1) * 128], qz_[:, j // 2, (j % 2) * 128:(j % 2 + 1) * 128], ident[:]),
                      reads=[F["qt"], g_const], writes=[PB[bV]])
            for j in range(2):
                P.add("pe", lambda e, j=j: e.transpose(Tps[:, 512 + j * 128:512 + (j + 1) * 128], kt_[:, j * 128:(j + 1) * 128], ident[:]),
                      reads=[F["kt"], g_const], writes=[PB[bV]])

        def s7():
            if not full:
                return
            P.add("dve", lambda e: e.tensor_copy(out=T_[:], in_=Tps[:, 0:768]), reads=[PB[bV]], writes=[Bb["T"]])

        def s8():
            if not full:
                return
            for pr in range(2):
                P.add("pe", lambda e, pr=pr: e.matmul(pb[bA][:, pr * 256:(pr + 1) * 256], lhsT=T_[:, 512 + pr * 128:512 + (pr + 1) * 128], rhs=T_[:, pr * 256:(pr + 1) * 256],
                                                      start=True, stop=True), reads=[Bb["T"]], writes=[PB[bA]])

        def s9():
            if not full:
                return
            P.add("dve", lambda e: e.tensor_tensor(out=att_[:], in0=pb[bA][:, :], in1=maskT[:, d, :], op=ALU.mult), reads=[PB[bA], g_const], writes=[Bb["att"]])

        return [s1, s2, s3, s4, s5, s6, s7, s8, s9]

    def gla_back(d, bs, full):
        Bb = bB[bs]
        et_, kh_, vb_, T_, att_ = g_et[bs], g_kh[bs], g_vb[bs], g_T[bs], g_att[bs]
        if full:
            for h in range(4):
                pr = h // 2
                P.add("pe", lambda e, h=h, pr=pr: e.matmul(pb[6][:, h * 128:(h + 1) * 128], lhsT=T_[:, h * 128:(h + 1) * 128], rhs=Sb[d][:, pr, :],
                                                           start=True, stop=False), reads=[Bb["T"], bSb[d]], writes=[PB[6]])
                P.add("pe", lambda e, h=h: e.matmul(pb[6][:, h * 128:(h + 1) * 128], lhsT=att_[:, h * 128:(h + 1) * 128], rhs=vb_[:, h * 128:(h + 1) * 128],
                                                    start=False, stop=True), reads=[Bb["att"], Bb["vb"]], writes=[PB[6]])
        for pr in range(2):
            P.add("pe", lambda e, pr=pr: e.matmul(pb[7][:, pr * 256:(pr + 1) * 256], lhsT=kh_[:, pr * 128:(pr + 1) * 128], rhs=vb_[:, pr * 256:(pr + 1) * 256],
                                                  start=True, stop=True), reads=[Bb["kh"], Bb["vb"]], writes=[PB[7]])
        for pr in range(2):
            for w in range(2):
                rsl = slice(w * 64, (w + 1) * 64)
                c0 = pr * 256 + w * 128
                P.add("dve", lambda e, pr=pr, rsl=rsl, c0=c0: e.scalar_tensor_tensor(out=S[d][rsl, pr, :], in0=S[d][rsl, pr, :], scalar=et_[rsl, pr:pr + 1],
                                                                                      in1=pb[7][rsl, c0:c0 + 128], op0=ALU.mult, op1=ALU.add),
                      reads=[bS[d], Bb["et"], PB[7]], writes=[bS[d]])
        P.add("act", lambda e: e.copy(out=Sb[d][:], in_=S[d][:]), reads=[bS[d]], writes=[bSb[d]])

    def gla_tile(d, order, after_back, mid=None):
        st = {}
        for n_, j in enumerate(order):
            st[j] = gla_stages(d, n_ % 2, n_ % 2, slice(j * 128, (j + 1) * 128), b_hxT[j], True)
        c0_, c1_, c2_, c3_ = order
        for k in range(9):
            st[c0_][k]()
            st[c1_][k]()
        if mid is None:
            st[c2_][0]()
            st[c3_][0]()
        for n_, j in enumerate((c0_, c1_)):
            gla_back(d, n_, True)
            if mid is not None:
                mid(j)
            after_back(j)
        for k in range(0 if mid is not None else 1, 9):
            st[c2_][k]()
            st[c3_][k]()
        for n_, j in enumerate((c2_, c3_)):
            gla_back(d, n_, True)
            if mid is not None:
                mid(j)
            after_back(j)

    ctx_v = ctx_d.rearrange("(j p) f -> p j f", p=128)
    for j in range(2):
        P.add("sp", lambda e, j=j: e.dma_start(out=xt[0][:, j, :], in_=ctx_v[:, j, :]), writes=[b_xt[0][j]], dma=f"x0{j}")
    for j in range(2):
        front(xt[0][:, j, :], b_xt[0][j], cA1, cB1, hxT, slice(j * 128, (j + 1) * 128), b_hxT[j])
    gate_lowrank(256)
    for d_, order in ((0, (0, 1)), (1, (1, 0))):
        for j in order:
            cols = slice(j * 128, (j + 1) * 128)
            for st_ in gla_stages(d_, 0, 0, cols, b_hxT[j], False):
                st_()
            gla_back(d_, 0, False)

    if stage == 1:
        return finish()
    x_v = x_d.rearrange("(n j p) f -> n p j f", p=128, j=4)
    ob_v = ob_d.rearrange("(n p) f -> n p f", p=128)
    x1_v = x1_d.rearrange("(n p) f -> n p f", p=128)

    def load_x(i, slot):
        for j in range(4):
            P.add("sp", lambda e, j=j: e.dma_start(out=xt[slot][:, j, :], in_=x_v[i, :, j, :]), writes=[b_xt[slot][j]], dma=f"x{slot}{j}")

    tseq = [i for i in range(NT - 1, -1, -1)] + [i for i in range(NT)]
    tcount = 0
    load_x(tseq[0], 0)
    for i in range(NT - 1, -1, -1):
        slot = tcount % 2
        tcount += 1
        if tcount < len(tseq) and stage >= 3 or tcount < NT:
            load_x(tseq[tcount], tcount % 2)
        front4(slot)
        gate_lowrank()
        def after_a(j, i=i):
            n = i * 4 + j
            P.add("act", lambda e, j=j: e.copy(out=obst[j % 2][:], in_=pb[6][:, :]), reads=[PB[6]], writes=[b_obst[j % 2]])
            P.add("sp", lambda e, j=j, n=n: e.dma_start(out=ob_v[n, :, :], in_=obst[j % 2][:]), reads=[b_obst[j % 2]], writes=[b_ob_d[n]], dma=f"obst{j % 2}")

        gla_tile(1, (3, 2, 1, 0), after_a)

    if stage == 2:
        return finish()
    hx2_v = hx2_d.rearrange("k p t -> p k t")
    for i in range(NT):
        slot = tcount % 2
        tcount += 1
        if tcount < len(tseq):
            load_x(tseq[tcount], tcount % 2)
        front4(slot)
        gate_lowrank()
        def sc_step(j):
            s2 = j % 2
            proj_fm(64 + j * 128, 128, 0)
            P.add("act", lambda e, s2=s2: e.copy(out=sbT[s2][:], in_=pb[0][:, :]), reads=[PB[0]], writes=[b_sbT[s2]])
            proj_fm(64 + 512 + j * 128, 128, 1)
            P.add("act", lambda e, s2=s2: e.copy(out=scT[s2][:], in_=pb[1][:, :]), reads=[PB[1]], writes=[b_scT[s2]])
            proj_fm(64 + 1024 + j * 128, 128, 2)
            P.add("dve", lambda e, j=j, s2=s2: e.tensor_tensor(out=zpad[:, j, :, 1:65], in0=pb[2][:, :].rearrange("p (r c) -> p r c", r=8),
                                                               in1=scT[s2][:].rearrange("p (r c) -> p r c", r=8), op=ALU.mult),
                  reads=[PB[2], b_scT[s2]], writes=[b_zp[j]])
            for t in range(3):
                P.add("pe", lambda e, j=j, t=t: e.matmul(pb[3][:, :], lhsT=dsc[:, t * 4 + j, :], rhs=zpad[:, j, :, t:t + 64], start=(t == 0), stop=(t == 2)),
                      reads=[b_zp[j], g_const], writes=[PB[3]])
            P.add("dve", lambda e, j=j, s2=s2: e.scalar_tensor_tensor(out=yxT[:, 4 + j, :], in0=pb[3][:, :], scalar=colp[:, Q_BSC + j:Q_BSC + j + 1], in1=sbT[s2][:],
                                                                      op0=ALU.add, op1=ALU.mult), reads=[PB[3], b_sbT[s2], g_const], writes=[b_yxs[j]])
        sgt = (tmpf, osum)
        sgb = (bg["tmpf"], bg["osum"])
        for j in range(4):
            cols = slice(j * 128, (j + 1) * 128)
            proj_tm(cols, 2, j, b_hxT[j])
            P.add("act", lambda e, j=j: e.activation(out=sgt[j % 2][:], in_=pb[j][:, :], func=AF.Silu), reads=[PB[j]], writes=[sgb[j % 2]])
            P.add("dve", lambda e, j=j: e.tensor_tensor(out=ghs4[:, j, :], in0=sgt[j % 2][:], in1=ghead[:], op=ALU.mult), reads=[sgb[j % 2], g_const], writes=[b_ghs[j]])

        def load_ob(j, i=i):
            n = i * 4 + j
            P.add("sp", lambda e, j=j, n=n: e.dma_start(out=obl[j % 2][:], in_=ob_v[n, :, :]), reads=[b_ob_d[n]], writes=[b_obl[j % 2]], dma=f"obl{j % 2}")

        load_ob(0)
        load_ob(1)

        def after_b(j, i=i):
            cols = slice(j * 128, (j + 1) * 128)
            P.add("dve", lambda e, j=j: e.tensor_tensor(out=osum[:], in0=pb[6][:, :], in1=obl[j % 2][:], op=ALU.add), reads=[PB[6], b_obl[j % 2]], writes=[bg["osum"]])
            if j < 2:
                load_ob(j + 2)
            c0, bs_ = stat_slot()
            P.add("dve", lambda e, c0=c0: e.memset(stat[:, c0:c0 + 8], 0.0), writes=[bs_])
            for h in range(4):
                P.add("act", lambda e, h=h, c0=c0: e.activation(out=junk[:, h * 128:(h + 1) * 128], in_=osum[:, h * 128:(h + 1) * 128], func=AF.Square,
                                                                accum_out=stat[:, c0 + h:c0 + h + 1]), reads=[bg["osum"], bs_], writes=[bg["junk"], bs_])
            rstd_from(stat[:, c0:c0 + 4], 128, stat[:, c0 + 4:c0 + 8], [bs_], [bs_])
            for h in range(4):
                P.add("dve", lambda e, h=h, c0=c0, j=j: e.scalar_tensor_tensor(out=yg[:, h * 128:(h + 1) * 128], in0=osum[:, h * 128:(h + 1) * 128],
                                                                               scalar=stat[:, c0 + 4 + h:c0 + 5 + h], in1=ghs4[:, j, h * 128:(h + 1) * 128],
                                                                               op0=ALU.mult, op1=ALU.mult), reads=[bg["osum"], bs_, b_ghs[j]], writes=[bg["yg"]])
            Tps = pb[7].bitcast(BF16)
            for h in range(4):
                P.add("pe", lambda e, h=h: e.transpose(Tps[:, h * 128:(h + 1) * 128], yg[:, h * 128:(h + 1) * 128], ident[:]), reads=[bg["yg"], g_const], writes=[PB[7]])
            P.add("act", lambda e, cols=cols: e.copy(out=yxT[:, 0:4, cols], in_=Tps[:, 0:512].rearrange("p (k t) -> p k t", k=4)), reads=[PB[7]], writes=[b_yxg[j]])

        gla_tile(0, (0, 1, 2, 3), after_b, mid=sc_step)
        def oproj_stages(j, i=i, slot=slot):
            cols = slice(j * 128, (j + 1) * 128)
            n = i * 4 + j
            ob_ = 3 * (j % 2)
            c0, bs_ = stat_slot()
            tmp2 = (tmpf, osum)
            tmpb = (bg["tmpf"], bg["osum"])

            def o1():
                for hf in range(2):
                    for k in range(8):
                        rb = [b_yxg[j]] if k < 4 else [b_yxs[k - 4]]
                        P.add("pe", lambda e, k=k, hf=hf: e.matmul(pb[ob_ + hf][:, :], lhsT=yxT[:, k, cols], rhs=wout[:, k, hf * 512:(hf + 1) * 512],
                                                                   start=(k == 0), stop=(k == 7)), reads=rb + [b_wout], writes=[PB[ob_ + hf]])

            def o2():
                P.add("dve", lambda e: e.memset(stat[:, c0:c0 + 4], 0.0), writes=[bs_])
                for hf in range(2):
                    P.add("act", lambda e, hf=hf: e.activation(out=junk[:, hf * 512:(hf + 1) * 512], in_=pb[ob_ + hf][:, :], func=AF.Square,
                                                               accum_out=stat[:, c0 + hf:c0 + hf + 1]), reads=[PB[ob_ + hf], bs_], writes=[bg["junk"], bs_])
                P.add("dve", lambda e: e.tensor_tensor(out=stat[:, c0 + 2:c0 + 3], in0=stat[:, c0:c0 + 1], in1=stat[:, c0 + 1:c0 + 2], op=ALU.add),
                      reads=[bs_], writes=[bs_])
                rstd_from(stat[:, c0 + 2:c0 + 3], D, stat[:, c0 + 3:c0 + 4], [bs_], [bs_])

            def o3():
                for hf in range(2):
                    hs = slice(hf * 512, (hf + 1) * 512)
                    P.add("dve", lambda e, hf=hf, hs=hs: e.scalar_tensor_tensor(out=tmp2[hf][:], in0=pb[ob_ + hf][:, :], scalar=stat[:, c0 + 3:c0 + 4], in1=G1[:, hs],
                                                                               op0=ALU.mult, op1=ALU.mult), reads=[PB[ob_ + hf], bs_, g_const], writes=[tmpb[hf]])
                    P.add("dve", lambda e, hf=hf, hs=hs: e.tensor_tensor(out=xt[slot][:, j, hs], in0=xt[slot][:, j, hs], in1=tmp2[hf][:], op=ALU.add),
                          reads=[tmpb[hf], b_xt[slot][j]], writes=[b_xt[slot][j]])
                P.add("sp", lambda e: e.dma_start(out=x1_v[n, :, :], in_=xt[slot][:, j, :]), reads=[b_xt[slot][j]], writes=[b_x1_d[n]], dma=f"xs{slot}{j}")

            fs = front_stages(xt[slot][:, j, :], b_xt[slot][j], A2, B2, hx2s[j % 2], slice(0, 128), b_hx2s[j % 2])

            def o7():
                fs[3]()
                P.add("sp", lambda e: e.dma_start(out=hx2_v[:, :, 64 + i * 512 + j * 128:64 + i * 512 + (j + 1) * 128], in_=hx2s[j % 2][:]),
                      reads=[b_hx2s[j % 2]], writes=[b_hx2_d[i]], dma=f"hx2s{j % 2}")

            return [o1, o2, o3, fs[0], fs[1], fs[2], o7]

        if fcnt[0] % 2:
            fcnt[0] += 1
        OS = [oproj_stages(j) for j in range(4)]
        order = [(0, 0), (1, 0), (0, 1), (0, 2), (0, 3), (1, 1), (2, 0), (1, 2), (0, 4), (1, 3), (0, 5), (2, 1), (3, 0), (0, 6), (2, 2), (1, 4), (2, 3), (1, 5),
                 (3, 1), (1, 6), (3, 2), (2, 4), (3, 3), (2, 5), (2, 6), (3, 4), (3, 5), (3, 6)]
        assert sorted(order) == [(j, k) for j in range(4) for k in range(7)]
        for (j, k) in order:
            OS[j][k]()
    P.barrier()
    if stage == 3:
        return finish()

    Zc = Alloc(nc, OV, SB_HI)
    wdn = Zc.t([128, NFC, D], BF16, "wdn")
    wupS = [Zc.t([128, 8, 512], BF16, "wupS") for _ in range(2)]
    hw = [Zc.t([128, 8, 640], BF16, "hw") for _ in range(2)]
    x1t = [Zc.t([128, 4, D], F32, "x1t") for _ in range(2)]
    hT = Zc.t([128, NFC, 512], BF16, "hT")
    upad = [Zc.t([128, 10, 66], BF16, "upad") for _ in range(2)]
    gts = [Zc.t([128, 512], BF16, "gts") for _ in range(2)]
    sil = [Zc.t([128, 512], BF16, "sil") for _ in range(2)]
    dg = [Zc.t([128, 9, 128], BF16, "dg") for _ in range(2)]
    tmpc = Zc.t([128, 512], F32, "tmpc")
    junkc = Zc.t([128, 512], BF16, "junkc")
    b_wdn = Buf("wdn")
    b_wupS = [Buf("wupS0"), Buf("wupS1")]
    b_hw = [Buf("hw0"), Buf("hw1")]
    b_x1t = [[Buf(f"x1t{s}{j}") for j in range(4)] for s in range(2)]
    b_hT = [Buf(f"hT{c}") for c in range(NFC)]
    b_upad = [Buf("upad0"), Buf("upad1")]
    b_upc = [Buf("upc0"), Buf("upc1")]
    ucar = Zc.t([128, NFC, 2, 64], BF16, "ucar")
    b_ucar = [Buf(f"ucar{c}") for c in range(NFC)]
    b_gts = [Buf("gts0"), Buf("gts1")]
    b_sil = [Buf("sil0"), Buf("sil1")]
    b_dg = [Buf("dg0"), Buf("dg1")]
    b_tmpc, b_junkc, b_statc = Buf("tmpc"), Buf("junkc"), Buf("statc")
    b_out = []

    wdn_v = wdown_d.rearrange("(c p) j -> p c j", p=128)
    for c in range(NFC):
        P.add("pool", lambda e, c=c: e.dma_start(out=wdn[:, c, :], in_=wdn_v[:, c, :]), writes=[b_wdn], dma="wdn", ndma=1)
    for s in range(2):
        P.add("dve", lambda e, s=s: e.memset(upad[s][:], 0.0), writes=[b_upad[s]])
    out_v = out_d.rearrange("(n p) f -> n p f", p=128)
    x1_v4 = x1_d.rearrange("(n j p) f -> n p j f", p=128, j=4)
    def load_c(i):
        slot = i % 2
        P.add("sp", lambda e: e.dma_start(out=hw[slot][:], in_=hx2_v[:, :, i * 512:i * 512 + 640]),
              reads=b_hx2_d[max(0, i - 1):i + 2] + [b_hx2], writes=[b_hw[slot]], dma=f"hw{slot}")
        for j in range(4):
            P.add("sp", lambda e, j=j: e.dma_start(out=x1t[slot][:, j, :], in_=x1_v4[i, :, j, :]), reads=[b_x1_d[i * 4 + j]],
                  writes=[b_x1t[slot][j]], dma=f"x1t{slot}{j}")

    NPC = NFC // 2

    def load_piece(g):
        ps = g % 2
        pc = g % NPC
        P.add("sp", lambda e: e.dma_start(out=wupS[ps][:], in_=wupb_d[pc, :, :, :]), reads=[b_wupb], writes=[b_wupS[ps]], dma=f"wupS{ps}")

    load_c(0)
    load_piece(0)
    piece = 0
    for i in range(NT):
        slot = i % 2
        for c in range(NFC):
            s2 = c % 2
            if c % 2 == 0:
                ps = piece % 2
                piece += 1
                if piece < NT * NPC:
                    load_piece(piece)
                if c == 2 and i + 1 < NT:
                    load_c(i + 1)
            wo = (c % 2) * 128
            for t in range(9):
                P.add("dve", lambda e, c=c, t=t, s2=s2: e.tensor_scalar(out=dg[s2][:, t, :], in0=ident[:], scalar1=colp[:, Q_WCF + c * 9 + t:Q_WCF + c * 9 + t + 1],
                                                                         scalar2=None, op0=ALU.mult), reads=[g_const], writes=[b_dg[s2]])
            ucol = 0 if i == 0 else 128
            for k in range(8):
                P.add("pe", lambda e, k=k, s2=s2, ps=ps, wo=wo, slot=slot, ucol=ucol: e.matmul(pb[s2][:, :], lhsT=wupS[ps][:, k, wo:wo + 128], rhs=hw[slot][:, k, ucol:ucol + 512],
                                                                                           start=(k == 0), stop=(k == 7)),
                      reads=[b_wupS[ps], b_hw[slot]], writes=[PB[s2]])
            if i == 0:
                for k in range(8):
                    P.add("pe", lambda e, k=k, s2=s2, ps=ps, wo=wo, slot=slot: e.matmul(pb[2 + s2][:, 0:128], lhsT=wupS[ps][:, k, wo:wo + 128], rhs=hw[slot][:, k, 512:640], start=(k == 0), stop=(k == 7)),
                          reads=[b_wupS[ps], b_hw[slot]], writes=[PB[2 + s2]])
            for k in range(8):
                P.add("pe", lambda e, k=k, s2=s2, ps=ps, wo=wo, slot=slot: e.matmul(pb[4 + s2][:, :], lhsT=wupS[ps][:, k, 256 + wo:256 + wo + 128], rhs=hw[slot][:, k, 64:576], start=(k == 0), stop=(k == 7)),
                      reads=[b_wupS[ps], b_hw[slot]], writes=[PB[4 + s2]])
            if i == 0:
                P.add("act", lambda e, s2=s2: e.copy(out=upad[s2][:, 0:8, 1:65], in_=pb[s2][:, :].rearrange("p (r c) -> p r c", r=8)), reads=[PB[s2]], writes=[b_upc[s2], b_upad[s2]])
                P.add("act", lambda e, s2=s2: e.copy(out=upad[s2][:, 8:10, 1:65], in_=pb[2 + s2][:, 0:128].rearrange("p (r c) -> p r c", r=2)), reads=[PB[2 + s2]], writes=[b_upad[s2]])
            else:
                P.add("pool", lambda e, c=c, s2=s2: e.tensor_copy(out=upad[s2][:, 0:2, 1:65], in_=ucar[:, c, :, :]), reads=[b_ucar[c]], writes=[b_upc[s2]])
                P.add("act", lambda e, s2=s2: e.copy(out=upad[s2][:, 2:10, 1:65], in_=pb[s2][:, :].rearrange("p (r c) -> p r c", r=8)), reads=[PB[s2]], writes=[b_upad[s2]])
            if i < NT - 1:
                P.add("pool", lambda e, c=c, s2=s2: e.tensor_copy(out=ucar[:, c, :, :], in_=upad[s2][:, 8:10, 1:65]), reads=[b_upad[s2]], writes=[b_ucar[c]])
            P.add("dve", lambda e, s2=s2: e.tensor_copy(out=gts[s2][:], in_=pb[4 + s2][:, :]), reads=[PB[4 + s2]], writes=[b_gts[s2]])
            for t in range(9):
                dr, dc = t // 3, t % 3
                P.add("pe", lambda e, t=t, dr=dr, dc=dc, s2=s2: e.matmul(pb[6 + s2][:, :], lhsT=dg[s2][:, t, :], rhs=upad[s2][:, dr:dr + 8, dc:dc + 64], start=(t == 0), stop=(t == 8)),
                      reads=[b_dg[s2], b_upad[s2], b_upc[s2]], writes=[PB[6 + s2]])
            P.add("act", lambda e, c=c, s2=s2: e.activation(out=sil[s2][:], in_=pb[6 + s2][:, :], func=AF.Silu, bias=colp[:, Q_BCF + c:Q_BCF + c + 1]),
                  reads=[PB[6 + s2], g_const], writes=[b_sil[s2]])
            P.add("dve", lambda e, c=c, s2=s2: e.tensor_tensor(out=hT[:, c, :], in0=sil[s2][:], in1=gts[s2][:], op=ALU.mult), reads=[b_sil[s2], b_gts[s2]], writes=[b_hT[c]])
        for j in range(4):
            cols = slice(j * 128, (j + 1) * 128)
            n = i * 4 + j
            for hf in range(2):
                bank = (2 * j + hf) % 8
                for c in range(NFC):
                    P.add("pe", lambda e, c=c, hf=hf, bank=bank, cols=cols: e.matmul(pb[bank][:, :], lhsT=hT[:, c, cols], rhs=wdn[:, c, hf * 512:(hf + 1) * 512],
                                                                                      start=(c == 0), stop=(c == NFC - 1)), reads=[b_hT[c], b_wdn], writes=[PB[bank]])
            c0, _unused = stat_slot()
            P.add("dve", lambda e, c0=c0: e.memset(stat[:, c0:c0 + 4], 0.0), writes=[b_statc])
            for hf in range(2):
                bank = (2 * j + hf) % 8
                P.add("act", lambda e, hf=hf, bank=bank, c0=c0: e.activation(out=junkc[:], in_=pb[bank][:, :], func=AF.Square, accum_out=stat[:, c0 + hf:c0 + hf + 1]),
                      reads=[PB[bank], b_statc], writes=[b_junkc, b_statc])
            P.add("dve", lambda e, c0=c0: e.tensor_tensor(out=stat[:, c0 + 2:c0 + 3], in0=stat[:, c0:c0 + 1], in1=stat[:, c0 + 1:c0 + 2], op=ALU.add),
                  reads=[b_statc], writes=[b_statc])
            rstd_from(stat[:, c0 + 2:c0 + 3], D, stat[:, c0 + 3:c0 + 4], [b_statc], [b_statc])
            for hf in range(2):
                bank = (2 * j + hf) % 8
                hs = slice(hf * 512, (hf + 1) * 512)
                P.add("dve", lambda e, hf=hf, bank=bank, hs=hs, c0=c0: e.scalar_tensor_tensor(out=tmpc[:], in0=pb[bank][:, :], scalar=stat[:, c0 + 3:c0 + 4], in1=G2[:, hs],
                                                                                             op0=ALU.mult, op1=ALU.mult), reads=[PB[bank], b_statc, g_const], writes=[b_tmpc])
                P.add("dve", lambda e, j=j, hs=hs, slot=slot: e.tensor_tensor(out=x1t[slot][:, j, hs], in0=x1t[slot][:, j, hs], in1=tmpc[:], op=ALU.add),
                      reads=[b_tmpc, b_x1t[slot][j]], writes=[b_x1t[slot][j]])
            bo = Buf(f"out{n}")
            tok = P.add("sp", lambda e, j=j, n=n, slot=slot: e.dma_start(out=out_v[n, :, :], in_=x1t[slot][:, j, :]), reads=[b_x1t[slot][j]], writes=[bo], dma=f"ot{slot}{j}")
            b_out.append(tok)
    last = {}
    for t in b_out:
        last[t[0]] = max(last.get(t[0], 0), t[1])
    fin = list(last.items())
    if debug:
        for k, v in P.dma_cum.items():
            fin.append(("dma:" + k, v))
    P.final_wait("sp", fin)
    P.emit(nc)
    return nc


def _consts():
    c = np.zeros((128, NCST), np.float32)
    m = np.arange(128)[:, None]
    t = np.arange(128)[None, :]
    c[:, K_ID:K_ID + 128] = (m == t)
    c[:, K_LFI:K_LFI + 128] = (m <= t) * (-1.0 / 16)
    c[:, K_LFR:K_LFR + 128] = (m > t) * (-1.0 / 16)
    c[:, K_LBI:K_LBI + 128] = (m >= t) * (-1.0 / 16)
    c[:, K_LBR:K_LBR + 128] = (m < t) * (-1.0 / 16)
    c[:, K_MF:K_MF + 128] = (m <= t)
    c[:, K_MB:K_MB + 128] = (m > t)
    c[:, K_NC] = -1.0 / 16
    return c


def _colmaj(v, n):
    return np.ascontiguousarray(np.asarray(v, np.float32).reshape(n, 128).T)


def make_in_maps(inputs, NT=16, ncores=8):
    f = lambda a: np.asarray(a, np.float32)
    w_in = f(inputs["w_in"])[0]
    w_tm = np.ascontiguousarray(np.concatenate([w_in[:, C_K:C_V], w_in[:, C_Q:C_OG], w_in[:, C_V:C_AF], w_in[:, C_OG:C_SB]], axis=1))
    gate = np.zeros((D, 64), np.float32)
    gate[:, 0:16] = w_in[:, C_AF:C_AB]
    gate[:, 32:48] = w_in[:, C_AB:C_Q]
    w_fm = np.ascontiguousarray(np.concatenate([gate, w_in[:, C_SB:C_SC], w_in[:, C_SC:C_SX], w_in[:, C_SX:]], axis=1))
    wg = np.zeros((96, 512), np.float32)
    wg[0:16, 0:256] = f(inputs["w_af"])[0]
    wg[32:48, 256:512] = f(inputs["w_ab"])[0]
    wg[64, 0:256] = f(inputs["b_af"])[0]
    wg[64, 256:512] = f(inputs["b_ab"])[0]
    rows1 = np.concatenate([f(inputs["g_pre_mix"])[0], f(inputs["g_post_mix"])[0], f(inputs["g_pre_ffn"])[0], f(inputs["g_post_ffn"])[0],
                            np.tile(f(inputs["g_head"])[0], 4), f(inputs["b_ada"])[0]])
    rows = np.ascontiguousarray(np.broadcast_to(rows1[None, :], (128, NROW)))
    w_sc = f(inputs["w_sc"])[0]
    w_cf = f(inputs["w_cf"])[0].reshape(9, DFF)
    cols_common = np.concatenate([
        _colmaj(f(inputs["b_sc"])[0], 4), _colmaj(f(inputs["b_cf"])[0], NFC),
        np.concatenate([_colmaj(w_sc[t], 4) for t in range(3)], axis=1),
        np.stack([_colmaj(w_cf[t], NFC) for t in range(9)], axis=2).reshape(128, NFC * 9),
    ], axis=1)
    cst = _consts()
    maps = []
    x = inputs["x"]
    for b in range(ncores):
        cols = np.ascontiguousarray(np.concatenate([_colmaj(f(inputs["c"])[b], 8), _colmaj(f(inputs["c_ctx"]), 8), cols_common], axis=1))
        maps.append({
            "x": np.ascontiguousarray(f(x[b])[:NT * 512]), "ctx": np.ascontiguousarray(f(inputs["ctx"][b])),
            "rows": rows, "cols": cols, "cst": cst,
            "w_ada": f(inputs["w_ada"])[0], "w_tm": w_tm, "w_fm": w_fm, "w_gate": wg,
            "w_out": f(inputs["w_out"])[0], "w_up": f(inputs["w_up"])[0], "w_down": f(inputs["w_down"])[0],
        })
    return maps


_NC_CACHE = {}


def kernel(**inputs):
    NT = 16
    if NT not in _NC_CACHE:
        _NC_CACHE[NT] = build(NT)
    nc = _NC_CACHE[NT]
    maps = make_in_maps(inputs, NT, 8)
    res = run_bass_kernel_spmd(nc, maps, core_ids=list(range(8)))
    return np.stack([np.asarray(r["out"], np.float32).reshape(NT * 512, D) for r in res.results], axis=0)
```

```python
import os
import numpy as np
import concourse.bass as bass
import concourse.mybir as mybir
from concourse.bass_utils import run_bass_kernel_spmd

F32 = mybir.dt.float32
BF16 = mybir.dt.bfloat16
AF = mybir.ActivationFunctionType
ALU = mybir.AluOpType

D = 1024
DFF = 2816
NFC = 22
CTX = 256
EPS = 1e-6
SB_LO = 16512
SB_HI = 229344
EPOCH = 24000
STRICT_SYNC = False

C_K, C_V, C_AF, C_AB, C_Q, C_OG, C_SB, C_SC, C_SX = 0, 256, 768, 784, 800, 1056, 1568, 2080, 2592

K_ID, K_LFI, K_LFR, K_LBI, K_LBR, K_MF, K_MB, K_NC = 0, 128, 256, 384, 512, 640, 768, 896
NCST = 897
R_GPM, R_GQM, R_GPF, R_GQF, R_GH, R_BADA = 0, 1024, 2048, 3072, 4096, 4608
NROW = 4608 + 6144
Q_C, Q_CC, Q_BSC, Q_BCF, Q_WSC, Q_WCF = 0, 8, 16, 20, 42, 54
NCOL = 54 + 198


class Buf:
    __slots__ = ("name", "w", "r")

    def __init__(self, name):
        self.name = name
        self.w = None
        self.r = {}


class Prog:
    ENGS = ("pe", "act", "dve", "pool", "sp")

    def __init__(self):
        self.ops = {e: [] for e in self.ENGS}
        self.waited = {e: {} for e in self.ENGS}
        self.dma_cum = {}
        self.n = 0

    def _need(self, eng, tok, waits):
        key, val = tok
        if self.waited[eng].get(key, -1) >= val:
            return
        self.waited[eng][key] = val
        waits.append(tok)
        if not key.startswith("dma:"):
            self.ops[key][val]["sig"] = True

    def add(self, eng, fn, reads=(), writes=(), dma=None, ndma=1):
        idx = len(self.ops[eng])
        waits = []
        mykey = ("dma:" + dma) if dma is not None else eng
        for b in reads:
            if b.w is not None:
                if b.w[0] == eng and eng == "pe" and dma is None:
                    continue
                self._need(eng, b.w, waits)
        relax = (eng == "pe" or dma is not None) if STRICT_SYNC else True
        for b in writes:
            if b.w is not None and not (b.w[0] == mykey and relax):
                self._need(eng, b.w, waits)
            for k, v in b.r.items():
                if k == mykey and relax:
                    continue
                self._need(eng, (k, v), waits)
        if dma is not None:
            cum = self.dma_cum.get(dma, 0) + 16 * ndma
            self.dma_cum[dma] = cum
            tok = ("dma:" + dma, cum)
        else:
            tok = (eng, idx)
        for b in reads:
            if b.r.get(tok[0], -1) < tok[1]:
                b.r[tok[0]] = tok[1]
        for b in writes:
            b.w = tok
            b.r = {}
        self.ops[eng].append({"fn": fn, "waits": waits, "sig": False, "dma": dma})
        self.n += 1
        return tok

    def barrier(self, skip=()):
        toks = []
        for e in self.ENGS:
            if self.ops[e]:
                for i in range(len(self.ops[e]) - 1, -1, -1):
                    if self.ops[e][i]["dma"] is None and self.ops[e][i]["fn"] is not None:
                        toks.append((e, i))
                        break
        for k, v in self.dma_cum.items():
            if k in skip:
                continue
            toks.append(("dma:" + k, v))
        for e in self.ENGS:
            waits = []
            for t in toks:
                if t[0] == e:
                    continue
                self._need(e, t, waits)
            self.ops[e].append({"fn": None, "waits": waits, "sig": False, "dma": None})

    def final_wait(self, eng, toks):
        waits = []
        for t in toks:
            self._need(eng, t, waits)
        self.ops[eng].append({"fn": None, "waits": waits, "sig": False, "dma": None})

    def emit(self, nc):
        engsem = {}
        signum = {}
        for e in self.ENGS:
            cnt = 0
            signum[e] = {}
            for i, op in enumerate(self.ops[e]):
                if op["sig"]:
                    signum[e][i] = cnt
                    cnt += 1
            nep = cnt // EPOCH + 1
            engsem[e] = [nc.alloc_semaphore(f"s_{e}_{j}") for j in range(nep)]
        dmasem = {k: nc.alloc_semaphore("d_" + k) for k in self.dma_cum}

        def run(ename, eng):
            for i, op in enumerate(self.ops[ename]):
                for key, val in op["waits"]:
                    if key.startswith("dma:"):
                        eng.wait_ge(dmasem[key[4:]], val)
                    else:
                        s = signum[key][val]
                        eng.wait_ge(engsem[key][s // EPOCH], s % EPOCH + 1)
                if op["fn"] is None:
                    continue
                ins = op["fn"](eng)
                if op["dma"] is not None:
                    ins.then_inc(dmasem[op["dma"]], 16)
                elif op["sig"]:
                    s = signum[ename][i]
                    ins.then_inc(engsem[ename][s // EPOCH], 1)

        with nc.Block() as block:
            @block.tensor
            def _(eng):
                run("pe", eng)

            @block.scalar
            def _(eng):
                run("act", eng)

            @block.vector
            def _(eng):
                run("dve", eng)

            @block.gpsimd
            def _(eng):
                run("pool", eng)

            @block.sync
            def _(eng):
                run("sp", eng)


class Alloc:
    def __init__(self, nc, lo, hi):
        self.nc, self.lo, self.hi, self.p, self.n = nc, lo, hi, lo, 0

    def t(self, shape, dt, name="t"):
        nb = 1
        for s in shape[1:]:
            nb *= s
        nb *= 2 if dt == BF16 else 4
        nb = (nb + 31) // 32 * 32
        off = self.p
        self.p += nb
        assert self.p <= self.hi, f"SBUF overflow {name} {self.p} > {self.hi}"
        self.n += 1
        return self.nc.alloc_sbuf_tensor_at(f"{name}{self.n}", list(shape), dt, offset=off)


def build(NT=16, debug=False, stage=9):
    nc = bass.Bass("TRN2", target_bir_lowering=False)
    SEQ = NT * 512
    P = Prog()

    def finish():
        fin = [("dma:" + k, v) for k, v in P.dma_cum.items()]
        P.final_wait("sp", fin)
        P.emit(nc)
        return nc

    def din(name, shape, dt=F32):
        return nc.dram_tensor(name, list(shape), dt, kind="ExternalInput").ap()

    x_d = din("x", [SEQ, D])
    ctx_d = din("ctx", [CTX, D])
    rows_d = din("rows", [128, NROW])
    cols_d = din("cols", [128, NCOL])
    cst_d = din("cst", [128, NCST])
    wada_d = din("w_ada", [D, 6 * D])
    wtm_d = din("w_tm", [D, 1536])
    wfm_d = din("w_fm", [D, 1600])
    wgate_d = din("w_gate", [96, 512])
    wout_d = din("w_out", [D, D])
    wup_d = din("w_up", [D, 2 * DFF])
    wdown_d = din("w_down", [DFF, D])
    out_d = nc.dram_tensor("out", [SEQ, D], F32, kind="ExternalOutput").ap()
    if debug:
        ob_d = nc.dram_tensor("ob_d", [SEQ, 512], F32, kind="ExternalOutput").ap()
        x1_d = nc.dram_tensor("x1_d", [SEQ, D], F32, kind="ExternalOutput").ap()
    else:
        ob_d = nc.dram_tensor("ob_d", [SEQ, 512], F32).ap()
        x1_d = nc.dram_tensor("x1_d", [SEQ, D], F32).ap()
    hx2_d = nc.dram_tensor("hx2_d", [8, 128, SEQ + 128], BF16).ap()
    wupb_d = nc.dram_tensor("wupb_d", [NFC // 2, 128, 8, 512], BF16).ap()

    pb = [nc.alloc_psum_tensor(f"pb{i}", [128, 512], F32) for i in range(8)]
    PB = [Buf(f"pb{i}") for i in range(8)]

    G = Alloc(nc, SB_LO, SB_HI)
    ident = G.t([128, 128], BF16, "ident")
    Lm = G.t([128, 4, 128], BF16, "Lm")
    maskT = G.t([128, 2, 512], BF16, "maskT")
    ncol = G.t([128, 1], BF16, "ncol")
    ones_r = G.t([1, 128], BF16, "ones_r")
    A1 = G.t([128, D], F32, "A1")
    cA1 = G.t([128, D], F32, "cA1")
    G1 = G.t([128, D], F32, "G1")
    A2 = G.t([128, D], F32, "A2")
    G2 = G.t([128, D], F32, "G2")
    ghead = G.t([128, 512], F32, "ghead")
    B1 = G.t([1, D], BF16, "B1")
    cB1 = G.t([1, D], BF16, "cB1")
    B2 = G.t([1, D], BF16, "B2")
    colp = G.t([128, NCOL], F32, "colp")
    Bc = G.t([128, 24], F32, "Bc")
    wgate = G.t([96, 512], BF16, "wgate")
    dsc = G.t([128, 12, 128], BF16, "dsc")
    S = [G.t([128, 2, 128], F32, "S") for _ in range(2)]
    Sb = [G.t([128, 2, 128], BF16, "Sb") for _ in range(2)]
    alT = G.t([96, 512], BF16, "alT")
    stat = G.t([128, 64], F32, "stat")
    fstat = G.t([128, 16], F32, "fstat")
    b_fst = [Buf(f"fst{i}") for i in range(8)]
    epsc = G.t([128, 1], F32, "epsc")
    g_const = Buf("consts")
    bS = [Buf("S0"), Buf("S1")]
    bSb = [Buf("Sb0"), Buf("Sb1")]
    OV = G.p

    Z = Alloc(nc, OV, SB_HI)
    cstf = Z.t([128, NCST], F32, "cstf")
    rowp = Z.t([128, NROW], F32, "rowp")
    wa = [Z.t([128, 8, D], F32, "wa") for _ in range(2)]
    modr = [Z.t([128, D], F32, "modr") for _ in range(2)]
    rep = Z.t([128, 16, 128], F32, "rep")
    onesf = Z.t([128, 128], F32, "onesf")
    scl = Z.t([128, 16], F32, "scl")
    wgf = Z.t([96, 512], F32, "wgf")
    zt = Z.t([128, 64], BF16, "zt")
    b_cst, b_row, b_col, b_wgf = Buf("cstf"), Buf("rowp"), Buf("colp"), Buf("wgf")
    b_wa = [Buf("wa0"), Buf("wa1")]
    b_modr = [Buf("modr0"), Buf("modr1")]
    b_rep, b_scl, b_zt = Buf("rep"), Buf("scl"), Buf("zt")

    P.add("sp", lambda e: e.dma_start(out=cstf[:], in_=cst_d[:, :]), writes=[b_cst], dma="ld0")
    P.add("sp", lambda e: e.dma_start(out=colp[:], in_=cols_d[:, :]), writes=[b_col], dma="ld1")
    P.add("sp", lambda e: e.dma_start(out=wgf[:], in_=wgate_d[:, :]), writes=[b_wgf], dma="ld2")
    P.add("sp", lambda e: e.dma_start(out=rowp[:, 0:4608], in_=rows_d[:, 0:4608]), writes=[b_row], dma="ld3", ndma=2)
    P.add("sp", lambda e: e.dma_start(out=rowp[:, 4608:NROW], in_=rows_d[:, 4608:NROW]), writes=[b_row], dma="ld3", ndma=0)

    b_wupb = Buf("wupb")
    wup_v = wup_d.rearrange("(k p) j -> p k j", p=128)
    for pc in range(NFC // 2):
        for ug in range(2):
            off = ug * DFF + pc * 256
            P.add("pool", lambda e, pc=pc, ug=ug, off=off: e.dma_start(out=wupb_d[pc, :, :, ug * 256:(ug + 1) * 256], in_=wup_v[:, :, off:off + 256]),
                  writes=[b_wupb], dma="wupc", ndma=1)

    P.add("dve", lambda e: e.tensor_copy(out=ident[:], in_=cstf[:, K_ID:K_ID + 128]), reads=[b_cst], writes=[g_const])
    for j, k0 in enumerate((K_LFI, K_LFR, K_LBI, K_LBR)):
        P.add("dve", lambda e, j=j, k0=k0: e.tensor_copy(out=Lm[:, j, :], in_=cstf[:, k0:k0 + 128]), reads=[b_cst], writes=[g_const])
    for d_, k0 in enumerate((K_MF, K_MB)):
        for h in range(4):
            P.add("dve", lambda e, d_=d_, k0=k0, h=h: e.tensor_copy(out=maskT[:, d_, h * 128:(h + 1) * 128], in_=cstf[:, k0:k0 + 128]),
                  reads=[b_cst], writes=[g_const])
    P.add("dve", lambda e: e.tensor_copy(out=ncol[:], in_=cstf[:, K_NC:K_NC + 1]), reads=[b_cst], writes=[g_const])
    P.add("dve", lambda e: e.memset(ones_r[:], 1.0), writes=[g_const])
    P.add("dve", lambda e: e.memset(epsc[:], EPS), writes=[g_const])
    P.add("dve", lambda e: e.memset(onesf[:], 1.0), writes=[b_rep])
    P.add("dve", lambda e: e.memset(zt[:], 0.0), writes=[b_zt])
    P.add("dve", lambda e: e.memset(alT[64:96, :], 1.0), writes=[g_const])
    P.add("dve", lambda e: e.tensor_copy(out=wgate[:], in_=wgf[:]), reads=[b_wgf], writes=[g_const])
    P.add("dve", lambda e: e.tensor_copy(out=ghead[:], in_=rowp[:, R_GH:R_GH + 512]), reads=[b_row], writes=[g_const])
    for d_ in range(2):
        P.add("dve", lambda e, d_=d_: e.memset(S[d_][:], 0.0), writes=[bS[d_]])
        P.add("dve", lambda e, d_=d_: e.memset(Sb[d_][:], 0.0), writes=[bSb[d_]])
    for t in range(3):
        for j in range(4):
            P.add("dve", lambda e, t=t, j=j: e.tensor_scalar(out=dsc[:, t * 4 + j, :], in0=ident[:], scalar1=colp[:, Q_WSC + t * 4 + j:Q_WSC + t * 4 + j + 1],
                                                              scalar2=None, op0=ALU.mult), reads=[g_const, b_col], writes=[g_const])
    b_hx2 = Buf("hx2_d")
    for k in range(8):
        P.add("sp", lambda e, k=k: e.dma_start(out=hx2_d[k, :, 0:64], in_=zt[:, 0:64]), reads=[b_zt], writes=[b_hx2], dma="zm", ndma=1)
        P.add("sp", lambda e, k=k: e.dma_start(out=hx2_d[k, :, SEQ + 64:SEQ + 128], in_=zt[:, 0:64]), reads=[b_zt], writes=[b_hx2], dma="zm", ndma=1)

    P.add("act", lambda e: e.activation(out=scl[:], in_=colp[:, Q_C:Q_C + 16], func=AF.Silu), reads=[b_col], writes=[b_scl])
    for j in range(16):
        P.add("dve", lambda e, j=j: e.tensor_scalar(out=rep[:, j, :], in0=onesf[:], scalar1=scl[:, j:j + 1], scalar2=None, op0=ALU.mult),
              reads=[b_scl, b_rep], writes=[b_rep])
    wada_v = wada_d.rearrange("(k p) j -> p k j", p=128)

    def mod_piece(m, v, slot):
        for hf in range(2):
            bank = 2 * slot + hf
            for k in range(8):
                P.add("pe", lambda e, k=k, hf=hf, bank=bank: e.matmul(pb[bank][:, :], lhsT=rep[:, v * 8 + k, :], rhs=wa[m % 2][:, k, hf * 512:(hf + 1) * 512],
                                                                     start=(k == 0), stop=(k == 7)),
                      reads=[b_rep, b_wa[m % 2]], writes=[PB[bank]])
            P.add("dve", lambda e, hf=hf, bank=bank: e.tensor_tensor(out=modr[slot][:, hf * 512:(hf + 1) * 512], in0=pb[bank][:, :],
                                                                      in1=rowp[:, R_BADA + m * D + hf * 512:R_BADA + m * D + (hf + 1) * 512], op=ALU.add),
                  reads=[PB[bank], b_row], writes=[b_modr[slot]])

    for m in range(6):
        for q in range(2):
            P.add("sp", lambda e, m=m, q=q: e.dma_start(out=wa[m % 2][:, q * 4:(q + 1) * 4, :], in_=wada_v[:, q * 4:(q + 1) * 4, m * D:(m + 1) * D]),
                  writes=[b_wa[m % 2]], dma=f"wa{m % 2}", ndma=1)
        mod_piece(m, 0, 0)
        if m < 2:
            mod_piece(m, 1, 1)
        if m == 0:
            P.add("act", lambda e: e.copy(out=B1[0:1, :], in_=modr[0][0:1, :]), reads=[b_modr[0]], writes=[g_const])
            P.add("act", lambda e: e.copy(out=cB1[0:1, :], in_=modr[1][0:1, :]), reads=[b_modr[1]], writes=[g_const])
        elif m == 1:
            P.add("dve", lambda e: e.scalar_tensor_tensor(out=A1[:], in0=modr[0][:], scalar=1.0, in1=rowp[:, R_GPM:R_GPM + D], op0=ALU.add, op1=ALU.mult),
                  reads=[b_modr[0], b_row], writes=[g_const])
            P.add("dve", lambda e: e.scalar_tensor_tensor(out=cA1[:], in0=modr[1][:], scalar=1.0, in1=rowp[:, R_GPM:R_GPM + D], op0=ALU.add, op1=ALU.mult),
                  reads=[b_modr[1], b_row], writes=[g_const])
        elif m == 2:
            P.add("dve", lambda e: e.tensor_tensor(out=G1[:], in0=modr[0][:], in1=rowp[:, R_GQM:R_GQM + D], op=ALU.mult), reads=[b_modr[0], b_row], writes=[g_const])
        elif m == 3:
            P.add("act", lambda e: e.copy(out=B2[0:1, :], in_=modr[0][0:1, :]), reads=[b_modr[0]], writes=[g_const])
        elif m == 4:
            P.add("dve", lambda e: e.scalar_tensor_tensor(out=A2[:], in0=modr[0][:], scalar=1.0, in1=rowp[:, R_GPF:R_GPF + D], op0=ALU.add, op1=ALU.mult),
                  reads=[b_modr[0], b_row], writes=[g_const])
        else:
            P.add("dve", lambda e: e.tensor_tensor(out=G2[:], in0=modr[0][:], in1=rowp[:, R_GQF:R_GQF + D], op=ALU.mult), reads=[b_modr[0], b_row], writes=[g_const])
    for v_, Br_ in enumerate((B1, cB1, B2)):
        for k in range(8):
            P.add("pe", lambda e, v_=v_, k=k, Br_=Br_: e.matmul(pb[4][:, v_ * 8 + k:v_ * 8 + k + 1], lhsT=Br_[0:1, k * 128:(k + 1) * 128], rhs=ones_r[0:1, 0:1],
                                                            start=True, stop=True), reads=[g_const], writes=[PB[4]])
    P.add("dve", lambda e: e.tensor_copy(out=Bc[:], in_=pb[4][:, 0:24]), reads=[PB[4]], writes=[g_const])
    P.barrier(skip=("wupc",))
    if stage == 0:
        return finish()

    Y = Alloc(nc, OV, SB_HI)
    wtm = Y.t([128, 8, 1536], BF16, "wtm")
    wfm = Y.t([128, 8, 1600], BF16, "wfm")
    wout = Y.t([128, 8, D], BF16, "wout")
    xt = [Y.t([128, 4, D], F32, "xt") for _ in range(2)]
    hxT = Y.t([128, 8, 512], BF16, "hxT")
    obl = [Y.t([128, 512], F32, "obl") for _ in range(2)]
    g_l = [Y.t([128, 256], BF16, "g_l") for _ in range(2)]
    g_eb = [Y.t([128, 256], F32, "g_eb") for _ in range(2)]
    g_enb = [Y.t([128, 256], F32, "g_enb") for _ in range(2)]
    g_er = [Y.t([128, 256], F32, "g_er") for _ in range(2)]
    g_e = g_er
    g_qz = [Y.t([128, 2, 256], BF16, "g_qz") for _ in range(2)]
    g_kt = [Y.t([128, 256], BF16, "g_kt") for _ in range(2)]
    kqs = [Y.t([128, 512], F32, "kqs") for _ in range(2)]
    g_et = [Y.t([128, 2], F32, "g_et") for _ in range(2)]
    g_kh = [Y.t([128, 256], BF16, "g_kh") for _ in range(2)]
    g_vb = [Y.t([128, 512], BF16, "g_vb") for _ in range(2)]
    g_T = [Y.t([128, 768], BF16, "g_T") for _ in range(2)]
    g_att = [Y.t([128, 512], BF16, "g_att") for _ in range(2)]
    obst = obl
    osum = Y.t([128, 512], F32, "osum")
    ghs4 = Y.t([128, 4, 512], BF16, "ghs4")
    b_ghs = [Buf(f"ghs{j}") for j in range(4)]
    yg = Y.t([128, 512], BF16, "yg")
    junk = Y.t([128, D], BF16, "junk")
    ybf = [Y.t([128, D], BF16, "ybf") for _ in range(2)]
    b_ybf = [Buf("ybf0"), Buf("ybf1")]
    yxT = Y.t([128, 8, 512], BF16, "yxT")
    sbT = [Y.t([128, 512], BF16, "sbT") for _ in range(2)]
    scT = [Y.t([128, 512], BF16, "scT") for _ in range(2)]
    zpad = Y.t([128, 4, 8, 66], BF16, "zpad")
    tmpf = Y.t([128, 512], F32, "tmpf")
    hx2s = [Y.t([128, 8, 128], BF16, "hx2s") for _ in range(2)]

    b_wtm, b_wfm, b_wout = Buf("wtm"), Buf("wfm"), Buf("wout")
    b_xt = [[Buf(f"xt{s}{j}") for j in range(4)] for s in range(2)]
    b_hxT = [Buf(f"hxT{j}") for j in range(4)]
    b_obl = [Buf("obl0"), Buf("obl1")]
    bg = {n: Buf(n) for n in ("osum", "sg", "ghs", "yg", "junk", "ybf", "tmpf", "alT", "stat")}
    bF = [{n: Buf(n + str(i)) for n in ("e", "l", "eb", "enb", "er", "qt", "kt", "qblk", "kqs")} for i in range(2)]
    bB = [{n: Buf(n + str(i)) for n in ("et", "kh", "vb", "T", "att")} for i in range(2)]
    b_obst = b_obl
    b_yxg = [Buf(f"yxg{j}") for j in range(4)]
    b_yxs = [Buf(f"yxs{j}") for j in range(4)]
    b_sbT = [Buf("sbT0"), Buf("sbT1")]
    b_scT = [Buf("scT0"), Buf("scT1")]
    b_zp = [Buf(f"zp{j}") for j in range(4)]
    b_hx2s = [Buf("hx2s0"), Buf("hx2s1")]
    b_ob_d = [Buf(f"ob_d{i}") for i in range(NT * 4)]
    b_x1_d = [Buf(f"x1_d{i}") for i in range(NT * 4)]
    b_hx2_d = [Buf(f"hx2_d{i}") for i in range(NT)]

    wtm_v = wtm_d.rearrange("(k p) j -> p k j", p=128)
    wfm_v = wfm_d.rearrange("(k p) j -> p k j", p=128)
    wout_v = wout_d.rearrange("(k p) j -> p k j", p=128)
    for k in range(8):
        P.add("pool", lambda e, k=k: e.dma_start(out=wtm[:, k, :], in_=wtm_v[:, k, :]), writes=[b_wtm], dma="wtm", ndma=1)
    for k in range(8):
        P.add("pool", lambda e, k=k: e.dma_start(out=wfm[:, k, :], in_=wfm_v[:, k, :]), writes=[b_wfm], dma="wfm", ndma=1)
    for k in range(8):
        P.add("pool", lambda e, k=k: e.dma_start(out=wout[:, k, :], in_=wout_v[:, k, :]), writes=[b_wout], dma="wout", ndma=1)
    P.add("dve", lambda e: e.memset(zpad[:], 0.0), writes=b_zp)
    for i_ in range(2):
        P.add("dve", lambda e, i_=i_: e.memset(g_qz[i_][:], 0.0), writes=[bF[i_]["qt"]])

    sc_i = [0]

    bst8 = [Buf(f"stat{i}") for i in range(8)]

    def stat_slot():
        k = sc_i[0] % 8
        sc_i[0] += 1
        return 8 * k, bst8[k]

    def rstd_from(ss_ap, nfeat, out_ap, rb, wb):
        P.add("act", lambda e: e.activation(out=out_ap, in_=ss_ap, func=AF.Ln, scale=1.0 / nfeat, bias=epsc[:, 0:1]), reads=rb + [g_const], writes=wb)
        P.add("act", lambda e: e.activation(out=out_ap, in_=out_ap, func=AF.Exp, scale=-0.5), reads=wb, writes=wb)

    fcnt = [0]

    def front_stages(src_ap, src_buf, Arow, Brow, dstT, dst_cols, dst_buf):
        n_ = fcnt[0]
        fcnt[0] += 1
        q_, par = n_ % 8, n_ % 2
        banks = (6, 7) if par == 0 else (2, 5)
        ss = fstat[:, 2 * q_:2 * q_ + 1]
        rs = fstat[:, 2 * q_ + 1:2 * q_ + 2]
        bst = b_fst[q_]
        yb = ybf[par]

        def fa():
            P.add("dve", lambda e: e.memset(fstat[:, 2 * q_:2 * q_ + 2], 0.0), writes=[bst])
            P.add("act", lambda e: e.activation(out=junk[:], in_=src_ap, func=AF.Square, accum_out=ss), reads=[src_buf, bst], writes=[bg["junk"], bst])
            rstd_from(ss, D, rs, [bst], [bst])

        def fb():
            P.add("dve", lambda e: e.scalar_tensor_tensor(out=yb[:], in0=src_ap, scalar=rs, in1=Arow[:], op0=ALU.mult, op1=ALU.mult),
                  reads=[src_buf, bst, g_const], writes=[b_ybf[par]])

        bv = {id(B1): 0, id(cB1): 8, id(B2): 16}[id(Brow)]

        def fc():
            for k in range(8):
                bank = banks[k // 4]
                o = pb[bank][:, (k % 4) * 128:(k % 4 + 1) * 128]
                P.add("pe", lambda e, o=o, k=k: e.matmul(o, lhsT=yb[:, k * 128:(k + 1) * 128], rhs=ident[:], start=True, stop=True),
                      reads=[b_ybf[par], g_const], writes=[PB[bank]])

        def fd():
            for k in range(8):
                bank = banks[k // 4]
                src = pb[bank][:, (k % 4) * 128:(k % 4 + 1) * 128]
                if par == 0:
                    P.add("act", lambda e, k=k, src=src: e.activation(out=dstT[:, k, dst_cols], in_=src, func=AF.Identity, bias=Bc[:, bv + k:bv + k + 1]),
                          reads=[PB[bank], g_const], writes=[dst_buf])
                else:
                    P.add("dve", lambda e, k=k, src=src: e.tensor_scalar(out=dstT[:, k, dst_cols], in0=src, scalar1=Bc[:, bv + k:bv + k + 1], scalar2=None, op0=ALU.add),
                          reads=[PB[bank], g_const], writes=[dst_buf])

        return [fa, fb, fc, fd]

    def front(src_ap, src_buf, Arow, Brow, dstT, dst_cols, dst_buf, banks=None):
        for st_ in front_stages(src_ap, src_buf, Arow, Brow, dstT, dst_cols, dst_buf):
            st_()

    pre = {}

    def prefront(slot):
        if fcnt[0] % 2:
            fcnt[0] += 1
        S_ = [front_stages(xt[slot][:, j, :], b_xt[slot][j], A1, B1, hxT, slice(j * 128, (j + 1) * 128), b_hxT[j]) for j in range(4)]
        for j in range(4):
            S_[j][0]()
        pre[slot] = S_

    def front4(slot):
        if slot not in pre:
            prefront(slot)
        S_ = pre.pop(slot)
        for (j, k) in ((0, 1), (1, 1), (0, 2), (0, 3), (2, 1), (1, 2), (1, 3), (3, 1), (2, 2), (2, 3), (3, 2), (3, 3)):
            S_[j][k]()

    def proj_tm(cols, grp, bank, hbuf):
        for k in range(8):
            P.add("pe", lambda e, k=k: e.matmul(pb[bank][:, :], lhsT=hxT[:, k, cols], rhs=wtm[:, k, grp * 512:(grp + 1) * 512], start=(k == 0), stop=(k == 7)),
                  reads=[hbuf, b_wtm], writes=[PB[bank]])

    def proj_fm(c0, M, bank, ncols=512):
        for k in range(8):
            P.add("pe", lambda e, k=k: e.matmul(pb[bank][0:M, 0:ncols], lhsT=wfm[:, k, c0:c0 + M], rhs=hxT[:, k, 0:ncols], start=(k == 0), stop=(k == 7)),
                  reads=b_hxT + [b_wfm], writes=[PB[bank]])

    def gate_lowrank(ncols=512):
        proj_fm(0, 64, 7, ncols)
        P.add("act", lambda e: e.copy(out=alT[0:64, 0:ncols], in_=pb[7][0:64, 0:ncols]), reads=[PB[7]], writes=[bg["alT"]])

    def gla_stages(d, sl, bs, cols, hbuf, full):
        bK, bV, bA = 3 * sl, 3 * sl + 1, 3 * sl + 2
        F, Bb = bF[sl], bB[bs]
        e_, l_, eb_, enb_, er_, qz_, kt_, kq_ = g_e[sl], g_l[sl], g_eb[sl], g_enb[sl], g_er[sl], g_qz[sl], g_kt[sl], kqs[sl]
        et_, kh_, vb_, T_, att_ = g_et[bs], g_kh[bs], g_vb[bs], g_T[bs], g_att[bs]
        Tps = pb[bV].bitcast(BF16)

        def s1():
            proj_tm(cols, 0, bK, hbuf)
            proj_tm(cols, 1, bV, hbuf)
            P.add("pe", lambda e: e.matmul(pb[bA][:, 0:256], lhsT=alT[0:96, cols], rhs=wgate[:, d * 256:(d + 1) * 256], start=True, stop=True),
                  reads=[bg["alT"], g_const], writes=[PB[bA]])

        def s2():
            P.add("act", lambda e: e.activation(out=e_[:], in_=pb[bA][:, 0:256], func=AF.Exp, scale=-1.0), reads=[PB[bA]], writes=[F["er"]])
            P.add("act", lambda e: e.activation(out=l_[:], in_=e_[:], func=AF.Ln, bias=1.0), reads=[F["er"]], writes=[F["l"]])
            P.add("act", lambda e: e.copy(out=kq_[:], in_=pb[bK][:, :]), reads=[PB[bK]], writes=[F["kqs"]])
            P.add("act", lambda e: e.copy(out=vb_[:], in_=pb[bV][:, :]), reads=[PB[bV]], writes=[Bb["vb"]])

        def s3():
            if full:
                P.add("pe", lambda e: e.matmul(pb[bK][:, 0:256], lhsT=Lm[:, 2 * d, :], rhs=l_[:], start=True, stop=True), reads=[F["l"], g_const], writes=[PB[bK]])
            P.add("pe", lambda e: e.matmul(pb[bK][:, 256:512], lhsT=Lm[:, 2 * d + 1, :], rhs=l_[:], start=True, stop=True), reads=[F["l"], g_const], writes=[PB[bK]])
            for p_ in range(2):
                P.add("pe", lambda e, p_=p_: e.matmul(pb[bA][:, 256 + p_:257 + p_], lhsT=l_[:, p_ * 128:(p_ + 1) * 128], rhs=ncol[:, 0:1], start=True, stop=True),
                      reads=[F["l"], g_const], writes=[PB[bA]])

        def s4():
            if full:
                P.add("act", lambda e: e.activation(out=eb_[:], in_=pb[bK][:, 0:256], func=AF.Exp), reads=[PB[bK]], writes=[F["eb"]])
                P.add("act", lambda e: e.activation(out=enb_[:], in_=pb[bK][:, 0:256], func=AF.Exp, scale=-1.0), reads=[PB[bK]], writes=[F["enb"]])
            P.add("act", lambda e: e.activation(out=er_[:], in_=pb[bK][:, 256:512], func=AF.Exp), reads=[PB[bK]], writes=[F["er"]])
            P.add("act", lambda e: e.activation(out=et_[:], in_=pb[bA][:, 256:258], func=AF.Exp), reads=[PB[bA]], writes=[Bb["et"]])

        def s5():
            if full:
                kq3 = kq_[:, 256:512].rearrange("p (a b) -> p a b", a=2)
                eb3 = eb_[:].rearrange("p (a b) -> p a b", a=2)
                for w in range(2):
                    P.add("dve", lambda e, w=w: e.scalar_tensor_tensor(out=qz_[:, :, w * 192:w * 192 + 64], in0=kq3[:, :, w * 64:(w + 1) * 64], scalar=0.125,
                                                                       in1=eb3[:, :, w * 64:(w + 1) * 64], op0=ALU.mult, op1=ALU.mult),
                          reads=[F["kqs"], F["eb"]], writes=[F["qt"]])
                P.add("dve", lambda e: e.tensor_tensor(out=kt_[:], in0=kq_[:, 0:256], in1=enb_[:], op=ALU.mult), reads=[F["kqs"], F["enb"]], writes=[F["kt"]])
            P.add("dve", lambda e: e.tensor_tensor(out=kh_[:], in0=kq_[:, 0:256], in1=er_[:], op=ALU.mult), reads=[F["kqs"], F["er"]], writes=[Bb["kh"]])

        def s6():
            if not full:
                return
            for j in range(4):
                P.add("pe", lambda e, j=j: e.transpose(Tps[:, j * 128:(j + 1) * 128], qz_[:, j // 2, (j % 2) * 128:(j % 2 + 1) * 128], ident[:]),
                      reads=[F["qt"], g_const], writes=[PB[bV]])
            for j in range(2):
                P.add("pe", lambda e, j=j: e.transpose(Tps[:, 512 + j * 128:512 + (j + 1) * 128], kt_[:, j * 128:(j + 1) * 128], ident[:]),
                      reads=[F["kt"], g_const], writes=[PB[bV]])

        def s7():
            if not full:
                return
            P.add("dve", lambda e: e.tensor_copy(out=T_[:], in_=Tps[:, 0:768]), reads=[PB[bV]], writes=[Bb["T"]])

        def s8():
            if not full:
                return
            for pr in range(2):
                P.add("pe", lambda e, pr=pr: e.matmul(pb[bA][:, pr * 256:(pr + 1) * 256], lhsT=T_[:, 512 + pr * 128:512 + (pr + 1) * 128], rhs=T_[:, pr * 256:(pr + 1) * 256],
                                                      start=True, stop=True), reads=[Bb["T"]], writes=[PB[bA]])

        def s9():
            if not full:
                return
            P.add("dve", lambda e: e.tensor_tensor(out=att_[:], in0=pb[bA][:, :], in1=maskT[:, d, :], op=ALU.mult), reads=[PB[bA], g_const], writes=[Bb["att"]])

        return [s1, s2, s3, s4, s5, s6, s7, s8, s9]

    def gla_back(d, bs, full):
        Bb = bB[bs]
        et_, kh_, vb_, T_, att_ = g_et[bs], g_kh[bs], g_vb[bs], g_T[bs], g_att[bs]
        if full:
            for h in range(4):
                pr = h // 2
                P.add("pe", lambda e, h=h, pr=pr: e.matmul(pb[6][:, h * 128:(h + 1) * 128], lhsT=T_[:, h * 128:(h + 1) * 128], rhs=Sb[d][:, pr, :],
                                                           start=True, stop=False), reads=[Bb["T"], bSb[d]], writes=[PB[6]])
                P.add("pe", lambda e, h=h: e.matmul(pb[6][:, h * 128:(h + 1) * 128], lhsT=att_[:, h * 128:(h + 1) * 128], rhs=vb_[:, h * 128:(h + 1) * 128],
                                                    start=False, stop=True), reads=[Bb["att"], Bb["vb"]], writes=[PB[6]])
        for pr in range(2):
            P.add("pe", lambda e, pr=pr: e.matmul(pb[7][:, pr * 256:(pr + 1) * 256], lhsT=kh_[:, pr * 128:(pr + 1) * 128], rhs=vb_[:, pr * 256:(pr + 1) * 256],
                                                  start=True, stop=True), reads=[Bb["kh"], Bb["vb"]], writes=[PB[7]])
        for pr in range(2):
            for w in range(2):
                rsl = slice(w * 64, (w + 1) * 64)
                c0 = pr * 256 + w * 128
                P.add("dve", lambda e, pr=pr, rsl=rsl, c0=c0: e.scalar_tensor_tensor(out=S[d][rsl, pr, :], in0=S[d][rsl, pr, :], scalar=et_[rsl, pr:pr + 1],
                                                                                      in1=pb[7][rsl, c0:c0 + 128], op0=ALU.mult, op1=ALU.add),
                      reads=[bS[d], Bb["et"], PB[7]], writes=[bS[d]])
        P.add("act", lambda e: e.copy(out=Sb[d][:], in_=S[d][:]), reads=[bS[d]], writes=[bSb[d]])

    def gla_tile(d, order, after_back, mid=None):
        st = {}
        for n_, j in enumerate(order):
            st[j] = gla_stages(d, n_ % 2, n_ % 2, slice(j * 128, (j + 1) * 128), b_hxT[j], True)
        c0_, c1_, c2_, c3_ = order
        for k in range(9):
            st[c0_][k]()
            st[c1_][k]()
        if mid is None:
            st[c2_][0]()
            st[c3_][0]()
        for n_, j in enumerate((c0_, c1_)):
            gla_back(d, n_, True)
            if mid is not None:
                mid(j)
            after_back(j)
        for k in range(0 if mid is not None else 1, 9):
            st[c2_][k]()
            st[c3_][k]()
        for n_, j in enumerate((c2_, c3_)):
            gla_back(d, n_, True)
            if mid is not None:
                mid(j)
            after_back(j)

    ctx_v = ctx_d.rearrange("(j p) f -> p j f", p=128)
    for j in range(2):
        P.add("sp", lambda e, j=j: e.dma_start(out=xt[0][:, j, :], in_=ctx_v[:, j, :]), writes=[b_xt[0][j]], dma=f"x0{j}")
    for j in range(2):
        front(xt[0][:, j, :], b_xt[0][j], cA1, cB1, hxT, slice(j * 128, (j + 1) * 128), b_hxT[j])
    gate_lowrank(256)
    for d_, order in ((0, (0, 1)), (1, (1, 0))):
        for j in order:
            cols = slice(j * 128, (j + 1) * 128)
            for st_ in gla_stages(d_, 0, 0, cols, b_hxT[j], False):
                st_()
            gla_back(d_, 0, False)

    if stage == 1:
        return finish()
    x_v = x_d.rearrange("(n j p) f -> n p j f", p=128, j=4)
    ob_v = ob_d.rearrange("(n p) f -> n p f", p=128)
    x1_v = x1_d.rearrange("(n p) f -> n p f", p=128)

    def load_x(i, slot):
        for j in range(4):
            P.add("sp", lambda e, j=j: e.dma_start(out=xt[slot][:, j, :], in_=x_v[i, :, j, :]), writes=[b_xt[slot][j]], dma=f"x{slot}{j}")

    tseq = [i for i in range(NT - 1, -1, -1)] + [i for i in range(NT)]
    tcount = 0
    load_x(tseq[0], 0)
    for i in range(NT - 1, -1, -1):
        slot = tcount % 2
        tcount += 1
        if tcount < len(tseq) and stage >= 3 or tcount < NT:
            load_x(tseq[tcount], tcount % 2)
        front4(slot)
        gate_lowrank()
        nxt_slot = tcount % 2 if tcount < len(tseq) else None

        def after_a(j, i=i, nxt_slot=nxt_slot):
            n = i * 4 + j
            P.add("act", lambda e, j=j: e.copy(out=obst[j % 2][:], in_=pb[6][:, :]), reads=[PB[6]], writes=[b_obst[j % 2]])
            P.add("sp", lambda e, j=j, n=n: e.dma_start(out=ob_v[n, :, :], in_=obst[j % 2][:]), reads=[b_obst[j % 2]], writes=[b_ob_d[n]], dma=f"obst{j % 2}")
            if j == 1 and nxt_slot is not None and stage >= 3:
                prefront(nxt_slot)

        gla_tile(1, (3, 2, 1, 0), after_a)

    if stage == 2:
        return finish()
    hx2_v = hx2_d.rearrange("k p t -> p k t")
    for i in range(NT):
        slot = tcount % 2
        tcount += 1
        if tcount < len(tseq):
            load_x(tseq[tcount], tcount % 2)
        front4(slot)
        gate_lowrank()
        def sc_step(j):
            s2 = j % 2
            proj_fm(64 + j * 128, 128, 0)
            P.add("act", lambda e, s2=s2: e.copy(out=sbT[s2][:], in_=pb[0][:, :]), reads=[PB[0]], writes=[b_sbT[s2]])
            proj_fm(64 + 512 + j * 128, 128, 1)
            P.add("act", lambda e, s2=s2: e.copy(out=scT[s2][:], in_=pb[1][:, :]), reads=[PB[1]], writes=[b_scT[s2]])
            proj_fm(64 + 1024 + j * 128, 128, 2)
            P.add("dve", lambda e, j=j, s2=s2: e.tensor_tensor(out=zpad[:, j, :, 1:65], in0=pb[2][:, :].rearrange("p (r c) -> p r c", r=8),
                                                               in1=scT[s2][:].rearrange("p (r c) -> p r c", r=8), op=ALU.mult),
                  reads=[PB[2], b_scT[s2]], writes=[b_zp[j]])
            for t in range(3):
                P.add("pe", lambda e, j=j, t=t: e.matmul(pb[3][:, :], lhsT=dsc[:, t * 4 + j, :], rhs=zpad[:, j, :, t:t + 64], start=(t == 0), stop=(t == 2)),
                      reads=[b_zp[j], g_const], writes=[PB[3]])
            P.add("dve", lambda e, j=j, s2=s2: e.scalar_tensor_tensor(out=yxT[:, 4 + j, :], in0=pb[3][:, :], scalar=colp[:, Q_BSC + j:Q_BSC + j + 1], in1=sbT[s2][:],
                                                                      op0=ALU.add, op1=ALU.mult), reads=[PB[3], b_sbT[s2], g_const], writes=[b_yxs[j]])
        sgt = (tmpf, osum)
        sgb = (bg["tmpf"], bg["osum"])
        for j in range(4):
            cols = slice(j * 128, (j + 1) * 128)
            proj_tm(cols, 2, j, b_hxT[j])
            P.add("act", lambda e, j=j: e.activation(out=sgt[j % 2][:], in_=pb[j][:, :], func=AF.Silu), reads=[PB[j]], writes=[sgb[j % 2]])
            P.add("dve", lambda e, j=j: e.tensor_tensor(out=ghs4[:, j, :], in0=sgt[j % 2][:], in1=ghead[:], op=ALU.mult), reads=[sgb[j % 2], g_const], writes=[b_ghs[j]])

        def load_ob(j, i=i):
            n = i * 4 + j
            P.add("sp", lambda e, j=j, n=n: e.dma_start(out=obl[j % 2][:], in_=ob_v[n, :, :]), reads=[b_ob_d[n]], writes=[b_obl[j % 2]], dma=f"obl{j % 2}")

        load_ob(0)
        load_ob(1)

        def after_b(j, i=i):
            cols = slice(j * 128, (j + 1) * 128)
            P.add("dve", lambda e, j=j: e.tensor_tensor(out=osum[:], in0=pb[6][:, :], in1=obl[j % 2][:], op=ALU.add), reads=[PB[6], b_obl[j % 2]], writes=[bg["osum"]])
            if j < 2:
                load_ob(j + 2)
            c0, bs_ = stat_slot()
            P.add("dve", lambda e, c0=c0: e.memset(stat[:, c0:c0 + 8], 0.0), writes=[bs_])
            for h in range(4):
                P.add("act", lambda e, h=h, c0=c0: e.activation(out=junk[:, h * 128:(h + 1) * 128], in_=osum[:, h * 128:(h + 1) * 128], func=AF.Square,
                                                                accum_out=stat[:, c0 + h:c0 + h + 1]), reads=[bg["osum"], bs_], writes=[bg["junk"], bs_])
            rstd_from(stat[:, c0:c0 + 4], 128, stat[:, c0 + 4:c0 + 8], [bs_], [bs_])
            for h in range(4):
                P.add("dve", lambda e, h=h, c0=c0, j=j: e.scalar_tensor_tensor(out=yg[:, h * 128:(h + 1) * 128], in0=osum[:, h * 128:(h + 1) * 128],
                                                                               scalar=stat[:, c0 + 4 + h:c0 + 5 + h], in1=ghs4[:, j, h * 128:(h + 1) * 128],
                                                                               op0=ALU.mult, op1=ALU.mult), reads=[bg["osum"], bs_, b_ghs[j]], writes=[bg["yg"]])
            Tps = pb[7].bitcast(BF16)
            for h in range(4):
                P.add("pe", lambda e, h=h: e.transpose(Tps[:, h * 128:(h + 1) * 128], yg[:, h * 128:(h + 1) * 128], ident[:]), reads=[bg["yg"], g_const], writes=[PB[7]])
            P.add("act", lambda e, cols=cols: e.copy(out=yxT[:, 0:4, cols], in_=Tps[:, 0:512].rearrange("p (k t) -> p k t", k=4)), reads=[PB[7]], writes=[b_yxg[j]])

        gla_tile(0, (0, 1, 2, 3), after_b, mid=sc_step)
        def oproj_stages(j, i=i, slot=slot):
            cols = slice(j * 128, (j + 1) * 128)
            n = i * 4 + j
            ob_ = 3 * (j % 2)
            c0, bs_ = stat_slot()
            tmp2 = (tmpf, osum)
            tmpb = (bg["tmpf"], bg["osum"])

            def o1():
                for hf in range(2):
                    for k in range(8):
                        rb = [b_yxg[j]] if k < 4 else [b_yxs[k - 4]]
                        P.add("pe", lambda e, k=k, hf=hf: e.matmul(pb[ob_ + hf][:, :], lhsT=yxT[:, k, cols], rhs=wout[:, k, hf * 512:(hf + 1) * 512],
                                                                   start=(k == 0), stop=(k == 7)), reads=rb + [b_wout], writes=[PB[ob_ + hf]])

            def o2():
                P.add("dve", lambda e: e.memset(stat[:, c0:c0 + 4], 0.0), writes=[bs_])
                for hf in range(2):
                    P.add("act", lambda e, hf=hf: e.activation(out=junk[:, hf * 512:(hf + 1) * 512], in_=pb[ob_ + hf][:, :], func=AF.Square,
                                                               accum_out=stat[:, c0 + hf:c0 + hf + 1]), reads=[PB[ob_ + hf], bs_], writes=[bg["junk"], bs_])
                P.add("dve", lambda e: e.tensor_tensor(out=stat[:, c0 + 2:c0 + 3], in0=stat[:, c0:c0 + 1], in1=stat[:, c0 + 1:c0 + 2], op=ALU.add),
                      reads=[bs_], writes=[bs_])
                rstd_from(stat[:, c0 + 2:c0 + 3], D, stat[:, c0 + 3:c0 + 4], [bs_], [bs_])

            def o3():
                for hf in range(2):
                    hs = slice(hf * 512, (hf + 1) * 512)
                    P.add("dve", lambda e, hf=hf, hs=hs: e.scalar_tensor_tensor(out=tmp2[hf][:], in0=pb[ob_ + hf][:, :], scalar=stat[:, c0 + 3:c0 + 4], in1=G1[:, hs],
                                                                               op0=ALU.mult, op1=ALU.mult), reads=[PB[ob_ + hf], bs_, g_const], writes=[tmpb[hf]])
                    P.add("dve", lambda e, hf=hf, hs=hs: e.tensor_tensor(out=xt[slot][:, j, hs], in0=xt[slot][:, j, hs], in1=tmp2[hf][:], op=ALU.add),
                          reads=[tmpb[hf], b_xt[slot][j]], writes=[b_xt[slot][j]])
                P.add("sp", lambda e: e.dma_start(out=x1_v[n, :, :], in_=xt[slot][:, j, :]), reads=[b_xt[slot][j]], writes=[b_x1_d[n]], dma=f"xs{slot}{j}")

            fs = front_stages(xt[slot][:, j, :], b_xt[slot][j], A2, B2, hx2s[j % 2], slice(0, 128), b_hx2s[j % 2])

            def o7():
                fs[3]()
                P.add("sp", lambda e: e.dma_start(out=hx2_v[:, :, 64 + i * 512 + j * 128:64 + i * 512 + (j + 1) * 128], in_=hx2s[j % 2][:]),
                      reads=[b_hx2s[j % 2]], writes=[b_hx2_d[i]], dma=f"hx2s{j % 2}")

            return [o1, o2, o3, fs[0], fs[1], fs[2], o7]

        if fcnt[0] % 2:
            fcnt[0] += 1
        OS = [oproj_stages(j) for j in range(4)]
        order = [(0, 0), (1, 0), (0, 1), (0, 2), (0, 3), (1, 1), (2, 0), (1, 2), (0, 4), (1, 3), (0, 5), (2, 1), (3, 0), (0, 6), (2, 2), (1, 4), (2, 3), (1, 5),
                 (3, 1), (1, 6), (3, 2), (2, 4), (3, 3), (2, 5), (2, 6), (3, 4), (3, 5), (3, 6)]
        assert sorted(order) == [(j, k) for j in range(4) for k in range(7)]
        for (j, k) in order:
            OS[j][k]()
        if tcount < len(tseq):
            prefront(tcount % 2)
    P.barrier()
    if stage == 3:
        return finish()

    Zc = Alloc(nc, OV, SB_HI)
    wdn = Zc.t([128, NFC, D], BF16, "wdn")
    wupS = [Zc.t([128, 8, 512], BF16, "wupS") for _ in range(2)]
    hw = [Zc.t([128, 8, 640], BF16, "hw") for _ in range(2)]
    x1t = [Zc.t([128, 4, D], F32, "x1t") for _ in range(2)]
    hT = Zc.t([128, NFC, 512], BF16, "hT")
    upad = [Zc.t([128, 10, 66], BF16, "upad") for _ in range(2)]
    gts = [Zc.t([128, 512], BF16, "gts") for _ in range(2)]
    sil = [Zc.t([128, 512], BF16, "sil") for _ in range(2)]
    dg = [Zc.t([128, 9, 128], BF16, "dg") for _ in range(2)]
    tmpc = Zc.t([128, 512], F32, "tmpc")
    junkc = Zc.t([128, 512], BF16, "junkc")
    b_wdn = Buf("wdn")
    b_wupS = [Buf("wupS0"), Buf("wupS1")]
    b_hw = [Buf("hw0"), Buf("hw1")]
    b_x1t = [[Buf(f"x1t{s}{j}") for j in range(4)] for s in range(2)]
    b_hT = [Buf(f"hT{c}") for c in range(NFC)]
    b_upad = [Buf("upad0"), Buf("upad1")]
    b_upc = [Buf("upc0"), Buf("upc1")]
    ucar = Zc.t([128, NFC, 2, 64], BF16, "ucar")
    b_ucar = [Buf(f"ucar{c}") for c in range(NFC)]
    b_gts = [Buf("gts0"), Buf("gts1")]
    b_sil = [Buf("sil0"), Buf("sil1")]
    b_dg = [Buf("dg0"), Buf("dg1")]
    b_tmpc, b_junkc, b_statc = Buf("tmpc"), Buf("junkc"), Buf("statc")
    b_out = []

    wdn_v = wdown_d.rearrange("(c p) j -> p c j", p=128)
    for c in range(NFC):
        P.add("pool", lambda e, c=c: e.dma_start(out=wdn[:, c, :], in_=wdn_v[:, c, :]), writes=[b_wdn], dma="wdn", ndma=1)
    for s in range(2):
        P.add("dve", lambda e, s=s: e.memset(upad[s][:], 0.0), writes=[b_upad[s]])
    out_v = out_d.rearrange("(n p) f -> n p f", p=128)
    x1_v4 = x1_d.rearrange("(n j p) f -> n p j f", p=128, j=4)
    def load_c(i):
        slot = i % 2
        P.add("sp", lambda e: e.dma_start(out=hw[slot][:], in_=hx2_v[:, :, i * 512:i * 512 + 640]),
              reads=b_hx2_d[max(0, i - 1):i + 2] + [b_hx2], writes=[b_hw[slot]], dma=f"hw{slot}")
        for j in range(4):
            P.add("sp", lambda e, j=j: e.dma_start(out=x1t[slot][:, j, :], in_=x1_v4[i, :, j, :]), reads=[b_x1_d[i * 4 + j]],
                  writes=[b_x1t[slot][j]], dma=f"x1t{slot}{j}")

    NPC = NFC // 2

    def load_piece(g):
        ps = g % 2
        pc = g % NPC
        P.add("sp", lambda e: e.dma_start(out=wupS[ps][:], in_=wupb_d[pc, :, :, :]), reads=[b_wupb], writes=[b_wupS[ps]], dma=f"wupS{ps}")

    load_c(0)
    load_piece(0)
    piece = 0
    for i in range(NT):
        slot = i % 2
        for c in range(NFC):
            s2 = c % 2
            if c % 2 == 0:
                ps = piece % 2
                piece += 1
                if piece < NT * NPC:
                    load_piece(piece)
                if c == 2 and i + 1 < NT:
                    load_c(i + 1)
            wo = (c % 2) * 128
            for t in range(9):
                P.add("dve", lambda e, c=c, t=t, s2=s2: e.tensor_scalar(out=dg[s2][:, t, :], in0=ident[:], scalar1=colp[:, Q_WCF + c * 9 + t:Q_WCF + c * 9 + t + 1],
                                                                         scalar2=None, op0=ALU.mult), reads=[g_const], writes=[b_dg[s2]])
            ucol = 0 if i == 0 else 128
            for k in range(8):
                P.add("pe", lambda e, k=k, s2=s2, ps=ps, wo=wo, slot=slot, ucol=ucol: e.matmul(pb[s2][:, :], lhsT=wupS[ps][:, k, wo:wo + 128], rhs=hw[slot][:, k, ucol:ucol + 512],
                                                                                           start=(k == 0), stop=(k == 7)),
                      reads=[b_wupS[ps], b_hw[slot]], writes=[PB[s2]])
            if i == 0:
                for k in range(8):
                    P.add("pe", lambda e, k=k, s2=s2, ps=ps, wo=wo, slot=slot: e.matmul(pb[2 + s2][:, 0:128], lhsT=wupS[ps][:, k, wo:wo + 128], rhs=hw[slot][:, k, 512:640], start=(k == 0), stop=(k == 7)),
                          reads=[b_wupS[ps], b_hw[slot]], writes=[PB[2 + s2]])
            for k in range(8):
                P.add("pe", lambda e, k=k, s2=s2, ps=ps, wo=wo, slot=slot: e.matmul(pb[4 + s2][:, :], lhsT=wupS[ps][:, k, 256 + wo:256 + wo + 128], rhs=hw[slot][:, k, 64:576], start=(k == 0), stop=(k == 7)),
                      reads=[b_wupS[ps], b_hw[slot]], writes=[PB[4 + s2]])
            if i == 0:
                P.add("act", lambda e, s2=s2: e.copy(out=upad[s2][:, 0:8, 1:65], in_=pb[s2][:, :].rearrange("p (r c) -> p r c", r=8)), reads=[PB[s2]], writes=[b_upc[s2], b_upad[s2]])
                P.add("act", lambda e, s2=s2: e.copy(out=upad[s2][:, 8:10, 1:65], in_=pb[2 + s2][:, 0:128].rearrange("p (r c) -> p r c", r=2)), reads=[PB[2 + s2]], writes=[b_upad[s2]])
            else:
                P.add("pool", lambda e, c=c, s2=s2: e.tensor_copy(out=upad[s2][:, 0:2, 1:65], in_=ucar[:, c, :, :]), reads=[b_ucar[c]], writes=[b_upc[s2]])
                P.add("act", lambda e, s2=s2: e.copy(out=upad[s2][:, 2:10, 1:65], in_=pb[s2][:, :].rearrange("p (r c) -> p r c", r=8)), reads=[PB[s2]], writes=[b_upad[s2]])
            if i < NT - 1:
                P.add("pool", lambda e, c=c, s2=s2: e.tensor_copy(out=ucar[:, c, :, :], in_=upad[s2][:, 8:10, 1:65]), reads=[b_upad[s2]], writes=[b_ucar[c]])
            P.add("dve", lambda e, s2=s2: e.tensor_copy(out=gts[s2][:], in_=pb[4 + s2][:, :]), reads=[PB[4 + s2]], writes=[b_gts[s2]])
            for t in range(9):
                dr, dc = t // 3, t % 3
                P.add("pe", lambda e, t=t, dr=dr, dc=dc, s2=s2: e.matmul(pb[6 + s2][:, :], lhsT=dg[s2][:, t, :], rhs=upad[s2][:, dr:dr + 8, dc:dc + 64], start=(t == 0), stop=(t == 8)),
                      reads=[b_dg[s2], b_upad[s2], b_upc[s2]], writes=[PB[6 + s2]])
            P.add("act", lambda e, c=c, s2=s2: e.activation(out=sil[s2][:], in_=pb[6 + s2][:, :], func=AF.Silu, bias=colp[:, Q_BCF + c:Q_BCF + c + 1]),
                  reads=[PB[6 + s2], g_const], writes=[b_sil[s2]])
            P.add("dve", lambda e, c=c, s2=s2: e.tensor_tensor(out=hT[:, c, :], in0=sil[s2][:], in1=gts[s2][:], op=ALU.mult), reads=[b_sil[s2], b_gts[s2]], writes=[b_hT[c]])
        for j in range(4):
            cols = slice(j * 128, (j + 1) * 128)
            n = i * 4 + j
            for hf in range(2):
                bank = (2 * j + hf) % 8
                for c in range(NFC):
                    P.add("pe", lambda e, c=c, hf=hf, bank=bank, cols=cols: e.matmul(pb[bank][:, :], lhsT=hT[:, c, cols], rhs=wdn[:, c, hf * 512:(hf + 1) * 512],
                                                                                      start=(c == 0), stop=(c == NFC - 1)), reads=[b_hT[c], b_wdn], writes=[PB[bank]])
            c0, _unused = stat_slot()
            P.add("dve", lambda e, c0=c0: e.memset(stat[:, c0:c0 + 4], 0.0), writes=[b_statc])
            for hf in range(2):
                bank = (2 * j + hf) % 8
                P.add("act", lambda e, hf=hf, bank=bank, c0=c0: e.activation(out=junkc[:], in_=pb[bank][:, :], func=AF.Square, accum_out=stat[:, c0 + hf:c0 + hf + 1]),
                      reads=[PB[bank], b_statc], writes=[b_junkc, b_statc])
            P.add("dve", lambda e, c0=c0: e.tensor_tensor(out=stat[:, c0 + 2:c0 + 3], in0=stat[:, c0:c0 + 1], in1=stat[:, c0 + 1:c0 + 2], op=ALU.add),
                  reads=[b_statc], writes=[b_statc])
            rstd_from(stat[:, c0 + 2:c0 + 3], D, stat[:, c0 + 3:c0 + 4], [b_statc], [b_statc])
            for hf in range(2):
                bank = (2 * j + hf) % 8
                hs = slice(hf * 512, (hf + 1) * 512)
                P.add("dve", lambda e, hf=hf, bank=bank, hs=hs, c0=c0: e.scalar_tensor_tensor(out=tmpc[:], in0=pb[bank][:, :], scalar=stat[:, c0 + 3:c0 + 4], in1=G2[:, hs],
                                                                                             op0=ALU.mult, op1=ALU.mult), reads=[PB[bank], b_statc, g_const], writes=[b_tmpc])
                P.add("dve", lambda e, j=j, hs=hs, slot=slot: e.tensor_tensor(out=x1t[slot][:, j, hs], in0=x1t[slot][:, j, hs], in1=tmpc[:], op=ALU.add),
                      reads=[b_tmpc, b_x1t[slot][j]], writes=[b_x1t[slot][j]])
            bo = Buf(f"out{n}")
            tok = P.add("sp", lambda e, j=j, n=n, slot=slot: e.dma_start(out=out_v[n, :, :], in_=x1t[slot][:, j, :]), reads=[b_x1t[slot][j]], writes=[bo], dma=f"ot{slot}{j}")
            b_out.append(tok)
    last = {}
    for t in b_out:
        last[t[0]] = max(last.get(t[0], 0), t[1])
    fin = list(last.items())
    if debug:
        for k, v in P.dma_cum.items():
            fin.append(("dma:" + k, v))
    P.final_wait("sp", fin)
    P.emit(nc)
    return nc


def _consts():
    c = np.zeros((128, NCST), np.float32)
    m = np.arange(128)[:, None]
    t = np.arange(128)[None, :]
    c[:, K_ID:K_ID + 128] = (m == t)
    c[:, K_LFI:K_LFI + 128] = (m <= t) * (-1.0 / 16)
    c[:, K_LFR:K_LFR + 128] = (m > t) * (-1.0 / 16)
    c[:, K_LBI:K_LBI + 128] = (m >= t) * (-1.0 / 16)
    c[:, K_LBR:K_LBR + 128] = (m < t) * (-1.0 / 16)
    c[:, K_MF:K_MF + 128] = (m <= t)
    c[:, K_MB:K_MB + 128] = (m > t)
    c[:, K_NC] = -1.0 / 16
    return c


def _colmaj(v, n):
    return np.ascontiguousarray(np.asarray(v, np.float32).reshape(n, 128).T)


def make_in_maps(inputs, NT=16, ncores=8):
    f = lambda a: np.asarray(a, np.float32)
    w_in = f(inputs["w_in"])[0]
    w_tm = np.ascontiguousarray(np.concatenate([w_in[:, C_K:C_V], w_in[:, C_Q:C_OG], w_in[:, C_V:C_AF], w_in[:, C_OG:C_SB]], axis=1))
    gate = np.zeros((D, 64), np.float32)
    gate[:, 0:16] = w_in[:, C_AF:C_AB]
    gate[:, 32:48] = w_in[:, C_AB:C_Q]
    w_fm = np.ascontiguousarray(np.concatenate([gate, w_in[:, C_SB:C_SC], w_in[:, C_SC:C_SX], w_in[:, C_SX:]], axis=1))
    wg = np.zeros((96, 512), np.float32)
    wg[0:16, 0:256] = f(inputs["w_af"])[0]
    wg[32:48, 256:512] = f(inputs["w_ab"])[0]
    wg[64, 0:256] = f(inputs["b_af"])[0]
    wg[64, 256:512] = f(inputs["b_ab"])[0]
    rows1 = np.concatenate([f(inputs["g_pre_mix"])[0], f(inputs["g_post_mix"])[0], f(inputs["g_pre_ffn"])[0], f(inputs["g_post_ffn"])[0],
                            np.tile(f(inputs["g_head"])[0], 4), f(inputs["b_ada"])[0]])
    rows = np.ascontiguousarray(np.broadcast_to(rows1[None, :], (128, NROW)))
    w_sc = f(inputs["w_sc"])[0]
    w_cf = f(inputs["w_cf"])[0].reshape(9, DFF)
    cols_common = np.concatenate([
        _colmaj(f(inputs["b_sc"])[0], 4), _colmaj(f(inputs["b_cf"])[0], NFC),
        np.concatenate([_colmaj(w_sc[t], 4) for t in range(3)], axis=1),
        np.stack([_colmaj(w_cf[t], NFC) for t in range(9)], axis=2).reshape(128, NFC * 9),
    ], axis=1)
    cst = _consts()
    maps = []
    x = inputs["x"]
    for b in range(ncores):
        cols = np.ascontiguousarray(np.concatenate([_colmaj(f(inputs["c"])[b], 8), _colmaj(f(inputs["c_ctx"]), 8), cols_common], axis=1))
        maps.append({
            "x": np.ascontiguousarray(f(x[b])[:NT * 512]), "ctx": np.ascontiguousarray(f(inputs["ctx"][b])),
            "rows": rows, "cols": cols, "cst": cst,
            "w_ada": f(inputs["w_ada"])[0], "w_tm": w_tm, "w_fm": w_fm, "w_gate": wg,
            "w_out": f(inputs["w_out"])[0], "w_up": f(inputs["w_up"])[0], "w_down": f(inputs["w_down"])[0],
        })
    return maps


_NC_CACHE = {}


def kernel(**inputs):
    NT = 16
    if NT not in _NC_CACHE:
        _NC_CACHE[NT] = build(NT)
    nc = _NC_CACHE[NT]
    maps = make_in_maps(inputs, NT, 8)
    res = run_bass_kernel_spmd(nc, maps, core_ids=list(range(8)))
    return np.stack([np.asarray(r["out"], np.float32).reshape(NT * 512, D) for r in res.results], axis=0)
```

```python
import os
import numpy as np
import concourse.bass as bass
import concourse.mybir as mybir
from concourse.bass_utils import run_bass_kernel_spmd

F32 = mybir.dt.float32
BF16 = mybir.dt.bfloat16
AF = mybir.ActivationFunctionType
ALU = mybir.AluOpType

D = 1024
DFF = 2816
NFC = 22
CTX = 256
EPS = 1e-6
SB_LO = 16512
SB_HI = 229344
EPOCH = 24000
STRICT_SYNC = False

C_K, C_V, C_AF, C_AB, C_Q, C_OG, C_SB, C_SC, C_SX = 0, 256, 768, 784, 800, 1056, 1568, 2080, 2592

K_ID, K_LFI, K_LFR, K_LBI, K_LBR, K_MF, K_MB, K_NC = 0, 128, 256, 384, 512, 640, 768, 896
NCST = 897
R_GPM, R_GQM, R_GPF, R_GQF, R_GH, R_BADA = 0, 1024, 2048, 3072, 4096, 4608
NROW = 4608 + 6144
Q_C, Q_CC, Q_BSC, Q_BCF, Q_WSC, Q_WCF = 0, 8, 16, 20, 42, 54
NCOL = 54 + 198


class Buf:
    __slots__ = ("name", "w", "r")

    def __init__(self, name):
        self.name = name
        self.w = None
        self.r = {}


class Prog:
    ENGS = ("pe", "act", "dve", "pool", "sp")

    def __init__(self):
        self.ops = {e: [] for e in self.ENGS}
        self.waited = {e: {} for e in self.ENGS}
        self.dma_cum = {}
        self.n = 0

    def _need(self, eng, tok, waits):
        key, val = tok
        if self.waited[eng].get(key, -1) >= val:
            return
        self.waited[eng][key] = val
        waits.append(tok)
        if not key.startswith("dma:"):
            self.ops[key][val]["sig"] = True

    def add(self, eng, fn, reads=(), writes=(), dma=None, ndma=1):
        idx = len(self.ops[eng])
        waits = []
        mykey = ("dma:" + dma) if dma is not None else eng
        for b in reads:
            if b.w is not None:
                if b.w[0] == eng and eng == "pe" and dma is None:
                    continue
                self._need(eng, b.w, waits)
        relax = (eng == "pe" or dma is not None) if STRICT_SYNC else True
        for b in writes:
            if b.w is not None and not (b.w[0] == mykey and relax):
                self._need(eng, b.w, waits)
            for k, v in b.r.items():
                if k == mykey and relax:
                    continue
                self._need(eng, (k, v), waits)
        if dma is not None:
            cum = self.dma_cum.get(dma, 0) + 16 * ndma
            self.dma_cum[dma] = cum
            tok = ("dma:" + dma, cum)
        else:
            tok = (eng, idx)
        for b in reads:
            if b.r.get(tok[0], -1) < tok[1]:
                b.r[tok[0]] = tok[1]
        for b in writes:
            b.w = tok
            b.r = {}
        self.ops[eng].append({"fn": fn, "waits": waits, "sig": False, "dma": dma})
        self.n += 1
        return tok

    def barrier(self, skip=()):
        toks = []
        for e in self.ENGS:
            if self.ops[e]:
                for i in range(len(self.ops[e]) - 1, -1, -1):
                    if self.ops[e][i]["dma"] is None and self.ops[e][i]["fn"] is not None:
                        toks.append((e, i))
                        break
        for k, v in self.dma_cum.items():
            if k in skip:
                continue
            toks.append(("dma:" + k, v))
        for e in self.ENGS:
            waits = []
            for t in toks:
                if t[0] == e:
                    continue
                self._need(e, t, waits)
            self.ops[e].append({"fn": None, "waits": waits, "sig": False, "dma": None})

    def final_wait(self, eng, toks):
        waits = []
        for t in toks:
            self._need(eng, t, waits)
        self.ops[eng].append({"fn": None, "waits": waits, "sig": False, "dma": None})

    def emit(self, nc):
        engsem = {}
        signum = {}
        for e in self.ENGS:
            cnt = 0
            signum[e] = {}
            for i, op in enumerate(self.ops[e]):
                if op["sig"]:
                    signum[e][i] = cnt
                    cnt += 1
            nep = cnt // EPOCH + 1
            engsem[e] = [nc.alloc_semaphore(f"s_{e}_{j}") for j in range(nep)]
        dmasem = {k: nc.alloc_semaphore("d_" + k) for k in self.dma_cum}

        def run(ename, eng):
            for i, op in enumerate(self.ops[ename]):
                for key, val in op["waits"]:
                    if key.startswith("dma:"):
                        eng.wait_ge(dmasem[key[4:]], val)
                    else:
                        s = signum[key][val]
                        eng.wait_ge(engsem[key][s // EPOCH], s % EPOCH + 1)
                if op["fn"] is None:
                    continue
                ins = op["fn"](eng)
                if op["dma"] is not None:
                    ins.then_inc(dmasem[op["dma"]], 16)
                elif op["sig"]:
                    s = signum[ename][i]
                    ins.then_inc(engsem[ename][s // EPOCH], 1)

        with nc.Block() as block:
            @block.tensor
            def _(eng):
                run("pe", eng)

            @block.scalar
            def _(eng):
                run("act", eng)

            @block.vector
            def _(eng):
                run("dve", eng)

            @block.gpsimd
            def _(eng):
                run("pool", eng)

            @block.sync
            def _(eng):
                run("sp", eng)


class Alloc:
    def __init__(self, nc, lo, hi):
        self.nc, self.lo, self.hi, self.p, self.n = nc, lo, hi, lo, 0

    def t(self, shape, dt, name="t"):
        nb = 1
        for s in shape[1:]:
            nb *= s
        nb *= 2 if dt == BF16 else 4
        nb = (nb + 31) // 32 * 32
        off = self.p
        self.p += nb
        assert self.p <= self.hi, f"SBUF overflow {name} {self.p} > {self.hi}"
        self.n += 1
        return self.nc.alloc_sbuf_tensor_at(f"{name}{self.n}", list(shape), dt, offset=off)


def build(NT=16, debug=False, stage=9):
    nc = bass.Bass("TRN2", target_bir_lowering=False)
    SEQ = NT * 512
    P = Prog()

    def finish():
        fin = [("dma:" + k, v) for k, v in P.dma_cum.items()]
        P.final_wait("sp", fin)
        P.emit(nc)
        return nc

    def din(name, shape, dt=F32):
        return nc.dram_tensor(name, list(shape), dt, kind="ExternalInput").ap()

    x_d = din("x", [SEQ, D])
    ctx_d = din("ctx", [CTX, D])
    rows_d = din("rows", [128, NROW])
    cols_d = din("cols", [128, NCOL])
    cst_d = din("cst", [128, NCST])
    wada_d = din("w_ada", [D, 6 * D])
    wtm_d = din("w_tm", [D, 1536])
    wfm_d = din("w_fm", [D, 1600])
    wgate_d = din("w_gate", [96, 512])
    wout_d = din("w_out", [D, D])
    wup_d = din("w_up", [D, 2 * DFF])
    wdown_d = din("w_down", [DFF, D])
    out_d = nc.dram_tensor("out", [SEQ, D], F32, kind="ExternalOutput").ap()
    if debug:
        ob_d = nc.dram_tensor("ob_d", [SEQ, 512], F32, kind="ExternalOutput").ap()
        x1_d = nc.dram_tensor("x1_d", [SEQ, D], F32, kind="ExternalOutput").ap()
    else:
        ob_d = nc.dram_tensor("ob_d", [SEQ, 512], F32).ap()
        x1_d = nc.dram_tensor("x1_d", [SEQ, D], F32).ap()
    hx2_d = nc.dram_tensor("hx2_d", [8, 128, SEQ + 128], BF16).ap()
    wupb_d = nc.dram_tensor("wupb_d", [NFC // 2, 128, 8, 512], BF16).ap()

    pb = [nc.alloc_psum_tensor(f"pb{i}", [128, 512], F32) for i in range(8)]
    PB = [Buf(f"pb{i}") for i in range(8)]

    G = Alloc(nc, SB_LO, SB_HI)
    ident = G.t([128, 128], BF16, "ident")
    Lm = G.t([128, 4, 128], BF16, "Lm")
    maskT = G.t([128, 2, 512], BF16, "maskT")
    ncol = G.t([128, 1], BF16, "ncol")
    ones_r = G.t([1, 128], BF16, "ones_r")
    A1 = G.t([128, D], F32, "A1")
    cA1 = G.t([128, D], F32, "cA1")
    G1 = G.t([128, D], F32, "G1")
    A2 = G.t([128, D], F32, "A2")
    G2 = G.t([128, D], F32, "G2")
    ghead = G.t([128, 512], F32, "ghead")
    B1 = G.t([1, D], BF16, "B1")
    cB1 = G.t([1, D], BF16, "cB1")
    B2 = G.t([1, D], BF16, "B2")
    colp = G.t([128, NCOL], F32, "colp")
    Bc = G.t([128, 24], F32, "Bc")
    wgate = G.t([96, 512], BF16, "wgate")
    dsc = G.t([128, 12, 128], BF16, "dsc")
    S = [G.t([128, 2, 128], F32, "S") for _ in range(2)]
    Sb = [G.t([128, 2, 128], BF16, "Sb") for _ in range(2)]
    alT = G.t([96, 512], BF16, "alT")
    stat = G.t([128, 64], F32, "stat")
    fstat = G.t([128, 8], F32, "fstat")
    b_fst = [Buf(f"fst{i}") for i in range(4)]
    epsc = G.t([128, 1], F32, "epsc")
    g_const = Buf("consts")
    bS = [Buf("S0"), Buf("S1")]
    bSb = [Buf("Sb0"), Buf("Sb1")]
    OV = G.p

    Z = Alloc(nc, OV, SB_HI)
    cstf = Z.t([128, NCST], F32, "cstf")
    rowp = Z.t([128, NROW], F32, "rowp")
    wa = [Z.t([128, 8, D], F32, "wa") for _ in range(2)]
    modr = [Z.t([128, D], F32, "modr") for _ in range(2)]
    rep = Z.t([128, 16, 128], F32, "rep")
    onesf = Z.t([128, 128], F32, "onesf")
    scl = Z.t([128, 16], F32, "scl")
    wgf = Z.t([96, 512], F32, "wgf")
    zt = Z.t([128, 64], BF16, "zt")
    b_cst, b_row, b_col, b_wgf = Buf("cstf"), Buf("rowp"), Buf("colp"), Buf("wgf")
    b_wa = [Buf("wa0"), Buf("wa1")]
    b_modr = [Buf("modr0"), Buf("modr1")]
    b_rep, b_scl, b_zt = Buf("rep"), Buf("scl"), Buf("zt")

    P.add("sp", lambda e: e.dma_start(out=cstf[:], in_=cst_d[:, :]), writes=[b_cst], dma="ld0")
    P.add("sp", lambda e: e.dma_start(out=colp[:], in_=cols_d[:, :]), writes=[b_col], dma="ld1")
    P.add("sp", lambda e: e.dma_start(out=wgf[:], in_=wgate_d[:, :]), writes=[b_wgf], dma="ld2")
    P.add("sp", lambda e: e.dma_start(out=rowp[:, 0:4608], in_=rows_d[:, 0:4608]), writes=[b_row], dma="ld3", ndma=2)
    P.add("sp", lambda e: e.dma_start(out=rowp[:, 4608:NROW], in_=rows_d[:, 4608:NROW]), writes=[b_row], dma="ld3", ndma=0)

    b_wupb = Buf("wupb")
    wup_v = wup_d.rearrange("(k p) j -> p k j", p=128)
    for pc in range(NFC // 2):
        for ug in range(2):
            off = ug * DFF + pc * 256
            P.add("pool", lambda e, pc=pc, ug=ug, off=off: e.dma_start(out=wupb_d[pc, :, :, ug * 256:(ug + 1) * 256], in_=wup_v[:, :, off:off + 256]),
                  writes=[b_wupb], dma="wupc", ndma=1)

    P.add("dve", lambda e: e.tensor_copy(out=ident[:], in_=cstf[:, K_ID:K_ID + 128]), reads=[b_cst], writes=[g_const])
    for j, k0 in enumerate((K_LFI, K_LFR, K_LBI, K_LBR)):
        P.add("dve", lambda e, j=j, k0=k0: e.tensor_copy(out=Lm[:, j, :], in_=cstf[:, k0:k0 + 128]), reads=[b_cst], writes=[g_const])
    for d_, k0 in enumerate((K_MF, K_MB)):
        for h in range(4):
            P.add("dve", lambda e, d_=d_, k0=k0, h=h: e.tensor_copy(out=maskT[:, d_, h * 128:(h + 1) * 128], in_=cstf[:, k0:k0 + 128]),
                  reads=[b_cst], writes=[g_const])
    P.add("dve", lambda e: e.tensor_copy(out=ncol[:], in_=cstf[:, K_NC:K_NC + 1]), reads=[b_cst], writes=[g_const])
    P.add("dve", lambda e: e.memset(ones_r[:], 1.0), writes=[g_const])
    P.add("dve", lambda e: e.memset(epsc[:], EPS), writes=[g_const])
    P.add("dve", lambda e: e.memset(onesf[:], 1.0), writes=[b_rep])
    P.add("dve", lambda e: e.memset(zt[:], 0.0), writes=[b_zt])
    P.add("dve", lambda e: e.memset(alT[64:96, :], 1.0), writes=[g_const])
    P.add("dve", lambda e: e.tensor_copy(out=wgate[:], in_=wgf[:]), reads=[b_wgf], writes=[g_const])
    P.add("dve", lambda e: e.tensor_copy(out=ghead[:], in_=rowp[:, R_GH:R_GH + 512]), reads=[b_row], writes=[g_const])
    for d_ in range(2):
        P.add("dve", lambda e, d_=d_: e.memset(S[d_][:], 0.0), writes=[bS[d_]])
        P.add("dve", lambda e, d_=d_: e.memset(Sb[d_][:], 0.0), writes=[bSb[d_]])
    for t in range(3):
        for j in range(4):
            P.add("dve", lambda e, t=t, j=j: e.tensor_scalar(out=dsc[:, t * 4 + j, :], in0=ident[:], scalar1=colp[:, Q_WSC + t * 4 + j:Q_WSC + t * 4 + j + 1],
                                                              scalar2=None, op0=ALU.mult), reads=[g_const, b_col], writes=[g_const])
    b_hx2 = Buf("hx2_d")
    for k in range(8):
        P.add("sp", lambda e, k=k: e.dma_start(out=hx2_d[k, :, 0:64], in_=zt[:, 0:64]), reads=[b_zt], writes=[b_hx2], dma="zm", ndma=1)
        P.add("sp", lambda e, k=k: e.dma_start(out=hx2_d[k, :, SEQ + 64:SEQ + 128], in_=zt[:, 0:64]), reads=[b_zt], writes=[b_hx2], dma="zm", ndma=1)

    P.add("act", lambda e: e.activation(out=scl[:], in_=colp[:, Q_C:Q_C + 16], func=AF.Silu), reads=[b_col], writes=[b_scl])
    for j in range(16):
        P.add("dve", lambda e, j=j: e.tensor_scalar(out=rep[:, j, :], in0=onesf[:], scalar1=scl[:, j:j + 1], scalar2=None, op0=ALU.mult),
              reads=[b_scl, b_rep], writes=[b_rep])
    wada_v = wada_d.rearrange("(k p) j -> p k j", p=128)

    def mod_piece(m, v, slot):
        for hf in range(2):
            bank = 2 * slot + hf
            for k in range(8):
                P.add("pe", lambda e, k=k, hf=hf, bank=bank: e.matmul(pb[bank][:, :], lhsT=rep[:, v * 8 + k, :], rhs=wa[m % 2][:, k, hf * 512:(hf + 1) * 512],
                                                                     start=(k == 0), stop=(k == 7)),
                      reads=[b_rep, b_wa[m % 2]], writes=[PB[bank]])
            P.add("dve", lambda e, hf=hf, bank=bank: e.tensor_tensor(out=modr[slot][:, hf * 512:(hf + 1) * 512], in0=pb[bank][:, :],
                                                                      in1=rowp[:, R_BADA + m * D + hf * 512:R_BADA + m * D + (hf + 1) * 512], op=ALU.add),
                  reads=[PB[bank], b_row], writes=[b_modr[slot]])

    for m in range(6):
        for q in range(2):
            P.add("sp", lambda e, m=m, q=q: e.dma_start(out=wa[m % 2][:, q * 4:(q + 1) * 4, :], in_=wada_v[:, q * 4:(q + 1) * 4, m * D:(m + 1) * D]),
                  writes=[b_wa[m % 2]], dma=f"wa{m % 2}", ndma=1)
        mod_piece(m, 0, 0)
        if m < 2:
            mod_piece(m, 1, 1)
        if m == 0:
            P.add("act", lambda e: e.copy(out=B1[0:1, :], in_=modr[0][0:1, :]), reads=[b_modr[0]], writes=[g_const])
            P.add("act", lambda e: e.copy(out=cB1[0:1, :], in_=modr[1][0:1, :]), reads=[b_modr[1]], writes=[g_const])
        elif m == 1:
            P.add("dve", lambda e: e.scalar_tensor_tensor(out=A1[:], in0=modr[0][:], scalar=1.0, in1=rowp[:, R_GPM:R_GPM + D], op0=ALU.add, op1=ALU.mult),
                  reads=[b_modr[0], b_row], writes=[g_const])
            P.add("dve", lambda e: e.scalar_tensor_tensor(out=cA1[:], in0=modr[1][:], scalar=1.0, in1=rowp[:, R_GPM:R_GPM + D], op0=ALU.add, op1=ALU.mult),
                  reads=[b_modr[1], b_row], writes=[g_const])
        elif m == 2:
            P.add("dve", lambda e: e.tensor_tensor(out=G1[:], in0=modr[0][:], in1=rowp[:, R_GQM:R_GQM + D], op=ALU.mult), reads=[b_modr[0], b_row], writes=[g_const])
        elif m == 3:
            P.add("act", lambda e: e.copy(out=B2[0:1, :], in_=modr[0][0:1, :]), reads=[b_modr[0]], writes=[g_const])
        elif m == 4:
            P.add("dve", lambda e: e.scalar_tensor_tensor(out=A2[:], in0=modr[0][:], scalar=1.0, in1=rowp[:, R_GPF:R_GPF + D], op0=ALU.add, op1=ALU.mult),
                  reads=[b_modr[0], b_row], writes=[g_const])
        else:
            P.add("dve", lambda e: e.tensor_tensor(out=G2[:], in0=modr[0][:], in1=rowp[:, R_GQF:R_GQF + D], op=ALU.mult), reads=[b_modr[0], b_row], writes=[g_const])
    for v_, Br_ in enumerate((B1, cB1, B2)):
        for k in range(8):
            P.add("pe", lambda e, v_=v_, k=k, Br_=Br_: e.matmul(pb[4][:, v_ * 8 + k:v_ * 8 + k + 1], lhsT=Br_[0:1, k * 128:(k + 1) * 128], rhs=ones_r[0:1, 0:1],
                                                            start=True, stop=True), reads=[g_const], writes=[PB[4]])
    P.add("dve", lambda e: e.tensor_copy(out=Bc[:], in_=pb[4][:, 0:24]), reads=[PB[4]], writes=[g_const])
    P.barrier(skip=("wupc",))
    if stage == 0:
        return finish()

    Y = Alloc(nc, OV, SB_HI)
    wtm = Y.t([128, 8, 1536], BF16, "wtm")
    wfm = Y.t([128, 8, 1600], BF16, "wfm")
    wout = Y.t([128, 8, D], BF16, "wout")
    xt = [Y.t([128, 4, D], F32, "xt") for _ in range(2)]
    hxT = Y.t([128, 8, 512], BF16, "hxT")
    obl = [Y.t([128, 512], F32, "obl") for _ in range(2)]
    g_l = [Y.t([128, 256], BF16, "g_l") for _ in range(2)]
    g_eb = [Y.t([128, 256], F32, "g_eb") for _ in range(2)]
    g_enb = [Y.t([128, 256], F32, "g_enb") for _ in range(2)]
    g_er = [Y.t([128, 256], F32, "g_er") for _ in range(2)]
    g_e = g_er
    g_qz = [Y.t([128, 2, 256], BF16, "g_qz") for _ in range(2)]
    g_kt = [Y.t([128, 256], BF16, "g_kt") for _ in range(2)]
    kqs = [Y.t([128, 512], F32, "kqs") for _ in range(2)]
    g_et = [Y.t([128, 2], F32, "g_et") for _ in range(2)]
    g_kh = [Y.t([128, 256], BF16, "g_kh") for _ in range(2)]
    g_vb = [Y.t([128, 512], BF16, "g_vb") for _ in range(2)]
    g_T = [Y.t([128, 768], BF16, "g_T") for _ in range(2)]
    g_att = [Y.t([128, 512], BF16, "g_att") for _ in range(2)]
    obst = obl
    osum = Y.t([128, 512], F32, "osum")
    ghs4 = Y.t([128, 4, 512], BF16, "ghs4")
    b_ghs = [Buf(f"ghs{j}") for j in range(4)]
    yg = Y.t([128, 512], BF16, "yg")
    junk = Y.t([128, D], BF16, "junk")
    ybf = [Y.t([128, D], BF16, "ybf") for _ in range(2)]
    b_ybf = [Buf("ybf0"), Buf("ybf1")]
    yxT = Y.t([128, 8, 512], BF16, "yxT")
    sbT = [Y.t([128, 512], BF16, "sbT") for _ in range(2)]
    scT = [Y.t([128, 512], BF16, "scT") for _ in range(2)]
    zpad = Y.t([128, 4, 8, 66], BF16, "zpad")
    tmpf = Y.t([128, 512], F32, "tmpf")
    hx2s = [Y.t([128, 8, 128], BF16, "hx2s") for _ in range(2)]

    b_wtm, b_wfm, b_wout = Buf("wtm"), Buf("wfm"), Buf("wout")
    b_xt = [[Buf(f"xt{s}{j}") for j in range(4)] for s in range(2)]
    b_hxT = [Buf(f"hxT{j}") for j in range(4)]
    b_obl = [Buf("obl0"), Buf("obl1")]
    bg = {n: Buf(n) for n in ("osum", "sg", "ghs", "yg", "junk", "ybf", "tmpf", "alT", "stat")}
    bF = [{n: Buf(n + str(i)) for n in ("e", "l", "eb", "enb", "er", "qt", "kt", "qblk", "kqs")} for i in range(2)]
    bB = [{n: Buf(n + str(i)) for n in ("et", "kh", "vb", "T", "att")} for i in range(2)]
    b_obst = b_obl
    b_yxg = [Buf(f"yxg{j}") for j in range(4)]
    b_yxs = [Buf(f"yxs{j}") for j in range(4)]
    b_sbT = [Buf("sbT0"), Buf("sbT1")]
    b_scT = [Buf("scT0"), Buf("scT1")]
    b_zp = [Buf(f"zp{j}") for j in range(4)]
    b_hx2s = [Buf("hx2s0"), Buf("hx2s1")]
    b_ob_d = [Buf(f"ob_d{i}") for i in range(NT * 4)]
    b_x1_d = [Buf(f"x1_d{i}") for i in range(NT * 4)]
    b_hx2_d = [Buf(f"hx2_d{i}") for i in range(NT)]

    wtm_v = wtm_d.rearrange("(k p) j -> p k j", p=128)
    wfm_v = wfm_d.rearrange("(k p) j -> p k j", p=128)
    wout_v = wout_d.rearrange("(k p) j -> p k j", p=128)
    for k in range(8):
        P.add("pool", lambda e, k=k: e.dma_start(out=wtm[:, k, :], in_=wtm_v[:, k, :]), writes=[b_wtm], dma="wtm", ndma=1)
    for k in range(8):
        P.add("pool", lambda e, k=k: e.dma_start(out=wfm[:, k, :], in_=wfm_v[:, k, :]), writes=[b_wfm], dma="wfm", ndma=1)
    for k in range(8):
        P.add("pool", lambda e, k=k: e.dma_start(out=wout[:, k, :], in_=wout_v[:, k, :]), writes=[b_wout], dma="wout", ndma=1)
    P.add("dve", lambda e: e.memset(zpad[:], 0.0), writes=b_zp)
    for i_ in range(2):
        P.add("dve", lambda e, i_=i_: e.memset(g_qz[i_][:], 0.0), writes=[bF[i_]["qt"]])

    sc_i = [0]

    bst8 = [Buf(f"stat{i}") for i in range(8)]

    def stat_slot():
        k = sc_i[0] % 8
        sc_i[0] += 1
        return 8 * k, bst8[k]

    def rstd_from(ss_ap, nfeat, out_ap, rb, wb):
        P.add("act", lambda e: e.activation(out=out_ap, in_=ss_ap, func=AF.Ln, scale=1.0 / nfeat, bias=epsc[:, 0:1]), reads=rb + [g_const], writes=wb)
        P.add("act", lambda e: e.activation(out=out_ap, in_=out_ap, func=AF.Exp, scale=-0.5), reads=wb, writes=wb)

    fcnt = [0]

    def front_stages(src_ap, src_buf, Arow, Brow, dstT, dst_cols, dst_buf):
        n_ = fcnt[0]
        fcnt[0] += 1
        q_, par = n_ % 4, n_ % 2
        banks = (6, 7) if par == 0 else (2, 5)
        ss = fstat[:, 2 * q_:2 * q_ + 1]
        rs = fstat[:, 2 * q_ + 1:2 * q_ + 2]
        bst = b_fst[q_]
        yb = ybf[par]

        def fa():
            P.add("dve", lambda e: e.memset(fstat[:, 2 * q_:2 * q_ + 2], 0.0), writes=[bst])
            P.add("act", lambda e: e.activation(out=junk[:], in_=src_ap, func=AF.Square, accum_out=ss), reads=[src_buf, bst], writes=[bg["junk"], bst])
            rstd_from(ss, D, rs, [bst], [bst])

        def fb():
            P.add("dve", lambda e: e.scalar_tensor_tensor(out=yb[:], in0=src_ap, scalar=rs, in1=Arow[:], op0=ALU.mult, op1=ALU.mult),
                  reads=[src_buf, bst, g_const], writes=[b_ybf[par]])

        bv = {id(B1): 0, id(cB1): 8, id(B2): 16}[id(Brow)]

        def fc():
            for k in range(8):
                bank = banks[k // 4]
                o = pb[bank][:, (k % 4) * 128:(k % 4 + 1) * 128]
                P.add("pe", lambda e, o=o, k=k: e.matmul(o, lhsT=yb[:, k * 128:(k + 1) * 128], rhs=ident[:], start=True, stop=True),
                      reads=[b_ybf[par], g_const], writes=[PB[bank]])

        def fd():
            for k in range(8):
                bank = banks[k // 4]
                src = pb[bank][:, (k % 4) * 128:(k % 4 + 1) * 128]
                if par == 0:
                    P.add("act", lambda e, k=k, src=src: e.activation(out=dstT[:, k, dst_cols], in_=src, func=AF.Identity, bias=Bc[:, bv + k:bv + k + 1]),
                          reads=[PB[bank], g_const], writes=[dst_buf])
                else:
                    P.add("dve", lambda e, k=k, src=src: e.tensor_scalar(out=dstT[:, k, dst_cols], in0=src, scalar1=Bc[:, bv + k:bv + k + 1], scalar2=None, op0=ALU.add),
                          reads=[PB[bank], g_const], writes=[dst_buf])

        return [fa, fb, fc, fd]

    def front(src_ap, src_buf, Arow, Brow, dstT, dst_cols, dst_buf, banks=None):
        for st_ in front_stages(src_ap, src_buf, Arow, Brow, dstT, dst_cols, dst_buf):
            st_()

    def front4(slot):
        if fcnt[0] % 2:
            fcnt[0] += 1
        S_ = [front_stages(xt[slot][:, j, :], b_xt[slot][j], A1, B1, hxT, slice(j * 128, (j + 1) * 128), b_hxT[j]) for j in range(4)]
        for (j, k) in ((0, 0), (1, 0), (0, 1), (1, 1), (2, 0), (3, 0), (0, 2), (0, 3), (2, 1), (1, 2), (1, 3), (3, 1), (2, 2), (2, 3), (3, 2), (3, 3)):
            S_[j][k]()

    def proj_tm(cols, grp, bank, hbuf):
        for k in range(8):
            P.add("pe", lambda e, k=k: e.matmul(pb[bank][:, :], lhsT=hxT[:, k, cols], rhs=wtm[:, k, grp * 512:(grp + 1) * 512], start=(k == 0), stop=(k == 7)),
                  reads=[hbuf, b_wtm], writes=[PB[bank]])

    def proj_fm(c0, M, bank, ncols=512):
        for k in range(8):
            P.add("pe", lambda e, k=k: e.matmul(pb[bank][0:M, 0:ncols], lhsT=wfm[:, k, c0:c0 + M], rhs=hxT[:, k, 0:ncols], start=(k == 0), stop=(k == 7)),
                  reads=b_hxT + [b_wfm], writes=[PB[bank]])

    def gate_lowrank(ncols=512):
        proj_fm(0, 64, 7, ncols)
        P.add("act", lambda e: e.copy(out=alT[0:64, 0:ncols], in_=pb[7][0:64, 0:ncols]), reads=[PB[7]], writes=[bg["alT"]])

    def gla_stages(d, sl, bs, cols, hbuf, full):
        bK, bV, bA = 3 * sl, 3 * sl + 1, 3 * sl + 2
        F, Bb = bF[sl], bB[bs]
        e_, l_, eb_, enb_, er_, qz_, kt_, kq_ = g_e[sl], g_l[sl], g_eb[sl], g_enb[sl], g_er[sl], g_qz[sl], g_kt[sl], kqs[sl]
        et_, kh_, vb_, T_, att_ = g_et[bs], g_kh[bs], g_vb[bs], g_T[bs], g_att[bs]
        Tps = pb[bV].bitcast(BF16)

        def s1():
            proj_tm(cols, 0, bK, hbuf)
            proj_tm(cols, 1, bV, hbuf)
            P.add("pe", lambda e: e.matmul(pb[bA][:, 0:256], lhsT=alT[0:96, cols], rhs=wgate[:, d * 256:(d + 1) * 256], start=True, stop=True),
                  reads=[bg["alT"], g_const], writes=[PB[bA]])

        def s2():
            P.add("act", lambda e: e.activation(out=e_[:], in_=pb[bA][:, 0:256], func=AF.Exp, scale=-1.0), reads=[PB[bA]], writes=[F["er"]])
            P.add("act", lambda e: e.activation(out=l_[:], in_=e_[:], func=AF.Ln, bias=1.0), reads=[F["er"]], writes=[F["l"]])
            P.add("act", lambda e: e.copy(out=kq_[:], in_=pb[bK][:, :]), reads=[PB[bK]], writes=[F["kqs"]])
            P.add("act", lambda e: e.copy(out=vb_[:], in_=pb[bV][:, :]), reads=[PB[bV]], writes=[Bb["vb"]])

        def s3():
            if full:
                P.add("pe", lambda e: e.matmul(pb[bK][:, 0:256], lhsT=Lm[:, 2 * d, :], rhs=l_[:], start=True, stop=True), reads=[F["l"], g_const], writes=[PB[bK]])
            P.add("pe", lambda e: e.matmul(pb[bK][:, 256:512], lhsT=Lm[:, 2 * d + 1, :], rhs=l_[:], start=True, stop=True), reads=[F["l"], g_const], writes=[PB[bK]])
            for p_ in range(2):
                P.add("pe", lambda e, p_=p_: e.matmul(pb[bA][:, 256 + p_:257 + p_], lhsT=l_[:, p_ * 128:(p_ + 1) * 128], rhs=ncol[:, 0:1], start=True, stop=True),
                      reads=[F["l"], g_const], writes=[PB[bA]])

        def s4():
            if full:
                P.add("act", lambda e: e.activation(out=eb_[:], in_=pb[bK][:, 0:256], func=AF.Exp), reads=[PB[bK]], writes=[F["eb"]])
                P.add("act", lambda e: e.activation(out=enb_[:], in_=pb[bK][:, 0:256], func=AF.Exp, scale=-1.0), reads=[PB[bK]], writes=[F["enb"]])
            P.add("act", lambda e: e.activation(out=er_[:], in_=pb[bK][:, 256:512], func=AF.Exp), reads=[PB[bK]], writes=[F["er"]])
            P.add("act", lambda e: e.activation(out=et_[:], in_=pb[bA][:, 256:258], func=AF.Exp), reads=[PB[bA]], writes=[Bb["et"]])

        def s5():
            if full:
                kq3 = kq_[:, 256:512].rearrange("p (a b) -> p a b", a=2)
                eb3 = eb_[:].rearrange("p (a b) -> p a b", a=2)
                for w in range(2):
                    P.add("dve", lambda e, w=w: e.scalar_tensor_tensor(out=qz_[:, :, w * 192:w * 192 + 64], in0=kq3[:, :, w * 64:(w + 1) * 64], scalar=0.125,
                                                                       in1=eb3[:, :, w * 64:(w + 1) * 64], op0=ALU.mult, op1=ALU.mult),
                          reads=[F["kqs"], F["eb"]], writes=[F["qt"]])
                P.add("dve", lambda e: e.tensor_tensor(out=kt_[:], in0=kq_[:, 0:256], in1=enb_[:], op=ALU.mult), reads=[F["kqs"], F["enb"]], writes=[F["kt"]])
            P.add("dve", lambda e: e.tensor_tensor(out=kh_[:], in0=kq_[:, 0:256], in1=er_[:], op=ALU.mult), reads=[F["kqs"], F["er"]], writes=[Bb["kh"]])

        def s6():
            if not full:
                return
            for j in range(4):
                P.add("pe", lambda e, j=j: e.transpose(Tps[:, j * 128:(j + 1) * 128], qz_[:, j // 2, (j % 2) * 128:(j % 2 + 1) * 128], ident[:]),
                      reads=[F["qt"], g_const], writes=[PB[bV]])
            for j in range(2):
                P.add("pe", lambda e, j=j: e.transpose(Tps[:, 512 + j * 128:512 + (j + 1) * 128], kt_[:, j * 128:(j + 1) * 128], ident[:]),
                      reads=[F["kt"], g_const], writes=[PB[bV]])

        def s7():
            if not full:
                return
            P.add("dve", lambda e: e.tensor_copy(out=T_[:], in_=Tps[:, 0:768]), reads=[PB[bV]], writes=[Bb["T"]])

        def s8():
            if not full:
                return
            for pr in range(2):
                P.add("pe", lambda e, pr=pr: e.matmul(pb[bA][:, pr * 256:(pr + 1) * 256], lhsT=T_[:, 512 + pr * 128:512 + (pr + 1) * 128], rhs=T_[:, pr * 256:(pr + 1) * 256],
                                                      start=True, stop=True), reads=[Bb["T"]], writes=[PB[bA]])

        def s9():
            if not full:
                return
            P.add("dve", lambda e: e.tensor_tensor(out=att_[:], in0=pb[bA][:, :], in1=maskT[:, d, :], op=ALU.mult), reads=[PB[bA], g_const], writes=[Bb["att"]])

        return [s1, s2, s3, s4, s5, s6, s7, s8, s9]

    def gla_back(d, bs, full):
        Bb = bB[bs]
        et_, kh_, vb_, T_, att_ = g_et[bs], g_kh[bs], g_vb[bs], g_T[bs], g_att[bs]
        if full:
            for h in range(4):
                pr = h // 2
                P.add("pe", lambda e, h=h, pr=pr: e.matmul(pb[6][:, h * 128:(h + 1) * 128], lhsT=T_[:, h * 128:(h + 1) * 128], rhs=Sb[d][:, pr, :],
                                                           start=True, stop=False), reads=[Bb["T"], bSb[d]], writes=[PB[6]])
                P.add("pe", lambda e, h=h: e.matmul(pb[6][:, h * 128:(h + 1) * 128], lhsT=att_[:, h * 128:(h + 1) * 128], rhs=vb_[:, h * 128:(h + 1) * 128],
                                                    start=False, stop=True), reads=[Bb["att"], Bb["vb"]], writes=[PB[6]])
        for pr in range(2):
            P.add("pe", lambda e, pr=pr: e.matmul(pb[7][:, pr * 256:(pr + 1) * 256], lhsT=kh_[:, pr * 128:(pr + 1) * 128], rhs=vb_[:, pr * 256:(pr + 1) * 256],
                                                  start=True, stop=True), reads=[Bb["kh"], Bb["vb"]], writes=[PB[7]])
        for pr in range(2):
            for w in range(2):
                rsl = slice(w * 64, (w + 1) * 64)
                c0 = pr * 256 + w * 128
                P.add("dve", lambda e, pr=pr, rsl=rsl, c0=c0: e.scalar_tensor_tensor(out=S[d][rsl, pr, :], in0=S[d][rsl, pr, :], scalar=et_[rsl, pr:pr + 1],
                                                                                      in1=pb[7][rsl, c0:c0 + 128], op0=ALU.mult, op1=ALU.add),
                      reads=[bS[d], Bb["et"], PB[7]], writes=[bS[d]])
        P.add("act", lambda e: e.copy(out=Sb[d][:], in_=S[d][:]), reads=[bS[d]], writes=[bSb[d]])

    def gla_tile(d, order, after_back, mid=None):
        st = {}
        for n_, j in enumerate(order):
            st[j] = gla_stages(d, n_ % 2, n_ % 2, slice(j * 128, (j + 1) * 128), b_hxT[j], True)
        c0_, c1_, c2_, c3_ = order
        for k in range(9):
            st[c0_][k]()
            st[c1_][k]()
        if mid is None:
            st[c2_][0]()
            st[c3_][0]()
        for n_, j in enumerate((c0_, c1_)):
            gla_back(d, n_, True)
            if mid is not None:
                mid(j)
            after_back(j)
        for k in range(0 if mid is not None else 1, 9):
            st[c2_][k]()
            st[c3_][k]()
        for n_, j in enumerate((c2_, c3_)):
            gla_back(d, n_, True)
            if mid is not None:
                mid(j)
            after_back(j)

    ctx_v = ctx_d.rearrange("(j p) f -> p j f", p=128)
    for j in range(2):
        P.add("sp", lambda e, j=j: e.dma_start(out=xt[0][:, j, :], in_=ctx_v[:, j, :]), writes=[b_xt[0][j]], dma=f"x0{j}")
    for j in range(2):
        front(xt[0][:, j, :], b_xt[0][j], cA1, cB1, hxT, slice(j * 128, (j + 1) * 128), b_hxT[j])
    gate_lowrank(256)
    for d_, order in ((0, (0, 1)), (1, (1, 0))):
        for j in order:
            cols = slice(j * 128, (j + 1) * 128)
            for st_ in gla_stages(d_, 0, 0, cols, b_hxT[j], False):
                st_()
            gla_back(d_, 0, False)

    if stage == 1:
        return finish()
    x_v = x_d.rearrange("(n j p) f -> n p j f", p=128, j=4)
    ob_v = ob_d.rearrange("(n p) f -> n p f", p=128)
    x1_v = x1_d.rearrange("(n p) f -> n p f", p=128)

    def load_x(i, slot):
        for j in range(4):
            P.add("sp", lambda e, j=j: e.dma_start(out=xt[slot][:, j, :], in_=x_v[i, :, j, :]), writes=[b_xt[slot][j]], dma=f"x{slot}{j}")

    tseq = [i for i in range(NT - 1, -1, -1)] + [i for i in range(NT)]
    tcount = 0
    load_x(tseq[0], 0)
    for i in range(NT - 1, -1, -1):
        slot = tcount % 2
        tcount += 1
        if tcount < len(tseq) and stage >= 3 or tcount < NT:
            load_x(tseq[tcount], tcount % 2)
        front4(slot)
        gate_lowrank()
        def after_a(j, i=i):
            n = i * 4 + j
            P.add("act", lambda e, j=j: e.copy(out=obst[j % 2][:], in_=pb[6][:, :]), reads=[PB[6]], writes=[b_obst[j % 2]])
            P.add("sp", lambda e, j=j, n=n: e.dma_start(out=ob_v[n, :, :], in_=obst[j % 2][:]), reads=[b_obst[j % 2]], writes=[b_ob_d[n]], dma=f"obst{j % 2}")

        gla_tile(1, (3, 2, 1, 0), after_a)

    if stage == 2:
        return finish()
    hx2_v = hx2_d.rearrange("k p t -> p k t")
    for i in range(NT):
        slot = tcount % 2
        tcount += 1
        if tcount < len(tseq):
            load_x(tseq[tcount], tcount % 2)
        front4(slot)
        gate_lowrank()
        def sc_step(j):
            s2 = j % 2
            proj_fm(64 + j * 128, 128, 0)
            P.add("act", lambda e, s2=s2: e.copy(out=sbT[s2][:], in_=pb[0][:, :]), reads=[PB[0]], writes=[b_sbT[s2]])
            proj_fm(64 + 512 + j * 128, 128, 1)
            P.add("act", lambda e, s2=s2: e.copy(out=scT[s2][:], in_=pb[1][:, :]), reads=[PB[1]], writes=[b_scT[s2]])
            proj_fm(64 + 1024 + j * 128, 128, 2)
            P.add("dve", lambda e, j=j, s2=s2: e.tensor_tensor(out=zpad[:, j, :, 1:65], in0=pb[2][:, :].rearrange("p (r c) -> p r c", r=8),
                                                               in1=scT[s2][:].rearrange("p (r c) -> p r c", r=8), op=ALU.mult),
                  reads=[PB[2], b_scT[s2]], writes=[b_zp[j]])
            for t in range(3):
                P.add("pe", lambda e, j=j, t=t: e.matmul(pb[3][:, :], lhsT=dsc[:, t * 4 + j, :], rhs=zpad[:, j, :, t:t + 64], start=(t == 0), stop=(t == 2)),
                      reads=[b_zp[j], g_const], writes=[PB[3]])
            P.add("dve", lambda e, j=j, s2=s2: e.scalar_tensor_tensor(out=yxT[:, 4 + j, :], in0=pb[3][:, :], scalar=colp[:, Q_BSC + j:Q_BSC + j + 1], in1=sbT[s2][:],
                                                                      op0=ALU.add, op1=ALU.mult), reads=[PB[3], b_sbT[s2], g_const], writes=[b_yxs[j]])
        sgt = (tmpf, osum)
        sgb = (bg["tmpf"], bg["osum"])
        for j in range(4):
            cols = slice(j * 128, (j + 1) * 128)
            proj_tm(cols, 2, j, b_hxT[j])
            P.add("act", lambda e, j=j: e.activation(out=sgt[j % 2][:], in_=pb[j][:, :], func=AF.Silu), reads=[PB[j]], writes=[sgb[j % 2]])
            P.add("dve", lambda e, j=j: e.tensor_tensor(out=ghs4[:, j, :], in0=sgt[j % 2][:], in1=ghead[:], op=ALU.mult), reads=[sgb[j % 2], g_const], writes=[b_ghs[j]])

        def load_ob(j, i=i):
            n = i * 4 + j
            P.add("sp", lambda e, j=j, n=n: e.dma_start(out=obl[j % 2][:], in_=ob_v[n, :, :]), reads=[b_ob_d[n]], writes=[b_obl[j % 2]], dma=f"obl{j % 2}")

        load_ob(0)
        load_ob(1)

        def after_b(j, i=i):
            cols = slice(j * 128, (j + 1) * 128)
            P.add("dve", lambda e, j=j: e.tensor_tensor(out=osum[:], in0=pb[6][:, :], in1=obl[j % 2][:], op=ALU.add), reads=[PB[6], b_obl[j % 2]], writes=[bg["osum"]])
            if j < 2:
                load_ob(j + 2)
            c0, bs_ = stat_slot()
            P.add("dve", lambda e, c0=c0: e.memset(stat[:, c0:c0 + 8], 0.0), writes=[bs_])
            for h in range(4):
                P.add("act", lambda e, h=h, c0=c0: e.activation(out=junk[:, h * 128:(h + 1) * 128], in_=osum[:, h * 128:(h + 1) * 128], func=AF.Square,
                                                                accum_out=stat[:, c0 + h:c0 + h + 1]), reads=[bg["osum"], bs_], writes=[bg["junk"], bs_])
            rstd_from(stat[:, c0:c0 + 4], 128, stat[:, c0 + 4:c0 + 8], [bs_], [bs_])
            for h in range(4):
                P.add("dve", lambda e, h=h, c0=c0, j=j: e.scalar_tensor_tensor(out=yg[:, h * 128:(h + 1) * 128], in0=osum[:, h * 128:(h + 1) * 128],
                                                                               scalar=stat[:, c0 + 4 + h:c0 + 5 + h], in1=ghs4[:, j, h * 128:(h + 1) * 128],
                                                                               op0=ALU.mult, op1=ALU.mult), reads=[bg["osum"], bs_, b_ghs[j]], writes=[bg["yg"]])
            Tps = pb[7].bitcast(BF16)
            for h in range(4):
                P.add("pe", lambda e, h=h: e.transpose(Tps[:, h * 128:(h + 1) * 128], yg[:, h * 128:(h + 1) * 128], ident[:]), reads=[bg["yg"], g_const], writes=[PB[7]])
            P.add("act", lambda e, cols=cols: e.copy(out=yxT[:, 0:4, cols], in_=Tps[:, 0:512].rearrange("p (k t) -> p k t", k=4)), reads=[PB[7]], writes=[b_yxg[j]])

        gla_tile(0, (0, 1, 2, 3), after_b, mid=sc_step)
        def oproj_stages(j, i=i, slot=slot):
            cols = slice(j * 128, (j + 1) * 128)
            n = i * 4 + j
            ob_ = 3 * (j % 2)
            c0, bs_ = stat_slot()
            tmp2 = (tmpf, osum)
            tmpb = (bg["tmpf"], bg["osum"])

            def o1():
                for hf in range(2):
                    for k in range(8):
                        rb = [b_yxg[j]] if k < 4 else [b_yxs[k - 4]]
                        P.add("pe", lambda e, k=k, hf=hf: e.matmul(pb[ob_ + hf][:, :], lhsT=yxT[:, k, cols], rhs=wout[:, k, hf * 512:(hf + 1) * 512],
                                                                   start=(k == 0), stop=(k == 7)), reads=rb + [b_wout], writes=[PB[ob_ + hf]])

            def o2():
                P.add("dve", lambda e: e.memset(stat[:, c0:c0 + 4], 0.0), writes=[bs_])
                for hf in range(2):
                    P.add("act", lambda e, hf=hf: e.activation(out=junk[:, hf * 512:(hf + 1) * 512], in_=pb[ob_ + hf][:, :], func=AF.Square,
                                                               accum_out=stat[:, c0 + hf:c0 + hf + 1]), reads=[PB[ob_ + hf], bs_], writes=[bg["junk"], bs_])
                P.add("dve", lambda e: e.tensor_tensor(out=stat[:, c0 + 2:c0 + 3], in0=stat[:, c0:c0 + 1], in1=stat[:, c0 + 1:c0 + 2], op=ALU.add),
                      reads=[bs_], writes=[bs_])
                rstd_from(stat[:, c0 + 2:c0 + 3], D, stat[:, c0 + 3:c0 + 4], [bs_], [bs_])

            def o3():
                for hf in range(2):
                    hs = slice(hf * 512, (hf + 1) * 512)
                    P.add("dve", lambda e, hf=hf, hs=hs: e.scalar_tensor_tensor(out=tmp2[hf][:], in0=pb[ob_ + hf][:, :], scalar=stat[:, c0 + 3:c0 + 4], in1=G1[:, hs],
                                                                               op0=ALU.mult, op1=ALU.mult), reads=[PB[ob_ + hf], bs_, g_const], writes=[tmpb[hf]])
                    P.add("dve", lambda e, hf=hf, hs=hs: e.tensor_tensor(out=xt[slot][:, j, hs], in0=xt[slot][:, j, hs], in1=tmp2[hf][:], op=ALU.add),
                          reads=[tmpb[hf], b_xt[slot][j]], writes=[b_xt[slot][j]])
                P.add("sp", lambda e: e.dma_start(out=x1_v[n, :, :], in_=xt[slot][:, j, :]), reads=[b_xt[slot][j]], writes=[b_x1_d[n]], dma=f"xs{slot}{j}")

            fs = front_stages(xt[slot][:, j, :], b_xt[slot][j], A2, B2, hx2s[j % 2], slice(0, 128), b_hx2s[j % 2])

            def o7():
                fs[3]()
                P.add("sp", lambda e: e.dma_start(out=hx2_v[:, :, 64 + i * 512 + j * 128:64 + i * 512 + (j + 1) * 128], in_=hx2s[j % 2][:]),
                      reads=[b_hx2s[j % 2]], writes=[b_hx2_d[i]], dma=f"hx2s{j % 2}")

            return [o1, o2, o3, fs[0], fs[1], fs[2], o7]

        if fcnt[0] % 2:
            fcnt[0] += 1
        OS = [oproj_stages(j) for j in range(4)]
        order = [(0, 0), (1, 0), (0, 1), (0, 2), (0, 3), (1, 1), (2, 0), (1, 2), (0, 4), (1, 3), (0, 5), (2, 1), (3, 0), (0, 6), (2, 2), (1, 4), (2, 3), (1, 5),
                 (3, 1), (1, 6), (3, 2), (2, 4), (3, 3), (2, 5), (2, 6), (3, 4), (3, 5), (3, 6)]
        assert sorted(order) == [(j, k) for j in range(4) for k in range(7)]
        for (j, k) in order:
            OS[j][k]()
    P.barrier()
    if stage == 3:
        return finish()

    Zc = Alloc(nc, OV, SB_HI)
    wdn = Zc.t([128, NFC, D], BF16, "wdn")
    wupS = [Zc.t([128, 8, 512], BF16, "wupS") for _ in range(2)]
    hw = [Zc.t([128, 8, 640], BF16, "hw") for _ in range(2)]
    x1t = [Zc.t([128, 4, D], F32, "x1t") for _ in range(2)]
    hT = Zc.t([128, NFC, 512], BF16, "hT")
    upad = [Zc.t([128, 10, 66], BF16, "upad") for _ in range(2)]
    gts = [Zc.t([128, 512], BF16, "gts") for _ in range(2)]
    sil = [Zc.t([128, 512], BF16, "sil") for _ in range(2)]
    dg = [Zc.t([128, 9, 128], BF16, "dg") for _ in range(2)]
    tmpc = Zc.t([128, 512], F32, "tmpc")
    junkc = Zc.t([128, 512], BF16, "junkc")
    b_wdn = Buf("wdn")
    b_wupS = [Buf("wupS0"), Buf("wupS1")]
    b_hw = [Buf("hw0"), Buf("hw1")]
    b_x1t = [[Buf(f"x1t{s}{j}") for j in range(4)] for s in range(2)]
    b_hT = [Buf(f"hT{c}") for c in range(NFC)]
    b_upad = [Buf("upad0"), Buf("upad1")]
    b_upc = [Buf("upc0"), Buf("upc1")]
    ucar = Zc.t([128, NFC, 2, 64], BF16, "ucar")
    b_ucar = [Buf(f"ucar{c}") for c in range(NFC)]
    b_gts = [Buf("gts0"), Buf("gts1")]
    b_sil = [Buf("sil0"), Buf("sil1")]
    b_dg = [Buf("dg0"), Buf("dg1")]
    b_tmpc, b_junkc, b_statc = Buf("tmpc"), Buf("junkc"), Buf("statc")
    b_out = []

    wdn_v = wdown_d.rearrange("(c p) j -> p c j", p=128)
    for c in range(NFC):
        P.add("pool", lambda e, c=c: e.dma_start(out=wdn[:, c, :], in_=wdn_v[:, c, :]), writes=[b_wdn], dma="wdn", ndma=1)
    for s in range(2):
        P.add("dve", lambda e, s=s: e.memset(upad[s][:], 0.0), writes=[b_upad[s]])
    out_v = out_d.rearrange("(n p) f -> n p f", p=128)
    x1_v4 = x1_d.rearrange("(n j p) f -> n p j f", p=128, j=4)
    def load_c(i):
        slot = i % 2
        P.add("sp", lambda e: e.dma_start(out=hw[slot][:], in_=hx2_v[:, :, i * 512:i * 512 + 640]),
              reads=b_hx2_d[max(0, i - 1):i + 2] + [b_hx2], writes=[b_hw[slot]], dma=f"hw{slot}")
        for j in range(4):
            P.add("sp", lambda e, j=j: e.dma_start(out=x1t[slot][:, j, :], in_=x1_v4[i, :, j, :]), reads=[b_x1_d[i * 4 + j]],
                  writes=[b_x1t[slot][j]], dma=f"x1t{slot}{j}")

    NPC = NFC // 2

    def load_piece(g):
        ps = g % 2
        pc = g % NPC
        P.add("sp", lambda e: e.dma_start(out=wupS[ps][:], in_=wupb_d[pc, :, :, :]), reads=[b_wupb], writes=[b_wupS[ps]], dma=f"wupS{ps}")

    def conv_part(c):
        s2 = c % 2
        for t in range(9):
            dr, dc = t // 3, t % 3
            P.add("pe", lambda e, t=t, dr=dr, dc=dc: e.matmul(pb[6 + s2][:, :], lhsT=dg[s2][:, t, :], rhs=upad[s2][:, dr:dr + 8, dc:dc + 64], start=(t == 0), stop=(t == 8)),
                  reads=[b_dg[s2], b_upad[s2], b_upc[s2]], writes=[PB[6 + s2]])
        P.add("act", lambda e: e.activation(out=sil[s2][:], in_=pb[6 + s2][:, :], func=AF.Silu, bias=colp[:, Q_BCF + c:Q_BCF + c + 1]),
              reads=[PB[6 + s2], g_const], writes=[b_sil[s2]])
        P.add("dve", lambda e: e.tensor_tensor(out=hT[:, c, :], in0=sil[s2][:], in1=gts[s2][:], op=ALU.mult), reads=[b_sil[s2], b_gts[s2]], writes=[b_hT[c]])

    load_c(0)
    load_piece(0)
    piece = 0
    for i in range(NT):
        slot = i % 2
        for c in range(NFC):
            s2 = c % 2
            if c % 2 == 0:
                ps = piece % 2
                piece += 1
                if piece < NT * NPC:
                    load_piece(piece)
                if c == 2 and i + 1 < NT:
                    load_c(i + 1)
            wo = (c % 2) * 128
            for t in range(9):
                P.add("dve", lambda e, c=c, t=t, s2=s2: e.tensor_scalar(out=dg[s2][:, t, :], in0=ident[:], scalar1=colp[:, Q_WCF + c * 9 + t:Q_WCF + c * 9 + t + 1],
                                                                         scalar2=None, op0=ALU.mult), reads=[g_const], writes=[b_dg[s2]])
            ucol = 0 if i == 0 else 128
            for k in range(8):
                P.add("pe", lambda e, k=k, s2=s2, ps=ps, wo=wo, slot=slot, ucol=ucol: e.matmul(pb[s2][:, :], lhsT=wupS[ps][:, k, wo:wo + 128], rhs=hw[slot][:, k, ucol:ucol + 512],
                                                                                           start=(k == 0), stop=(k == 7)),
                      reads=[b_wupS[ps], b_hw[slot]], writes=[PB[s2]])
            if i == 0:
                for k in range(8):
                    P.add("pe", lambda e, k=k, s2=s2, ps=ps, wo=wo, slot=slot: e.matmul(pb[2 + s2][:, 0:128], lhsT=wupS[ps][:, k, wo:wo + 128], rhs=hw[slot][:, k, 512:640], start=(k == 0), stop=(k == 7)),
                          reads=[b_wupS[ps], b_hw[slot]], writes=[PB[2 + s2]])
            for k in range(8):
                P.add("pe", lambda e, k=k, s2=s2, ps=ps, wo=wo, slot=slot: e.matmul(pb[4 + s2][:, :], lhsT=wupS[ps][:, k, 256 + wo:256 + wo + 128], rhs=hw[slot][:, k, 64:576], start=(k == 0), stop=(k == 7)),
                      reads=[b_wupS[ps], b_hw[slot]], writes=[PB[4 + s2]])
            if i == 0:
                P.add("act", lambda e, s2=s2: e.copy(out=upad[s2][:, 0:8, 1:65], in_=pb[s2][:, :].rearrange("p (r c) -> p r c", r=8)), reads=[PB[s2]], writes=[b_upc[s2], b_upad[s2]])
                P.add("act", lambda e, s2=s2: e.copy(out=upad[s2][:, 8:10, 1:65], in_=pb[2 + s2][:, 0:128].rearrange("p (r c) -> p r c", r=2)), reads=[PB[2 + s2]], writes=[b_upad[s2]])
            else:
                P.add("pool", lambda e, c=c, s2=s2: e.tensor_copy(out=upad[s2][:, 0:2, 1:65], in_=ucar[:, c, :, :]), reads=[b_ucar[c]], writes=[b_upc[s2]])
                P.add("act", lambda e, s2=s2: e.copy(out=upad[s2][:, 2:10, 1:65], in_=pb[s2][:, :].rearrange("p (r c) -> p r c", r=8)), reads=[PB[s2]], writes=[b_upad[s2]])
            if i < NT - 1:
                P.add("pool", lambda e, c=c, s2=s2: e.tensor_copy(out=ucar[:, c, :, :], in_=upad[s2][:, 8:10, 1:65]), reads=[b_upad[s2]], writes=[b_ucar[c]])
            P.add("dve", lambda e, s2=s2: e.tensor_copy(out=gts[s2][:], in_=pb[4 + s2][:, :]), reads=[PB[4 + s2]], writes=[b_gts[s2]])
            if c >= 1:
                conv_part(c - 1)
        conv_part(NFC - 1)
        for j in range(4):
            cols = slice(j * 128, (j + 1) * 128)
            n = i * 4 + j
            for hf in range(2):
                bank = (2 * j + hf) % 8
                for c in range(NFC):
                    P.add("pe", lambda e, c=c, hf=hf, bank=bank, cols=cols: e.matmul(pb[bank][:, :], lhsT=hT[:, c, cols], rhs=wdn[:, c, hf * 512:(hf + 1) * 512],
                                                                                      start=(c == 0), stop=(c == NFC - 1)), reads=[b_hT[c], b_wdn], writes=[PB[bank]])
            c0, _unused = stat_slot()
            P.add("dve", lambda e, c0=c0: e.memset(stat[:, c0:c0 + 4], 0.0), writes=[b_statc])
            for hf in range(2):
                bank = (2 * j + hf) % 8
                P.add("act", lambda e, hf=hf, bank=bank, c0=c0: e.activation(out=junkc[:], in_=pb[bank][:, :], func=AF.Square, accum_out=stat[:, c0 + hf:c0 + hf + 1]),
                      reads=[PB[bank], b_statc], writes=[b_junkc, b_statc])
            P.add("dve", lambda e, c0=c0: e.tensor_tensor(out=stat[:, c0 + 2:c0 + 3], in0=stat[:, c0:c0 + 1], in1=stat[:, c0 + 1:c0 + 2], op=ALU.add),
                  reads=[b_statc], writes=[b_statc])
            rstd_from(stat[:, c0 + 2:c0 + 3], D, stat[:, c0 + 3:c0 + 4], [b_statc], [b_statc])
            for hf in range(2):
                bank = (2 * j + hf) % 8
                hs = slice(hf * 512, (hf + 1) * 512)
                P.add("dve", lambda e, hf=hf, bank=bank, hs=hs, c0=c0: e.scalar_tensor_tensor(out=tmpc[:], in0=pb[bank][:, :], scalar=stat[:, c0 + 3:c0 + 4], in1=G2[:, hs],
                                                                                             op0=ALU.mult, op1=ALU.mult), reads=[PB[bank], b_statc, g_const], writes=[b_tmpc])
                P.add("dve", lambda e, j=j, hs=hs, slot=slot: e.tensor_tensor(out=x1t[slot][:, j, hs], in0=x1t[slot][:, j, hs], in1=tmpc[:], op=ALU.add),
                      reads=[b_tmpc, b_x1t[slot][j]], writes=[b_x1t[slot][j]])
            bo = Buf(f"out{n}")
            tok = P.add("sp", lambda e, j=j, n=n, slot=slot: e.dma_start(out=out_v[n, :, :], in_=x1t[slot][:, j, :]), reads=[b_x1t[slot][j]], writes=[bo], dma=f"ot{slot}{j}")
            b_out.append(tok)
    last = {}
    for t in b_out:
        last[t[0]] = max(last.get(t[0], 0), t[1])
    fin = list(last.items())
    if debug:
        for k, v in P.dma_cum.items():
            fin.append(("dma:" + k, v))
    P.final_wait("sp", fin)
    P.emit(nc)
    return nc


def _consts():
    c = np.zeros((128, NCST), np.float32)
    m = np.arange(128)[:, None]
    t = np.arange(128)[None, :]
    c[:, K_ID:K_ID + 128] = (m == t)
    c[:, K_LFI:K_LFI + 128] = (m <= t) * (-1.0 / 16)
    c[:, K_LFR:K_LFR + 128] = (m > t) * (-1.0 / 16)
    c[:, K_LBI:K_LBI + 128] = (m >= t) * (-1.0 / 16)
    c[:, K_LBR:K_LBR + 128] = (m < t) * (-1.0 / 16)
    c[:, K_MF:K_MF + 128] = (m <= t)
    c[:, K_MB:K_MB + 128] = (m > t)
    c[:, K_NC] = -1.0 / 16
    return c


def _colmaj(v, n):
    return np.ascontiguousarray(np.asarray(v, np.float32).reshape(n, 128).T)


def make_in_maps(inputs, NT=16, ncores=8):
    f = lambda a: np.asarray(a, np.float32)
    w_in = f(inputs["w_in"])[0]
    w_tm = np.ascontiguousarray(np.concatenate([w_in[:, C_K:C_V], w_in[:, C_Q:C_OG], w_in[:, C_V:C_AF], w_in[:, C_OG:C_SB]], axis=1))
    gate = np.zeros((D, 64), np.float32)
    gate[:, 0:16] = w_in[:, C_AF:C_AB]
    gate[:, 32:48] = w_in[:, C_AB:C_Q]
    w_fm = np.ascontiguousarray(np.concatenate([gate, w_in[:, C_SB:C_SC], w_in[:, C_SC:C_SX], w_in[:, C_SX:]], axis=1))
    wg = np.zeros((96, 512), np.float32)
    wg[0:16, 0:256] = f(inputs["w_af"])[0]
    wg[32:48, 256:512] = f(inputs["w_ab"])[0]
    wg[64, 0:256] = f(inputs["b_af"])[0]
    wg[64, 256:512] = f(inputs["b_ab"])[0]
    rows1 = np.concatenate([f(inputs["g_pre_mix"])[0], f(inputs["g_post_mix"])[0], f(inputs["g_pre_ffn"])[0], f(inputs["g_post_ffn"])[0],
                            np.tile(f(inputs["g_head"])[0], 4), f(inputs["b_ada"])[0]])
    rows = np.ascontiguousarray(np.broadcast_to(rows1[None, :], (128, NROW)))
    w_sc = f(inputs["w_sc"])[0]
    w_cf = f(inputs["w_cf"])[0].reshape(9, DFF)
    cols_common = np.concatenate([
        _colmaj(f(inputs["b_sc"])[0], 4), _colmaj(f(inputs["b_cf"])[0], NFC),
        np.concatenate([_colmaj(w_sc[t], 4) for t in range(3)], axis=1),
        np.stack([_colmaj(w_cf[t], NFC) for t in range(9)], axis=2).reshape(128, NFC * 9),
    ], axis=1)
    cst = _consts()
    maps = []
    x = inputs["x"]
    for b in range(ncores):
        cols = np.ascontiguousarray(np.concatenate([_colmaj(f(inputs["c"])[b], 8), _colmaj(f(inputs["c_ctx"]), 8), cols_common], axis=1))
        maps.append({
            "x": np.ascontiguousarray(f(x[b])[:NT * 512]), "ctx": np.ascontiguousarray(f(inputs["ctx"][b])),
            "rows": rows, "cols": cols, "cst": cst,
            "w_ada": f(inputs["w_ada"])[0], "w_tm": w_tm, "w_fm": w_fm, "w_gate": wg,
            "w_out": f(inputs["w_out"])[0], "w_up": f(inputs["w_up"])[0], "w_down": f(inputs["w_down"])[0],
        })
    return maps


_NC_CACHE = {}


def kernel(**inputs):
    NT = 16
    if NT not in _NC_CACHE:
        _NC_CACHE[NT] = build(NT)
    nc = _NC_CACHE[NT]
    maps = make_in_maps(inputs, NT, 8)
    res = run_bass_kernel_spmd(nc, maps, core_ids=list(range(8)))
    return np.stack([np.asarray(r["out"], np.float32).reshape(NT * 512, D) for r in res.results], axis=0)
```

```python
import os
import numpy as np
import concourse.bass as bass
import concourse.mybir as mybir
from concourse.bass_utils import run_bass_kernel_spmd

F32 = mybir.dt.float32
BF16 = mybir.dt.bfloat16
AF = mybir.ActivationFunctionType
ALU = mybir.AluOpType

D = 1024
DFF = 2816
NFC = 22
CTX = 256
EPS = 1e-6
SB_LO = 16512
SB_HI = 229344
EPOCH = 24000
STRICT_SYNC = False

C_K, C_V, C_AF, C_AB, C_Q, C_OG, C_SB, C_SC, C_SX = 0, 256, 768, 784, 800, 1056, 1568, 2080, 2592

K_ID, K_LFI, K_LFR, K_LBI, K_LBR, K_MF, K_MB, K_NC = 0, 128, 256, 384, 512, 640, 768, 896
NCST = 897
R_GPM, R_GQM, R_GPF, R_GQF, R_GH, R_BADA = 0, 1024, 2048, 3072, 4096, 4608
NROW = 4608 + 6144
Q_C, Q_CC, Q_BSC, Q_BCF, Q_WSC, Q_WCF = 0, 8, 16, 20, 42, 54
NCOL = 54 + 198


class Buf:
    __slots__ = ("name", "w", "r")

    def __init__(self, name):
        self.name = name
        self.w = None
        self.r = {}


class Prog:
    ENGS = ("pe", "act", "dve", "pool", "sp")

    def __init__(self):
        self.ops = {e: [] for e in self.ENGS}
        self.waited = {e: {} for e in self.ENGS}
        self.dma_cum = {}
        self.n = 0

    def _need(self, eng, tok, waits):
        key, val = tok
        if self.waited[eng].get(key, -1) >= val:
            return
        self.waited[eng][key] = val
        waits.append(tok)
        if not key.startswith("dma:"):
            self.ops[key][val]["sig"] = True

    def add(self, eng, fn, reads=(), writes=(), dma=None, ndma=1):
        idx = len(self.ops[eng])
        waits = []
        mykey = ("dma:" + dma) if dma is not None else eng
        for b in reads:
            if b.w is not None:
                if b.w[0] == eng and eng == "pe" and dma is None:
                    continue
                self._need(eng, b.w, waits)
        relax = (eng == "pe" or dma is not None) if STRICT_SYNC else True
        for b in writes:
            if b.w is not None and not (b.w[0] == mykey and relax):
                self._need(eng, b.w, waits)
            for k, v in b.r.items():
                if k == mykey and relax:
                    continue
                self._need(eng, (k, v), waits)
        if dma is not None:
            cum = self.dma_cum.get(dma, 0) + 16 * ndma
            self.dma_cum[dma] = cum
            tok = ("dma:" + dma, cum)
        else:
            tok = (eng, idx)
        for b in reads:
            if b.r.get(tok[0], -1) < tok[1]:
                b.r[tok[0]] = tok[1]
        for b in writes:
            b.w = tok
            b.r = {}
        self.ops[eng].append({"fn": fn, "waits": waits, "sig": False, "dma": dma})
        self.n += 1
        return tok

    def barrier(self, skip=()):
        toks = []
        for e in self.ENGS:
            if self.ops[e]:
                for i in range(len(self.ops[e]) - 1, -1, -1):
                    if self.ops[e][i]["dma"] is None and self.ops[e][i]["fn"] is not None:
                        toks.append((e, i))
                        break
        for k, v in self.dma_cum.items():
            if k in skip:
                continue
            toks.append(("dma:" + k, v))
        for e in self.ENGS:
            waits = []
            for t in toks:
                if t[0] == e:
                    continue
                self._need(e, t, waits)
            self.ops[e].append({"fn": None, "waits": waits, "sig": False, "dma": None})

    def final_wait(self, eng, toks):
        waits = []
        for t in toks:
            self._need(eng, t, waits)
        self.ops[eng].append({"fn": None, "waits": waits, "sig": False, "dma": None})

    def emit(self, nc):
        engsem = {}
        signum = {}
        for e in self.ENGS:
            cnt = 0
            signum[e] = {}
            for i, op in enumerate(self.ops[e]):
                if op["sig"]:
                    signum[e][i] = cnt
                    cnt += 1
            nep = cnt // EPOCH + 1
            engsem[e] = [nc.alloc_semaphore(f"s_{e}_{j}") for j in range(nep)]
        dmasem = {k: nc.alloc_semaphore("d_" + k) for k in self.dma_cum}

        def run(ename, eng):
            for i, op in enumerate(self.ops[ename]):
                for key, val in op["waits"]:
                    if key.startswith("dma:"):
                        eng.wait_ge(dmasem[key[4:]], val)
                    else:
                        s = signum[key][val]
                        eng.wait_ge(engsem[key][s // EPOCH], s % EPOCH + 1)
                if op["fn"] is None:
                    continue
                ins = op["fn"](eng)
                if op["dma"] is not None:
                    ins.then_inc(dmasem[op["dma"]], 16)
                elif op["sig"]:
                    s = signum[ename][i]
                    ins.then_inc(engsem[ename][s // EPOCH], 1)

        with nc.Block() as block:
            @block.tensor
            def _(eng):
                run("pe", eng)

            @block.scalar
            def _(eng):
                run("act", eng)

            @block.vector
            def _(eng):
                run("dve", eng)

            @block.gpsimd
            def _(eng):
                run("pool", eng)

            @block.sync
            def _(eng):
                run("sp", eng)


class Alloc:
    def __init__(self, nc, lo, hi):
        self.nc, self.lo, self.hi, self.p, self.n = nc, lo, hi, lo, 0

    def t(self, shape, dt, name="t"):
        nb = 1
        for s in shape[1:]:
            nb *= s
        nb *= 2 if dt == BF16 else 4
        nb = (nb + 31) // 32 * 32
        off = self.p
        self.p += nb
        assert self.p <= self.hi, f"SBUF overflow {name} {self.p} > {self.hi}"
        self.n += 1
        return self.nc.alloc_sbuf_tensor_at(f"{name}{self.n}", list(shape), dt, offset=off)


def build(NT=16, debug=False, stage=9):
    nc = bass.Bass("TRN2", target_bir_lowering=False)
    SEQ = NT * 512
    P = Prog()

    def finish():
        fin = [("dma:" + k, v) for k, v in P.dma_cum.items()]
        P.final_wait("sp", fin)
        P.emit(nc)
        return nc

    def din(name, shape, dt=F32):
        return nc.dram_tensor(name, list(shape), dt, kind="ExternalInput").ap()

    x_d = din("x", [SEQ, D])
    ctx_d = din("ctx", [CTX, D])
    rows_d = din("rows", [128, NROW])
    cols_d = din("cols", [128, NCOL])
    cst_d = din("cst", [128, NCST])
    wada_d = din("w_ada", [D, 6 * D])
    wtm_d = din("w_tm", [D, 1536])
    wfm_d = din("w_fm", [D, 1600])
    wgate_d = din("w_gate", [96, 512])
    wout_d = din("w_out", [D, D])
    wup_d = din("w_up", [D, 2 * DFF])
    wdown_d = din("w_down", [DFF, D])
    out_d = nc.dram_tensor("out", [SEQ, D], F32, kind="ExternalOutput").ap()
    if debug:
        ob_d = nc.dram_tensor("ob_d", [SEQ, 512], F32, kind="ExternalOutput").ap()
        x1_d = nc.dram_tensor("x1_d", [SEQ, D], F32, kind="ExternalOutput").ap()
    else:
        ob_d = nc.dram_tensor("ob_d", [SEQ, 512], F32).ap()
        x1_d = nc.dram_tensor("x1_d", [SEQ, D], F32).ap()
    hx2_d = nc.dram_tensor("hx2_d", [8, 128, SEQ + 128], BF16).ap()
    wupb_d = nc.dram_tensor("wupb_d", [NFC // 2, 128, 8, 512], BF16).ap()

    pb = [nc.alloc_psum_tensor(f"pb{i}", [128, 512], F32) for i in range(8)]
    PB = [Buf(f"pb{i}") for i in range(8)]

    G = Alloc(nc, SB_LO, SB_HI)
    ident = G.t([128, 128], BF16, "ident")
    Lm = G.t([128, 4, 128], BF16, "Lm")
    maskT = G.t([128, 2, 512], BF16, "maskT")
    ncol = G.t([128, 1], BF16, "ncol")
    ones_r = G.t([1, 128], BF16, "ones_r")
    A1 = G.t([128, D], F32, "A1")
    cA1 = G.t([128, D], F32, "cA1")
    G1 = G.t([128, D], F32, "G1")
    A2 = G.t([128, D], F32, "A2")
    G2 = G.t([128, D], F32, "G2")
    ghead = G.t([128, 512], F32, "ghead")
    B1 = G.t([1, D], BF16, "B1")
    cB1 = G.t([1, D], BF16, "cB1")
    B2 = G.t([1, D], BF16, "B2")
    colp = G.t([128, NCOL], F32, "colp")
    Bc = G.t([128, 24], F32, "Bc")
    wgate = G.t([96, 512], BF16, "wgate")
    dsc = G.t([128, 12, 128], BF16, "dsc")
    S = [G.t([128, 2, 128], F32, "S") for _ in range(2)]
    Sb = [G.t([128, 2, 128], BF16, "Sb") for _ in range(2)]
    alT = G.t([96, 512], BF16, "alT")
    stat = G.t([128, 64], F32, "stat")
    fstat = G.t([128, 16], F32, "fstat")
    b_fst = [Buf(f"fst{i}") for i in range(8)]
    epsc = G.t([128, 1], F32, "epsc")
    g_const = Buf("consts")
    bS = [Buf("S0"), Buf("S1")]
    bSb = [Buf("Sb0"), Buf("Sb1")]
    OV = G.p

    Z = Alloc(nc, OV, SB_HI)
    cstf = Z.t([128, NCST], F32, "cstf")
    rowp = Z.t([128, NROW], F32, "rowp")
    wa = [Z.t([128, 8, D], F32, "wa") for _ in range(2)]
    modr = [Z.t([128, D], F32, "modr") for _ in range(2)]
    rep = Z.t([128, 16, 128], F32, "rep")
    onesf = Z.t([128, 128], F32, "onesf")
    scl = Z.t([128, 16], F32, "scl")
    wgf = Z.t([96, 512], F32, "wgf")
    zt = Z.t([128, 64], BF16, "zt")
    b_cst, b_row, b_col, b_wgf = Buf("cstf"), Buf("rowp"), Buf("colp"), Buf("wgf")
    b_wa = [Buf("wa0"), Buf("wa1")]
    b_modr = [Buf("modr0"), Buf("modr1")]
    b_rep, b_scl, b_zt = Buf("rep"), Buf("scl"), Buf("zt")

    P.add("sp", lambda e: e.dma_start(out=cstf[:], in_=cst_d[:, :]), writes=[b_cst], dma="ld0")
    P.add("sp", lambda e: e.dma_start(out=colp[:], in_=cols_d[:, :]), writes=[b_col], dma="ld1")
    P.add("sp", lambda e: e.dma_start(out=wgf[:], in_=wgate_d[:, :]), writes=[b_wgf], dma="ld2")
    P.add("sp", lambda e: e.dma_start(out=rowp[:, 0:4608], in_=rows_d[:, 0:4608]), writes=[b_row], dma="ld3", ndma=2)
    P.add("sp", lambda e: e.dma_start(out=rowp[:, 4608:NROW], in_=rows_d[:, 4608:NROW]), writes=[b_row], dma="ld3", ndma=0)

    b_wupb = Buf("wupb")
    wup_v = wup_d.rearrange("(k p) j -> p k j", p=128)
    for pc in range(NFC // 2):
        for ug in range(2):
            off = ug * DFF + pc * 256
            P.add("pool", lambda e, pc=pc, ug=ug, off=off: e.dma_start(out=wupb_d[pc, :, :, ug * 256:(ug + 1) * 256], in_=wup_v[:, :, off:off + 256]),
                  writes=[b_wupb], dma="wupc", ndma=1)

    P.add("dve", lambda e: e.tensor_copy(out=ident[:], in_=cstf[:, K_ID:K_ID + 128]), reads=[b_cst], writes=[g_const])
    for j, k0 in enumerate((K_LFI, K_LFR, K_LBI, K_LBR)):
        P.add("dve", lambda e, j=j, k0=k0: e.tensor_copy(out=Lm[:, j, :], in_=cstf[:, k0:k0 + 128]), reads=[b_cst], writes=[g_const])
    for d_, k0 in enumerate((K_MF, K_MB)):
        for h in range(4):
            P.add("dve", lambda e, d_=d_, k0=k0, h=h: e.tensor_copy(out=maskT[:, d_, h * 128:(h + 1) * 128], in_=cstf[:, k0:k0 + 128]),
                  reads=[b_cst], writes=[g_const])
    P.add("dve", lambda e: e.tensor_copy(out=ncol[:], in_=cstf[:, K_NC:K_NC + 1]), reads=[b_cst], writes=[g_const])
    P.add("dve", lambda e: e.memset(ones_r[:], 1.0), writes=[g_const])
    P.add("dve", lambda e: e.memset(epsc[:], EPS), writes=[g_const])
    P.add("dve", lambda e: e.memset(onesf[:], 1.0), writes=[b_rep])
    P.add("dve", lambda e: e.memset(zt[:], 0.0), writes=[b_zt])
    P.add("dve", lambda e: e.memset(alT[64:96, :], 1.0), writes=[g_const])
    P.add("dve", lambda e: e.tensor_copy(out=wgate[:], in_=wgf[:]), reads=[b_wgf], writes=[g_const])
    P.add("dve", lambda e: e.tensor_copy(out=ghead[:], in_=rowp[:, R_GH:R_GH + 512]), reads=[b_row], writes=[g_const])
    for d_ in range(2):
        P.add("dve", lambda e, d_=d_: e.memset(S[d_][:], 0.0), writes=[bS[d_]])
        P.add("dve", lambda e, d_=d_: e.memset(Sb[d_][:], 0.0), writes=[bSb[d_]])
    for t in range(3):
        for j in range(4):
            P.add("dve", lambda e, t=t, j=j: e.tensor_scalar(out=dsc[:, t * 4 + j, :], in0=ident[:], scalar1=colp[:, Q_WSC + t * 4 + j:Q_WSC + t * 4 + j + 1],
                                                              scalar2=None, op0=ALU.mult), reads=[g_const, b_col], writes=[g_const])
    b_hx2 = Buf("hx2_d")
    for k in range(8):
        P.add("sp", lambda e, k=k: e.dma_start(out=hx2_d[k, :, 0:64], in_=zt[:, 0:64]), reads=[b_zt], writes=[b_hx2], dma="zm", ndma=1)
        P.add("sp", lambda e, k=k: e.dma_start(out=hx2_d[k, :, SEQ + 64:SEQ + 128], in_=zt[:, 0:64]), reads=[b_zt], writes=[b_hx2], dma="zm", ndma=1)

    P.add("act", lambda e: e.activation(out=scl[:], in_=colp[:, Q_C:Q_C + 16], func=AF.Silu), reads=[b_col], writes=[b_scl])
    for j in range(16):
        P.add("dve", lambda e, j=j: e.tensor_scalar(out=rep[:, j, :], in0=onesf[:], scalar1=scl[:, j:j + 1], scalar2=None, op0=ALU.mult),
              reads=[b_scl, b_rep], writes=[b_rep])
    wada_v = wada_d.rearrange("(k p) j -> p k j", p=128)

    def mod_piece(m, v, slot):
        for hf in range(2):
            bank = 2 * slot + hf
            for k in range(8):
                P.add("pe", lambda e, k=k, hf=hf, bank=bank: e.matmul(pb[bank][:, :], lhsT=rep[:, v * 8 + k, :], rhs=wa[m % 2][:, k, hf * 512:(hf + 1) * 512],
                                                                     start=(k == 0), stop=(k == 7)),
                      reads=[b_rep, b_wa[m % 2]], writes=[PB[bank]])
            P.add("dve", lambda e, hf=hf, bank=bank: e.tensor_tensor(out=modr[slot][:, hf * 512:(hf + 1) * 512], in0=pb[bank][:, :],
                                                                      in1=rowp[:, R_BADA + m * D + hf * 512:R_BADA + m * D + (hf + 1) * 512], op=ALU.add),
                  reads=[PB[bank], b_row], writes=[b_modr[slot]])

    for m in range(6):
        for q in range(2):
            P.add("sp", lambda e, m=m, q=q: e.dma_start(out=wa[m % 2][:, q * 4:(q + 1) * 4, :], in_=wada_v[:, q * 4:(q + 1) * 4, m * D:(m + 1) * D]),
                  writes=[b_wa[m % 2]], dma=f"wa{m % 2}", ndma=1)
        mod_piece(m, 0, 0)
        if m < 2:
            mod_piece(m, 1, 1)
        if m == 0:
            P.add("act", lambda e: e.copy(out=B1[0:1, :], in_=modr[0][0:1, :]), reads=[b_modr[0]], writes=[g_const])
            P.add("act", lambda e: e.copy(out=cB1[0:1, :], in_=modr[1][0:1, :]), reads=[b_modr[1]], writes=[g_const])
        elif m == 1:
            P.add("dve", lambda e: e.scalar_tensor_tensor(out=A1[:], in0=modr[0][:], scalar=1.0, in1=rowp[:, R_GPM:R_GPM + D], op0=ALU.add, op1=ALU.mult),
                  reads=[b_modr[0], b_row], writes=[g_const])
            P.add("dve", lambda e: e.scalar_tensor_tensor(out=cA1[:], in0=modr[1][:], scalar=1.0, in1=rowp[:, R_GPM:R_GPM + D], op0=ALU.add, op1=ALU.mult),
                  reads=[b_modr[1], b_row], writes=[g_const])
        elif m == 2:
            P.add("dve", lambda e: e.tensor_tensor(out=G1[:], in0=modr[0][:], in1=rowp[:, R_GQM:R_GQM + D], op=ALU.mult), reads=[b_modr[0], b_row], writes=[g_const])
        elif m == 3:
            P.add("act", lambda e: e.copy(out=B2[0:1, :], in_=modr[0][0:1, :]), reads=[b_modr[0]], writes=[g_const])
        elif m == 4:
            P.add("dve", lambda e: e.scalar_tensor_tensor(out=A2[:], in0=modr[0][:], scalar=1.0, in1=rowp[:, R_GPF:R_GPF + D], op0=ALU.add, op1=ALU.mult),
                  reads=[b_modr[0], b_row], writes=[g_const])
        else:
            P.add("dve", lambda e: e.tensor_tensor(out=G2[:], in0=modr[0][:], in1=rowp[:, R_GQF:R_GQF + D], op=ALU.mult), reads=[b_modr[0], b_row], writes=[g_const])
    for v_, Br_ in enumerate((B1, cB1, B2)):
        for k in range(8):
            P.add("pe", lambda e, v_=v_, k=k, Br_=Br_: e.matmul(pb[4][:, v_ * 8 + k:v_ * 8 + k + 1], lhsT=Br_[0:1, k * 128:(k + 1) * 128], rhs=ones_r[0:1, 0:1],
                                                            start=True, stop=True), reads=[g_const], writes=[PB[4]])
    P.add("dve", lambda e: e.tensor_copy(out=Bc[:], in_=pb[4][:, 0:24]), reads=[PB[4]], writes=[g_const])
    P.barrier(skip=("wupc",))
    if stage == 0:
        return finish()

    Y = Alloc(nc, OV, SB_HI)
    wtm = Y.t([128, 8, 1536], BF16, "wtm")
    wfm = Y.t([128, 8, 1600], BF16, "wfm")
    wout = Y.t([128, 8, D], BF16, "wout")
    xt = [Y.t([128, 4, D], F32, "xt") for _ in range(2)]
    hxT = Y.t([128, 8, 512], BF16, "hxT")
    obl = [Y.t([128, 512], F32, "obl") for _ in range(2)]
    g_l = [Y.t([128, 256], BF16, "g_l") for _ in range(2)]
    g_eb = [Y.t([128, 256], F32, "g_eb") for _ in range(2)]
    g_enb = [Y.t([128, 256], F32, "g_enb") for _ in range(2)]
    g_er = [Y.t([128, 256], F32, "g_er") for _ in range(2)]
    g_e = g_er
    g_qz = [Y.t([128, 2, 256], BF16, "g_qz") for _ in range(2)]
    g_kt = [Y.t([128, 256], BF16, "g_kt") for _ in range(2)]
    kqs = [Y.t([128, 512], F32, "kqs") for _ in range(2)]
    g_et = [Y.t([128, 2], F32, "g_et") for _ in range(2)]
    g_kh = [Y.t([128, 256], BF16, "g_kh") for _ in range(2)]
    g_vb = [Y.t([128, 512], BF16, "g_vb") for _ in range(2)]
    g_T = [Y.t([128, 768], BF16, "g_T") for _ in range(2)]
    g_att = [Y.t([128, 512], BF16, "g_att") for _ in range(2)]
    obst = obl
    osum = Y.t([128, 512], F32, "osum")
    ghs4 = Y.t([128, 4, 512], BF16, "ghs4")
    b_ghs = [Buf(f"ghs{j}") for j in range(4)]
    yg = Y.t([128, 512], BF16, "yg")
    junk = Y.t([128, D], BF16, "junk")
    ybf = [Y.t([128, D], BF16, "ybf") for _ in range(2)]
    b_ybf = [Buf("ybf0"), Buf("ybf1")]
    yxT = Y.t([128, 8, 512], BF16, "yxT")
    sbT = [Y.t([128, 512], BF16, "sbT") for _ in range(2)]
    scT = [Y.t([128, 512], BF16, "scT") for _ in range(2)]
    zpad = Y.t([128, 4, 8, 66], BF16, "zpad")
    tmpf = Y.t([128, 512], F32, "tmpf")
    hx2s = [Y.t([128, 8, 128], BF16, "hx2s") for _ in range(2)]

    b_wtm, b_wfm, b_wout = Buf("wtm"), Buf("wfm"), Buf("wout")
    b_xt = [[Buf(f"xt{s}{j}") for j in range(4)] for s in range(2)]
    b_hxT = [Buf(f"hxT{j}") for j in range(4)]
    b_obl = [Buf("obl0"), Buf("obl1")]
    bg = {n: Buf(n) for n in ("osum", "sg", "ghs", "yg", "junk", "ybf", "tmpf", "alT", "stat")}
    bF = [{n: Buf(n + str(i)) for n in ("e", "l", "eb", "enb", "er", "qt", "kt", "qblk", "kqs")} for i in range(2)]
    bB = [{n: Buf(n + str(i)) for n in ("et", "kh", "vb", "T", "att")} for i in range(2)]
    b_obst = b_obl
    b_yxg = [Buf(f"yxg{j}") for j in range(4)]
    b_yxs = [Buf(f"yxs{j}") for j in range(4)]
    b_sbT = [Buf("sbT0"), Buf("sbT1")]
    b_scT = [Buf("scT0"), Buf("scT1")]
    b_zp = [Buf(f"zp{j}") for j in range(4)]
    b_hx2s = [Buf("hx2s0"), Buf("hx2s1")]
    b_ob_d = [Buf(f"ob_d{i}") for i in range(NT * 4)]
    b_x1_d = [Buf(f"x1_d{i}") for i in range(NT * 4)]
    b_hx2_d = [Buf(f"hx2_d{i}") for i in range(NT)]

    wtm_v = wtm_d.rearrange("(k p) j -> p k j", p=128)
    wfm_v = wfm_d.rearrange("(k p) j -> p k j", p=128)
    wout_v = wout_d.rearrange("(k p) j -> p k j", p=128)
    for k in range(8):
        P.add("pool", lambda e, k=k: e.dma_start(out=wtm[:, k, :], in_=wtm_v[:, k, :]), writes=[b_wtm], dma="wtm", ndma=1)
    for k in range(8):
        P.add("pool", lambda e, k=k: e.dma_start(out=wfm[:, k, :], in_=wfm_v[:, k, :]), writes=[b_wfm], dma="wfm", ndma=1)
    for k in range(8):
        P.add("pool", lambda e, k=k: e.dma_start(out=wout[:, k, :], in_=wout_v[:, k, :]), writes=[b_wout], dma="wout", ndma=1)
    P.add("dve", lambda e: e.memset(zpad[:], 0.0), writes=b_zp)
    for i_ in range(2):
        P.add("dve", lambda e, i_=i_: e.memset(g_qz[i_][:], 0.0), writes=[bF[i_]["qt"]])

    sc_i = [0]

    bst8 = [Buf(f"stat{i}") for i in range(8)]

    def stat_slot():
        k = sc_i[0] % 8
        sc_i[0] += 1
        return 8 * k, bst8[k]

    def rstd_from(ss_ap, nfeat, out_ap, rb, wb):
        P.add("act", lambda e: e.activation(out=out_ap, in_=ss_ap, func=AF.Ln, scale=1.0 / nfeat, bias=epsc[:, 0:1]), reads=rb + [g_const], writes=wb)
        P.add("act", lambda e: e.activation(out=out_ap, in_=out_ap, func=AF.Exp, scale=-0.5), reads=wb, writes=wb)

    fcnt = [0]

    def front_stages(src_ap, src_buf, Arow, Brow, dstT, dst_cols, dst_buf):
        n_ = fcnt[0]
        fcnt[0] += 1
        q_, par = n_ % 8, n_ % 2
        banks = (6, 7) if par == 0 else (2, 5)
        ss = fstat[:, 2 * q_:2 * q_ + 1]
        rs = fstat[:, 2 * q_ + 1:2 * q_ + 2]
        bst = b_fst[q_]
        yb = ybf[par]

        def fa():
            P.add("dve", lambda e: e.memset(fstat[:, 2 * q_:2 * q_ + 2], 0.0), writes=[bst])
            P.add("act", lambda e: e.activation(out=junk[:], in_=src_ap, func=AF.Square, accum_out=ss), reads=[src_buf, bst], writes=[bg["junk"], bst])
            rstd_from(ss, D, rs, [bst], [bst])

        def fb():
            P.add("dve", lambda e: e.scalar_tensor_tensor(out=yb[:], in0=src_ap, scalar=rs, in1=Arow[:], op0=ALU.mult, op1=ALU.mult),
                  reads=[src_buf, bst, g_const], writes=[b_ybf[par]])

        bv = {id(B1): 0, id(cB1): 8, id(B2): 16}[id(Brow)]

        def fc():
            for k in range(8):
                bank = banks[k // 4]
                o = pb[bank][:, (k % 4) * 128:(k % 4 + 1) * 128]
                P.add("pe", lambda e, o=o, k=k: e.matmul(o, lhsT=yb[:, k * 128:(k + 1) * 128], rhs=ident[:], start=True, stop=True),
                      reads=[b_ybf[par], g_const], writes=[PB[bank]])

        def fd():
            for k in range(8):
                bank = banks[k // 4]
                src = pb[bank][:, (k % 4) * 128:(k % 4 + 1) * 128]
                if par == 0:
                    P.add("act", lambda e, k=k, src=src: e.activation(out=dstT[:, k, dst_cols], in_=src, func=AF.Identity, bias=Bc[:, bv + k:bv + k + 1]),
                          reads=[PB[bank], g_const], writes=[dst_buf])
                else:
                    P.add("dve", lambda e, k=k, src=src: e.tensor_scalar(out=dstT[:, k, dst_cols], in0=src, scalar1=Bc[:, bv + k:bv + k + 1], scalar2=None, op0=ALU.add),
                          reads=[PB[bank], g_const], writes=[dst_buf])

        return [fa, fb, fc, fd]

    def front(src_ap, src_buf, Arow, Brow, dstT, dst_cols, dst_buf, banks=None):
        for st_ in front_stages(src_ap, src_buf, Arow, Brow, dstT, dst_cols, dst_buf):
            st_()

    pre = {}

    def prefront(slot):
        if fcnt[0] % 2:
            fcnt[0] += 1
        S_ = [front_stages(xt[slot][:, j, :], b_xt[slot][j], A1, B1, hxT, slice(j * 128, (j + 1) * 128), b_hxT[j]) for j in range(4)]
        for j in range(4):
            S_[j][0]()
        pre[slot] = S_

    def front4(slot):
        if slot not in pre:
            prefront(slot)
        S_ = pre.pop(slot)
        for (j, k) in ((0, 1), (1, 1), (0, 2), (0, 3), (2, 1), (1, 2), (1, 3), (3, 1), (2, 2), (2, 3), (3, 2), (3, 3)):
            S_[j][k]()

    def proj_tm(cols, grp, bank, hbuf):
        for k in range(8):
            P.add("pe", lambda e, k=k: e.matmul(pb[bank][:, :], lhsT=hxT[:, k, cols], rhs=wtm[:, k, grp * 512:(grp + 1) * 512], start=(k == 0), stop=(k == 7)),
                  reads=[hbuf, b_wtm], writes=[PB[bank]])

    def proj_fm(c0, M, bank, ncols=512):
        for k in range(8):
            P.add("pe", lambda e, k=k: e.matmul(pb[bank][0:M, 0:ncols], lhsT=wfm[:, k, c0:c0 + M], rhs=hxT[:, k, 0:ncols], start=(k == 0), stop=(k == 7)),
                  reads=b_hxT + [b_wfm], writes=[PB[bank]])

    def gate_lowrank(ncols=512):
        proj_fm(0, 64, 7, ncols)
        P.add("act", lambda e: e.copy(out=alT[0:64, 0:ncols], in_=pb[7][0:64, 0:ncols]), reads=[PB[7]], writes=[bg["alT"]])

    def gla_stages(d, sl, bs, cols, hbuf, full):
        bK, bV, bA = 3 * sl, 3 * sl + 1, 3 * sl + 2
        F, Bb = bF[sl], bB[bs]
        e_, l_, eb_, enb_, er_, qz_, kt_, kq_ = g_e[sl], g_l[sl], g_eb[sl], g_enb[sl], g_er[sl], g_qz[sl], g_kt[sl], kqs[sl]
        et_, kh_, vb_, T_, att_ = g_et[bs], g_kh[bs], g_vb[bs], g_T[bs], g_att[bs]
        Tps = pb[bV].bitcast(BF16)

        def s1():
            proj_tm(cols, 0, bK, hbuf)
            proj_tm(cols, 1, bV, hbuf)
            P.add("pe", lambda e: e.matmul(pb[bA][:, 0:256], lhsT=alT[0:96, cols], rhs=wgate[:, d * 256:(d + 1) * 256], start=True, stop=True),
                  reads=[bg["alT"], g_const], writes=[PB[bA]])

        def s2():
            P.add("act", lambda e: e.activation(out=e_[:], in_=pb[bA][:, 0:256], func=AF.Exp, scale=-1.0), reads=[PB[bA]], writes=[F["er"]])
            P.add("act", lambda e: e.activation(out=l_[:], in_=e_[:], func=AF.Ln, bias=1.0), reads=[F["er"]], writes=[F["l"]])
            P.add("act", lambda e: e.copy(out=kq_[:], in_=pb[bK][:, :]), reads=[PB[bK]], writes=[F["kqs"]])
            P.add("act", lambda e: e.copy(out=vb_[:], in_=pb[bV][:, :]), reads=[PB[bV]], writes=[Bb["vb"]])

        def s3():
            if full:
                P.add("pe", lambda e: e.matmul(pb[bK][:, 0:256], lhsT=Lm[:, 2 * d, :], rhs=l_[:], start=True, stop=True), reads=[F["l"], g_const], writes=[PB[bK]])
            P.add("pe", lambda e: e.matmul(pb[bK][:, 256:512], lhsT=Lm[:, 2 * d + 1, :], rhs=l_[:], start=True, stop=True), reads=[F["l"], g_const], writes=[PB[bK]])
            for p_ in range(2):
                P.add("pe", lambda e, p_=p_: e.matmul(pb[bA][:, 256 + p_:257 + p_], lhsT=l_[:, p_ * 128:(p_ + 1) * 128], rhs=ncol[:, 0:1], start=True, stop=True),
                      reads=[F["l"], g_const], writes=[PB[bA]])

        def s4():
            if full:
                P.add("act", lambda e: e.activation(out=eb_[:], in_=pb[bK][:, 0:256], func=AF.Exp), reads=[PB[bK]], writes=[F["eb"]])
                P.add("act", lambda e: e.activation(out=enb_[:], in_=pb[bK][:, 0:256], func=AF.Exp, scale=-1.0), reads=[PB[bK]], writes=[F["enb"]])
            P.add("act", lambda e: e.activation(out=er_[:], in_=pb[bK][:, 256:512], func=AF.Exp), reads=[PB[bK]], writes=[F["er"]])
            P.add("act", lambda e: e.activation(out=et_[:], in_=pb[bA][:, 256:258], func=AF.Exp), reads=[PB[bA]], writes=[Bb["et"]])

        def s5():
            if full:
                kq3 = kq_[:, 256:512].rearrange("p (a b) -> p a b", a=2)
                eb3 = eb_[:].rearrange("p (a b) -> p a b", a=2)
                for w in range(2):
                    P.add("dve", lambda e, w=w: e.scalar_tensor_tensor(out=qz_[:, :, w * 192:w * 192 + 64], in0=kq3[:, :, w * 64:(w + 1) * 64], scalar=0.125,
                                                                       in1=eb3[:, :, w * 64:(w + 1) * 64], op0=ALU.mult, op1=ALU.mult),
                          reads=[F["kqs"], F["eb"]], writes=[F["qt"]])
                P.add("dve", lambda e: e.tensor_tensor(out=kt_[:], in0=kq_[:, 0:256], in1=enb_[:], op=ALU.mult), reads=[F["kqs"], F["enb"]], writes=[F["kt"]])
            P.add("dve", lambda e: e.tensor_tensor(out=kh_[:], in0=kq_[:, 0:256], in1=er_[:], op=ALU.mult), reads=[F["kqs"], F["er"]], writes=[Bb["kh"]])

        def s6():
            if not full:
                return
            for j in range(4):
                P.add("pe", lambda e, j=j: e.transpose(Tps[:, j * 128:(j + 1) * 128], qz_[:, j // 2, (j % 2) * 128:(j % 2 + 1) * 128], ident[:]),
                      reads=[F["qt"], g_const], writes=[PB[bV]])
            for j in range(2):
                P.add("pe", lambda e, j=j: e.transpose(Tps[:, 512 + j * 128:512 + (j + 1) * 128], kt_[:, j * 128:(j + 1) * 128], ident[:]),
                      reads=[F["kt"], g_const], writes=[PB[bV]])

        def s7():
            if not full:
                return
            P.add("dve", lambda e: e.tensor_copy(out=T_[:], in_=Tps[:, 0:768]), reads=[PB[bV]], writes=[Bb["T"]])

        def s8():
            if not full:
                return
            for pr in range(2):
                P.add("pe", lambda e, pr=pr: e.matmul(pb[bA][:, pr * 256:(pr + 1) * 256], lhsT=T_[:, 512 + pr * 128:512 + (pr + 1) * 128], rhs=T_[:, pr * 256:(pr + 1) * 256],
                                                      start=True, stop=True), reads=[Bb["T"]], writes=[PB[bA]])

        def s9():
            if not full:
                return
            P.add("dve", lambda e: e.tensor_tensor(out=att_[:], in0=pb[bA][:, :], in1=maskT[:, d, :], op=ALU.mult), reads=[PB[bA], g_const], writes=[Bb["att"]])

        return [s1, s2, s3, s4, s5, s6, s7, s8, s9]

    def gla_back(d, bs, full):
        Bb = bB[bs]
        et_, kh_, vb_, T_, att_ = g_et[bs], g_kh[bs], g_vb[bs], g_T[bs], g_att[bs]
        if full:
            for h in range(4):
                pr = h // 2
                P.add("pe", lambda e, h=h, pr=pr: e.matmul(pb[6][:, h * 128:(h + 1) * 128], lhsT=T_[:, h * 128:(h + 1) * 128], rhs=Sb[d][:, pr, :],
                                                           start=True, stop=False), reads=[Bb["T"], bSb[d]], writes=[PB[6]])
                P.add("pe", lambda e, h=h: e.matmul(pb[6][:, h * 128:(h + 1) * 128], lhsT=att_[:, h * 128:(h + 1) * 128], rhs=vb_[:, h * 128:(h + 1) * 128],
                                                    start=False, stop=True), reads=[Bb["att"], Bb["vb"]], writes=[PB[6]])
        for pr in range(2):
            P.add("pe", lambda e, pr=pr: e.matmul(pb[7][:, pr * 256:(pr + 1) * 256], lhsT=kh_[:, pr * 128:(pr + 1) * 128], rhs=vb_[:, pr * 256:(pr + 1) * 256],
                                                  start=True, stop=True), reads=[Bb["kh"], Bb["vb"]], writes=[PB[7]])
        for pr in range(2):
            for w in range(2):
                rsl = slice(w * 64, (w + 1) * 64)
                c0 = pr * 256 + w * 128
                P.add("dve", lambda e, pr=pr, rsl=rsl, c0=c0: e.scalar_tensor_tensor(out=S[d][rsl, pr, :], in0=S[d][rsl, pr, :], scalar=et_[rsl, pr:pr + 1],
                                                                                      in1=pb[7][rsl, c0:c0 + 128], op0=ALU.mult, op1=ALU.add),
                      reads=[bS[d], Bb["et"], PB[7]], writes=[bS[d]])
        P.add("act", lambda e: e.copy(out=Sb[d][:], in_=S[d][:]), reads=[bS[d]], writes=[bSb[d]])

    def gla_tile(d, order, after_back, mid=None):
        st = {}
        for n_, j in enumerate(order):
            st[j] = gla_stages(d, n_ % 2, n_ % 2, slice(j * 128, (j + 1) * 128), b_hxT[j], True)
        c0_, c1_, c2_, c3_ = order
        for k in range(9):
            st[c0_][k]()
            st[c1_][k]()
        if mid is None:
            st[c2_][0]()
            st[c3_][0]()
        for n_, j in enumerate((c0_, c1_)):
            gla_back(d, n_, True)
            if mid is not None:
                mid(j)
            after_back(j)
        for k in range(0 if mid is not None else 1, 9):
            st[c2_][k]()
            st[c3_][k]()
        for n_, j in enumerate((c2_, c3_)):
            gla_back(d, n_, True)
            if mid is not None:
                mid(j)
            after_back(j)

    ctx_v = ctx_d.rearrange("(j p) f -> p j f", p=128)
    for j in range(2):
        P.add("sp", lambda e, j=j: e.dma_start(out=xt[0][:, j, :], in_=ctx_v[:, j, :]), writes=[b_xt[0][j]], dma=f"x0{j}")
    for j in range(2):
        front(xt[0][:, j, :], b_xt[0][j], cA1, cB1, hxT, slice(j * 128, (j + 1) * 128), b_hxT[j])
    gate_lowrank(256)
    for d_, order in ((0, (0, 1)), (1, (1, 0))):
        for j in order:
            cols = slice(j * 128, (j + 1) * 128)
            for st_ in gla_stages(d_, 0, 0, cols, b_hxT[j], False):
                st_()
            gla_back(d_, 0, False)

    if stage == 1:
        return finish()
    x_v = x_d.rearrange("(n j p) f -> n p j f", p=128, j=4)
    ob_v = ob_d.rearrange("(n p) f -> n p f", p=128)
    x1_v = x1_d.rearrange("(n p) f -> n p f", p=128)

    def load_x(i, slot):
        for j in range(4):
            P.add("sp", lambda e, j=j: e.dma_start(out=xt[slot][:, j, :], in_=x_v[i, :, j, :]), writes=[b_xt[slot][j]], dma=f"x{slot}{j}")

    tseq = [i for i in range(NT - 1, -1, -1)] + [i for i in range(NT)]
    tcount = 0
    load_x(tseq[0], 0)
    for i in range(NT - 1, -1, -1):
        slot = tcount % 2
        tcount += 1
        if tcount < len(tseq) and stage >= 3 or tcount < NT:
            load_x(tseq[tcount], tcount % 2)
        front4(slot)
        gate_lowrank()
        nxt_slot = tcount % 2 if tcount < len(tseq) else None

        def after_a(j, i=i, nxt_slot=nxt_slot):
            n = i * 4 + j
            P.add("act", lambda e, j=j: e.copy(out=obst[j % 2][:], in_=pb[6][:, :]), reads=[PB[6]], writes=[b_obst[j % 2]])
            P.add("sp", lambda e, j=j, n=n: e.dma_start(out=ob_v[n, :, :], in_=obst[j % 2][:]), reads=[b_obst[j % 2]], writes=[b_ob_d[n]], dma=f"obst{j % 2}")
            if j == 1 and nxt_slot is not None and stage >= 3:
                prefront(nxt_slot)

        gla_tile(1, (3, 2, 1, 0), after_a)

    if stage == 2:
        return finish()
    hx2_v = hx2_d.rearrange("k p t -> p k t")
    for i in range(NT):
        slot = tcount % 2
        tcount += 1
        if tcount < len(tseq):
            load_x(tseq[tcount], tcount % 2)
        front4(slot)
        gate_lowrank()
        def sc_step(j):
            s2 = j % 2
            proj_fm(64 + j * 128, 128, 0)
            P.add("act", lambda e, s2=s2: e.copy(out=sbT[s2][:], in_=pb[0][:, :]), reads=[PB[0]], writes=[b_sbT[s2]])
            proj_fm(64 + 512 + j * 128, 128, 1)
            P.add("act", lambda e, s2=s2: e.copy(out=scT[s2][:], in_=pb[1][:, :]), reads=[PB[1]], writes=[b_scT[s2]])
            proj_fm(64 + 1024 + j * 128, 128, 2)
            P.add("dve", lambda e, j=j, s2=s2: e.tensor_tensor(out=zpad[:, j, :, 1:65], in0=pb[2][:, :].rearrange("p (r c) -> p r c", r=8),
                                                               in1=scT[s2][:].rearrange("p (r c) -> p r c", r=8), op=ALU.mult),
                  reads=[PB[2], b_scT[s2]], writes=[b_zp[j]])
            for t in range(3):
                P.add("pe", lambda e, j=j, t=t: e.matmul(pb[3][:, :], lhsT=dsc[:, t * 4 + j, :], rhs=zpad[:, j, :, t:t + 64], start=(t == 0), stop=(t == 2)),
                      reads=[b_zp[j], g_const], writes=[PB[3]])
            P.add("dve", lambda e, j=j, s2=s2: e.scalar_tensor_tensor(out=yxT[:, 4 + j, :], in0=pb[3][:, :], scalar=colp[:, Q_BSC + j:Q_BSC + j + 1], in1=sbT[s2][:],
                                                                      op0=ALU.add, op1=ALU.mult), reads=[PB[3], b_sbT[s2], g_const], writes=[b_yxs[j]])
        sgt = (tmpf, osum)
        sgb = (bg["tmpf"], bg["osum"])
        for j in range(4):
            cols = slice(j * 128, (j + 1) * 128)
            proj_tm(cols, 2, j, b_hxT[j])
            P.add("act", lambda e, j=j: e.activation(out=sgt[j % 2][:], in_=pb[j][:, :], func=AF.Silu), reads=[PB[j]], writes=[sgb[j % 2]])
            P.add("dve", lambda e, j=j: e.tensor_tensor(out=ghs4[:, j, :], in0=sgt[j % 2][:], in1=ghead[:], op=ALU.mult), reads=[sgb[j % 2], g_const], writes=[b_ghs[j]])

        def load_ob(j, i=i):
            n = i * 4 + j
            P.add("sp", lambda e, j=j, n=n: e.dma_start(out=obl[j % 2][:], in_=ob_v[n, :, :]), reads=[b_ob_d[n]], writes=[b_obl[j % 2]], dma=f"obl{j % 2}")

        load_ob(0)
        load_ob(1)

        def after_b(j, i=i):
            cols = slice(j * 128, (j + 1) * 128)
            P.add("dve", lambda e, j=j: e.tensor_tensor(out=osum[:], in0=pb[6][:, :], in1=obl[j % 2][:], op=ALU.add), reads=[PB[6], b_obl[j % 2]], writes=[bg["osum"]])
            if j < 2:
                load_ob(j + 2)
            c0, bs_ = stat_slot()
            P.add("dve", lambda e, c0=c0: e.memset(stat[:, c0:c0 + 8], 0.0), writes=[bs_])
            for h in range(4):
                P.add("act", lambda e, h=h, c0=c0: e.activation(out=junk[:, h * 128:(h + 1) * 128], in_=osum[:, h * 128:(h + 1) * 128], func=AF.Square,
                                                                accum_out=stat[:, c0 + h:c0 + h + 1]), reads=[bg["osum"], bs_], writes=[bg["junk"], bs_])
            rstd_from(stat[:, c0:c0 + 4], 128, stat[:, c0 + 4:c0 + 8], [bs_], [bs_])
            for h in range(4):
                P.add("dve", lambda e, h=h, c0=c0, j=j: e.scalar_tensor_tensor(out=yg[:, h * 128:(h + 1) * 128], in0=osum[:, h * 128:(h + 1) * 128],
                                                                               scalar=stat[:, c0 + 4 + h:c0 + 5 + h], in1=ghs4[:, j, h * 128:(h + 1) * 128],
                                                                               op0=ALU.mult, op1=ALU.mult), reads=[bg["osum"], bs_, b_ghs[j]], writes=[bg["yg"]])
            Tps = pb[7].bitcast(BF16)
            for h in range(4):
                P.add("pe", lambda e, h=h: e.transpose(Tps[:, h * 128:(h + 1) * 128], yg[:, h * 128:(h + 1) * 128], ident[:]), reads=[bg["yg"], g_const], writes=[PB[7]])
            P.add("act", lambda e, cols=cols: e.copy(out=yxT[:, 0:4, cols], in_=Tps[:, 0:512].rearrange("p (k t) -> p k t", k=4)), reads=[PB[7]], writes=[b_yxg[j]])

        gla_tile(0, (0, 1, 2, 3), after_b, mid=sc_step)
        def oproj_stages(j, i=i, slot=slot):
            cols = slice(j * 128, (j + 1) * 128)
            n = i * 4 + j
            ob_ = 3 * (j % 2)
            c0, bs_ = stat_slot()
            tmp2 = (tmpf, osum)
            tmpb = (bg["tmpf"], bg["osum"])

            def o1():
                for hf in range(2):
                    for k in range(8):
                        rb = [b_yxg[j]] if k < 4 else [b_yxs[k - 4]]
                        P.add("pe", lambda e, k=k, hf=hf: e.matmul(pb[ob_ + hf][:, :], lhsT=yxT[:, k, cols], rhs=wout[:, k, hf * 512:(hf + 1) * 512],
                                                                   start=(k == 0), stop=(k == 7)), reads=rb + [b_wout], writes=[PB[ob_ + hf]])

            def o2():
                P.add("dve", lambda e: e.memset(stat[:, c0:c0 + 4], 0.0), writes=[bs_])
                for hf in range(2):
                    P.add("act", lambda e, hf=hf: e.activation(out=junk[:, hf * 512:(hf + 1) * 512], in_=pb[ob_ + hf][:, :], func=AF.Square,
                                                               accum_out=stat[:, c0 + hf:c0 + hf + 1]), reads=[PB[ob_ + hf], bs_], writes=[bg["junk"], bs_])
                P.add("dve", lambda e: e.tensor_tensor(out=stat[:, c0 + 2:c0 + 3], in0=stat[:, c0:c0 + 1], in1=stat[:, c0 + 1:c0 + 2], op=ALU.add),
                      reads=[bs_], writes=[bs_])
                rstd_from(stat[:, c0 + 2:c0 + 3], D, stat[:, c0 + 3:c0 + 4], [bs_], [bs_])

            def o3():
                for hf in range(2):
                    hs = slice(hf * 512, (hf + 1) * 512)
                    P.add("dve", lambda e, hf=hf, hs=hs: e.scalar_tensor_tensor(out=tmp2[hf][:], in0=pb[ob_ + hf][:, :], scalar=stat[:, c0 + 3:c0 + 4], in1=G1[:, hs],
                                                                               op0=ALU.mult, op1=ALU.mult), reads=[PB[ob_ + hf], bs_, g_const], writes=[tmpb[hf]])
                    P.add("dve", lambda e, hf=hf, hs=hs: e.tensor_tensor(out=xt[slot][:, j, hs], in0=xt[slot][:, j, hs], in1=tmp2[hf][:], op=ALU.add),
                          reads=[tmpb[hf], b_xt[slot][j]], writes=[b_xt[slot][j]])
                P.add("sp", lambda e: e.dma_start(out=x1_v[n, :, :], in_=xt[slot][:, j, :]), reads=[b_xt[slot][j]], writes=[b_x1_d[n]], dma=f"xs{slot}{j}")

            fs = front_stages(xt[slot][:, j, :], b_xt[slot][j], A2, B2, hx2s[j % 2], slice(0, 128), b_hx2s[j % 2])

            def o7():
                fs[3]()
                P.add("sp", lambda e: e.dma_start(out=hx2_v[:, :, 64 + i * 512 + j * 128:64 + i * 512 + (j + 1) * 128], in_=hx2s[j % 2][:]),
                      reads=[b_hx2s[j % 2]], writes=[b_hx2_d[i]], dma=f"hx2s{j % 2}")

            return [o1, o2, o3, fs[0], fs[1], fs[2], o7]

        if fcnt[0] % 2:
            fcnt[0] += 1
        OS = [oproj_stages(j) for j in range(4)]
        order = [(0, 0), (1, 0), (0, 1), (0, 2), (0, 3), (1, 1), (2, 0), (1, 2), (0, 4), (1, 3), (0, 5), (2, 1), (3, 0), (0, 6), (2, 2), (1, 4), (2, 3), (1, 5),
                 (3, 1), (1, 6), (3, 2), (2, 4), (3, 3), (2, 5), (2, 6), (3, 4), (3, 5), (3, 6)]
        assert sorted(order) == [(j, k) for j in range(4) for k in range(7)]
        for (j, k) in order:
            OS[j][k]()
        if tcount < len(tseq):
            prefront(tcount % 2)
    P.barrier()
    if stage == 3:
        return finish()

    Zc = Alloc(nc, OV, SB_HI)
    wdn = Zc.t([128, NFC, D], BF16, "wdn")
    wupS = [Zc.t([128, 8, 512], BF16, "wupS") for _ in range(2)]
    hw = [Zc.t([128, 8, 640], BF16, "hw") for _ in range(2)]
    x1t = [Zc.t([128, 4, D], F32, "x1t") for _ in range(2)]
    hT = Zc.t([128, NFC, 512], BF16, "hT")
    upad = [Zc.t([128, 10, 66], BF16, "upad") for _ in range(2)]
    gts = [Zc.t([128, 512], BF16, "gts") for _ in range(2)]
    sil = [Zc.t([128, 512], BF16, "sil") for _ in range(2)]
    dg = [Zc.t([128, 9, 128], BF16, "dg") for _ in range(2)]
    tmpc = Zc.t([128, 512], F32, "tmpc")
    junkc = Zc.t([128, 512], BF16, "junkc")
    b_wdn = Buf("wdn")
    b_wupS = [Buf("wupS0"), Buf("wupS1")]
    b_hw = [Buf("hw0"), Buf("hw1")]
    b_x1t = [[Buf(f"x1t{s}{j}") for j in range(4)] for s in range(2)]
    b_hT = [Buf(f"hT{c}") for c in range(NFC)]
    b_upad = [Buf("upad0"), Buf("upad1")]
    b_upc = [Buf("upc0"), Buf("upc1")]
    ucar = Zc.t([128, NFC, 2, 64], BF16, "ucar")
    b_ucar = [Buf(f"ucar{c}") for c in range(NFC)]
    b_gts = [Buf("gts0"), Buf("gts1")]
    b_sil = [Buf("sil0"), Buf("sil1")]
    b_dg = [Buf("dg0"), Buf("dg1")]
    b_tmpc, b_junkc, b_statc = Buf("tmpc"), Buf("junkc"), Buf("statc")
    b_out = []

    wdn_v = wdown_d.rearrange("(c p) j -> p c j", p=128)
    for c in range(NFC):
        P.add("pool", lambda e, c=c: e.dma_start(out=wdn[:, c, :], in_=wdn_v[:, c, :]), writes=[b_wdn], dma="wdn", ndma=1)
    for s in range(2):
        P.add("dve", lambda e, s=s: e.memset(upad[s][:], 0.0), writes=[b_upad[s]])
    out_v = out_d.rearrange("(n p) f -> n p f", p=128)
    x1_v4 = x1_d.rearrange("(n j p) f -> n p j f", p=128, j=4)
    def load_c(i):
        slot = i % 2
        P.add("sp", lambda e: e.dma_start(out=hw[slot][:], in_=hx2_v[:, :, i * 512:i * 512 + 640]),
              reads=b_hx2_d[max(0, i - 1):i + 2] + [b_hx2], writes=[b_hw[slot]], dma=f"hw{slot}")
        for j in range(4):
            P.add("sp", lambda e, j=j: e.dma_start(out=x1t[slot][:, j, :], in_=x1_v4[i, :, j, :]), reads=[b_x1_d[i * 4 + j]],
                  writes=[b_x1t[slot][j]], dma=f"x1t{slot}{j}")

    NPC = NFC // 2

    def load_piece(g):
        ps = g % 2
        pc = g % NPC
        P.add("sp", lambda e: e.dma_start(out=wupS[ps][:], in_=wupb_d[pc, :, :, :]), reads=[b_wupb], writes=[b_wupS[ps]], dma=f"wupS{ps}")

    def conv_part(c):
        s2 = c % 2
        for t in range(9):
            dr, dc = t // 3, t % 3
            P.add("pe", lambda e, t=t, dr=dr, dc=dc: e.matmul(pb[6 + s2][:, :], lhsT=dg[s2][:, t, :], rhs=upad[s2][:, dr:dr + 8, dc:dc + 64], start=(t == 0), stop=(t == 8)),
                  reads=[b_dg[s2], b_upad[s2], b_upc[s2]], writes=[PB[6 + s2]])
        P.add("act", lambda e: e.activation(out=sil[s2][:], in_=pb[6 + s2][:, :], func=AF.Silu, bias=colp[:, Q_BCF + c:Q_BCF + c + 1]),
              reads=[PB[6 + s2], g_const], writes=[b_sil[s2]])
        P.add("dve", lambda e: e.tensor_tensor(out=hT[:, c, :], in0=sil[s2][:], in1=gts[s2][:], op=ALU.mult), reads=[b_sil[s2], b_gts[s2]], writes=[b_hT[c]])

    load_c(0)
    load_piece(0)
    piece = 0
    for i in range(NT):
        slot = i % 2
        for c in range(NFC):
            s2 = c % 2
            if c % 2 == 0:
                ps = piece % 2
                piece += 1
                if piece < NT * NPC:
                    load_piece(piece)
                if c == 2 and i + 1 < NT:
                    load_c(i + 1)
            wo = (c % 2) * 128
            for t in range(9):
                P.add("dve", lambda e, c=c, t=t, s2=s2: e.tensor_scalar(out=dg[s2][:, t, :], in0=ident[:], scalar1=colp[:, Q_WCF + c * 9 + t:Q_WCF + c * 9 + t + 1],
                                                                         scalar2=None, op0=ALU.mult), reads=[g_const], writes=[b_dg[s2]])
            ucol = 0 if i == 0 else 128
            for k in range(8):
                P.add("pe", lambda e, k=k, s2=s2, ps=ps, wo=wo, slot=slot, ucol=ucol: e.matmul(pb[s2][:, :], lhsT=wupS[ps][:, k, wo:wo + 128], rhs=hw[slot][:, k, ucol:ucol + 512],
                                                                                           start=(k == 0), stop=(k == 7)),
                      reads=[b_wupS[ps], b_hw[slot]], writes=[PB[s2]])
            if i == 0:
                for k in range(8):
                    P.add("pe", lambda e, k=k, s2=s2, ps=ps, wo=wo, slot=slot: e.matmul(pb[2 + s2][:, 0:128], lhsT=wupS[ps][:, k, wo:wo + 128], rhs=hw[slot][:, k, 512:640], start=(k == 0), stop=(k == 7)),
                          reads=[b_wupS[ps], b_hw[slot]], writes=[PB[2 + s2]])
            for k in range(8):
                P.add("pe", lambda e, k=k, s2=s2, ps=ps, wo=wo, slot=slot: e.matmul(pb[4 + s2][:, :], lhsT=wupS[ps][:, k, 256 + wo:256 + wo + 128], rhs=hw[slot][:, k, 64:576], start=(k == 0), stop=(k == 7)),
                      reads=[b_wupS[ps], b_hw[slot]], writes=[PB[4 + s2]])
            if i == 0:
                P.add("act", lambda e, s2=s2: e.copy(out=upad[s2][:, 0:8, 1:65], in_=pb[s2][:, :].rearrange("p (r c) -> p r c", r=8)), reads=[PB[s2]], writes=[b_upc[s2], b_upad[s2]])
                P.add("act", lambda e, s2=s2: e.copy(out=upad[s2][:, 8:10, 1:65], in_=pb[2 + s2][:, 0:128].rearrange("p (r c) -> p r c", r=2)), reads=[PB[2 + s2]], writes=[b_upad[s2]])
            else:
                P.add("pool", lambda e, c=c, s2=s2: e.tensor_copy(out=upad[s2][:, 0:2, 1:65], in_=ucar[:, c, :, :]), reads=[b_ucar[c]], writes=[b_upc[s2]])
                P.add("act", lambda e, s2=s2: e.copy(out=upad[s2][:, 2:10, 1:65], in_=pb[s2][:, :].rearrange("p (r c) -> p r c", r=8)), reads=[PB[s2]], writes=[b_upad[s2]])
            if i < NT - 1:
                P.add("pool", lambda e, c=c, s2=s2: e.tensor_copy(out=ucar[:, c, :, :], in_=upad[s2][:, 8:10, 1:65]), reads=[b_upad[s2]], writes=[b_ucar[c]])
            P.add("dve", lambda e, s2=s2: e.tensor_copy(out=gts[s2][:], in_=pb[4 + s2][:, :]), reads=[PB[4 + s2]], writes=[b_gts[s2]])
            if c >= 1:
                conv_part(c - 1)
        conv_part(NFC - 1)
        for j in range(4):
            cols = slice(j * 128, (j + 1) * 128)
            n = i * 4 + j
            for hf in range(2):
                bank = (2 * j + hf) % 8
                for c in range(NFC):
                    P.add("pe", lambda e, c=c, hf=hf, bank=bank, cols=cols: e.matmul(pb[bank][:, :], lhsT=hT[:, c, cols], rhs=wdn[:, c, hf * 512:(hf + 1) * 512],
                                                                                      start=(c == 0), stop=(c == NFC - 1)), reads=[b_hT[c], b_wdn], writes=[PB[bank]])
            c0, _unused = stat_slot()
            P.add("dve", lambda e, c0=c0: e.memset(stat[:, c0:c0 + 4], 0.0), writes=[b_statc])
            for hf in range(2):
                bank = (2 * j + hf) % 8
                P.add("act", lambda e, hf=hf, bank=bank, c0=c0: e.activation(out=junkc[:], in_=pb[bank][:, :], func=AF.Square, accum_out=stat[:, c0 + hf:c0 + hf + 1]),
                      reads=[PB[bank], b_statc], writes=[b_junkc, b_statc])
            P.add("dve", lambda e, c0=c0: e.tensor_tensor(out=stat[:, c0 + 2:c0 + 3], in0=stat[:, c0:c0 + 1], in1=stat[:, c0 + 1:c0 + 2], op=ALU.add),
                  reads=[b_statc], writes=[b_statc])
            rstd_from(stat[:, c0 + 2:c0 + 3], D, stat[:, c0 + 3:c0 + 4], [b_statc], [b_statc])
            for hf in range(2):
                bank = (2 * j + hf) % 8
                hs = slice(hf * 512, (hf + 1) * 512)
                P.add("dve", lambda e, hf=hf, bank=bank, hs=hs, c0=c0: e.scalar_tensor_tensor(out=tmpc[:], in0=pb[bank][:, :], scalar=stat[:, c0 + 3:c0 + 4], in1=G2[:, hs],
                                                                                             op0=ALU.mult, op1=ALU.mult), reads=[PB[bank], b_statc, g_const], writes=[b_tmpc])
                P.add("dve", lambda e, j=j, hs=hs, slot=slot: e.tensor_tensor(out=x1t[slot][:, j, hs], in0=x1t[slot][:, j, hs], in1=tmpc[:], op=ALU.add),
                      reads=[b_tmpc, b_x1t[slot][j]], writes=[b_x1t[slot][j]])
            bo = Buf(f"out{n}")
            tok = P.add("sp", lambda e, j=j, n=n, slot=slot: e.dma_start(out=out_v[n, :, :], in_=x1t[slot][:, j, :]), reads=[b_x1t[slot][j]], writes=[bo], dma=f"ot{slot}{j}")
            b_out.append(tok)
    last = {}
    for t in b_out:
        last[t[0]] = max(last.get(t[0], 0), t[1])
    fin = list(last.items())
    if debug:
        for k, v in P.dma_cum.items():
            fin.append(("dma:" + k, v))
    P.final_wait("sp", fin)
    P.emit(nc)
    return nc


def _consts():
    c = np.zeros((128, NCST), np.float32)
    m = np.arange(128)[:, None]
    t = np.arange(128)[None, :]
    c[:, K_ID:K_ID + 128] = (m == t)
    c[:, K_LFI:K_LFI + 128] = (m <= t) * (-1.0 / 16)
    c[:, K_LFR:K_LFR + 128] = (m > t) * (-1.0 / 16)
    c[:, K_LBI:K_LBI + 128] = (m >= t) * (-1.0 / 16)
    c[:, K_LBR:K_LBR + 128] = (m < t) * (-1.0 / 16)
    c[:, K_MF:K_MF + 128] = (m <= t)
    c[:, K_MB:K_MB + 128] = (m > t)
    c[:, K_NC] = -1.0 / 16
    return c


def _colmaj(v, n):
    return np.ascontiguousarray(np.asarray(v, np.float32).reshape(n, 128).T)


def make_in_maps(inputs, NT=16, ncores=8):
    f = lambda a: np.asarray(a, np.float32)
    w_in = f(inputs["w_in"])[0]
    w_tm = np.ascontiguousarray(np.concatenate([w_in[:, C_K:C_V], w_in[:, C_Q:C_OG], w_in[:, C_V:C_AF], w_in[:, C_OG:C_SB]], axis=1))
    gate = np.zeros((D, 64), np.float32)
    gate[:, 0:16] = w_in[:, C_AF:C_AB]
    gate[:, 32:48] = w_in[:, C_AB:C_Q]
    w_fm = np.ascontiguousarray(np.concatenate([gate, w_in[:, C_SB:C_SC], w_in[:, C_SC:C_SX], w_in[:, C_SX:]], axis=1))
    wg = np.zeros((96, 512), np.float32)
    wg[0:16, 0:256] = f(inputs["w_af"])[0]
    wg[32:48, 256:512] = f(inputs["w_ab"])[0]
    wg[64, 0:256] = f(inputs["b_af"])[0]
    wg[64, 256:512] = f(inputs["b_ab"])[0]
    rows1 = np.concatenate([f(inputs["g_pre_mix"])[0], f(inputs["g_post_mix"])[0], f(inputs["g_pre_ffn"])[0], f(inputs["g_post_ffn"])[0],
                            np.tile(f(inputs["g_head"])[0], 4), f(inputs["b_ada"])[0]])
    rows = np.ascontiguousarray(np.broadcast_to(rows1[None, :], (128, NROW)))
    w_sc = f(inputs["w_sc"])[0]
    w_cf = f(inputs["w_cf"])[0].reshape(9, DFF)
    cols_common = np.concatenate([
        _colmaj(f(inputs["b_sc"])[0], 4), _colmaj(f(inputs["b_cf"])[0], NFC),
        np.concatenate([_colmaj(w_sc[t], 4) for t in range(3)], axis=1),
        np.stack([_colmaj(w_cf[t], NFC) for t in range(9)], axis=2).reshape(128, NFC * 9),
    ], axis=1)
    cst = _consts()
    maps = []
    x = inputs["x"]
    for b in range(ncores):
        cols = np.ascontiguousarray(np.concatenate([_colmaj(f(inputs["c"])[b], 8), _colmaj(f(inputs["c_ctx"]), 8), cols_common], axis=1))
        maps.append({
            "x": np.ascontiguousarray(f(x[b])[:NT * 512]), "ctx": np.ascontiguousarray(f(inputs["ctx"][b])),
            "rows": rows, "cols": cols, "cst": cst,
            "w_ada": f(inputs["w_ada"])[0], "w_tm": w_tm, "w_fm": w_fm, "w_gate": wg,
            "w_out": f(inputs["w_out"])[0], "w_up": f(inputs["w_up"])[0], "w_down": f(inputs["w_down"])[0],
        })
    return maps


_NC_CACHE = {}


def kernel(**inputs):
    NT = 16
    if NT not in _NC_CACHE:
        _NC_CACHE[NT] = build(NT)
    nc = _NC_CACHE[NT]
    maps = make_in_maps(inputs, NT, 8)
    res = run_bass_kernel_spmd(nc, maps, core_ids=list(range(8)))
    return np.stack([np.asarray(r["out"], np.float32).reshape(NT * 512, D) for r in res.results], axis=0)
```
